# Optimizing a Trainium2 kernel written in Bass

```python
import math
import jax, jax.numpy as jnp
from jax import lax
import numpy as np

D_MODEL = 2048
BATCH = 4
SEQ = 8192
DEPTH = 4
DEC_BATCH = 8
DEC_SEQ = 4096
PAST_LEN = 128

N_META = 16
CHUNK = 64
ROPE_THETA = 10000.0
EPS = 1e-6
NEG = -1e30

A_HEADS = 16
A_KV_HEADS = 2
A_GROUP = A_HEADS // A_KV_HEADS
A_HEAD_DIM = 64
WINDOW = 128
A_BLOCK = 128
A_Q = A_HEADS * A_HEAD_DIM
A_KV = A_KV_HEADS * A_HEAD_DIM

B_HEADS = 4
B_DK = 128
B_DV = 256
B_RANK = 16
B_TAU = 16.0
B_QK = B_HEADS * B_DK
B_V = B_HEADS * B_DV

C_HEADS = 16
C_DK = 128
C_DV = 128
CONV_K = 5
C_QK = C_HEADS * C_DK
C_V = C_HEADS * C_DV
C_CONV = 2 * C_QK + C_V

EVEN_SPLITS = (A_Q, A_KV, A_KV, A_Q, B_QK, B_QK, B_V, B_V, B_RANK, B_RANK)
ODD_SPLITS = (C_CONV, C_V, C_HEADS, C_HEADS, C_HEADS, C_HEADS)
EVEN_IN = sum(EVEN_SPLITS)
ODD_IN = sum(ODD_SPLITS)
EVEN_OUT = A_Q + B_V
N_EVEN = (DEPTH + 1) // 2
N_ODD = DEPTH // 2

kernel_name = 'hybrid_bidir_swa_gla_gdn_encoder'


def _split(z, sizes):
    idx = np.cumsum(sizes)[:-1].tolist()
    return jnp.split(z, idx, axis=-1)


def _rmsnorm(x, g):
    xf = x.astype(jnp.float32)
    y = xf * lax.rsqrt(jnp.mean(xf * xf, axis=-1, keepdims=True) + EPS)
    return (y * g.astype(jnp.float32)).astype(x.dtype)


def _l2norm(x):
    xf = x.astype(jnp.float32)
    return xf * lax.rsqrt(jnp.sum(xf * xf, axis=-1, keepdims=True) + EPS)


def _rope_tables(length):
    inv = 1.0 / (ROPE_THETA ** (jnp.arange(0, A_HEAD_DIM, 2, dtype=jnp.float32) / A_HEAD_DIM))
    ang = jnp.arange(length, dtype=jnp.float32)[:, None] * inv[None]
    return jnp.cos(ang), jnp.sin(ang)


def _rope(x, cos, sin):
    x1, x2 = jnp.split(x.astype(jnp.float32), 2, axis=-1)
    c = cos[None, :, None]
    s = sin[None, :, None]
    return jnp.concatenate([x1 * c - x2 * s, x2 * c + x1 * s], axis=-1).astype(x.dtype)


def _sink_attend(q, k, v, mask, sink):
    s = jnp.einsum('bihgd,bjhd->bhgij', q, k, preferred_element_type=jnp.float32) * (q.shape[-1] ** -0.5)
    s = jnp.where(mask, s, NEG)
    sk = jnp.broadcast_to(sink.astype(jnp.float32)[None, :, :, None, None], s.shape[:-1] + (1,))
    p = jax.nn.softmax(jnp.concatenate([s, sk], axis=-1), axis=-1)[..., :-1]
    return jnp.einsum('bhgij,bjhd->bihgd', p, v.astype(jnp.float32))


def _window_attention(q, k, v, sink):
    bsz, L, _, hd = q.shape
    s = L - N_META
    nb = s // A_BLOCK
    sink = sink.reshape(A_KV_HEADS, A_GROUP)
    q = q.reshape(bsz, L, A_KV_HEADS, A_GROUP, hd)
    qm, qr = q[:, :N_META], q[:, N_META:]
    km, kr = k[:, :N_META], k[:, N_META:]
    vm, vr = v[:, :N_META], v[:, N_META:]
    pos_k = jnp.arange(N_META + A_BLOCK)
    pos_q = jnp.arange(N_META)
    mask_m = (pos_k[None] - pos_q[:, None]) <= WINDOW
    om = _sink_attend(qm, jnp.concatenate([km, kr[:, :A_BLOCK]], axis=1),
                      jnp.concatenate([vm, vr[:, :A_BLOCK]], axis=1), mask_m, sink)

    def nbr(t):
        tb = jnp.pad(t.reshape(bsz, nb, A_BLOCK, A_KV_HEADS, hd), ((0, 0), (1, 1), (0, 0), (0, 0), (0, 0)))
        return jnp.concatenate([tb[:, :-2], tb[:, 1:-1], tb[:, 2:]], axis=2)
    kn = jnp.moveaxis(nbr(kr), 1, 0)
    vn = jnp.moveaxis(nbr(vr), 1, 0)
    qb = jnp.moveaxis(qr.reshape(bsz, nb, A_BLOCK, A_KV_HEADS, A_GROUP, hd), 1, 0)
    qi = jnp.arange(A_BLOCK)[:, None]
    kj = jnp.arange(3 * A_BLOCK)[None] - A_BLOCK
    band = jnp.abs(kj - qi) <= WINDOW
    kabs = jnp.arange(nb)[:, None, None] * A_BLOCK + kj[None]
    mask_r = band[None] & (kabs >= 0) & (kabs < s)
    mask_r = jnp.concatenate([jnp.ones((nb, A_BLOCK, N_META), dtype=bool), mask_r], axis=-1)

    def block(args):
        qn, kn_n, vn_n, m_n = args
        return _sink_attend(qn, jnp.concatenate([km, kn_n], axis=1),
                            jnp.concatenate([vm, vn_n], axis=1), m_n, sink)
    orr = lax.map(block, (qb, kn, vn, mask_r))
    orr = jnp.moveaxis(orr, 0, 1).reshape(bsz, s, A_KV_HEADS, A_GROUP, hd)
    return jnp.concatenate([om, orr], axis=1).reshape(bsz, L, A_HEADS, hd)


def _to_chunks(t, front):
    pad = CHUNK - N_META
    widths = [(0, 0)] * t.ndim
    widths[1] = (pad, 0) if front else (0, pad)
    t = jnp.pad(t, widths)
    bsz, lp = t.shape[:2]
    t = t.reshape((bsz, lp // CHUNK, CHUNK) + t.shape[2:])
    return jnp.moveaxis(t, 3, 1)


def _from_chunks(o, front):
    o = jnp.moveaxis(o, 1, 3)
    bsz, n, c = o.shape[:3]
    o = o.reshape((bsz, n * c) + o.shape[3:])
    pad = CHUNK - N_META
    return o[:, pad:] if front else o[:, :-pad]


def _bidirectional(chunk_fn, shared, fwd_extra, bwd_extra):
    o_f = _from_chunks(chunk_fn(*[_to_chunks(a, True) for a in shared + fwd_extra]), True)
    rev = [a[:, ::-1] for a in shared + bwd_extra]
    o_b = _from_chunks(chunk_fn(*[_to_chunks(a, False) for a in rev]), False)[:, ::-1]
    return o_f + o_b


def _gla_chunks(q, k, v, g):
    q, k, v, g = (t.astype(jnp.float32) for t in (q, k, v, g))
    bsz, nh, _, c, dk = q.shape
    dv = v.shape[-1]
    b = jnp.cumsum(g, axis=3)
    qd = q * jnp.exp(b)
    kd = k * jnp.exp(-b)
    causal = jnp.tril(jnp.ones((c, c), dtype=bool))
    att = jnp.where(causal, jnp.einsum('bhnid,bhnjd->bhnij', qd, kd), 0.0)
    o_intra = jnp.einsum('bhnij,bhnje->bhnie', att, v)
    b_last = b[:, :, :, -1:]
    d_state = jnp.einsum('bhncd,bhnce->bhnde', k * jnp.exp(b_last - b), v)
    decay = jnp.exp(b_last[:, :, :, 0])

    def step(S, inp):
        ds_n, a_n = inp
        return a_n[..., None] * S + ds_n, S
    s0 = jnp.zeros((bsz, nh, dk, dv), jnp.float32)
    _, s_prev = lax.scan(step, s0, (jnp.moveaxis(d_state, 2, 0), jnp.moveaxis(decay, 2, 0)))
    s_prev = jnp.moveaxis(s_prev, 0, 2)
    return o_intra + jnp.einsum('bhncd,bhnde->bhnce', qd, s_prev)


def _delta_chunks(q, k, v, beta, g):
    q, k, v, beta, g = (t.astype(jnp.float32) for t in (q, k, v, beta, g))
    bsz, nh, _, c, dk = q.shape
    dv = v.shape[-1]
    gam = jnp.cumsum(g, axis=-1)
    idx = jnp.arange(c)
    incl = idx[:, None] >= idx[None]
    strict = idx[:, None] > idx[None]
    dec = jnp.exp(jnp.where(incl, gam[..., :, None] - gam[..., None, :], -jnp.inf))
    kb = k * beta[..., None]
    m = jnp.einsum('bhnid,bhnjd->bhnij', kb, k) * jnp.where(strict, dec, 0.0)
    t_mat = m + jnp.eye(c, dtype=jnp.float32)
    u = lax.linalg.triangular_solve(t_mat, v * beta[..., None], left_side=True, lower=True, unit_diagonal=True)
    w = lax.linalg.triangular_solve(t_mat, kb * jnp.exp(gam)[..., None], left_side=True, lower=True, unit_diagonal=True)
    a_qk = jnp.einsum('bhnid,bhnjd->bhnij', q, k) * dec
    qg = q * jnp.exp(gam)[..., None]
    g_last = gam[..., -1]
    kg = k * jnp.exp(g_last[..., None] - gam)[..., None]

    def step(S, inp):
        u_n, w_n, a_n, qg_n, kg_n, gl_n = inp
        v_new = u_n - jnp.einsum('bhcd,bhde->bhce', w_n, S)
        o_n = jnp.einsum('bhcd,bhde->bhce', qg_n, S) + jnp.einsum('bhij,bhje->bhie', a_n, v_new)
        S = jnp.exp(gl_n)[..., None, None] * S + jnp.einsum('bhcd,bhce->bhde', kg_n, v_new)
        return S, o_n
    s0 = jnp.zeros((bsz, nh, dk, dv), jnp.float32)
    xs = tuple(jnp.moveaxis(t, 2, 0) for t in (u, w, a_qk, qg, kg, g_last))
    _, o = lax.scan(step, s0, xs)
    return jnp.moveaxis(o, 0, 2)


def _even_mixer(u, w_in, sink, gu_f, gb_f, gu_b, gb_b, head_norm, w_out, cos, sin):
    bsz, L, _ = u.shape
    z = jnp.einsum('bld,de->ble', u, w_in)
    qa, ka, va, za, qb, kb, vb, zb, lf, lb = _split(z, EVEN_SPLITS)
    qa = _rope(qa.reshape(bsz, L, A_HEADS, A_HEAD_DIM), cos, sin)
    ka = _rope(ka.reshape(bsz, L, A_KV_HEADS, A_HEAD_DIM), cos, sin)
    va = va.reshape(bsz, L, A_KV_HEADS, A_HEAD_DIM)
    oa = _window_attention(qa, ka, va, sink).reshape(bsz, L, A_Q) * jax.nn.silu(za.astype(jnp.float32))

    def gla_gate(lr, up, bias):
        pre = jnp.einsum('blr,re->ble', lr, up) + bias
        return (jax.nn.log_sigmoid(pre.astype(jnp.float32)) / B_TAU).reshape(bsz, L, B_HEADS, B_DK)
    qb = qb.reshape(bsz, L, B_HEADS, B_DK) * (B_DK ** -0.5)
    kb = kb.reshape(bsz, L, B_HEADS, B_DK)
    vb = vb.reshape(bsz, L, B_HEADS, B_DV)
    ob = _bidirectional(_gla_chunks, (qb, kb, vb), (gla_gate(lf, gu_f, gb_f),), (gla_gate(lb, gu_b, gb_b),))
    ob = _rmsnorm(ob, head_norm).reshape(bsz, L, B_V) * jax.nn.silu(zb.astype(jnp.float32))
    o = jnp.concatenate([oa, ob], axis=-1).astype(u.dtype)
    return jnp.einsum('ble,ed->bld', o, w_out)


def _odd_mixer(u, w_in, conv_w, alog_f, dtb_f, alog_b, dtb_b, head_norm, w_out):
    bsz, L, _ = u.shape
    z = jnp.einsum('bld,de->ble', u, w_in)
    xc, zc, af, bf, ab, bb = _split(z, ODD_SPLITS)
    xc = jax.nn.silu(lax.conv_general_dilated(
        xc, conv_w[:, None, :].astype(xc.dtype), (1,), ((CONV_K // 2, CONV_K // 2),),
        dimension_numbers=('NWC', 'WIO', 'NWC'), feature_group_count=C_CONV))
    qc, kc, vc = _split(xc, (C_QK, C_QK, C_V))
    q = _l2norm(qc.reshape(bsz, L, C_HEADS, C_DK)) * (C_DK ** -0.5)
    k = _l2norm(kc.reshape(bsz, L, C_HEADS, C_DK))
    v = vc.reshape(bsz, L, C_HEADS, C_DV)

    def decay(a, alog, dtb):
        return -jnp.exp(alog.astype(jnp.float32)) * jax.nn.softplus(a.astype(jnp.float32) + dtb.astype(jnp.float32))
    o = _bidirectional(_delta_chunks, (q, k, v),
                       (jax.nn.sigmoid(bf.astype(jnp.float32)), decay(af, alog_f, dtb_f)),
                       (jax.nn.sigmoid(bb.astype(jnp.float32)), decay(ab, alog_b, dtb_b)))
    o = _rmsnorm(o, head_norm).reshape(bsz, L, C_V) * jax.nn.silu(zc.astype(jnp.float32))
    return jnp.einsum('ble,ed->bld', o.astype(u.dtype), w_out)


def _trunk(x, meta_tokens, norm_even, w_in_even, a_sink, b_gate_up_fwd, b_gate_bias_fwd,
           b_gate_up_bwd, b_gate_bias_bwd, b_head_norm, w_out_even, norm_odd, w_in_odd, c_conv,
           c_a_log_fwd, c_dt_bias_fwd, c_a_log_bwd, c_dt_bias_bwd, c_head_norm, w_out_odd, norm_final):
    bsz, s, d = x.shape
    meta = jnp.broadcast_to(meta_tokens.astype(x.dtype)[None], (bsz, N_META, d))
    h = jnp.concatenate([meta, x], axis=1)
    cos, sin = _rope_tables(N_META + s)
    for layer in range(DEPTH):
        j = layer // 2
        if layer % 2 == 0:
            h = h + _even_mixer(_rmsnorm(h, norm_even[j]), w_in_even[j], a_sink[j], b_gate_up_fwd[j],
                                b_gate_bias_fwd[j], b_gate_up_bwd[j], b_gate_bias_bwd[j], b_head_norm[j],
                                w_out_even[j], cos, sin)
        else:
            h = h + _odd_mixer(_rmsnorm(h, norm_odd[j]), w_in_odd[j], c_conv[j], c_a_log_fwd[j],
                               c_dt_bias_fwd[j], c_a_log_bwd[j], c_dt_bias_bwd[j], c_head_norm[j], w_out_odd[j])
    return _rmsnorm(h, norm_final)[:, N_META:]


def setup_inputs(seed: int = 0) -> dict:
    key = jax.random.key(seed)
    ks = jax.random.split(key, 24)
    f32 = jnp.float32

    def nrm(k, shape, scale):
        return scale * jax.random.normal(k, shape, f32)

    def dt_bias(k, shape):
        dt = jnp.exp(jax.random.uniform(k, shape, f32, math.log(1e-3), math.log(1e-1)))
        return dt + jnp.log(-jnp.expm1(-dt))

    def a_log(k, shape):
        return jnp.log(jax.random.uniform(k, shape, f32, 1.0, 16.0))

    return {
        'x_prompt': nrm(ks[0], (BATCH, SEQ, D_MODEL), 1.0),
        'x_sample': nrm(ks[1], (DEC_BATCH, DEC_SEQ, D_MODEL), 1.0),
        'meta_tokens': nrm(ks[2], (N_META, D_MODEL), 1.0),
        'norm_even': 1.0 + nrm(ks[3], (N_EVEN, D_MODEL), 0.02),
        'w_in_even': nrm(ks[4], (N_EVEN, D_MODEL, EVEN_IN), D_MODEL ** -0.5),
        'a_sink': nrm(ks[5], (N_EVEN, A_HEADS), 1.0),
        'b_gate_up_fwd': nrm(ks[6], (N_EVEN, B_RANK, B_QK), B_RANK ** -0.5),
        'b_gate_bias_fwd': nrm(ks[7], (N_EVEN, B_QK), 0.1),
        'b_gate_up_bwd': nrm(ks[8], (N_EVEN, B_RANK, B_QK), B_RANK ** -0.5),
        'b_gate_bias_bwd': nrm(ks[9], (N_EVEN, B_QK), 0.1),
        'b_head_norm': 1.0 + nrm(ks[10], (N_EVEN, B_DV), 0.02),
        'w_out_even': nrm(ks[11], (N_EVEN, EVEN_OUT, D_MODEL), EVEN_OUT ** -0.5),
        'norm_odd': 1.0 + nrm(ks[12], (N_ODD, D_MODEL), 0.02),
        'w_in_odd': nrm(ks[13], (N_ODD, D_MODEL, ODD_IN), D_MODEL ** -0.5),
        'c_conv': nrm(ks[14], (N_ODD, CONV_K, C_CONV), CONV_K ** -0.5),
        'c_a_log_fwd': a_log(ks[15], (N_ODD, C_HEADS)),
        'c_dt_bias_fwd': dt_bias(ks[16], (N_ODD, C_HEADS)),
        'c_a_log_bwd': a_log(ks[17], (N_ODD, C_HEADS)),
        'c_dt_bias_bwd': dt_bias(ks[18], (N_ODD, C_HEADS)),
        'c_head_norm': 1.0 + nrm(ks[19], (N_ODD, C_DV), 0.02),
        'w_out_odd': nrm(ks[20], (N_ODD, C_V, D_MODEL), C_V ** -0.5),
        'norm_final': 1.0 + nrm(ks[21], (D_MODEL,), 0.02),
    }


def reference(x_prompt, x_sample, meta_tokens, norm_even, w_in_even, a_sink, b_gate_up_fwd, b_gate_bias_fwd,
              b_gate_up_bwd, b_gate_bias_bwd, b_head_norm, w_out_even, norm_odd, w_in_odd, c_conv,
              c_a_log_fwd, c_dt_bias_fwd, c_a_log_bwd, c_dt_bias_bwd, c_head_norm, w_out_odd, norm_final):
    y_prompt = _trunk(x_prompt, meta_tokens, norm_even, w_in_even, a_sink, b_gate_up_fwd, b_gate_bias_fwd,
                      b_gate_up_bwd, b_gate_bias_bwd, b_head_norm, w_out_even, norm_odd, w_in_odd, c_conv,
                      c_a_log_fwd, c_dt_bias_fwd, c_a_log_bwd, c_dt_bias_bwd, c_head_norm, w_out_odd, norm_final)
    y_sample = _trunk(x_sample, meta_tokens, norm_even, w_in_even, a_sink, b_gate_up_fwd, b_gate_bias_fwd,
                      b_gate_up_bwd, b_gate_bias_bwd, b_head_norm, w_out_even, norm_odd, w_in_odd, c_conv,
                      c_a_log_fwd, c_dt_bias_fwd, c_a_log_bwd, c_dt_bias_bwd, c_head_norm, w_out_odd, norm_final)
    return (y_prompt, y_sample)
```

```python
import numpy as np
from contextlib import ExitStack
import concourse.bass as bass
import concourse.mybir as mybir
from concourse.bass_utils import run_bass_kernel_spmd

F32 = mybir.dt.float32
BF16 = mybir.dt.bfloat16
AF = mybir.ActivationFunctionType
ALU = mybir.AluOpType
AX = mybir.AxisListType

D = 2048
EPS = 1e-6
N_META = 16
EVEN_IN = 5408
ODD_IN = 8256
G = 6
NDMA = 48


class Buf:
    __slots__ = ("t", "w", "r", "ex")

    def __init__(self, t, ex=False):
        self.t = t
        self.w = None
        self.r = []
        self.ex = ex


class Ring:
    def __init__(self, bufs):
        self.b = bufs
        self.i = 0

    def next(self):
        b = self.b[self.i % len(self.b)]
        self.i += 1
        return b


class K:
    def __init__(self, nc, es):
        self.nc = nc
        self.es = es
        self.eng = {"pe": nc.tensor, "dve": nc.vector, "act": nc.scalar, "pool": nc.gpsimd, "sp": nc.sync}
        self.sem = {n: es.enter_context(nc.semaphore("s_" + n)) for n in ["pe", "dve", "act", "pool"]}
        self.cnt = {n: 0 for n in self.sem}
        self.waited = {}
        self.dslots = [[es.enter_context(nc.semaphore("d%d" % i)), 0] for i in range(NDMA)]
        self.ndma = 0
        self.nins = 0
        import os
        self.limit = int(os.environ.get("KLIMIT", "0")) or None
        self.nop_ = 0
        self.dbgops = set(int(x) for x in os.environ.get("KDBG", "").split(",") if x)

    def _wait(self, on, deps):
        e = self.eng[on]
        for d in deps:
            if d is None:
                continue
            key, val = d
            if key == on and on == "pe":
                continue
            if self.waited.get((on, key), 0) >= val:
                continue
            sem = self.sem[key] if isinstance(key, str) else self.dslots[key][0]
            if self.nop_ in self.dbgops:
                print("DBGWAIT op", self.nop_, "on", on, "waits", key, val, "cnt", dict(self.cnt))
            e.wait_ge(sem, val)
            self.nins += 1
            self.waited[(on, key)] = val

    def _deps(self, R, W):
        deps = []
        for b in R:
            deps.append(b.w)
            if b.ex:
                deps.extend(b.r)
        for b in W:
            deps.append(b.w)
            deps.extend(b.r)
        return deps

    def _commit(self, tok, R, W):
        for b in R:
            b.r = [t for t in b.r if t[0] != tok[0]] + [tok]
        for b in W:
            b.w = tok
            b.r = []

    def op(self, on, fn, R=(), W=()):
        self.nop_ += 1
        if self.limit is not None and self.nop_ > self.limit:
            return None
        self._wait(on, self._deps(R, W))
        ins = fn(self.eng[on])
        self.cnt[on] += 1
        ins.then_inc(self.sem[on], 1)
        self.nins += 1
        tok = (on, self.cnt[on])
        self._commit(tok, R, W)
        return tok

    def dma(self, on, out, in_, R=(), W=(), **kw):
        self.nop_ += 1
        if self.limit is not None and self.nop_ > self.limit and not str(getattr(out.tensor, "name", "")).startswith("dbg_"):
            return None
        i = self.ndma % NDMA
        self.ndma += 1
        slot = self.dslots[i]
        self._wait(on, self._deps(R, W) + [(i, slot[1])])
        slot[1] += 16
        self.eng[on].dma_start(out=out, in_=in_, **kw).then_inc(slot[0], 16)
        self.nins += 1
        tok = (i, slot[1])
        self._commit(tok, R, W)
        return tok

    def barrier(self):
        deps = [(n, c) for n, c in self.cnt.items() if c > 0]
        deps += [(i, s[1]) for i, s in enumerate(self.dslots) if s[1] > 0]
        for on in self.eng:
            self._wait(on, deps)


class _Stop(Exception):
    pass


def build(NT_SEG, debug=False, stop_at=None):
    NTILE = 2 * (NT_SEG + 1)
    T = NTILE * 128
    assert NTILE % G == 0
    NSUP = NTILE // G
    M0, M1 = 0, NT_SEG + 1
    LA, LB = NT_SEG, NT_SEG + 2
    assert LA // G == LB // G
    NREAL = 2 * NT_SEG

    nc = bass.Bass("TRN2", target_bir_lowering=False)

    def din(name, shape, dt=F32):
        return nc.dram_tensor(name, list(shape), dt, kind="ExternalInput").ap()

    def dscr(name, shape, dt):
        t = nc.dram_tensor(name, list(shape), dt, kind="Internal").ap()
        scr_all[name] = t
        return t

    scr_all = {}

    phase_no = [0]

    def phase_done():
        phase_no[0] += 1
        if stop_at is not None and phase_no[0] >= stop_at:
            stopped[0] = True
        return stopped[0]

    stopped = [False]

    hin = din("hin", [T, D])
    cosr = din("cosr", [T, 32])
    sinr = din("sinr", [T, 32])
    linkc_d = din("linkc", [128, 1])
    vmask_d = din("vmask", [128, 2])
    ident_d = din("ident", [128, 128])
    tri_d = din("tri", [5, 128, 128])
    norm_even = din("norm_even", [2, D])
    w_in_even = din("w_in_even", [2, D, EVEN_IN])
    a_sink = din("a_sink", [2, 16])
    gu_f = din("b_gate_up_fwd", [2, 16, 512])
    gb_f = din("b_gate_bias_fwd", [2, 512])
    gu_b = din("b_gate_up_bwd", [2, 16, 512])
    gb_b = din("b_gate_bias_bwd", [2, 512])
    b_hnorm = din("b_head_norm", [2, 256])
    w_out_even = din("w_out_even", [2, D, D])
    norm_odd = din("norm_odd", [2, D])
    w_in_odd = din("w_in_odd", [2, D, ODD_IN])
    c_conv = din("c_conv", [2, 5, 6144])
    alog_f = din("c_a_log_fwd", [2, 16])
    dtb_f = din("c_dt_bias_fwd", [2, 16])
    alog_b = din("c_a_log_bwd", [2, 16])
    dtb_b = din("c_dt_bias_bwd", [2, 16])
    c_hnorm = din("c_head_norm", [2, 128])
    w_out_odd = din("w_out_odd", [2, D, D])
    norm_final = din("norm_final", [1, D])
    y = nc.dram_tensor("y", [NREAL * 128, D], F32, kind="ExternalOutput").ap()

    hs = dscr("hs", [T, D], F32)
    zt = dscr("zt", [T, EVEN_IN], F32)
    xcT = dscr("xcT", [6144, T], F32)
    qTs = dscr("qTs", [NTILE, 128, 16, 128], BF16)
    kTs = dscr("kTs", [NTILE, 128, 16, 128], BF16)
    ktok = dscr("ktok", [T, D], BF16)
    vtok = dscr("vtok", [T, D], BF16)
    ofs = dscr("ofs", [T, D], F32)
    ogs = dscr("ogs", [T, D], BF16)
    wie = dscr("wie", [2, D, EVEN_IN], BF16)
    woe = dscr("woe", [2, D, D], BF16)
    wio = dscr("wio", [2, D, ODD_IN], BF16)
    woo = dscr("woo", [2, D, D], BF16)

    es0 = ExitStack()
    with es0:
        k = K(nc, es0)

        uid = [0]

        def uname(name):
            uid[0] += 1
            return "%s_u%d" % (name, uid[0])

        def sb(es, name, shape, dt):
            return Buf(es.enter_context(nc.sbuf_tensor(uname("sb_" + name), list(shape), dt)))

        def psb(es, name, shape, dt=F32):
            return Buf(es.enter_context(nc.psum_tensor(uname("ps_" + name), list(shape), dt)), ex=True)

        identb = sb(es0, "identb", [128, 128], BF16)
        tri = sb(es0, "tri", [128, 5, 128], F32)
        onesf = sb(es0, "onesf", [128, 128], F32)
        trib = sb(es0, "trib", [128, 5, 128], BF16)
        onesb = sb(es0, "onesb", [128, 128], BF16)
        linkc = sb(es0, "linkc", [128, 1], F32)
        vmask = sb(es0, "vmask", [128, 2], F32)
        amask = sb(es0, "amask", [128, 5, 128], BF16)
        k.dma("pool", identb.t[:], ident_d, W=[identb])
        k.dma("sp", tri.t[:], tri_d.rearrange("a p c -> p a c"), W=[tri])
        k.dma("sp", linkc.t[:], linkc_d, W=[linkc])
        k.dma("sp", vmask.t[:], vmask_d, W=[vmask])
        k.op("dve", lambda e: e.memset(onesf.t[:], 1.0), W=[onesf])
        k.op("dve", lambda e: e.memset(onesb.t[:], 1.0), W=[onesb])
        k.op("dve", lambda e: e.tensor_copy(trib.t[:], tri.t[:]), R=[tri], W=[trib])
        U_LE, U_LT, U_GE, U_GT, BDM = 0, 1, 2, 3, 4
        k.op("dve", lambda e: e.tensor_copy(amask.t[:, 0, :], tri.t[:, U_GE, :]), R=[tri], W=[amask])
        k.op("dve", lambda e: e.tensor_copy(amask.t[:, 1, :], tri.t[:, U_LE, :]), R=[tri], W=[amask])
        k.op("dve", lambda e: e.tensor_scalar(amask.t[:, 2, :], tri.t[:, U_GE, :], linkc.t[:, 0:1], None, ALU.mult), R=[tri, linkc], W=[amask])
        k.op("dve", lambda e: e.tensor_scalar(amask.t[:, 3, :], tri.t[:, U_LE, :], linkc.t[:, 0:1], None, ALU.mult), R=[tri, linkc], W=[amask])
        k.op("dve", lambda e: e.tensor_scalar(amask.t[:, 4, :], onesf.t[:], linkc.t[:, 0:1], None, ALU.mult), R=[onesf, linkc], W=[amask])
        AM_PREV, AM_NEXT, AM_PREVL, AM_NEXTL, AM_FULLL = 0, 1, 2, 3, 4

        for j in range(2):
            for (dst, src) in ((wie, w_in_even), (woe, w_out_even), (wio, w_in_odd), (woo, w_out_odd)):
                for r in range(0, D, 128):
                    k.dma("pool", dst[j, r:r + 128, :], src[j, r:r + 128, :])
        k.barrier()

        evq = [0]

        def evac(dst_ap, src_ap, R, W, engines=("dve", "act")):
            on = engines[evq[0] % len(engines)]
            evq[0] += 1
            if on == "act":
                return k.op("act", lambda e: e.activation(dst_ap, src_ap, AF.Copy), R=R, W=W)
            return k.op(on, lambda e: e.tensor_copy(dst_ap, src_ap), R=R, W=W)

        def rstd_from_ss(rs, scale):
            k.op("dve", lambda e: e.tensor_scalar(rs.t[:], rs.t[:], scale, EPS, ALU.mult, ALU.add), R=[rs], W=[rs])
            k.op("act", lambda e: e.activation(rs.t[:], rs.t[:], AF.Sqrt), R=[rs], W=[rs])
            k.op("dve", lambda e: e.reciprocal(rs.t[:], rs.t[:]), R=[rs], W=[rs])

        def phase_in(hsrc, gamma_row, Wb, E, feat_cols):
            with ExitStack() as es:
                gam = sb(es, "in_gam", [128, D], F32)
                k.dma("sp", gam.t[:], gamma_row.partition_broadcast(128), W=[gam])
                hring = Ring([sb(es, "in_h%d" % i, [128, D], F32) for i in range(2)])
                sqj = sb(es, "in_sq", [128, D], BF16)
                ssr = Ring([sb(es, "in_ss%d" % i, [128, 1], F32) for i in range(2)])
                ubr = Ring([sb(es, "in_ub%d" % i, [128, D], BF16) for i in range(2)])
                uTr = Ring([sb(es, "in_uT%d" % i, [128, 16, G * 128], BF16) for i in range(2)])
                wring = Ring([sb(es, "in_w%d" % i, [128, 16, 512], BF16) for i in range(2)])
                stage = Ring([sb(es, "in_st%d" % i, [128, 512], F32) for i in range(4)])
                psr = Ring([psb(es, "in_ps%d" % i, [128, 512]) for i in range(8)])
                for sp in range(NSUP):
                    uT = uTr.next()
                    for tl in range(G):
                        tau = sp * G + tl
                        hb = hring.next()
                        k.dma("sp", hb.t[:], hsrc[tau * 128:(tau + 1) * 128, :], W=[hb])
                        ss = ssr.next()
                        k.op("dve", lambda e: e.memset(ss.t[:], 0.0), W=[ss])
                        k.op("act", lambda e: e.activation(sqj.t[:], hb.t[:], AF.Square, accum_out=ss.t[:, 0:1]), R=[hb], W=[sqj, ss])
                        rstd_from_ss(ss, 1.0 / D)
                        ub = ubr.next()
                        k.op("dve", lambda e: e.scalar_tensor_tensor(ub.t[:], hb.t[:], ss.t[:, 0:1], gam.t[:], ALU.mult, ALU.mult), R=[hb, ss, gam], W=[ub])
                        for q in range(2):
                            pt = psr.next()
                            ptv = pt.t[:].bitcast(BF16).rearrange("p (a b) -> p a b", a=8)
                            for j in range(8):
                                kc = q * 8 + j
                                k.op("pe", lambda e: e.transpose(ptv[:, j, :], ub.t[:, kc * 128:(kc + 1) * 128], identb.t[:]), R=[ub, identb], W=[pt])
                            evac(uT.t[:, q * 8:(q + 1) * 8, tl * 128:(tl + 1) * 128], ptv, R=[pt], W=[uT])
                    c0 = 0
                    while c0 < E:
                        cw = min(512, E - c0)
                        wt = wring.next()
                        k.dma("sp", wt.t[:, :, 0:cw], Wb[:, c0:c0 + cw].rearrange("(kc p) c -> p kc c", p=128), W=[wt])
                        if c0 < feat_cols:
                            for j in range(cw // 128):
                                for half in range(2):
                                    ps = psr.next()
                                    nn = G * 64
                                    for kc in range(16):
                                        k.op("pe", lambda e: e.matmul(ps.t[:, 0:nn], wt.t[:, kc, j * 128:(j + 1) * 128], uT.t[:, kc, half * nn:(half + 1) * nn], start=(kc == 0), stop=(kc == 15)), R=[wt, uT], W=[ps])
                                    st = stage.next()
                                    evac(st.t[:, 0:nn], ps.t[:, 0:nn], R=[ps], W=[st])
                                    col = sp * G * 128 + half * nn
                                    k.dma("sp", xcT[c0 + j * 128:c0 + (j + 1) * 128, col:col + nn], st.t[:, 0:nn], R=[st])
                        else:
                            for tl in range(G):
                                tau = sp * G + tl
                                ps = psr.next()
                                for kc in range(16):
                                    k.op("pe", lambda e: e.matmul(ps.t[:, 0:cw], uT.t[:, kc, tl * 128:(tl + 1) * 128], wt.t[:, kc, 0:cw], start=(kc == 0), stop=(kc == 15)), R=[wt, uT], W=[ps])
                                st = stage.next()
                                evac(st.t[:, 0:cw], ps.t[:, 0:cw], R=[ps], W=[st])
                                k.dma("sp", zt[tau * 128:(tau + 1) * 128, c0 - feat_cols:c0 - feat_cols + cw], st.t[:, 0:cw], R=[st])
                        c0 += cw
                k.barrier()
            if phase_done():
                return

        def phase_out(hsrc, Wb, final):
            with ExitStack() as es:
                wout = sb(es, "o_w", [128, 16, D], BF16)
                for q in range(4):
                    k.dma("sp", wout.t[:, q * 4:(q + 1) * 4, :], Wb[q * 512:(q + 1) * 512, :].rearrange("(kc p) c -> p kc c", p=128), W=[wout])
                gfin = None
                if final:
                    gfin = sb(es, "o_gf", [128, D], F32)
                    k.dma("sp", gfin.t[:], norm_final.partition_broadcast(128), W=[gfin])
                    sqj = sb(es, "o_sq", [128, D], BF16)
                    ssr = Ring([sb(es, "o_ss%d" % i, [128, 1], F32) for i in range(2)])
                    yr = Ring([sb(es, "o_y%d" % i, [128, D], F32) for i in range(2)])
                ogr = Ring([sb(es, "o_og%d" % i, [128, D], BF16) for i in range(2)])
                hor = Ring([sb(es, "o_ho%d" % i, [128, D], F32) for i in range(2)])
                hnr = Ring([sb(es, "o_hn%d" % i, [128, D], F32) for i in range(2)])
                oTr = Ring([sb(es, "o_oT%d" % i, [128, 16, 128], BF16) for i in range(2)])
                psr = Ring([psb(es, "o_ps%d" % i, [128, 512]) for i in range(8)])
                for tau in range(NTILE):
                    og = ogr.next()
                    k.dma("sp", og.t[:], ogs[tau * 128:(tau + 1) * 128, :], W=[og])
                    ho = hor.next()
                    k.dma("sp", ho.t[:], hsrc[tau * 128:(tau + 1) * 128, :], W=[ho])
                    oT = oTr.next()
                    for q in range(2):
                        pt = psr.next()
                        ptv = pt.t[:].bitcast(BF16).rearrange("p (a b) -> p a b", a=8)
                        for j in range(8):
                            kc = q * 8 + j
                            k.op("pe", lambda e: e.transpose(ptv[:, j, :], og.t[:, kc * 128:(kc + 1) * 128], identb.t[:]), R=[og, identb], W=[pt])
                        evac(oT.t[:, q * 8:(q + 1) * 8, :], ptv, R=[pt], W=[oT])
                    hn = hnr.next()
                    for cg in range(4):
                        ps = psr.next()
                        for kc in range(16):
                            k.op("pe", lambda e: e.matmul(ps.t[:], oT.t[:, kc, :], wout.t[:, kc, cg * 512:(cg + 1) * 512], start=(kc == 0), stop=(kc == 15)), R=[oT, wout], W=[ps])
                        k.op("dve", lambda e: e.tensor_tensor(hn.t[:, cg * 512:(cg + 1) * 512], ps.t[:], ho.t[:, cg * 512:(cg + 1) * 512], ALU.add), R=[ps, ho], W=[hn])
                    if not final:
                        k.dma("sp", hs[tau * 128:(tau + 1) * 128, :], hn.t[:], R=[hn])
                    elif tau not in (M0, M1):
                        ss = ssr.next()
                        k.op("dve", lambda e: e.memset(ss.t[:], 0.0), W=[ss])
                        k.op("act", lambda e: e.activation(sqj.t[:], hn.t[:], AF.Square, accum_out=ss.t[:, 0:1]), R=[hn], W=[sqj, ss])
                        rstd_from_ss(ss, 1.0 / D)
                        yb = yr.next()
                        k.op("dve", lambda e: e.scalar_tensor_tensor(yb.t[:], hn.t[:], ss.t[:, 0:1], gfin.t[:], ALU.mult, ALU.mult), R=[hn, ss, gfin], W=[yb])
                        ry = (tau - 1) if tau <= NT_SEG else (tau - 2)
                        k.dma("sp", y[ry * 128:(ry + 1) * 128, :], yb.t[:], R=[yb])
                k.barrier()
            if phase_done():
                return

        def rope(dst4, src4, cs, nh, tA, tB, R, W):
            c = cs.t[:, 0:1, :].to_broadcast([128, nh, 32])
            s = cs.t[:, 1:2, :].to_broadcast([128, nh, 32])
            k.op("dve", lambda e: e.tensor_tensor(tA.t[:, :, 0, :], src4[:, :, 0, :], c, ALU.mult), R=R + [cs], W=[tA])
            k.op("dve", lambda e: e.tensor_tensor(tA.t[:, :, 1, :], src4[:, :, 1, :], s, ALU.mult), R=R + [cs], W=[tA])
            k.op("pool", lambda e: e.tensor_tensor(tB.t[:, :, 0, :], src4[:, :, 1, :], c, ALU.mult), R=R + [cs], W=[tB])
            k.op("pool", lambda e: e.tensor_tensor(tB.t[:, :, 1, :], src4[:, :, 0, :], s, ALU.mult), R=R + [cs], W=[tB])
            k.op("dve", lambda e: e.tensor_tensor(dst4[:, :, 0, :], tA.t[:, :, 0, :], tA.t[:, :, 1, :], ALU.subtract), R=[tA], W=W)
            k.op("pool", lambda e: e.tensor_tensor(dst4[:, :, 1, :], tB.t[:, :, 0, :], tB.t[:, :, 1, :], ALU.add), R=[tB], W=W)

        def phase_even(j):
            with ExitStack() as es:
                KTall = es.enter_context(nc.sbuf_tensor(uname("sb_e_KT"), [128, NTILE, 2, 128], BF16))
                VAall = es.enter_context(nc.sbuf_tensor(uname("sb_e_VA"), [128, NTILE, 2, 65], BF16))
                KTb = [Buf(KTall) for _ in range(NTILE)]
                VAb = [Buf(VAall) for _ in range(NTILE)]
                esink = sb(es, "e_esink", [128, 16], F32)
                k.dma("sp", esink.t[:], a_sink[j:j + 1, :].partition_broadcast(128), W=[esink])
                k.op("act", lambda e: e.activation(esink.t[:], esink.t[:], AF.Exp), R=[esink], W=[esink])
                gub = [sb(es, "e_gu%d" % d, [16, 512], BF16) for d in range(2)]
                gbias = [sb(es, "e_gb%d" % d, [128, 512], F32) for d in range(2)]
                k.dma("pool", gub[0].t[:], gu_f[j], W=[gub[0]])
                k.dma("pool", gub[1].t[:], gu_b[j], W=[gub[1]])
                k.dma("sp", gbias[0].t[:], gb_f[j:j + 1, :].partition_broadcast(128), W=[gbias[0]])
                k.dma("sp", gbias[1].t[:], gb_b[j:j + 1, :].partition_broadcast(128), W=[gbias[1]])
                hnb = sb(es, "e_hn", [128, 1, 256], F32)
                k.dma("sp", hnb.t[:, 0, :], b_hnorm[j:j + 1, :].partition_broadcast(128), W=[hnb])
                psr = Ring([psb(es, "e_ps%d" % i, [128, 1024]) for i in range(4)])
                csr = Ring([sb(es, "e_cs%d" % i, [128, 2, 32], F32) for i in range(2)])

                def load_cs(tau):
                    cs = csr.next()
                    k.dma("sp", cs.t[:, 0, :], cosr[tau * 128:(tau + 1) * 128, :], W=[cs])
                    k.dma("sp", cs.t[:, 1, :], sinr[tau * 128:(tau + 1) * 128, :], W=[cs])
                    return cs

                with ExitStack() as e1:
                    kvr = Ring([sb(e1, "e1_kv%d" % i, [128, 256], F32) for i in range(2)])
                    tA = sb(e1, "e1_tA", [128, 2, 2, 32], F32)
                    tB = sb(e1, "e1_tB", [128, 2, 2, 32], F32)
                    krr = Ring([sb(e1, "e1_kr%d" % i, [128, 2, 2, 64], BF16) for i in range(2)])
                    for tau in range(NTILE):
                        kv = kvr.next()
                        k.dma("sp", kv.t[:], zt[tau * 128:(tau + 1) * 128, 1024:1280], W=[kv])
                        cs = load_cs(tau)
                        kr = krr.next()
                        src4 = kv.t[:, 0:128].rearrange("p (h a b) -> p h a b", h=2, a=2)
                        dst4 = kr.t[:, :, 0, :].rearrange("p h (a b) -> p h a b", a=2)
                        rope(dst4, src4, cs, 2, tA, tB, [kv], [kr])
                        k.op("dve", lambda e: e.tensor_copy(kr.t[:, :, 1, :], kr.t[:, :, 0, :]), R=[kr], W=[kr])
                        pt = psr.next()
                        ptv = pt.t[:, 0:128].bitcast(BF16).rearrange("p (a b) -> p a b", a=2)
                        for g in range(2):
                            k.op("pe", lambda e: e.transpose(ptv[:, g, :], kr.t[:, g, :, :].rearrange("p a b -> p (a b)"), identb.t[:]), R=[kr, identb], W=[pt])
                        evac(KTall[:, tau, :, :], ptv, R=[pt], W=[KTb[tau]])
                        k.op("dve", lambda e: e.tensor_copy(VAall[:, tau, :, 0:64], kv.t[:, 128:256].rearrange("p (g d) -> p g d", g=2)), R=[kv], W=[VAb[tau]])
                        if tau == M0 or tau == M1:
                            mi = 0 if tau == M0 else 1
                            k.op("dve", lambda e: e.tensor_copy(VAall[:, tau, :, 64:65], vmask.t[:, mi:mi + 1].unsqueeze(1).to_broadcast([128, 2, 1])), R=[vmask], W=[VAb[tau]])
                        else:
                            k.op("dve", lambda e: e.memset(VAall[:, tau, :, 64:65], 1.0), W=[VAb[tau]])
                    k.barrier()

                gl = ExitStack()
                es.enter_context(gl)
                qkr = Ring([sb(gl, "g_qk%d" % i, [128, 1024], F32) for i in range(2)])
                vr = Ring([sb(gl, "g_v%d" % i, [128, 1024], F32) for i in range(2)])
                lr = Ring([sb(gl, "g_l%d" % i, [128, 16], F32) for i in range(2)])
                l16 = sb(gl, "g_l16", [128, 16], BF16)
                lT = sb(gl, "g_lT", [16, 128], BF16)
                gt = sb(gl, "g_gt", [128, 512], F32)
                gg = sb(gl, "g_gg", [128, 512], F32)
                ghl = sb(gl, "g_ghl", [128, 2, 512], BF16)
                eb = sb(gl, "g_eb", [128, 512], F32)
                enb = sb(gl, "g_enb", [128, 512], F32)
                ek = sb(gl, "g_ek", [128, 512], F32)
                dec = sb(gl, "g_dec", [128, 4], F32)
                qd = sb(gl, "g_qd", [128, 512], BF16)
                kd = sb(gl, "g_kd", [128, 512], BF16)
                kdec = sb(gl, "g_kdec", [128, 512], BF16)
                v16 = sb(gl, "g_v16", [128, 1024], BF16)
                qkT = sb(gl, "g_qkT", [128, 8, 128], BF16)
                att = sb(gl, "g_att", [128, 4, 128], BF16)
                Sf = sb(gl, "g_S", [128, 1024], F32)
                Sb = sb(gl, "g_Sb", [128, 1024], BF16)
                ofr = Ring([sb(gl, "g_of%d" % i, [128, 1024], F32) for i in range(2)])

                def gla_reset():
                    k.op("dve", lambda e: e.memset(Sf.t[:], 0.0), W=[Sf])
                    k.op("dve", lambda e: e.memset(Sb.t[:], 0.0), W=[Sb])

                def gla_link():
                    k.op("dve", lambda e: e.tensor_scalar(Sf.t[:], Sf.t[:], linkc.t[:, 0:1], None, ALU.mult), R=[Sf, linkc], W=[Sf])
                    k.op("act", lambda e: e.activation(Sb.t[:], Sf.t[:], AF.Copy), R=[Sf], W=[Sb])

                def gla_load(tau, d):
                    qk = qkr.next()
                    k.dma("sp", qk.t[:], zt[tau * 128:(tau + 1) * 128, 2304:3328], W=[qk])
                    vv = vr.next()
                    k.dma("sp", vv.t[:], zt[tau * 128:(tau + 1) * 128, 3328:4352], W=[vv])
                    ll = lr.next()
                    k.dma("sp", ll.t[:], zt[tau * 128:(tau + 1) * 128, 5376 + 16 * d:5392 + 16 * d], W=[ll])
                    return qk, vv, ll

                def gla_step(tau, d, loaded):
                    qk, vv, ll = loaded
                    TA = U_LE if d == 0 else U_GE
                    TB = U_GT if d == 0 else U_LT
                    k.op("dve", lambda e: e.tensor_copy(l16.t[:], ll.t[:]), R=[ll], W=[l16])
                    p0 = psr.next()
                    p0b = p0.t[0:16, 0:64].bitcast(BF16)
                    k.op("pe", lambda e: e.transpose(p0b, l16.t[:], identb.t[:]), R=[l16, identb], W=[p0])
                    k.op("dve", lambda e: e.tensor_copy(lT.t[:], p0b), R=[p0], W=[lT])
                    p1 = psr.next()
                    k.op("pe", lambda e: e.matmul(p1.t[:, 0:512], lT.t[:], gub[d].t[:], start=True, stop=True), R=[lT, gub[d]], W=[p1])
                    k.op("dve", lambda e: e.tensor_tensor(gt.t[:], p1.t[:, 0:512], gbias[d].t[:], ALU.add), R=[p1, gbias[d]], W=[gt])
                    k.op("act", lambda e: e.activation(gt.t[:], gt.t[:], AF.Exp, scale=-1.0), R=[gt], W=[gt])
                    k.op("act", lambda e: e.activation(gt.t[:], gt.t[:], AF.Ln, bias=1.0), R=[gt], W=[gt])
                    if tau in (M0, M1):
                        mi = 0 if tau == M0 else 1
                        k.op("dve", lambda e: e.tensor_scalar(gg.t[:], gt.t[:], -1.0 / 16.0, vmask.t[:, mi:mi + 1], ALU.mult, ALU.mult), R=[gt, vmask], W=[gg])
                    else:
                        k.op("dve", lambda e: e.tensor_scalar(gg.t[:], gt.t[:], -1.0 / 16.0, None, ALU.mult), R=[gt], W=[gg])
                    k.op("dve", lambda e: e.tensor_copy(ghl.t[:, 0, :], gg.t[:]), R=[gg], W=[ghl])
                    k.op("dve", lambda e: e.tensor_tensor(ghl.t[:, 1, :], gg.t[:], ghl.t[:, 0, :], ALU.subtract), R=[gg, ghl], W=[ghl])
                    p2 = psr.next()
                    for hl in range(2):
                        k.op("pe", lambda e: e.matmul(p2.t[:, 0:512], trib.t[:, TA, :], ghl.t[:, hl, :], start=(hl == 0), stop=(hl == 1)), R=[trib, ghl], W=[p2])
                    for hl in range(2):
                        k.op("pe", lambda e: e.matmul(p2.t[:, 512:1024], trib.t[:, TB, :], ghl.t[:, hl, :], start=(hl == 0), stop=(hl == 1)), R=[trib, ghl], W=[p2])
                    p3 = psr.next()
                    for h in range(4):
                        for hl in range(2):
                            k.op("pe", lambda e: e.matmul(p3.t[:, h:h + 1], ghl.t[:, hl, h * 128:(h + 1) * 128], onesb.t[:, 0:1], start=(hl == 0), stop=(hl == 1)), R=[ghl, onesb], W=[p3])
                    k.op("act", lambda e: e.activation(eb.t[:], p2.t[:, 0:512], AF.Exp), R=[p2], W=[eb])
                    k.op("act", lambda e: e.activation(enb.t[:], p2.t[:, 0:512], AF.Exp, scale=-1.0), R=[p2], W=[enb])
                    k.op("act", lambda e: e.activation(ek.t[:], p2.t[:, 512:1024], AF.Exp), R=[p2], W=[ek])
                    k.op("act", lambda e: e.activation(dec.t[:], p3.t[:, 0:4], AF.Exp), R=[p3], W=[dec])
                    k.op("dve", lambda e: e.scalar_tensor_tensor(qd.t[:], qk.t[:, 0:512], 128.0 ** -0.5, eb.t[:], ALU.mult, ALU.mult), R=[qk, eb], W=[qd])
                    k.op("pool", lambda e: e.tensor_tensor(kd.t[:], qk.t[:, 512:1024], enb.t[:], ALU.mult), R=[qk, enb], W=[kd])
                    k.op("pool", lambda e: e.tensor_tensor(kdec.t[:], qk.t[:, 512:1024], ek.t[:], ALU.mult), R=[qk, ek], W=[kdec])
                    k.op("act", lambda e: e.activation(v16.t[:], vv.t[:], AF.Copy), R=[vv], W=[v16])
                    p4 = psr.next()
                    p4v = p4.t[:, 0:512].bitcast(BF16).rearrange("p (a b) -> p a b", a=8)
                    for h in range(4):
                        k.op("pe", lambda e: e.transpose(p4v[:, h, :], qd.t[:, h * 128:(h + 1) * 128], identb.t[:]), R=[qd, identb], W=[p4])
                        k.op("pe", lambda e: e.transpose(p4v[:, 4 + h, :], kd.t[:, h * 128:(h + 1) * 128], identb.t[:]), R=[kd, identb], W=[p4])
                    k.op("dve", lambda e: e.tensor_copy(qkT.t[:], p4v), R=[p4], W=[qkT])
                    p5 = psr.next()
                    p5v = p5.t[:, 0:512].rearrange("p (a b) -> p a b", a=4)
                    for h in range(4):
                        k.op("pe", lambda e: e.matmul(p5v[:, h, :], qkT.t[:, 4 + h, :], qkT.t[:, h, :], start=True, stop=True), R=[qkT], W=[p5])
                    k.op("dve", lambda e: e.tensor_tensor(att.t[:], p5v, tri.t[:, TA:TA + 1, :].to_broadcast([128, 4, 128]), ALU.mult), R=[p5, tri], W=[att])
                    po = psr.next()
                    for h in range(4):
                        k.op("pe", lambda e: e.matmul(po.t[:, h * 256:(h + 1) * 256], att.t[:, h, :], v16.t[:, h * 256:(h + 1) * 256], start=True, stop=False), R=[att, v16], W=[po])
                        k.op("pe", lambda e: e.matmul(po.t[:, h * 256:(h + 1) * 256], qkT.t[:, h, :], Sb.t[:, h * 256:(h + 1) * 256], start=False, stop=True), R=[qkT, Sb], W=[po])
                    pS = psr.next()
                    for h in range(4):
                        k.op("pe", lambda e: e.matmul(pS.t[:, h * 256:(h + 1) * 256], kdec.t[:, h * 128:(h + 1) * 128], v16.t[:, h * 256:(h + 1) * 256], start=True, stop=True), R=[kdec, v16], W=[pS])
                    for h in range(4):
                        k.op("dve", lambda e: e.scalar_tensor_tensor(Sf.t[:, h * 256:(h + 1) * 256], Sf.t[:, h * 256:(h + 1) * 256], dec.t[:, h:h + 1], pS.t[:, h * 256:(h + 1) * 256], ALU.mult, ALU.add), R=[Sf, dec, pS], W=[Sf])
                    k.op("act", lambda e: e.activation(Sb.t[:], Sf.t[:], AF.Copy), R=[Sf], W=[Sb])
                    return po

                gla_reset()
                nxt = gla_load(0, 0)
                for tau in range(NTILE):
                    cur = nxt
                    if tau + 1 < NTILE:
                        nxt = gla_load(tau + 1, 0)
                    if tau == M1:
                        gla_link()
                    po = gla_step(tau, 0, cur)
                    ob = ofr.next()
                    k.op("act", lambda e: e.activation(ob.t[:], po.t[:], AF.Copy), R=[po], W=[ob])
                    k.dma("sp", ofs[tau * 128:(tau + 1) * 128, 0:1024], ob.t[:], R=[ob])
                k.barrier()
                if phase_done():
                    return

                with ExitStack() as e3:
                    zbr = Ring([sb(e3, "e3_zb%d" % i, [128, 1024], F32) for i in range(2)])
                    osum = sb(e3, "e3_osum", [128, 4, 256], F32)
                    osq = sb(e3, "e3_osq", [128, 4, 256], F32)
                    oss = sb(e3, "e3_oss", [128, 4], F32)
                    ogr = Ring([sb(e3, "e3_og%d" % i, [128, D], BF16) for i in range(2)])
                    qar = Ring([sb(e3, "e3_qa%d" % i, [128, 1024], F32) for i in range(2)])
                    zar = Ring([sb(e3, "e3_za%d" % i, [128, 1024], F32) for i in range(2)])
                    tA = sb(e3, "e3_tA", [128, 16, 2, 32], F32)
                    tB = sb(e3, "e3_tB", [128, 16, 2, 32], F32)
                    qr = sb(e3, "e3_qr", [128, 1024], BF16)
                    QT = sb(e3, "e3_QT", [128, 8, 128], BF16)
                    PTr = Ring([sb(e3, "e3_PT%d" % i, [128, 2, 4, 128], BF16) for i in range(6)])
                    oall = sb(e3, "e3_oall", [128, 16, 65], F32)
                    den = sb(e3, "e3_den", [128, 16], F32)
                    onr = sb(e3, "e3_on", [128, 16, 64], F32)

                    def e3_load(tau):
                        ld = gla_load(tau, 1)
                        zb = zbr.next()
                        k.dma("sp", zb.t[:], zt[tau * 128:(tau + 1) * 128, 4352:5376], W=[zb])
                        of_t = ofr.next()
                        k.dma("sp", of_t.t[:], ofs[tau * 128:(tau + 1) * 128, 0:1024], W=[of_t])
                        qa = qar.next()
                        k.dma("sp", qa.t[:], zt[tau * 128:(tau + 1) * 128, 0:1024], W=[qa])
                        za = zar.next()
                        k.dma("sp", za.t[:], zt[tau * 128:(tau + 1) * 128, 1280:2304], W=[za])
                        cs = load_cs(tau)
                        return ld, zb, of_t, qa, za, cs

                    def keylist(tau):
                        if tau == M0 or tau == M1:
                            return [(tau, None), (tau + 1, AM_NEXT)]
                        ks = []
                        seg1 = tau > M1
                        if seg1:
                            ks.append((M0, AM_FULLL))
                            ks.append((M1, None))
                        else:
                            ks.append((M0, None))
                        first = (tau == 1) or (tau == LB)
                        last = (tau == LA) or (tau == NTILE - 1)
                        if not first:
                            ks.append((tau - 1, AM_PREV))
                        elif tau == LB:
                            ks.append((LA, AM_PREVL))
                        ks.append((tau, None))
                        if not last:
                            ks.append((tau + 1, AM_NEXT))
                        elif tau == LA:
                            ks.append((LB, AM_NEXTL))
                        return ks

                    gla_reset()
                    nxt = e3_load(NTILE - 1)
                    for tau in range(NTILE - 1, -1, -1):
                        ld, zb, of_t, qa, za, cs = nxt
                        if tau - 1 >= 0:
                            nxt = e3_load(tau - 1)
                        if tau == LA:
                            gla_link()
                        og = ogr.next()
                        po = gla_step(tau, 1, ld)
                        k.op("dve", lambda e: e.tensor_tensor(osum.t[:], po.t[:].rearrange("p (a b) -> p a b", a=4), of_t.t[:].rearrange("p (a b) -> p a b", a=4), ALU.add), R=[po, of_t], W=[osum])
                        k.op("pool", lambda e: e.tensor_tensor(osq.t[:], osum.t[:], osum.t[:], ALU.mult), R=[osum], W=[osq])
                        k.op("dve", lambda e: e.tensor_reduce(oss.t[:], osq.t[:], AX.X, ALU.add), R=[osq], W=[oss])
                        rstd_from_ss(oss, 1.0 / 256.0)
                        k.op("dve", lambda e: e.tensor_tensor(osum.t[:], osum.t[:], oss.t[:].unsqueeze(2).to_broadcast([128, 4, 256]), ALU.mult), R=[osum, oss], W=[osum])
                        k.op("pool", lambda e: e.tensor_tensor(osum.t[:], osum.t[:], hnb.t[:, 0:1, :].to_broadcast([128, 4, 256]), ALU.mult), R=[osum, hnb], W=[osum])
                        k.op("act", lambda e: e.activation(zb.t[:], zb.t[:], AF.Silu), R=[zb], W=[zb])
                        k.op("dve", lambda e: e.tensor_tensor(og.t[:, 1024:2048], osum.t[:].rearrange("p a b -> p (a b)"), zb.t[:], ALU.mult), R=[osum, zb], W=[og])
                        src4 = qa.t[:].rearrange("p (h a b) -> p h a b", h=16, a=2)
                        dst4 = qr.t[:].rearrange("p (h a b) -> p h a b", h=16, a=2)
                        rope(dst4, src4, cs, 16, tA, tB, [qa], [qr])
                        pq = psr.next()
                        pqv = pq.t[:, 0:512].bitcast(BF16).rearrange("p (a b) -> p a b", a=8)
                        for jj in range(8):
                            k.op("pe", lambda e: e.transpose(pqv[:, jj, :], qr.t[:, jj * 128:(jj + 1) * 128], identb.t[:]), R=[qr, identb], W=[pq])
                        k.op("dve", lambda e: e.tensor_copy(QT.t[:], pqv), R=[pq], W=[QT])
                        keys = keylist(tau)
                        for g in range(2):
                            pts = []
                            for (c, mk) in keys:
                                pst = psr.next()
                                for par in range(2):
                                    k.op("pe", lambda e: e.matmul(pst.t[:, par * 512:(par + 1) * 512], KTall[par * 64:(par + 1) * 64, c, g, :], QT.t[par * 64:(par + 1) * 64, 4 * g:4 * g + 4, :], start=True, stop=True), R=[KTb[c], QT], W=[pst])
                                pt = PTr.next()
                                k.op("act", lambda e: e.activation(pt.t[:].rearrange("p a b c -> p (a b c)"), pst.t[:], AF.Exp, scale=0.125), R=[pst], W=[pt])
                                if mk is not None:
                                    k.op("pool", lambda e: e.tensor_tensor(pt.t[:].rearrange("p a b c -> p (a b) c"), pt.t[:].rearrange("p a b c -> p (a b) c"), amask.t[:, mk:mk + 1, :].to_broadcast([128, 8, 128]), ALU.mult), R=[pt, amask], W=[pt])
                                pts.append((pt, c))
                            pso = psr.next()
                            psov = pso.t[:].rearrange("p (a b) -> p a b", a=8)
                            for jj in range(4):
                                for par in range(2):
                                    hh = 2 * jj + par
                                    for ci, (pt, c) in enumerate(pts):
                                        k.op("pe", lambda e: e.matmul(psov[:, hh, 0:65], pt.t[:, par, jj, :], VAall[:, c, g, :], start=(ci == 0), stop=(ci == len(pts) - 1)), R=[pt, VAb[c]], W=[pso])
                            k.op("act", lambda e: e.activation(oall.t[:, 8 * g:8 * g + 8, :], psov[:, :, 0:65], AF.Copy), R=[pso], W=[oall])
                        k.op("dve", lambda e: e.tensor_tensor(den.t[:], oall.t[:, :, 64], esink.t[:], ALU.add), R=[oall, esink], W=[den])
                        k.op("dve", lambda e: e.reciprocal(den.t[:], den.t[:]), R=[den], W=[den])
                        k.op("dve", lambda e: e.tensor_tensor(onr.t[:], oall.t[:, :, 0:64], den.t[:].unsqueeze(2).to_broadcast([128, 16, 64]), ALU.mult), R=[oall, den], W=[onr])
                        k.op("act", lambda e: e.activation(za.t[:], za.t[:], AF.Silu), R=[za], W=[za])
                        k.op("dve", lambda e: e.tensor_tensor(og.t[:, 0:1024], onr.t[:].rearrange("p a b -> p (a b)"), za.t[:], ALU.mult), R=[onr, za], W=[og])
                        k.dma("sp", ogs[tau * 128:(tau + 1) * 128, :], og.t[:], R=[og])
                k.barrier()
            if phase_done():
                return

        def phase_odd(j):
            with ExitStack() as es:
                NW = G * 128
                cw = sb(es, "o1_cw", [128, 48, 5], F32)
                with nc.allow_non_contiguous_dma("conv weights, tiny"):
                    for kk in range(5):
                        k.dma("sp", cw.t[:, :, kk], c_conv[j, kk, :].rearrange("(c p) -> p c", p=128), W=[cw])
                xr = Ring([sb(es, "o1_x%d" % i, [128, NW + 4], F32) for i in range(2)])
                acr = Ring([sb(es, "o1_ac%d" % i, [128, NW], F32) for i in range(2)])
                xl = sb(es, "o1_xl", [128, 4], F32)
                sq = sb(es, "o1_sq", [128, NW], BF16)
                rn = sb(es, "o1_rn", [128, NW], F32)
                xnr = Ring([sb(es, "o1_xn%d" % i, [128, G, 128], BF16) for i in range(2)])
                tmr = Ring([sb(es, "o1_tm%d" % i, [128, G, 128], BF16) for i in range(2)])
                psr = Ring([psb(es, "o1_ps%d" % i, [128, 1024]) for i in range(4)])
                for sp in range(NSUP):
                    col0 = sp * NW
                    for cc in range(48):
                        xb = xr.next()
                        lo = col0 - 2
                        hi = col0 + NW + 2
                        if sp == 0:
                            k.op("pool", lambda e: e.memset(xb.t[:, 0:2], 0.0), W=[xb])
                            lo = col0
                        if sp == NSUP - 1:
                            k.op("pool", lambda e: e.memset(xb.t[:, NW + 2:NW + 4], 0.0), W=[xb])
                            hi = col0 + NW
                        k.dma("sp", xb.t[:, lo - (col0 - 2):hi - (col0 - 2)], xcT[cc * 128:(cc + 1) * 128, lo:hi], W=[xb])
                        ac = acr.next()
                        k.op("dve", lambda e: e.tensor_scalar(ac.t[:], xb.t[:, 0:NW], cw.t[:, cc, 0:1], None, ALU.mult), R=[xb, cw], W=[ac])
                        for kk in range(1, 5):
                            k.op("dve", lambda e: e.scalar_tensor_tensor(ac.t[:], xb.t[:, kk:kk + NW], cw.t[:, cc, kk:kk + 1], ac.t[:], ALU.mult, ALU.add), R=[xb, cw, ac], W=[ac])
                        if sp == LA // G:
                            cA = (LA + 1) * 128 - 1 - col0
                            cB = LB * 128 - col0
                            k.op("dve", lambda e: e.tensor_scalar(xl.t[:, 0:2], xb.t[:, cA + 1:cA + 3], linkc.t[:, 0:1], None, ALU.mult), R=[xb, linkc], W=[xl])
                            k.op("dve", lambda e: e.tensor_scalar(xl.t[:, 2:4], xb.t[:, cB + 2:cB + 4], linkc.t[:, 0:1], None, ALU.mult), R=[xb, linkc], W=[xl])
                            fix = [(cA, 2, 3), (cA, 3, 4), (cA - 1, 2, 4), (cB, 1, 1), (cB, 0, 0), (cB + 1, 1, 0)]
                            for (col, xi, wk) in fix:
                                k.op("dve", lambda e: e.scalar_tensor_tensor(ac.t[:, col:col + 1], xl.t[:, xi:xi + 1], cw.t[:, cc, wk:wk + 1], ac.t[:, col:col + 1], ALU.mult, ALU.add), R=[xl, cw, ac], W=[ac])
                        k.op("act", lambda e: e.activation(ac.t[:], ac.t[:], AF.Silu), R=[ac], W=[ac])
                        xn = xnr.next()
                        xnf = xn.t[:].rearrange("p a b -> p (a b)")
                        if cc < 32:
                            head = cc % 16
                            k.op("pool", lambda e: e.tensor_tensor(sq.t[:], ac.t[:], ac.t[:], ALU.mult), R=[ac], W=[sq])
                            ps = psr.next()
                            nn = NW // 2
                            for half in range(2):
                                k.op("pe", lambda e: e.matmul(ps.t[:, half * 512:half * 512 + nn], onesb.t[:], sq.t[:, half * nn:(half + 1) * nn], start=True, stop=True), R=[onesb, sq], W=[ps])
                            rnv = rn.t[:].rearrange("p (a b) -> p a b", a=2)
                            psv = ps.t[:].rearrange("p (a b) -> p a b", a=2)[:, :, 0:nn]
                            k.op("act", lambda e: e.activation(rnv, psv, AF.Sqrt, bias=EPS), R=[ps], W=[rn])
                            k.op("dve", lambda e: e.reciprocal(rn.t[:], rn.t[:]), R=[rn], W=[rn])
                            scl = (128.0 ** -0.5) if cc < 16 else 1.0
                            k.op("dve", lambda e: e.scalar_tensor_tensor(xnf, ac.t[:], scl, rn.t[:], ALU.mult, ALU.mult), R=[ac, rn], W=[xn])
                            dst = qTs if cc < 16 else kTs
                            k.dma("sp", dst[sp * G:(sp + 1) * G, :, head, :].rearrange("t p c -> p t c"), xn.t[:], R=[xn])
                        else:
                            head = cc - 32
                            k.op("dve", lambda e: e.tensor_copy(xnf, ac.t[:]), R=[ac], W=[xn])
                        if cc >= 16:
                            ps = psr.next()
                            psv = ps.t[:, 0:G * 64].bitcast(BF16).rearrange("p (a b) -> p a b", a=G)
                            for tl in range(G):
                                k.op("pe", lambda e: e.transpose(psv[:, tl, :], xn.t[:, tl, :], identb.t[:]), R=[xn, identb], W=[ps])
                            tm = tmr.next()
                            evac(tm.t[:], psv, R=[ps], W=[tm])
                            dst = ktok if cc < 32 else vtok
                            k.dma("sp", dst[sp * NW:(sp + 1) * NW, head * 128:(head + 1) * 128].rearrange("(t p) c -> p t c", p=128), tm.t[:], R=[tm])
                k.barrier()
            if phase_done():
                return

            with ExitStack() as es:
                cst = sb(es, "d_cst", [128, 4, 16], F32)
                k.dma("sp", cst.t[:, 0, :], alog_f[j:j + 1, :].partition_broadcast(128), W=[cst])
                k.dma("sp", cst.t[:, 1, :], dtb_f[j:j + 1, :].partition_broadcast(128), W=[cst])
                k.dma("sp", cst.t[:, 2, :], alog_b[j:j + 1, :].partition_broadcast(128), W=[cst])
                k.dma("sp", cst.t[:, 3, :], dtb_b[j:j + 1, :].partition_broadcast(128), W=[cst])
                for a in (0, 2):
                    k.op("act", lambda e: e.activation(cst.t[:, a, :], cst.t[:, a, :], AF.Exp), R=[cst], W=[cst])
                    k.op("dve", lambda e: e.tensor_scalar(cst.t[:, a, :], cst.t[:, a, :], -1.0, None, ALU.mult), R=[cst], W=[cst])
                hnb = sb(es, "d_hn", [128, 1, 128], F32)
                k.dma("sp", hnb.t[:, 0, :], c_hnorm[j:j + 1, :].partition_broadcast(128), W=[hnb])
                psr = Ring([psb(es, "d_ps%d" % i, [128, 1024]) for i in range(4)])
                QKr = Ring([sb(es, "d_QK%d" % i, [128, 16, 2, 128], BF16) for i in range(2)])
                ktr = Ring([sb(es, "d_kt%d" % i, [128, 16, 128], BF16) for i in range(2)])
                Rbr = Ring([sb(es, "d_Rb%d" % i, [128, 16, 256], BF16) for i in range(2)])
                zzr = Ring([sb(es, "d_zz%d" % i, [128, 64], F32) for i in range(2)])
                gt = sb(es, "d_gt", [128, 16], F32)
                gg = sb(es, "d_gg", [128, 16], F32)
                ghl = sb(es, "d_ghl", [128, 2, 16], BF16)
                GUl = sb(es, "d_GUl", [128, 16, 128], BF16)
                GUh = sb(es, "d_GUh", [128, 16, 128], BF16)
                bt = sb(es, "d_bt", [128, 16], F32)
                nbt = sb(es, "d_nbt", [128, 16], F32)
                ecum = sb(es, "d_ecum", [128, 48], F32)
                GU = sb(es, "d_GU", [128, 16, 128], F32)
                EX = sb(es, "d_EX", [128, 16, 128], F32)
                Ys = [[sb(es, "d_Y%d%d" % (a, b), [128, 16, 128], BF16) for b in range(2)] for a in range(2)]
                AQ = sb(es, "d_AQ", [128, 16, 128], BF16)
                Ao = sb(es, "d_Ao", [128, 16, 128], BF16)
                Qm = sb(es, "d_Qm", [128, 16, 128], BF16)
                Yx = sb(es, "d_Yx", [128, 16, 128], BF16)
                Ub = sb(es, "d_Ub", [128, 16, 256], BF16)
                Rf = sb(es, "d_Rf", [128, 16, 256], F32)
                Wb_ = sb(es, "d_Wb", [128, 16, 128], BF16)
                WT = sb(es, "d_WT", [128, 16, 128], BF16)
                KG = sb(es, "d_KG", [128, 16, 128], BF16)
                vnb = sb(es, "d_vnb", [128, 16, 128], BF16)
                osr = Ring([sb(es, "d_os%d" % i, [128, 16, 128], F32) for i in range(2)])
                Sf = sb(es, "d_S", [128, 16, 128], F32)
                Sb = sb(es, "d_Sb", [128, 16, 128], BF16)

                def dn_reset():
                    k.op("dve", lambda e: e.memset(Sf.t[:], 0.0), W=[Sf])
                    k.op("dve", lambda e: e.memset(Sb.t[:], 0.0), W=[Sb])

                def dn_link():
                    k.op("dve", lambda e: e.tensor_scalar(Sf.t[:], Sf.t[:], linkc.t[:, 0:1], None, ALU.mult), R=[Sf, linkc], W=[Sf])
                    k.op("act", lambda e: e.activation(Sb.t[:], Sf.t[:], AF.Copy), R=[Sf], W=[Sb])

                def dn_load(tau):
                    QK = QKr.next()
                    k.dma("sp", QK.t[:, :, 0, :], kTs[tau], W=[QK])
                    k.dma("sp", QK.t[:, :, 1, :], qTs[tau], W=[QK])
                    kt = ktr.next()
                    k.dma("sp", kt.t[:], ktok[tau * 128:(tau + 1) * 128, :].rearrange("p (h c) -> p h c", h=16), W=[kt])
                    Rb = Rbr.next()
                    k.dma("sp", Rb.t[:, :, 0:128], vtok[tau * 128:(tau + 1) * 128, :].rearrange("p (h c) -> p h c", h=16), W=[Rb])
                    zz = zzr.next()
                    k.dma("sp", zz.t[:], zt[tau * 128:(tau + 1) * 128, 2048:2112], W=[zz])
                    return QK, kt, Rb, zz

                def dn_step(tau, d, loaded):
                    QK, kt, Rb, zz = loaded
                    TA = U_LE if d == 0 else U_GE
                    TB = U_GT if d == 0 else U_LT
                    TS = U_LT if d == 0 else U_GT
                    a_ap = zz.t[:, 32 * d:32 * d + 16]
                    b_ap = zz.t[:, 32 * d + 16:32 * d + 32]
                    isM = tau in (M0, M1)
                    mi = 0 if tau == M0 else 1
                    k.op("dve", lambda e: e.tensor_tensor(gt.t[:], a_ap, cst.t[:, 2 * d + 1, :], ALU.add), R=[zz, cst], W=[gt])
                    k.op("act", lambda e: e.activation(gt.t[:], gt.t[:], AF.Exp), R=[gt], W=[gt])
                    k.op("act", lambda e: e.activation(gt.t[:], gt.t[:], AF.Ln, bias=1.0), R=[gt], W=[gt])
                    k.op("dve", lambda e: e.tensor_tensor(gg.t[:], gt.t[:], cst.t[:, 2 * d, :], ALU.mult), R=[gt, cst], W=[gg])
                    k.op("act", lambda e: e.activation(bt.t[:], b_ap, AF.Exp, scale=-1.0), R=[zz], W=[bt])
                    k.op("dve", lambda e: e.tensor_scalar(bt.t[:], bt.t[:], 1.0, None, ALU.add), R=[bt], W=[bt])
                    k.op("dve", lambda e: e.reciprocal(bt.t[:], bt.t[:]), R=[bt], W=[bt])
                    if isM:
                        k.op("dve", lambda e: e.tensor_scalar(gg.t[:], gg.t[:], vmask.t[:, mi:mi + 1], None, ALU.mult), R=[gg, vmask], W=[gg])
                        k.op("dve", lambda e: e.tensor_scalar(bt.t[:], bt.t[:], vmask.t[:, mi:mi + 1], None, ALU.mult), R=[bt, vmask], W=[bt])
                    k.op("dve", lambda e: e.tensor_scalar(nbt.t[:], bt.t[:], -1.0, None, ALU.mult), R=[bt], W=[nbt])
                    pc = psr.next()
                    k.op("dve", lambda e: e.tensor_copy(ghl.t[:, 0, :], gg.t[:]), R=[gg], W=[ghl])
                    k.op("dve", lambda e: e.tensor_tensor(ghl.t[:, 1, :], gg.t[:], ghl.t[:, 0, :], ALU.subtract), R=[gg, ghl], W=[ghl])
                    for hl in range(2):
                        k.op("pe", lambda e: e.matmul(pc.t[:, 0:16], trib.t[:, TA, :], ghl.t[:, hl, :], start=(hl == 0), stop=(hl == 1)), R=[trib, ghl], W=[pc])
                    for hl in range(2):
                        k.op("pe", lambda e: e.matmul(pc.t[:, 16:32], trib.t[:, TB, :], ghl.t[:, hl, :], start=(hl == 0), stop=(hl == 1)), R=[trib, ghl], W=[pc])
                    for hl in range(2):
                        k.op("pe", lambda e: e.matmul(pc.t[:, 32:48], onesb.t[:], ghl.t[:, hl, :], start=(hl == 0), stop=(hl == 1)), R=[onesb, ghl], W=[pc])
                    k.op("act", lambda e: e.activation(ecum.t[:], pc.t[:, 0:48], AF.Exp), R=[pc], W=[ecum])
                    egam = ecum.t[:, 0:16]
                    erest = ecum.t[:, 16:32]
                    etot = ecum.t[:, 32:48]
                    k.op("pool", lambda e: e.tensor_tensor(GUh.t[:], trib.t[:, TA:TA + 1, :].to_broadcast([128, 16, 128]), ghl.t[:, 0, :].unsqueeze(2).to_broadcast([128, 16, 128]), ALU.mult), R=[trib, ghl], W=[GUh])
                    k.op("pool", lambda e: e.tensor_tensor(GUl.t[:], trib.t[:, TA:TA + 1, :].to_broadcast([128, 16, 128]), ghl.t[:, 1, :].unsqueeze(2).to_broadcast([128, 16, 128]), ALU.mult), R=[trib, ghl], W=[GUl])
                    pe_ = [psr.next(), psr.next()]
                    for hq in range(4):
                        pp = pe_[hq // 2]
                        k.op("pe", lambda e: e.matmul(pp.t[:, (hq % 2) * 512:(hq % 2) * 512 + 512], trib.t[:, TB, :], GUh.t[:, 4 * hq:4 * hq + 4, :], start=True, stop=False), R=[trib, GUh], W=[pp])
                        k.op("pe", lambda e: e.matmul(pp.t[:, (hq % 2) * 512:(hq % 2) * 512 + 512], trib.t[:, TB, :], GUl.t[:, 4 * hq:4 * hq + 4, :], start=False, stop=True), R=[trib, GUl], W=[pp])
                    for i2 in range(2):
                        k.op("act", lambda e: e.activation(EX.t[:, 8 * i2:8 * i2 + 8, :].rearrange("p a b -> p (a b)"), pe_[i2].t[:], AF.Exp), R=[pe_[i2]], W=[EX])
                    k.op("dve", lambda e: e.tensor_tensor(GU.t[:], EX.t[:], tri.t[:, TS:TS + 1, :].to_broadcast([128, 16, 128]), ALU.mult), R=[EX, tri], W=[GU])
                    k.op("pool", lambda e: e.tensor_tensor(GU.t[:], GU.t[:], nbt.t[:].unsqueeze(2).to_broadcast([128, 16, 128]), ALU.mult), R=[GU, nbt], W=[GU])
                    k.op("dve", lambda e: e.tensor_tensor(EX.t[:], EX.t[:], tri.t[:, TA:TA + 1, :].to_broadcast([128, 16, 128]), ALU.mult), R=[EX, tri], W=[EX])
                    YT0, Y0 = Ys[0]
                    for hq in range(4):
                        pk = psr.next()
                        pkv = pk.t[:].rearrange("p (a b c) -> p a b c", a=4, b=2)
                        for hl in range(4):
                            h = 4 * hq + hl
                            k.op("pe", lambda e: e.matmul(pk.t[:, hl * 256:(hl + 1) * 256], QK.t[:, h, 0, :], QK.t[:, h, :, :].rearrange("p a b -> p (a b)"), start=True, stop=True), R=[QK], W=[pk])
                        k.op("dve", lambda e: e.tensor_tensor(YT0.t[:, 4 * hq:4 * hq + 4, :], pkv[:, :, 0, :], GU.t[:, 4 * hq:4 * hq + 4, :], ALU.mult), R=[pk, GU], W=[YT0])
                        k.op("dve", lambda e: e.tensor_tensor(AQ.t[:, 4 * hq:4 * hq + 4, :], pkv[:, :, 1, :], EX.t[:, 4 * hq:4 * hq + 4, :], ALU.mult), R=[pk, EX], W=[AQ])
                    k.op("pool", lambda e: e.tensor_tensor(Ao.t[:], YT0.t[:], trib.t[:, BDM:BDM + 1, :].to_broadcast([128, 16, 128]), ALU.mult), R=[YT0, trib], W=[Ao])
                    k.op("dve", lambda e: e.tensor_tensor(YT0.t[:], YT0.t[:], Ao.t[:], ALU.subtract), R=[YT0, Ao], W=[YT0])
                    AdT, AoT = Ao, YT0
                    YTc, Yc = Ys[1]
                    for i2 in range(2):
                        pt = psr.next()
                        ptv = pt.t[:, 0:512].bitcast(BF16).rearrange("p (a b) -> p a b", a=8)
                        for hl in range(8):
                            h = 8 * i2 + hl
                            k.op("pe", lambda e: e.transpose(ptv[:, hl, :], AdT.t[:, h, :], identb.t[:]), R=[AdT, identb], W=[pt])
                        evac(Yc.t[:, 8 * i2:8 * i2 + 8, :], ptv, R=[pt], W=[Yc])
                    k.op("pool", lambda e: e.tensor_copy(YTc.t[:], AdT.t[:]), R=[AdT], W=[YTc])
                    k.op("dve", lambda e: e.tensor_tensor(Qm.t[:], AdT.t[:], identb.t[:].unsqueeze(1).to_broadcast([128, 16, 128]), ALU.add), R=[AdT, identb], W=[Qm])
                    k.op("act", lambda e: e.activation(Rf.t[:, :, 0:128], Rb.t[:, :, 0:128], AF.Copy), R=[Rb], W=[Rf])
                    k.op("dve", lambda e: e.tensor_tensor(Rf.t[:, :, 128:256], kt.t[:], egam.unsqueeze(2).to_broadcast([128, 16, 128]), ALU.mult), R=[kt, ecum], W=[Rf])
                    k.op("pool", lambda e: e.tensor_copy(Rb.t[:, :, 128:256], Rf.t[:, :, 128:256]), R=[Rf], W=[Rb])
                    cur = 1
                    ysets = [(Yx, Ys[0][1]), Ys[1]]
                    for lv in range(1, 5):
                        YT, Y = ysets[cur]
                        YTn, Yn = ysets[1 - cur]
                        for hq in range(4):
                            py = psr.next()
                            pyv = py.t[:].rearrange("p (a b c) -> p a b c", a=4, b=2)
                            for hl in range(4):
                                h = 4 * hq + hl
                                k.op("pe", lambda e: e.matmul(pyv[:, hl, 0, :], Y.t[:, h, :], YT.t[:, h, :], start=True, stop=True), R=[Y, YT], W=[py])
                                k.op("pe", lambda e: e.matmul(pyv[:, hl, 1, :], YT.t[:, h, :], Y.t[:, h, :], start=True, stop=True), R=[Y, YT], W=[py])
                            k.op("act", lambda e: e.activation(YTn.t[:, 4 * hq:4 * hq + 4, :], pyv[:, :, 0, :], AF.Copy), R=[py], W=[YTn])
                            k.op("dve", lambda e: e.tensor_copy(Yn.t[:, 4 * hq:4 * hq + 4, :], pyv[:, :, 1, :]), R=[py], W=[Yn])
                        for hq in range(4):
                            pq_ = psr.next()
                            pqv = pq_.t[:, 0:512].rearrange("p (a b) -> p a b", a=4)
                            for hl in range(4):
                                h = 4 * hq + hl
                                k.op("pe", lambda e: e.matmul(pqv[:, hl, :], Yn.t[:, h, :], Qm.t[:, h, :], start=True, stop=True), R=[Yn, Qm], W=[pq_])
                            k.op("dve", lambda e: e.tensor_tensor(Qm.t[:, 4 * hq:4 * hq + 4, :], Qm.t[:, 4 * hq:4 * hq + 4, :], pqv, ALU.add), R=[Qm, pq_], W=[Qm])
                        cur = 1 - cur
                    for it in range(4):
                        for hq in range(4):
                            if it == 0:
                                zsrc = Rb
                            else:
                                pz = psr.next()
                                pzv = pz.t[:].rearrange("p (a b) -> p a b", a=4)
                                for hl in range(4):
                                    h = 4 * hq + hl
                                    k.op("pe", lambda e: e.matmul(pzv[:, hl, :], AoT.t[:, h, :], Ub.t[:, h, :], start=True, stop=True), R=[AoT, Ub], W=[pz])
                                k.op("dve", lambda e: e.tensor_tensor(Ub.t[:, 4 * hq:4 * hq + 4, :], Rf.t[:, 4 * hq:4 * hq + 4, :], pzv, ALU.add), R=[Rf, pz], W=[Ub])
                                zsrc = Ub
                            pu = psr.next()
                            puv = pu.t[:].rearrange("p (a b) -> p a b", a=4)
                            for hl in range(4):
                                h = 4 * hq + hl
                                k.op("pe", lambda e: e.matmul(puv[:, hl, :], Qm.t[:, h, :], zsrc.t[:, h, :], start=True, stop=True), R=[Qm, zsrc], W=[pu])
                            if it < 3:
                                k.op("act", lambda e: e.activation(Ub.t[:, 4 * hq:4 * hq + 4, :], puv, AF.Copy), R=[pu], W=[Ub])
                            else:
                                k.op("act", lambda e: e.activation(Rf.t[:, 4 * hq:4 * hq + 4, :], puv, AF.Copy), R=[pu], W=[Rf])
                    bbc = bt.t[:].unsqueeze(2).to_broadcast([128, 16, 128])
                    k.op("dve", lambda e: e.tensor_tensor(Rf.t[:, :, 0:128], Rf.t[:, :, 0:128], bbc, ALU.mult), R=[Rf, bt], W=[Rf])
                    k.op("pool", lambda e: e.tensor_tensor(Wb_.t[:], Rf.t[:, :, 128:256], bbc, ALU.mult), R=[Rf, bt], W=[Wb_])
                    for i2 in range(2):
                        pt = psr.next()
                        ptv = pt.t[:, 0:512].bitcast(BF16).rearrange("p (a b) -> p a b", a=8)
                        for hl in range(8):
                            h = 8 * i2 + hl
                            k.op("pe", lambda e: e.transpose(ptv[:, hl, :], Wb_.t[:, h, :], identb.t[:]), R=[Wb_, identb], W=[pt])
                        evac(WT.t[:, 8 * i2:8 * i2 + 8, :], ptv, R=[pt], W=[WT])
                    k.op("pool", lambda e: e.tensor_tensor(KG.t[:], kt.t[:], erest.unsqueeze(2).to_broadcast([128, 16, 128]), ALU.mult), R=[kt, ecum], W=[KG])
                    osb = osr.next()
                    p12 = []
                    for hq in range(4):
                        pp = psr.next()
                        ppv = pp.t[:].rearrange("p (x a b) -> p x a b", x=2, a=4)
                        for hl in range(4):
                            h = 4 * hq + hl
                            k.op("pe", lambda e: e.matmul(ppv[:, 0, hl, :], WT.t[:, h, :], Sb.t[:, h, :], start=True, stop=True), R=[WT, Sb], W=[pp])
                            k.op("pe", lambda e: e.matmul(ppv[:, 1, hl, :], QK.t[:, h, 1, :], Sb.t[:, h, :], start=True, stop=True), R=[QK, Sb], W=[pp])
                        k.op("dve", lambda e: e.tensor_tensor(vnb.t[:, 4 * hq:4 * hq + 4, :], Rf.t[:, 4 * hq:4 * hq + 4, 0:128], ppv[:, 0, :, :], ALU.subtract), R=[Rf, pp], W=[vnb])
                        k.op("dve", lambda e: e.tensor_tensor(osb.t[:, 4 * hq:4 * hq + 4, :], ppv[:, 1, :, :], egam[:, 4 * hq:4 * hq + 4].unsqueeze(2).to_broadcast([128, 4, 128]), ALU.mult), R=[pp, ecum], W=[osb])
                        p12.append(pp)
                    for hq in range(4):
                        pp = psr.next()
                        ppv = pp.t[:].rearrange("p (x a b) -> p x a b", x=2, a=4)
                        for hl in range(4):
                            h = 4 * hq + hl
                            k.op("pe", lambda e: e.matmul(ppv[:, 0, hl, :], AQ.t[:, h, :], vnb.t[:, h, :], start=True, stop=True), R=[AQ, vnb], W=[pp])
                            k.op("pe", lambda e: e.matmul(ppv[:, 1, hl, :], KG.t[:, h, :], vnb.t[:, h, :], start=True, stop=True), R=[KG, vnb], W=[pp])
                        k.op("dve", lambda e: e.tensor_tensor(osb.t[:, 4 * hq:4 * hq + 4, :], osb.t[:, 4 * hq:4 * hq + 4, :], ppv[:, 0, :, :], ALU.add), R=[osb, pp], W=[osb])
                        k.op("pool", lambda e: e.tensor_tensor(Sf.t[:, 4 * hq:4 * hq + 4, :], Sf.t[:, 4 * hq:4 * hq + 4, :], etot[:, 4 * hq:4 * hq + 4].unsqueeze(2).to_broadcast([128, 4, 128]), ALU.mult), R=[Sf, ecum], W=[Sf])
                        k.op("dve", lambda e: e.tensor_tensor(Sf.t[:, 4 * hq:4 * hq + 4, :], Sf.t[:, 4 * hq:4 * hq + 4, :], ppv[:, 1, :, :], ALU.add), R=[Sf, pp], W=[Sf])
                        k.op("act", lambda e: e.activation(Sb.t[:, 4 * hq:4 * hq + 4, :], Sf.t[:, 4 * hq:4 * hq + 4, :], AF.Copy), R=[Sf], W=[Sb])
                    return osb

                dn_reset()
                nxt = dn_load(0)
                for tau in range(NTILE):
                    cur = nxt
                    if tau + 1 < NTILE:
                        nxt = dn_load(tau + 1)
                    if tau == M1:
                        dn_link()
                    osb = dn_step(tau, 0, cur)
                    k.dma("sp", ofs[tau * 128:(tau + 1) * 128, :], osb.t[:].rearrange("p a b -> p (a b)"), R=[osb])
                k.barrier()
                if phase_done():
                    return

                ofr = Ring([sb(es, "d_of%d" % i, [128, 16, 128], F32) for i in range(1)])
                zcr = Ring([sb(es, "d_zc%d" % i, [128, D], F32) for i in range(1)])
                osq = GU
                oss = sb(es, "d_oss", [128, 16], F32)
                ogr = Ring([sb(es, "d_og%d" % i, [128, D], BF16) for i in range(2)])

                dn_reset()
                nxt = dn_load(NTILE - 1)
                for tau in range(NTILE - 1, -1, -1):
                    ld = nxt
                    if tau - 1 >= 0:
                        nxt = dn_load(tau - 1)
                    of_t = ofr.next()
                    k.dma("sp", of_t.t[:].rearrange("p a b -> p (a b)"), ofs[tau * 128:(tau + 1) * 128, :], W=[of_t])
                    zc = zcr.next()
                    k.dma("sp", zc.t[:], zt[tau * 128:(tau + 1) * 128, 0:2048], W=[zc])
                    if tau == LA:
                        dn_link()
                    osb = dn_step(tau, 1, ld)
                    k.op("dve", lambda e: e.tensor_tensor(osb.t[:], osb.t[:], of_t.t[:], ALU.add), R=[osb, of_t], W=[osb])
                    k.op("pool", lambda e: e.tensor_tensor(osq.t[:], osb.t[:], osb.t[:], ALU.mult), R=[osb], W=[osq])
                    k.op("dve", lambda e: e.tensor_reduce(oss.t[:], osq.t[:], AX.X, ALU.add), R=[osq], W=[oss])
                    rstd_from_ss(oss, 1.0 / 128.0)
                    k.op("dve", lambda e: e.tensor_tensor(osb.t[:], osb.t[:], oss.t[:].unsqueeze(2).to_broadcast([128, 16, 128]), ALU.mult), R=[osb, oss], W=[osb])
                    k.op("pool", lambda e: e.tensor_tensor(osb.t[:], osb.t[:], hnb.t[:, 0:1, :].to_broadcast([128, 16, 128]), ALU.mult), R=[osb, hnb], W=[osb])
                    k.op("act", lambda e: e.activation(zc.t[:], zc.t[:], AF.Silu), R=[zc], W=[zc])
                    og = ogr.next()
                    k.op("dve", lambda e: e.tensor_tensor(og.t[:], osb.t[:].rearrange("p a b -> p (a b)"), zc.t[:], ALU.mult), R=[osb, zc], W=[og])
                    k.dma("sp", ogs[tau * 128:(tau + 1) * 128, :], og.t[:], R=[og])
                k.barrier()
            if phase_done():
                return

        try:
          for layer in range(4):
            j = layer // 2
            hsrc = hin if layer == 0 else hs
            if layer % 2 == 0:
                for ph in (lambda: phase_in(hsrc, norm_even[j:j + 1, :], wie[j], EVEN_IN, 0), lambda: phase_even(j), lambda: phase_out(hsrc, woe[j], False)):
                    if not stopped[0]:
                        ph()
            else:
                for ph in (lambda: phase_in(hsrc, norm_odd[j:j + 1, :], wio[j], ODD_IN, 6144), lambda: phase_odd(j), lambda: phase_out(hsrc, woo[j], layer == 3)):
                    if not stopped[0]:
                        ph()
        except _Stop:
            pass
        k.barrier()
        if debug:
            for nm in debug:
                src = scr_all[nm]
                dst = nc.dram_tensor("dbg_" + nm, list(src.shape), src.dtype, kind="ExternalOutput").ap()
                n0 = src.shape[0]
                stp = max(1, n0 // 8)
                for r in range(0, n0, stp):
                    k.dma("sp", dst[r:r + stp], src[r:r + stp])
            k.barrier()
    build.ninstr = k.nins
    build.nop = k.nop_
    return nc


def _core_inputs(xs, meta, S_seg, is_prompt):
    NT_SEG = S_seg // 128
    NTILE = 2 * (NT_SEG + 1)
    T = NTILE * 128
    hin = np.zeros((T, D), np.float32)
    pos = np.zeros((T,), np.float32)
    m0 = 0
    m1 = (NT_SEG + 1) * 128
    r0 = 128
    r1 = (NT_SEG + 2) * 128
    hin[m0 + 112:m0 + 128] = meta
    pos[m0 + 112:m0 + 128] = np.arange(16)
    vm = np.zeros((128, 2), np.float32)
    vm[112:, 0] = 1.0
    if is_prompt:
        x = xs[0]
        hin[r0:r0 + S_seg] = x[:S_seg]
        hin[r1:r1 + S_seg] = x[S_seg:]
        pos[r0:r0 + S_seg] = 16 + np.arange(S_seg)
        pos[r1:r1 + S_seg] = 16 + S_seg + np.arange(S_seg)
        link = 1.0
    else:
        hin[r0:r0 + S_seg] = xs[0]
        hin[r1:r1 + S_seg] = xs[1]
        hin[m1 + 112:m1 + 128] = meta
        pos[m1 + 112:m1 + 128] = np.arange(16)
        pos[r0:r0 + S_seg] = 16 + np.arange(S_seg)
        pos[r1:r1 + S_seg] = 16 + np.arange(S_seg)
        vm[112:, 1] = 1.0
        link = 0.0
    inv = (1.0 / (np.float32(10000.0) ** (np.arange(0, 64, 2, dtype=np.float32) / np.float32(64)))).astype(np.float32)
    ang = pos[:, None].astype(np.float32) * inv[None]
    return {
        "hin": hin,
        "cosr": np.cos(ang).astype(np.float32),
        "sinr": np.sin(ang).astype(np.float32),
        "linkc": np.full((128, 1), link, np.float32),
        "vmask": vm,
    }


def _consts():
    s = np.arange(128)[:, None]
    i = np.arange(128)[None, :]
    tri = np.stack([(s <= i), (s < i), (s >= i), (s > i), (s // 32 == i // 32)]).astype(np.float32)
    return {"ident": np.eye(128, dtype=np.float32), "tri": tri}


_NC_CACHE = {}


def kernel(**inputs):
    xp = np.asarray(inputs["x_prompt"], np.float32)
    xsm = np.asarray(inputs["x_sample"], np.float32)
    nb_p, seq, _ = xp.shape
    nb_s, dseq, _ = xsm.shape
    assert seq == 2 * dseq and nb_s % 2 == 0
    S_seg = dseq
    NT_SEG = S_seg // 128
    meta = np.asarray(inputs["meta_tokens"], np.float32)
    shared = _consts()
    for name in ["norm_even", "w_in_even", "a_sink", "b_gate_up_fwd", "b_gate_bias_fwd", "b_gate_up_bwd",
                 "b_gate_bias_bwd", "b_head_norm", "w_out_even", "norm_odd", "w_in_odd", "c_conv", "c_a_log_fwd",
                 "c_dt_bias_fwd", "c_a_log_bwd", "c_dt_bias_bwd", "c_head_norm", "w_out_odd"]:
        shared[name] = np.ascontiguousarray(np.asarray(inputs[name], np.float32))
    shared["norm_final"] = np.ascontiguousarray(np.asarray(inputs["norm_final"], np.float32).reshape(1, D))
    in_maps = []
    for p in range(nb_p):
        m = dict(shared)
        m.update(_core_inputs([xp[p]], meta, S_seg, True))
        in_maps.append(m)
    for s in range(nb_s // 2):
        m = dict(shared)
        m.update(_core_inputs([xsm[2 * s], xsm[2 * s + 1]], meta, S_seg, False))
        in_maps.append(m)
    ncores = len(in_maps)
    if NT_SEG not in _NC_CACHE:
        _NC_CACHE[NT_SEG] = build(NT_SEG)
    nc = _NC_CACHE[NT_SEG]
    res = run_bass_kernel_spmd(nc, in_maps, core_ids=list(range(ncores)))
    outs = [np.asarray(r["y"], np.float32) for r in res.results]
    y_prompt = np.stack([outs[p].reshape(seq, D) for p in range(nb_p)], axis=0)
    y_sample = np.stack([outs[nb_p + s // 2].reshape(2, dseq, D)[s % 2] for s in range(nb_s)], axis=0)
    return (y_prompt, y_sample)
```

```python
import numpy as np
from contextlib import ExitStack
import concourse.bass as bass
import concourse.mybir as mybir
from concourse.bass_utils import run_bass_kernel_spmd

F32 = mybir.dt.float32
BF16 = mybir.dt.bfloat16
AF = mybir.ActivationFunctionType
ALU = mybir.AluOpType
AX = mybir.AxisListType

D = 2048
EPS = 1e-6
N_META = 16
EVEN_IN = 5408
ODD_IN = 8256
G = 6
NDMA = 48


class Buf:
    __slots__ = ("t", "w", "r", "ex", "q")

    def __init__(self, t, ex=False):
        self.t = t
        self.w = None
        self.r = []
        self.ex = ex
        self.q = None


class Ring:
    def __init__(self, bufs):
        self.b = bufs
        self.i = 0

    def next(self):
        b = self.b[self.i % len(self.b)]
        self.i += 1
        return b


class K:
    def __init__(self, nc, es):
        self.nc = nc
        self.es = es
        self.eng = {"pe": nc.tensor, "dve": nc.vector, "act": nc.scalar, "pool": nc.gpsimd, "sp": nc.sync}
        self.sem = {n: es.enter_context(nc.semaphore("s_" + n)) for n in ["pe", "dve", "act", "pool"]}
        self.cnt = {n: 0 for n in self.sem}
        self.waited = {}
        self.dslots = [[es.enter_context(nc.semaphore("d%d" % i)), 0] for i in range(NDMA)]
        self.ndma = 0
        self.nins = 0
        import os
        self.limit = int(os.environ.get("KLIMIT", "0")) or None
        self.nop_ = 0
        self.dbgops = set(int(x) for x in os.environ.get("KDBG", "").split(",") if x)

    def _wait(self, on, deps):
        e = self.eng[on]
        for d in deps:
            if d is None:
                continue
            key, val = d
            if key == on and on == "pe":
                continue
            if self.waited.get((on, key), 0) >= val:
                continue
            sem = self.sem[key] if isinstance(key, str) else self.dslots[key][0]
            if self.nop_ in self.dbgops:
                print("DBGWAIT op", self.nop_, "on", on, "waits", key, val, "cnt", dict(self.cnt))
            e.wait_ge(sem, val)
            self.nins += 1
            self.waited[(on, key)] = val

    def _deps(self, R, W):
        deps = []
        for b in R:
            deps.append(b.w)
            if b.ex:
                deps.extend(b.r)
        for b in W:
            deps.append(b.w)
            deps.extend(b.r)
        return deps

    def _commit(self, tok, R, W):
        for b in R:
            b.r = [t for t in b.r if t[0] != tok[0]] + [tok]
        for b in W:
            b.w = tok
            b.r = []

    def op(self, on, fn, R=(), W=()):
        self.nop_ += 1
        if self.limit is not None and self.nop_ > self.limit:
            return None
        self._wait(on, self._deps(R, W))
        ins = fn(self.eng[on])
        self.cnt[on] += 1
        ins.then_inc(self.sem[on], 1)
        self.nins += 1
        tok = (on, self.cnt[on])
        self._commit(tok, R, W)
        return tok

    def dma(self, on, out, in_, R=(), W=(), **kw):
        self.nop_ += 1
        if self.limit is not None and self.nop_ > self.limit and not str(getattr(out.tensor, "name", "")).startswith("dbg_"):
            return None
        i = self.ndma % NDMA
        self.ndma += 1
        slot = self.dslots[i]
        self._wait(on, self._deps(R, W) + [(i, slot[1])])
        slot[1] += 16
        self.eng[on].dma_start(out=out, in_=in_, **kw).then_inc(slot[0], 16)
        self.nins += 1
        tok = (i, slot[1])
        self._commit(tok, R, W)
        return tok

    def barrier(self):
        deps = [(n, c) for n, c in self.cnt.items() if c > 0]
        deps += [(i, s[1]) for i, s in enumerate(self.dslots) if s[1] > 0]
        for on in self.eng:
            self._wait(on, deps)


class _Stop(Exception):
    pass


def build(NT_SEG, debug=False, stop_at=None):
    NTILE = 2 * (NT_SEG + 1)
    T = NTILE * 128
    assert NTILE % G == 0
    NSUP = NTILE // G
    M0, M1 = 0, NT_SEG + 1
    LA, LB = NT_SEG, NT_SEG + 2
    assert LA // G == LB // G
    NREAL = 2 * NT_SEG

    nc = bass.Bass("TRN2", target_bir_lowering=False)

    def din(name, shape, dt=F32):
        return nc.dram_tensor(name, list(shape), dt, kind="ExternalInput").ap()

    def dscr(name, shape, dt):
        t = nc.dram_tensor(name, list(shape), dt, kind="Internal").ap()
        scr_all[name] = t
        return t

    scr_all = {}

    phase_no = [0]

    def phase_done():
        phase_no[0] += 1
        if stop_at is not None and phase_no[0] >= stop_at:
            stopped[0] = True
        return stopped[0]

    stopped = [False]

    hin = din("hin", [T, D])
    cosr = din("cosr", [T, 32])
    sinr = din("sinr", [T, 32])
    linkc_d = din("linkc", [128, 1])
    vmask_d = din("vmask", [128, 2])
    ident_d = din("ident", [128, 128])
    tri_d = din("tri", [5, 128, 128])
    norm_even = din("norm_even", [2, D])
    w_in_even = din("w_in_even", [2, D, EVEN_IN])
    a_sink = din("a_sink", [2, 16])
    gu_f = din("b_gate_up_fwd", [2, 16, 512])
    gb_f = din("b_gate_bias_fwd", [2, 512])
    gu_b = din("b_gate_up_bwd", [2, 16, 512])
    gb_b = din("b_gate_bias_bwd", [2, 512])
    b_hnorm = din("b_head_norm", [2, 256])
    w_out_even = din("w_out_even", [2, D, D])
    norm_odd = din("norm_odd", [2, D])
    w_in_odd = din("w_in_odd", [2, D, ODD_IN])
    c_conv = din("c_conv", [2, 5, 6144])
    alog_f = din("c_a_log_fwd", [2, 16])
    dtb_f = din("c_dt_bias_fwd", [2, 16])
    alog_b = din("c_a_log_bwd", [2, 16])
    dtb_b = din("c_dt_bias_bwd", [2, 16])
    c_hnorm = din("c_head_norm", [2, 128])
    w_out_odd = din("w_out_odd", [2, D, D])
    norm_final = din("norm_final", [1, D])
    y = nc.dram_tensor("y", [NREAL * 128, D], F32, kind="ExternalOutput").ap()

    hs = dscr("hs", [T, D], F32)
    zt = dscr("zt", [T, EVEN_IN], F32)
    xcT = dscr("xcT", [6144, T], F32)
    qTs = dscr("qTs", [NTILE, 128, 16, 128], BF16)
    kTs = dscr("kTs", [NTILE, 128, 16, 128], BF16)
    ktok = dscr("ktok", [T, D], BF16)
    vtok = dscr("vtok", [T, D], BF16)
    ofs = dscr("ofs", [T, D], F32)
    ogs = dscr("ogs", [T, D], BF16)
    wie = dscr("wie", [2, D, EVEN_IN], BF16)
    woe = dscr("woe", [2, D, D], BF16)
    wio = dscr("wio", [2, D, ODD_IN], BF16)
    woo = dscr("woo", [2, D, D], BF16)

    es0 = ExitStack()
    with es0:
        k = K(nc, es0)

        uid = [0]

        def uname(name):
            uid[0] += 1
            return "%s_u%d" % (name, uid[0])

        def sb(es, name, shape, dt):
            return Buf(es.enter_context(nc.sbuf_tensor(uname("sb_" + name), list(shape), dt)))

        def psb(es, name, shape, dt=F32):
            return Buf(es.enter_context(nc.psum_tensor(uname("ps_" + name), list(shape), dt)), ex=True)

        identb = sb(es0, "identb", [128, 128], BF16)
        tri = sb(es0, "tri", [128, 5, 128], F32)
        onesf = sb(es0, "onesf", [128, 128], F32)
        trib = sb(es0, "trib", [128, 5, 128], BF16)
        onesb = sb(es0, "onesb", [128, 128], BF16)
        linkc = sb(es0, "linkc", [128, 1], F32)
        vmask = sb(es0, "vmask", [128, 2], F32)
        amask = sb(es0, "amask", [128, 5, 128], BF16)
        k.dma("pool", identb.t[:], ident_d, W=[identb])
        k.dma("sp", tri.t[:], tri_d.rearrange("a p c -> p a c"), W=[tri])
        k.dma("sp", linkc.t[:], linkc_d, W=[linkc])
        k.dma("sp", vmask.t[:], vmask_d, W=[vmask])
        k.op("dve", lambda e: e.memset(onesf.t[:], 1.0), W=[onesf])
        k.op("dve", lambda e: e.memset(onesb.t[:], 1.0), W=[onesb])
        k.op("dve", lambda e: e.tensor_copy(trib.t[:], tri.t[:]), R=[tri], W=[trib])
        U_LE, U_LT, U_GE, U_GT, BDM = 0, 1, 2, 3, 4
        k.op("dve", lambda e: e.tensor_copy(amask.t[:, 0, :], tri.t[:, U_GE, :]), R=[tri], W=[amask])
        k.op("dve", lambda e: e.tensor_copy(amask.t[:, 1, :], tri.t[:, U_LE, :]), R=[tri], W=[amask])
        k.op("dve", lambda e: e.tensor_scalar(amask.t[:, 2, :], tri.t[:, U_GE, :], linkc.t[:, 0:1], None, ALU.mult), R=[tri, linkc], W=[amask])
        k.op("dve", lambda e: e.tensor_scalar(amask.t[:, 3, :], tri.t[:, U_LE, :], linkc.t[:, 0:1], None, ALU.mult), R=[tri, linkc], W=[amask])
        k.op("dve", lambda e: e.tensor_scalar(amask.t[:, 4, :], onesf.t[:], linkc.t[:, 0:1], None, ALU.mult), R=[onesf, linkc], W=[amask])
        AM_PREV, AM_NEXT, AM_PREVL, AM_NEXTL, AM_FULLL = 0, 1, 2, 3, 4

        for j in range(2):
            for (dst, src) in ((wie, w_in_even), (woe, w_out_even), (wio, w_in_odd), (woo, w_out_odd)):
                for r in range(0, D, 128):
                    k.dma("pool", dst[j, r:r + 128, :], src[j, r:r + 128, :])
        k.barrier()

        evq = [0]

        def evac(dst_ap, src_ap, R, W, engines=("dve", "act")):
            on = engines[evq[0] % len(engines)]
            evq[0] += 1
            if on == "act":
                return k.op("act", lambda e: e.activation(dst_ap, src_ap, AF.Copy), R=R, W=W)
            return k.op(on, lambda e: e.tensor_copy(dst_ap, src_ap), R=R, W=W)

        def rstd_from_ss(rs, scale):
            k.op("dve", lambda e: e.tensor_scalar(rs.t[:], rs.t[:], scale, EPS, ALU.mult, ALU.add), R=[rs], W=[rs])
            k.op("act", lambda e: e.activation(rs.t[:], rs.t[:], AF.Sqrt), R=[rs], W=[rs])
            k.op("dve", lambda e: e.reciprocal(rs.t[:], rs.t[:]), R=[rs], W=[rs])

        def phase_in(hsrc, gamma_row, Wb, E, feat_cols):
            with ExitStack() as es:
                gam = sb(es, "in_gam", [128, D], F32)
                k.dma("sp", gam.t[:], gamma_row.partition_broadcast(128), W=[gam])
                hring = Ring([sb(es, "in_h%d" % i, [128, D], F32) for i in range(2)])
                sqj = sb(es, "in_sq", [128, D], BF16)
                ssr = Ring([sb(es, "in_ss%d" % i, [128, 1], F32) for i in range(2)])
                ubr = Ring([sb(es, "in_ub%d" % i, [128, D], BF16) for i in range(2)])
                uTr = Ring([sb(es, "in_uT%d" % i, [128, 16, G * 128], BF16) for i in range(2)])
                wring = Ring([sb(es, "in_w%d" % i, [128, 16, 512], BF16) for i in range(2)])
                stage = Ring([sb(es, "in_st%d" % i, [128, 512], F32) for i in range(4)])
                psr = Ring([psb(es, "in_ps%d" % i, [128, 512]) for i in range(8)])
                for sp in range(NSUP):
                    uT = uTr.next()
                    for tl in range(G):
                        tau = sp * G + tl
                        hb = hring.next()
                        k.dma("sp", hb.t[:], hsrc[tau * 128:(tau + 1) * 128, :], W=[hb])
                        ss = ssr.next()
                        k.op("dve", lambda e: e.memset(ss.t[:], 0.0), W=[ss])
                        k.op("act", lambda e: e.activation(sqj.t[:], hb.t[:], AF.Square, accum_out=ss.t[:, 0:1]), R=[hb], W=[sqj, ss])
                        rstd_from_ss(ss, 1.0 / D)
                        ub = ubr.next()
                        k.op("dve", lambda e: e.scalar_tensor_tensor(ub.t[:], hb.t[:], ss.t[:, 0:1], gam.t[:], ALU.mult, ALU.mult), R=[hb, ss, gam], W=[ub])
                        for q in range(2):
                            pt = psr.next()
                            ptv = pt.t[:].bitcast(BF16).rearrange("p (a b) -> p a b", a=8)
                            for j in range(8):
                                kc = q * 8 + j
                                k.op("pe", lambda e: e.transpose(ptv[:, j, :], ub.t[:, kc * 128:(kc + 1) * 128], identb.t[:]), R=[ub, identb], W=[pt])
                            evac(uT.t[:, q * 8:(q + 1) * 8, tl * 128:(tl + 1) * 128], ptv, R=[pt], W=[uT])
                    c0 = 0
                    while c0 < E:
                        cw = min(512, E - c0)
                        wt = wring.next()
                        k.dma("sp", wt.t[:, :, 0:cw], Wb[:, c0:c0 + cw].rearrange("(kc p) c -> p kc c", p=128), W=[wt])
                        if c0 < feat_cols:
                            for j in range(cw // 128):
                                for half in range(2):
                                    ps = psr.next()
                                    nn = G * 64
                                    for kc in range(16):
                                        k.op("pe", lambda e: e.matmul(ps.t[:, 0:nn], wt.t[:, kc, j * 128:(j + 1) * 128], uT.t[:, kc, half * nn:(half + 1) * nn], start=(kc == 0), stop=(kc == 15)), R=[wt, uT], W=[ps])
                                    st = stage.next()
                                    evac(st.t[:, 0:nn], ps.t[:, 0:nn], R=[ps], W=[st])
                                    col = sp * G * 128 + half * nn
                                    k.dma("sp", xcT[c0 + j * 128:c0 + (j + 1) * 128, col:col + nn], st.t[:, 0:nn], R=[st])
                        else:
                            for tl in range(G):
                                tau = sp * G + tl
                                ps = psr.next()
                                for kc in range(16):
                                    k.op("pe", lambda e: e.matmul(ps.t[:, 0:cw], uT.t[:, kc, tl * 128:(tl + 1) * 128], wt.t[:, kc, 0:cw], start=(kc == 0), stop=(kc == 15)), R=[wt, uT], W=[ps])
                                st = stage.next()
                                evac(st.t[:, 0:cw], ps.t[:, 0:cw], R=[ps], W=[st])
                                k.dma("sp", zt[tau * 128:(tau + 1) * 128, c0 - feat_cols:c0 - feat_cols + cw], st.t[:, 0:cw], R=[st])
                        c0 += cw
                k.barrier()
            if phase_done():
                return

        def phase_out(hsrc, Wb, final):
            with ExitStack() as es:
                wout = sb(es, "o_w", [128, 16, D], BF16)
                for q in range(4):
                    k.dma("sp", wout.t[:, q * 4:(q + 1) * 4, :], Wb[q * 512:(q + 1) * 512, :].rearrange("(kc p) c -> p kc c", p=128), W=[wout])
                gfin = None
                if final:
                    gfin = sb(es, "o_gf", [128, D], F32)
                    k.dma("sp", gfin.t[:], norm_final.partition_broadcast(128), W=[gfin])
                    sqj = sb(es, "o_sq", [128, D], BF16)
                    ssr = Ring([sb(es, "o_ss%d" % i, [128, 1], F32) for i in range(2)])
                    yr = Ring([sb(es, "o_y%d" % i, [128, D], F32) for i in range(2)])
                ogr = Ring([sb(es, "o_og%d" % i, [128, D], BF16) for i in range(2)])
                hor = Ring([sb(es, "o_ho%d" % i, [128, D], F32) for i in range(2)])
                hnr = Ring([sb(es, "o_hn%d" % i, [128, D], F32) for i in range(2)])
                oTr = Ring([sb(es, "o_oT%d" % i, [128, 16, 128], BF16) for i in range(2)])
                psr = Ring([psb(es, "o_ps%d" % i, [128, 512]) for i in range(8)])
                for tau in range(NTILE):
                    og = ogr.next()
                    k.dma("sp", og.t[:], ogs[tau * 128:(tau + 1) * 128, :], W=[og])
                    ho = hor.next()
                    k.dma("sp", ho.t[:], hsrc[tau * 128:(tau + 1) * 128, :], W=[ho])
                    oT = oTr.next()
                    for q in range(2):
                        pt = psr.next()
                        ptv = pt.t[:].bitcast(BF16).rearrange("p (a b) -> p a b", a=8)
                        for j in range(8):
                            kc = q * 8 + j
                            k.op("pe", lambda e: e.transpose(ptv[:, j, :], og.t[:, kc * 128:(kc + 1) * 128], identb.t[:]), R=[og, identb], W=[pt])
                        evac(oT.t[:, q * 8:(q + 1) * 8, :], ptv, R=[pt], W=[oT])
                    hn = hnr.next()
                    for cg in range(4):
                        ps = psr.next()
                        for kc in range(16):
                            k.op("pe", lambda e: e.matmul(ps.t[:], oT.t[:, kc, :], wout.t[:, kc, cg * 512:(cg + 1) * 512], start=(kc == 0), stop=(kc == 15)), R=[oT, wout], W=[ps])
                        k.op("dve", lambda e: e.tensor_tensor(hn.t[:, cg * 512:(cg + 1) * 512], ps.t[:], ho.t[:, cg * 512:(cg + 1) * 512], ALU.add), R=[ps, ho], W=[hn])
                    if not final:
                        k.dma("sp", hs[tau * 128:(tau + 1) * 128, :], hn.t[:], R=[hn])
                    elif tau not in (M0, M1):
                        ss = ssr.next()
                        k.op("dve", lambda e: e.memset(ss.t[:], 0.0), W=[ss])
                        k.op("act", lambda e: e.activation(sqj.t[:], hn.t[:], AF.Square, accum_out=ss.t[:, 0:1]), R=[hn], W=[sqj, ss])
                        rstd_from_ss(ss, 1.0 / D)
                        yb = yr.next()
                        k.op("dve", lambda e: e.scalar_tensor_tensor(yb.t[:], hn.t[:], ss.t[:, 0:1], gfin.t[:], ALU.mult, ALU.mult), R=[hn, ss, gfin], W=[yb])
                        ry = (tau - 1) if tau <= NT_SEG else (tau - 2)
                        k.dma("sp", y[ry * 128:(ry + 1) * 128, :], yb.t[:], R=[yb])
                k.barrier()
            if phase_done():
                return

        def rope(dst4, src4, cs, nh, tA, tB, R, W):
            c = cs.t[:, 0:1, :].to_broadcast([128, nh, 32])
            s = cs.t[:, 1:2, :].to_broadcast([128, nh, 32])
            k.op("dve", lambda e: e.tensor_tensor(tA.t[:, :, 0, :], src4[:, :, 0, :], c, ALU.mult), R=R + [cs], W=[tA])
            k.op("dve", lambda e: e.tensor_tensor(tA.t[:, :, 1, :], src4[:, :, 1, :], s, ALU.mult), R=R + [cs], W=[tA])
            k.op("pool", lambda e: e.tensor_tensor(tB.t[:, :, 0, :], src4[:, :, 1, :], c, ALU.mult), R=R + [cs], W=[tB])
            k.op("pool", lambda e: e.tensor_tensor(tB.t[:, :, 1, :], src4[:, :, 0, :], s, ALU.mult), R=R + [cs], W=[tB])
            k.op("dve", lambda e: e.tensor_tensor(dst4[:, :, 0, :], tA.t[:, :, 0, :], tA.t[:, :, 1, :], ALU.subtract), R=[tA], W=W)
            k.op("pool", lambda e: e.tensor_tensor(dst4[:, :, 1, :], tB.t[:, :, 0, :], tB.t[:, :, 1, :], ALU.add), R=[tB], W=W)

        def phase_even(j):
            with ExitStack() as es:
                KTall = es.enter_context(nc.sbuf_tensor(uname("sb_e_KT"), [128, NTILE, 2, 128], BF16))
                VAall = es.enter_context(nc.sbuf_tensor(uname("sb_e_VA"), [128, NTILE, 2, 65], BF16))
                KTb = [Buf(KTall) for _ in range(NTILE)]
                VAb = [Buf(VAall) for _ in range(NTILE)]
                esink = sb(es, "e_esink", [128, 16], F32)
                k.dma("sp", esink.t[:], a_sink[j:j + 1, :].partition_broadcast(128), W=[esink])
                k.op("act", lambda e: e.activation(esink.t[:], esink.t[:], AF.Exp), R=[esink], W=[esink])
                gub = [sb(es, "e_gu%d" % d, [16, 512], BF16) for d in range(2)]
                gbias = [sb(es, "e_gb%d" % d, [128, 512], F32) for d in range(2)]
                k.dma("pool", gub[0].t[:], gu_f[j], W=[gub[0]])
                k.dma("pool", gub[1].t[:], gu_b[j], W=[gub[1]])
                k.dma("sp", gbias[0].t[:], gb_f[j:j + 1, :].partition_broadcast(128), W=[gbias[0]])
                k.dma("sp", gbias[1].t[:], gb_b[j:j + 1, :].partition_broadcast(128), W=[gbias[1]])
                hnb = sb(es, "e_hn", [128, 1, 256], F32)
                k.dma("sp", hnb.t[:, 0, :], b_hnorm[j:j + 1, :].partition_broadcast(128), W=[hnb])
                psr = Ring([psb(es, "e_ps%d" % i, [128, 1024]) for i in range(4)])
                csr = Ring([sb(es, "e_cs%d" % i, [128, 2, 32], F32) for i in range(2)])

                def load_cs(tau):
                    cs = csr.next()
                    k.dma("sp", cs.t[:, 0, :], cosr[tau * 128:(tau + 1) * 128, :], W=[cs])
                    k.dma("sp", cs.t[:, 1, :], sinr[tau * 128:(tau + 1) * 128, :], W=[cs])
                    return cs

                with ExitStack() as e1:
                    kvr = Ring([sb(e1, "e1_kv%d" % i, [128, 256], F32) for i in range(2)])
                    tA = sb(e1, "e1_tA", [128, 2, 2, 32], F32)
                    tB = sb(e1, "e1_tB", [128, 2, 2, 32], F32)
                    krr = Ring([sb(e1, "e1_kr%d" % i, [128, 2, 2, 64], BF16) for i in range(2)])
                    for tau in range(NTILE):
                        kv = kvr.next()
                        k.dma("sp", kv.t[:], zt[tau * 128:(tau + 1) * 128, 1024:1280], W=[kv])
                        cs = load_cs(tau)
                        kr = krr.next()
                        src4 = kv.t[:, 0:128].rearrange("p (h a b) -> p h a b", h=2, a=2)
                        dst4 = kr.t[:, :, 0, :].rearrange("p h (a b) -> p h a b", a=2)
                        rope(dst4, src4, cs, 2, tA, tB, [kv], [kr])
                        k.op("dve", lambda e: e.tensor_copy(kr.t[:, :, 1, :], kr.t[:, :, 0, :]), R=[kr], W=[kr])
                        pt = psr.next()
                        ptv = pt.t[:, 0:128].bitcast(BF16).rearrange("p (a b) -> p a b", a=2)
                        for g in range(2):
                            k.op("pe", lambda e: e.transpose(ptv[:, g, :], kr.t[:, g, :, :].rearrange("p a b -> p (a b)"), identb.t[:]), R=[kr, identb], W=[pt])
                        evac(KTall[:, tau, :, :], ptv, R=[pt], W=[KTb[tau]])
                        k.op("dve", lambda e: e.tensor_copy(VAall[:, tau, :, 0:64], kv.t[:, 128:256].rearrange("p (g d) -> p g d", g=2)), R=[kv], W=[VAb[tau]])
                        if tau == M0 or tau == M1:
                            mi = 0 if tau == M0 else 1
                            k.op("dve", lambda e: e.tensor_copy(VAall[:, tau, :, 64:65], vmask.t[:, mi:mi + 1].unsqueeze(1).to_broadcast([128, 2, 1])), R=[vmask], W=[VAb[tau]])
                        else:
                            k.op("dve", lambda e: e.memset(VAall[:, tau, :, 64:65], 1.0), W=[VAb[tau]])
                    k.barrier()

                gl = ExitStack()
                es.enter_context(gl)
                qkr = Ring([sb(gl, "g_qk%d" % i, [128, 1024], F32) for i in range(2)])
                vr = Ring([sb(gl, "g_v%d" % i, [128, 1024], F32) for i in range(2)])
                lr = Ring([sb(gl, "g_l%d" % i, [128, 16], F32) for i in range(2)])
                l16 = sb(gl, "g_l16", [128, 16], BF16)
                lT = sb(gl, "g_lT", [16, 128], BF16)
                gt = sb(gl, "g_gt", [128, 512], F32)
                gg = sb(gl, "g_gg", [128, 512], F32)
                ghl = sb(gl, "g_ghl", [128, 2, 512], BF16)
                eb = sb(gl, "g_eb", [128, 512], F32)
                enb = sb(gl, "g_enb", [128, 512], F32)
                ek = sb(gl, "g_ek", [128, 512], F32)
                dec = sb(gl, "g_dec", [128, 4], F32)
                qd = sb(gl, "g_qd", [128, 512], BF16)
                kd = sb(gl, "g_kd", [128, 512], BF16)
                kdec = sb(gl, "g_kdec", [128, 512], BF16)
                v16 = sb(gl, "g_v16", [128, 1024], BF16)
                qkT = sb(gl, "g_qkT", [128, 8, 128], BF16)
                att = sb(gl, "g_att", [128, 4, 128], BF16)
                Sf = sb(gl, "g_S", [128, 1024], F32)
                Sb = sb(gl, "g_Sb", [128, 1024], BF16)
                ofr = Ring([sb(gl, "g_of%d" % i, [128, 1024], F32) for i in range(2)])

                def gla_reset():
                    k.op("dve", lambda e: e.memset(Sf.t[:], 0.0), W=[Sf])
                    k.op("dve", lambda e: e.memset(Sb.t[:], 0.0), W=[Sb])

                def gla_link():
                    k.op("dve", lambda e: e.tensor_scalar(Sf.t[:], Sf.t[:], linkc.t[:, 0:1], None, ALU.mult), R=[Sf, linkc], W=[Sf])
                    k.op("act", lambda e: e.activation(Sb.t[:], Sf.t[:], AF.Copy), R=[Sf], W=[Sb])

                def gla_load(tau, d):
                    qk = qkr.next()
                    k.dma("sp", qk.t[:], zt[tau * 128:(tau + 1) * 128, 2304:3328], W=[qk])
                    vv = vr.next()
                    k.dma("sp", vv.t[:], zt[tau * 128:(tau + 1) * 128, 3328:4352], W=[vv])
                    ll = lr.next()
                    k.dma("sp", ll.t[:], zt[tau * 128:(tau + 1) * 128, 5376 + 16 * d:5392 + 16 * d], W=[ll])
                    return qk, vv, ll

                def gla_step(tau, d, loaded):
                    qk, vv, ll = loaded
                    TA = U_LE if d == 0 else U_GE
                    TB = U_GT if d == 0 else U_LT
                    k.op("dve", lambda e: e.tensor_copy(l16.t[:], ll.t[:]), R=[ll], W=[l16])
                    p0 = psr.next()
                    p0b = p0.t[0:16, 0:64].bitcast(BF16)
                    k.op("pe", lambda e: e.transpose(p0b, l16.t[:], identb.t[:]), R=[l16, identb], W=[p0])
                    k.op("dve", lambda e: e.tensor_copy(lT.t[:], p0b), R=[p0], W=[lT])
                    p1 = psr.next()
                    k.op("pe", lambda e: e.matmul(p1.t[:, 0:512], lT.t[:], gub[d].t[:], start=True, stop=True), R=[lT, gub[d]], W=[p1])
                    k.op("dve", lambda e: e.tensor_tensor(gt.t[:], p1.t[:, 0:512], gbias[d].t[:], ALU.add), R=[p1, gbias[d]], W=[gt])
                    k.op("act", lambda e: e.activation(gt.t[:], gt.t[:], AF.Exp, scale=-1.0), R=[gt], W=[gt])
                    k.op("act", lambda e: e.activation(gt.t[:], gt.t[:], AF.Ln, bias=1.0), R=[gt], W=[gt])
                    if tau in (M0, M1):
                        mi = 0 if tau == M0 else 1
                        k.op("dve", lambda e: e.tensor_scalar(gg.t[:], gt.t[:], -1.0 / 16.0, vmask.t[:, mi:mi + 1], ALU.mult, ALU.mult), R=[gt, vmask], W=[gg])
                    else:
                        k.op("dve", lambda e: e.tensor_scalar(gg.t[:], gt.t[:], -1.0 / 16.0, None, ALU.mult), R=[gt], W=[gg])
                    k.op("dve", lambda e: e.tensor_copy(ghl.t[:, 0, :], gg.t[:]), R=[gg], W=[ghl])
                    k.op("dve", lambda e: e.tensor_tensor(ghl.t[:, 1, :], gg.t[:], ghl.t[:, 0, :], ALU.subtract), R=[gg, ghl], W=[ghl])
                    p2 = psr.next()
                    for hl in range(2):
                        k.op("pe", lambda e: e.matmul(p2.t[:, 0:512], trib.t[:, TA, :], ghl.t[:, hl, :], start=(hl == 0), stop=(hl == 1)), R=[trib, ghl], W=[p2])
                    for hl in range(2):
                        k.op("pe", lambda e: e.matmul(p2.t[:, 512:1024], trib.t[:, TB, :], ghl.t[:, hl, :], start=(hl == 0), stop=(hl == 1)), R=[trib, ghl], W=[p2])
                    p3 = psr.next()
                    for h in range(4):
                        for hl in range(2):
                            k.op("pe", lambda e: e.matmul(p3.t[:, h:h + 1], ghl.t[:, hl, h * 128:(h + 1) * 128], onesb.t[:, 0:1], start=(hl == 0), stop=(hl == 1)), R=[ghl, onesb], W=[p3])
                    k.op("act", lambda e: e.activation(eb.t[:], p2.t[:, 0:512], AF.Exp), R=[p2], W=[eb])
                    k.op("act", lambda e: e.activation(enb.t[:], p2.t[:, 0:512], AF.Exp, scale=-1.0), R=[p2], W=[enb])
                    k.op("act", lambda e: e.activation(ek.t[:], p2.t[:, 512:1024], AF.Exp), R=[p2], W=[ek])
                    k.op("act", lambda e: e.activation(dec.t[:], p3.t[:, 0:4], AF.Exp), R=[p3], W=[dec])
                    k.op("dve", lambda e: e.scalar_tensor_tensor(qd.t[:], qk.t[:, 0:512], 128.0 ** -0.5, eb.t[:], ALU.mult, ALU.mult), R=[qk, eb], W=[qd])
                    k.op("pool", lambda e: e.tensor_tensor(kd.t[:], qk.t[:, 512:1024], enb.t[:], ALU.mult), R=[qk, enb], W=[kd])
                    k.op("pool", lambda e: e.tensor_tensor(kdec.t[:], qk.t[:, 512:1024], ek.t[:], ALU.mult), R=[qk, ek], W=[kdec])
                    k.op("act", lambda e: e.activation(v16.t[:], vv.t[:], AF.Copy), R=[vv], W=[v16])
                    p4 = psr.next()
                    p4v = p4.t[:, 0:512].bitcast(BF16).rearrange("p (a b) -> p a b", a=8)
                    for h in range(4):
                        k.op("pe", lambda e: e.transpose(p4v[:, h, :], qd.t[:, h * 128:(h + 1) * 128], identb.t[:]), R=[qd, identb], W=[p4])
                        k.op("pe", lambda e: e.transpose(p4v[:, 4 + h, :], kd.t[:, h * 128:(h + 1) * 128], identb.t[:]), R=[kd, identb], W=[p4])
                    k.op("dve", lambda e: e.tensor_copy(qkT.t[:], p4v), R=[p4], W=[qkT])
                    p5 = psr.next()
                    p5v = p5.t[:, 0:512].rearrange("p (a b) -> p a b", a=4)
                    for h in range(4):
                        k.op("pe", lambda e: e.matmul(p5v[:, h, :], qkT.t[:, 4 + h, :], qkT.t[:, h, :], start=True, stop=True), R=[qkT], W=[p5])
                    k.op("dve", lambda e: e.tensor_tensor(att.t[:], p5v, tri.t[:, TA:TA + 1, :].to_broadcast([128, 4, 128]), ALU.mult), R=[p5, tri], W=[att])
                    po = psr.next()
                    for h in range(4):
                        k.op("pe", lambda e: e.matmul(po.t[:, h * 256:(h + 1) * 256], att.t[:, h, :], v16.t[:, h * 256:(h + 1) * 256], start=True, stop=False), R=[att, v16], W=[po])
                        k.op("pe", lambda e: e.matmul(po.t[:, h * 256:(h + 1) * 256], qkT.t[:, h, :], Sb.t[:, h * 256:(h + 1) * 256], start=False, stop=True), R=[qkT, Sb], W=[po])
                    pS = psr.next()
                    for h in range(4):
                        k.op("pe", lambda e: e.matmul(pS.t[:, h * 256:(h + 1) * 256], kdec.t[:, h * 128:(h + 1) * 128], v16.t[:, h * 256:(h + 1) * 256], start=True, stop=True), R=[kdec, v16], W=[pS])
                    for h in range(4):
                        k.op("dve", lambda e: e.scalar_tensor_tensor(Sf.t[:, h * 256:(h + 1) * 256], Sf.t[:, h * 256:(h + 1) * 256], dec.t[:, h:h + 1], pS.t[:, h * 256:(h + 1) * 256], ALU.mult, ALU.add), R=[Sf, dec, pS], W=[Sf])
                    k.op("act", lambda e: e.activation(Sb.t[:], Sf.t[:], AF.Copy), R=[Sf], W=[Sb])
                    return po

                gla_reset()
                nxt = gla_load(0, 0)
                for tau in range(NTILE):
                    cur = nxt
                    if tau + 1 < NTILE:
                        nxt = gla_load(tau + 1, 0)
                    if tau == M1:
                        gla_link()
                    po = gla_step(tau, 0, cur)
                    ob = ofr.next()
                    k.op("act", lambda e: e.activation(ob.t[:], po.t[:], AF.Copy), R=[po], W=[ob])
                    k.dma("sp", ofs[tau * 128:(tau + 1) * 128, 0:1024], ob.t[:], R=[ob])
                k.barrier()
                if phase_done():
                    return

                with ExitStack() as e3:
                    zbr = Ring([sb(e3, "e3_zb%d" % i, [128, 1024], F32) for i in range(2)])
                    osum = sb(e3, "e3_osum", [128, 4, 256], F32)
                    osq = sb(e3, "e3_osq", [128, 4, 256], F32)
                    oss = sb(e3, "e3_oss", [128, 4], F32)
                    ogr = Ring([sb(e3, "e3_og%d" % i, [128, D], BF16) for i in range(2)])
                    qar = Ring([sb(e3, "e3_qa%d" % i, [128, 1024], F32) for i in range(2)])
                    zar = Ring([sb(e3, "e3_za%d" % i, [128, 1024], F32) for i in range(2)])
                    tA = sb(e3, "e3_tA", [128, 16, 2, 32], F32)
                    tB = sb(e3, "e3_tB", [128, 16, 2, 32], F32)
                    qr = sb(e3, "e3_qr", [128, 1024], BF16)
                    QT = sb(e3, "e3_QT", [128, 8, 128], BF16)
                    PTr = Ring([sb(e3, "e3_PT%d" % i, [128, 2, 4, 128], BF16) for i in range(6)])
                    oall = sb(e3, "e3_oall", [128, 16, 65], F32)
                    den = sb(e3, "e3_den", [128, 16], F32)
                    onr = sb(e3, "e3_on", [128, 16, 64], F32)

                    def e3_load(tau):
                        ld = gla_load(tau, 1)
                        zb = zbr.next()
                        k.dma("sp", zb.t[:], zt[tau * 128:(tau + 1) * 128, 4352:5376], W=[zb])
                        of_t = ofr.next()
                        k.dma("sp", of_t.t[:], ofs[tau * 128:(tau + 1) * 128, 0:1024], W=[of_t])
                        qa = qar.next()
                        k.dma("sp", qa.t[:], zt[tau * 128:(tau + 1) * 128, 0:1024], W=[qa])
                        za = zar.next()
                        k.dma("sp", za.t[:], zt[tau * 128:(tau + 1) * 128, 1280:2304], W=[za])
                        cs = load_cs(tau)
                        return ld, zb, of_t, qa, za, cs

                    def keylist(tau):
                        if tau == M0 or tau == M1:
                            return [(tau, None), (tau + 1, AM_NEXT)]
                        ks = []
                        seg1 = tau > M1
                        if seg1:
                            ks.append((M0, AM_FULLL))
                            ks.append((M1, None))
                        else:
                            ks.append((M0, None))
                        first = (tau == 1) or (tau == LB)
                        last = (tau == LA) or (tau == NTILE - 1)
                        if not first:
                            ks.append((tau - 1, AM_PREV))
                        elif tau == LB:
                            ks.append((LA, AM_PREVL))
                        ks.append((tau, None))
                        if not last:
                            ks.append((tau + 1, AM_NEXT))
                        elif tau == LA:
                            ks.append((LB, AM_NEXTL))
                        return ks

                    gla_reset()
                    nxt = e3_load(NTILE - 1)
                    for tau in range(NTILE - 1, -1, -1):
                        ld, zb, of_t, qa, za, cs = nxt
                        if tau - 1 >= 0:
                            nxt = e3_load(tau - 1)
                        if tau == LA:
                            gla_link()
                        og = ogr.next()
                        po = gla_step(tau, 1, ld)
                        k.op("dve", lambda e: e.tensor_tensor(osum.t[:], po.t[:].rearrange("p (a b) -> p a b", a=4), of_t.t[:].rearrange("p (a b) -> p a b", a=4), ALU.add), R=[po, of_t], W=[osum])
                        k.op("pool", lambda e: e.tensor_tensor(osq.t[:], osum.t[:], osum.t[:], ALU.mult), R=[osum], W=[osq])
                        k.op("dve", lambda e: e.tensor_reduce(oss.t[:], osq.t[:], AX.X, ALU.add), R=[osq], W=[oss])
                        rstd_from_ss(oss, 1.0 / 256.0)
                        k.op("dve", lambda e: e.tensor_tensor(osum.t[:], osum.t[:], oss.t[:].unsqueeze(2).to_broadcast([128, 4, 256]), ALU.mult), R=[osum, oss], W=[osum])
                        k.op("pool", lambda e: e.tensor_tensor(osum.t[:], osum.t[:], hnb.t[:, 0:1, :].to_broadcast([128, 4, 256]), ALU.mult), R=[osum, hnb], W=[osum])
                        k.op("act", lambda e: e.activation(zb.t[:], zb.t[:], AF.Silu), R=[zb], W=[zb])
                        k.op("dve", lambda e: e.tensor_tensor(og.t[:, 1024:2048], osum.t[:].rearrange("p a b -> p (a b)"), zb.t[:], ALU.mult), R=[osum, zb], W=[og])
                        src4 = qa.t[:].rearrange("p (h a b) -> p h a b", h=16, a=2)
                        dst4 = qr.t[:].rearrange("p (h a b) -> p h a b", h=16, a=2)
                        rope(dst4, src4, cs, 16, tA, tB, [qa], [qr])
                        pq = psr.next()
                        pqv = pq.t[:, 0:512].bitcast(BF16).rearrange("p (a b) -> p a b", a=8)
                        for jj in range(8):
                            k.op("pe", lambda e: e.transpose(pqv[:, jj, :], qr.t[:, jj * 128:(jj + 1) * 128], identb.t[:]), R=[qr, identb], W=[pq])
                        k.op("dve", lambda e: e.tensor_copy(QT.t[:], pqv), R=[pq], W=[QT])
                        keys = keylist(tau)
                        for g in range(2):
                            pts = []
                            for (c, mk) in keys:
                                pst = psr.next()
                                for par in range(2):
                                    k.op("pe", lambda e: e.matmul(pst.t[:, par * 512:(par + 1) * 512], KTall[par * 64:(par + 1) * 64, c, g, :], QT.t[par * 64:(par + 1) * 64, 4 * g:4 * g + 4, :], start=True, stop=True), R=[KTb[c], QT], W=[pst])
                                pt = PTr.next()
                                k.op("act", lambda e: e.activation(pt.t[:].rearrange("p a b c -> p (a b c)"), pst.t[:], AF.Exp, scale=0.125), R=[pst], W=[pt])
                                if mk is not None:
                                    k.op("pool", lambda e: e.tensor_tensor(pt.t[:].rearrange("p a b c -> p (a b) c"), pt.t[:].rearrange("p a b c -> p (a b) c"), amask.t[:, mk:mk + 1, :].to_broadcast([128, 8, 128]), ALU.mult), R=[pt, amask], W=[pt])
                                pts.append((pt, c))
                            pso = psr.next()
                            psov = pso.t[:].rearrange("p (a b) -> p a b", a=8)
                            for jj in range(4):
                                for par in range(2):
                                    hh = 2 * jj + par
                                    for ci, (pt, c) in enumerate(pts):
                                        k.op("pe", lambda e: e.matmul(psov[:, hh, 0:65], pt.t[:, par, jj, :], VAall[:, c, g, :], start=(ci == 0), stop=(ci == len(pts) - 1)), R=[pt, VAb[c]], W=[pso])
                            k.op("act", lambda e: e.activation(oall.t[:, 8 * g:8 * g + 8, :], psov[:, :, 0:65], AF.Copy), R=[pso], W=[oall])
                        k.op("dve", lambda e: e.tensor_tensor(den.t[:], oall.t[:, :, 64], esink.t[:], ALU.add), R=[oall, esink], W=[den])
                        k.op("dve", lambda e: e.reciprocal(den.t[:], den.t[:]), R=[den], W=[den])
                        k.op("dve", lambda e: e.tensor_tensor(onr.t[:], oall.t[:, :, 0:64], den.t[:].unsqueeze(2).to_broadcast([128, 16, 64]), ALU.mult), R=[oall, den], W=[onr])
                        k.op("act", lambda e: e.activation(za.t[:], za.t[:], AF.Silu), R=[za], W=[za])
                        k.op("dve", lambda e: e.tensor_tensor(og.t[:, 0:1024], onr.t[:].rearrange("p a b -> p (a b)"), za.t[:], ALU.mult), R=[onr, za], W=[og])
                        k.dma("sp", ogs[tau * 128:(tau + 1) * 128, :], og.t[:], R=[og])
                k.barrier()
            if phase_done():
                return

        def phase_odd(j):
            with ExitStack() as es:
                NW = G * 128
                cw = sb(es, "o1_cw", [128, 48, 5], F32)
                with nc.allow_non_contiguous_dma("conv weights, tiny"):
                    for kk in range(5):
                        k.dma("sp", cw.t[:, :, kk], c_conv[j, kk, :].rearrange("(c p) -> p c", p=128), W=[cw])
                xr = Ring([sb(es, "o1_x%d" % i, [128, NW + 4], F32) for i in range(2)])
                acr = Ring([sb(es, "o1_ac%d" % i, [128, NW], F32) for i in range(2)])
                xl = sb(es, "o1_xl", [128, 4], F32)
                sq = sb(es, "o1_sq", [128, NW], BF16)
                rn = sb(es, "o1_rn", [128, NW], F32)
                xnr = Ring([sb(es, "o1_xn%d" % i, [128, G, 128], BF16) for i in range(2)])
                tmr = Ring([sb(es, "o1_tm%d" % i, [128, G, 128], BF16) for i in range(2)])
                psr = Ring([psb(es, "o1_ps%d" % i, [128, 1024]) for i in range(4)])
                for sp in range(NSUP):
                    col0 = sp * NW
                    for cc in range(48):
                        xb = xr.next()
                        lo = col0 - 2
                        hi = col0 + NW + 2
                        if sp == 0:
                            k.op("pool", lambda e: e.memset(xb.t[:, 0:2], 0.0), W=[xb])
                            lo = col0
                        if sp == NSUP - 1:
                            k.op("pool", lambda e: e.memset(xb.t[:, NW + 2:NW + 4], 0.0), W=[xb])
                            hi = col0 + NW
                        k.dma("sp", xb.t[:, lo - (col0 - 2):hi - (col0 - 2)], xcT[cc * 128:(cc + 1) * 128, lo:hi], W=[xb])
                        ac = acr.next()
                        k.op("dve", lambda e: e.tensor_scalar(ac.t[:], xb.t[:, 0:NW], cw.t[:, cc, 0:1], None, ALU.mult), R=[xb, cw], W=[ac])
                        for kk in range(1, 5):
                            k.op("dve", lambda e: e.scalar_tensor_tensor(ac.t[:], xb.t[:, kk:kk + NW], cw.t[:, cc, kk:kk + 1], ac.t[:], ALU.mult, ALU.add), R=[xb, cw, ac], W=[ac])
                        if sp == LA // G:
                            cA = (LA + 1) * 128 - 1 - col0
                            cB = LB * 128 - col0
                            k.op("dve", lambda e: e.tensor_scalar(xl.t[:, 0:2], xb.t[:, cA + 1:cA + 3], linkc.t[:, 0:1], None, ALU.mult), R=[xb, linkc], W=[xl])
                            k.op("dve", lambda e: e.tensor_scalar(xl.t[:, 2:4], xb.t[:, cB + 2:cB + 4], linkc.t[:, 0:1], None, ALU.mult), R=[xb, linkc], W=[xl])
                            fix = [(cA, 2, 3), (cA, 3, 4), (cA - 1, 2, 4), (cB, 1, 1), (cB, 0, 0), (cB + 1, 1, 0)]
                            for (col, xi, wk) in fix:
                                k.op("dve", lambda e: e.scalar_tensor_tensor(ac.t[:, col:col + 1], xl.t[:, xi:xi + 1], cw.t[:, cc, wk:wk + 1], ac.t[:, col:col + 1], ALU.mult, ALU.add), R=[xl, cw, ac], W=[ac])
                        k.op("act", lambda e: e.activation(ac.t[:], ac.t[:], AF.Silu), R=[ac], W=[ac])
                        xn = xnr.next()
                        xnf = xn.t[:].rearrange("p a b -> p (a b)")
                        if cc < 32:
                            head = cc % 16
                            k.op("pool", lambda e: e.tensor_tensor(sq.t[:], ac.t[:], ac.t[:], ALU.mult), R=[ac], W=[sq])
                            ps = psr.next()
                            nn = NW // 2
                            for half in range(2):
                                k.op("pe", lambda e: e.matmul(ps.t[:, half * 512:half * 512 + nn], onesb.t[:], sq.t[:, half * nn:(half + 1) * nn], start=True, stop=True), R=[onesb, sq], W=[ps])
                            rnv = rn.t[:].rearrange("p (a b) -> p a b", a=2)
                            psv = ps.t[:].rearrange("p (a b) -> p a b", a=2)[:, :, 0:nn]
                            k.op("act", lambda e: e.activation(rnv, psv, AF.Sqrt, bias=EPS), R=[ps], W=[rn])
                            k.op("dve", lambda e: e.reciprocal(rn.t[:], rn.t[:]), R=[rn], W=[rn])
                            scl = (128.0 ** -0.5) if cc < 16 else 1.0
                            k.op("dve", lambda e: e.scalar_tensor_tensor(xnf, ac.t[:], scl, rn.t[:], ALU.mult, ALU.mult), R=[ac, rn], W=[xn])
                            dst = qTs if cc < 16 else kTs
                            k.dma("sp", dst[sp * G:(sp + 1) * G, :, head, :].rearrange("t p c -> p t c"), xn.t[:], R=[xn])
                        else:
                            head = cc - 32
                            k.op("dve", lambda e: e.tensor_copy(xnf, ac.t[:]), R=[ac], W=[xn])
                        if cc >= 16:
                            ps = psr.next()
                            psv = ps.t[:, 0:G * 64].bitcast(BF16).rearrange("p (a b) -> p a b", a=G)
                            for tl in range(G):
                                k.op("pe", lambda e: e.transpose(psv[:, tl, :], xn.t[:, tl, :], identb.t[:]), R=[xn, identb], W=[ps])
                            tm = tmr.next()
                            evac(tm.t[:], psv, R=[ps], W=[tm])
                            dst = ktok if cc < 32 else vtok
                            k.dma("sp", dst[sp * NW:(sp + 1) * NW, head * 128:(head + 1) * 128].rearrange("(t p) c -> p t c", p=128), tm.t[:], R=[tm])
                k.barrier()
            if phase_done():
                return

            with ExitStack() as es:
                cst = sb(es, "d_cst", [128, 4, 16], F32)
                k.dma("sp", cst.t[:, 0, :], alog_f[j:j + 1, :].partition_broadcast(128), W=[cst])
                k.dma("sp", cst.t[:, 1, :], dtb_f[j:j + 1, :].partition_broadcast(128), W=[cst])
                k.dma("sp", cst.t[:, 2, :], alog_b[j:j + 1, :].partition_broadcast(128), W=[cst])
                k.dma("sp", cst.t[:, 3, :], dtb_b[j:j + 1, :].partition_broadcast(128), W=[cst])
                for a in (0, 2):
                    k.op("act", lambda e: e.activation(cst.t[:, a, :], cst.t[:, a, :], AF.Exp), R=[cst], W=[cst])
                    k.op("dve", lambda e: e.tensor_scalar(cst.t[:, a, :], cst.t[:, a, :], -1.0, None, ALU.mult), R=[cst], W=[cst])
                hnb = sb(es, "d_hn", [128, 1, 128], F32)
                k.dma("sp", hnb.t[:, 0, :], c_hnorm[j:j + 1, :].partition_broadcast(128), W=[hnb])
                psr = Ring([psb(es, "d_ps%d" % i, [128, 1024]) for i in range(4)])
                QKr = Ring([sb(es, "d_QK%d" % i, [128, 16, 2, 128], BF16) for i in range(2)])
                ktr = Ring([sb(es, "d_kt%d" % i, [128, 16, 128], BF16) for i in range(2)])
                Rbr = Ring([sb(es, "d_Rb%d" % i, [128, 16, 256], BF16) for i in range(2)])
                zzr = Ring([sb(es, "d_zz%d" % i, [128, 64], F32) for i in range(2)])
                gt = sb(es, "d_gt", [128, 16], F32)
                gg = sb(es, "d_gg", [128, 16], F32)
                ghl = sb(es, "d_ghl", [128, 2, 16], BF16)
                GUl = sb(es, "d_GUl", [128, 16, 128], BF16)
                GUh = sb(es, "d_GUh", [128, 16, 128], BF16)
                bt = sb(es, "d_bt", [128, 16], F32)
                nbt = sb(es, "d_nbt", [128, 16], F32)
                ecum = sb(es, "d_ecum", [128, 48], F32)
                GU = sb(es, "d_GU", [128, 16, 128], F32)
                EX = sb(es, "d_EX", [128, 16, 128], F32)
                Ys = [[sb(es, "d_Y%d%d" % (a, b), [128, 16, 128], BF16) for b in range(2)] for a in range(2)]
                AQ = sb(es, "d_AQ", [128, 16, 128], BF16)
                Ao = sb(es, "d_Ao", [128, 16, 128], BF16)
                Qm = sb(es, "d_Qm", [128, 16, 128], BF16)
                Yx = sb(es, "d_Yx", [128, 16, 128], BF16)
                Ub = sb(es, "d_Ub", [128, 16, 256], BF16)
                Rf = sb(es, "d_Rf", [128, 16, 256], F32)
                Wb_ = sb(es, "d_Wb", [128, 16, 128], BF16)
                WT = sb(es, "d_WT", [128, 16, 128], BF16)
                KG = sb(es, "d_KG", [128, 16, 128], BF16)
                vnb = sb(es, "d_vnb", [128, 16, 128], BF16)
                osr = Ring([sb(es, "d_os%d" % i, [128, 16, 128], F32) for i in range(2)])
                Sf = sb(es, "d_S", [128, 16, 128], F32)
                Sb = sb(es, "d_Sb", [128, 16, 128], BF16)
                for b_ in [Ys[0][0], Ys[0][1], Ys[1][0], Ys[1][1], Yx, Qm, Ub, Rf, vnb, Sf, Sb, AQ, Ao] + osr.b:
                    b_.q = [Buf(b_.t) for _ in range(4)]

                def dn_reset():
                    k.op("dve", lambda e: e.memset(Sf.t[:], 0.0), W=Sf.q)
                    k.op("dve", lambda e: e.memset(Sb.t[:], 0.0), W=Sb.q)

                def dn_link():
                    k.op("dve", lambda e: e.tensor_scalar(Sf.t[:], Sf.t[:], linkc.t[:, 0:1], None, ALU.mult), R=Sf.q + [linkc], W=Sf.q)
                    k.op("act", lambda e: e.activation(Sb.t[:], Sf.t[:], AF.Copy), R=Sf.q, W=Sb.q)

                def dn_load(tau):
                    QK = QKr.next()
                    k.dma("sp", QK.t[:, :, 0, :], kTs[tau], W=[QK])
                    k.dma("sp", QK.t[:, :, 1, :], qTs[tau], W=[QK])
                    kt = ktr.next()
                    k.dma("sp", kt.t[:], ktok[tau * 128:(tau + 1) * 128, :].rearrange("p (h c) -> p h c", h=16), W=[kt])
                    Rb = Rbr.next()
                    k.dma("sp", Rb.t[:, :, 0:128], vtok[tau * 128:(tau + 1) * 128, :].rearrange("p (h c) -> p h c", h=16), W=[Rb])
                    zz = zzr.next()
                    k.dma("sp", zz.t[:], zt[tau * 128:(tau + 1) * 128, 2048:2112], W=[zz])
                    return QK, kt, Rb, zz

                def dn_step(tau, d, loaded):
                    QK, kt, Rb, zz = loaded
                    TA = U_LE if d == 0 else U_GE
                    TB = U_GT if d == 0 else U_LT
                    TS = U_LT if d == 0 else U_GT
                    a_ap = zz.t[:, 32 * d:32 * d + 16]
                    b_ap = zz.t[:, 32 * d + 16:32 * d + 32]
                    isM = tau in (M0, M1)
                    mi = 0 if tau == M0 else 1
                    k.op("dve", lambda e: e.tensor_tensor(gt.t[:], a_ap, cst.t[:, 2 * d + 1, :], ALU.add), R=[zz, cst], W=[gt])
                    k.op("act", lambda e: e.activation(gt.t[:], gt.t[:], AF.Exp), R=[gt], W=[gt])
                    k.op("act", lambda e: e.activation(gt.t[:], gt.t[:], AF.Ln, bias=1.0), R=[gt], W=[gt])
                    k.op("dve", lambda e: e.tensor_tensor(gg.t[:], gt.t[:], cst.t[:, 2 * d, :], ALU.mult), R=[gt, cst], W=[gg])
                    k.op("act", lambda e: e.activation(bt.t[:], b_ap, AF.Exp, scale=-1.0), R=[zz], W=[bt])
                    k.op("dve", lambda e: e.tensor_scalar(bt.t[:], bt.t[:], 1.0, None, ALU.add), R=[bt], W=[bt])
                    k.op("dve", lambda e: e.reciprocal(bt.t[:], bt.t[:]), R=[bt], W=[bt])
                    if isM:
                        k.op("dve", lambda e: e.tensor_scalar(gg.t[:], gg.t[:], vmask.t[:, mi:mi + 1], None, ALU.mult), R=[gg, vmask], W=[gg])
                        k.op("dve", lambda e: e.tensor_scalar(bt.t[:], bt.t[:], vmask.t[:, mi:mi + 1], None, ALU.mult), R=[bt, vmask], W=[bt])
                    k.op("dve", lambda e: e.tensor_scalar(nbt.t[:], bt.t[:], -1.0, None, ALU.mult), R=[bt], W=[nbt])
                    pc = psr.next()
                    k.op("dve", lambda e: e.tensor_copy(ghl.t[:, 0, :], gg.t[:]), R=[gg], W=[ghl])
                    k.op("dve", lambda e: e.tensor_tensor(ghl.t[:, 1, :], gg.t[:], ghl.t[:, 0, :], ALU.subtract), R=[gg, ghl], W=[ghl])
                    for hl in range(2):
                        k.op("pe", lambda e: e.matmul(pc.t[:, 0:16], trib.t[:, TA, :], ghl.t[:, hl, :], start=(hl == 0), stop=(hl == 1)), R=[trib, ghl], W=[pc])
                    for hl in range(2):
                        k.op("pe", lambda e: e.matmul(pc.t[:, 16:32], trib.t[:, TB, :], ghl.t[:, hl, :], start=(hl == 0), stop=(hl == 1)), R=[trib, ghl], W=[pc])
                    for hl in range(2):
                        k.op("pe", lambda e: e.matmul(pc.t[:, 32:48], onesb.t[:], ghl.t[:, hl, :], start=(hl == 0), stop=(hl == 1)), R=[onesb, ghl], W=[pc])
                    k.op("act", lambda e: e.activation(ecum.t[:], pc.t[:, 0:48], AF.Exp), R=[pc], W=[ecum])
                    egam = ecum.t[:, 0:16]
                    erest = ecum.t[:, 16:32]
                    etot = ecum.t[:, 32:48]
                    k.op("pool", lambda e: e.tensor_tensor(GUh.t[:], trib.t[:, TA:TA + 1, :].to_broadcast([128, 16, 128]), ghl.t[:, 0, :].unsqueeze(2).to_broadcast([128, 16, 128]), ALU.mult), R=[trib, ghl], W=[GUh])
                    k.op("pool", lambda e: e.tensor_tensor(GUl.t[:], trib.t[:, TA:TA + 1, :].to_broadcast([128, 16, 128]), ghl.t[:, 1, :].unsqueeze(2).to_broadcast([128, 16, 128]), ALU.mult), R=[trib, ghl], W=[GUl])
                    pe_ = [psr.next(), psr.next()]
                    for hq in range(4):
                        pp = pe_[hq // 2]
                        k.op("pe", lambda e: e.matmul(pp.t[:, (hq % 2) * 512:(hq % 2) * 512 + 512], trib.t[:, TB, :], GUh.t[:, 4 * hq:4 * hq + 4, :], start=True, stop=False), R=[trib, GUh], W=[pp])
                        k.op("pe", lambda e: e.matmul(pp.t[:, (hq % 2) * 512:(hq % 2) * 512 + 512], trib.t[:, TB, :], GUl.t[:, 4 * hq:4 * hq + 4, :], start=False, stop=True), R=[trib, GUl], W=[pp])
                    for i2 in range(2):
                        k.op("act", lambda e: e.activation(EX.t[:, 8 * i2:8 * i2 + 8, :].rearrange("p a b -> p (a b)"), pe_[i2].t[:], AF.Exp), R=[pe_[i2]], W=[EX])
                    k.op("dve", lambda e: e.tensor_tensor(GU.t[:], EX.t[:], tri.t[:, TS:TS + 1, :].to_broadcast([128, 16, 128]), ALU.mult), R=[EX, tri], W=[GU])
                    k.op("pool", lambda e: e.tensor_tensor(GU.t[:], GU.t[:], nbt.t[:].unsqueeze(2).to_broadcast([128, 16, 128]), ALU.mult), R=[GU, nbt], W=[GU])
                    k.op("dve", lambda e: e.tensor_tensor(EX.t[:], EX.t[:], tri.t[:, TA:TA + 1, :].to_broadcast([128, 16, 128]), ALU.mult), R=[EX, tri], W=[EX])
                    YT0, Y0 = Ys[0]
                    for hq in range(4):
                        pk = psr.next()
                        pkv = pk.t[:].rearrange("p (a b c) -> p a b c", a=4, b=2)
                        for hl in range(4):
                            h = 4 * hq + hl
                            k.op("pe", lambda e: e.matmul(pk.t[:, hl * 256:(hl + 1) * 256], QK.t[:, h, 0, :], QK.t[:, h, :, :].rearrange("p a b -> p (a b)"), start=True, stop=True), R=[QK], W=[pk])
                        k.op("dve", lambda e: e.tensor_tensor(YT0.t[:, 4 * hq:4 * hq + 4, :], pkv[:, :, 0, :], GU.t[:, 4 * hq:4 * hq + 4, :], ALU.mult), R=[pk, GU], W=[YT0.q[hq]])
                        k.op("dve", lambda e: e.tensor_tensor(AQ.t[:, 4 * hq:4 * hq + 4, :], pkv[:, :, 1, :], EX.t[:, 4 * hq:4 * hq + 4, :], ALU.mult), R=[pk, EX], W=[AQ.q[hq]])
                    k.op("pool", lambda e: e.tensor_tensor(Ao.t[:], YT0.t[:], trib.t[:, BDM:BDM + 1, :].to_broadcast([128, 16, 128]), ALU.mult), R=YT0.q + [trib], W=Ao.q)
                    k.op("dve", lambda e: e.tensor_tensor(YT0.t[:], YT0.t[:], Ao.t[:], ALU.subtract), R=YT0.q + Ao.q, W=YT0.q)
                    AdT, AoT = Ao, YT0
                    YTc, Yc = Ys[1]
                    for i2 in range(2):
                        pt = psr.next()
                        ptv = pt.t[:, 0:512].bitcast(BF16).rearrange("p (a b) -> p a b", a=8)
                        for hl in range(8):
                            h = 8 * i2 + hl
                            k.op("pe", lambda e: e.transpose(ptv[:, hl, :], AdT.t[:, h, :], identb.t[:]), R=[AdT.q[2 * i2], AdT.q[2 * i2 + 1], identb], W=[pt])
                        evac(Yc.t[:, 8 * i2:8 * i2 + 8, :], ptv, R=[pt], W=[Yc.q[2 * i2], Yc.q[2 * i2 + 1]])
                    k.op("pool", lambda e: e.tensor_copy(YTc.t[:], AdT.t[:]), R=AdT.q, W=YTc.q)
                    k.op("dve", lambda e: e.tensor_tensor(Qm.t[:], AdT.t[:], identb.t[:].unsqueeze(1).to_broadcast([128, 16, 128]), ALU.add), R=AdT.q + [identb], W=Qm.q)
                    k.op("act", lambda e: e.activation(Rf.t[:, :, 0:128], Rb.t[:, :, 0:128], AF.Copy), R=[Rb], W=Rf.q)
                    k.op("dve", lambda e: e.tensor_tensor(Rf.t[:, :, 128:256], kt.t[:], egam.unsqueeze(2).to_broadcast([128, 16, 128]), ALU.mult), R=[kt, ecum], W=Rf.q)
                    k.op("pool", lambda e: e.tensor_copy(Rb.t[:, :, 128:256], Rf.t[:, :, 128:256]), R=Rf.q, W=[Rb])
                    cur = 1
                    ysets = [(Yx, Ys[0][1]), Ys[1]]
                    for lv in range(1, 5):
                        YT, Y = ysets[cur]
                        YTn, Yn = ysets[1 - cur]
                        for hq in range(4):
                            py = psr.next()
                            pyv = py.t[:].rearrange("p (a b c) -> p a b c", a=4, b=2)
                            for hl in range(4):
                                h = 4 * hq + hl
                                k.op("pe", lambda e: e.matmul(pyv[:, hl, 0, :], Y.t[:, h, :], YT.t[:, h, :], start=True, stop=True), R=[Y.q[hq], YT.q[hq]], W=[py])
                                k.op("pe", lambda e: e.matmul(pyv[:, hl, 1, :], YT.t[:, h, :], Y.t[:, h, :], start=True, stop=True), R=[Y.q[hq], YT.q[hq]], W=[py])
                            k.op("act", lambda e: e.activation(YTn.t[:, 4 * hq:4 * hq + 4, :], pyv[:, :, 0, :], AF.Copy), R=[py], W=[YTn.q[hq]])
                            k.op("dve", lambda e: e.tensor_copy(Yn.t[:, 4 * hq:4 * hq + 4, :], pyv[:, :, 1, :]), R=[py], W=[Yn.q[hq]])
                        for hq in range(4):
                            pq_ = psr.next()
                            pqv = pq_.t[:, 0:512].rearrange("p (a b) -> p a b", a=4)
                            for hl in range(4):
                                h = 4 * hq + hl
                                k.op("pe", lambda e: e.matmul(pqv[:, hl, :], Yn.t[:, h, :], Qm.t[:, h, :], start=True, stop=True), R=[Yn.q[hq], Qm.q[hq]], W=[pq_])
                            k.op("dve", lambda e: e.tensor_tensor(Qm.t[:, 4 * hq:4 * hq + 4, :], Qm.t[:, 4 * hq:4 * hq + 4, :], pqv, ALU.add), R=[Qm.q[hq], pq_], W=[Qm.q[hq]])
                        cur = 1 - cur
                    for it in range(4):
                        for hq in range(4):
                            if it == 0:
                                zsrc = Rb
                            else:
                                pz = psr.next()
                                pzv = pz.t[:].rearrange("p (a b) -> p a b", a=4)
                                for hl in range(4):
                                    h = 4 * hq + hl
                                    k.op("pe", lambda e: e.matmul(pzv[:, hl, :], AoT.t[:, h, :], Ub.t[:, h, :], start=True, stop=True), R=[AoT.q[hq], Ub.q[hq]], W=[pz])
                                k.op("dve", lambda e: e.tensor_tensor(Ub.t[:, 4 * hq:4 * hq + 4, :], Rf.t[:, 4 * hq:4 * hq + 4, :], pzv, ALU.add), R=[Rf.q[hq], pz], W=[Ub.q[hq]])
                                zsrc = Ub
                            pu = psr.next()
                            puv = pu.t[:].rearrange("p (a b) -> p a b", a=4)
                            for hl in range(4):
                                h = 4 * hq + hl
                                k.op("pe", lambda e: e.matmul(puv[:, hl, :], Qm.t[:, h, :], zsrc.t[:, h, :], start=True, stop=True), R=[Qm.q[hq], (zsrc.q[hq] if zsrc.q else zsrc)], W=[pu])
                            if it < 3:
                                k.op("act", lambda e: e.activation(Ub.t[:, 4 * hq:4 * hq + 4, :], puv, AF.Copy), R=[pu], W=[Ub.q[hq]])
                            else:
                                k.op("act", lambda e: e.activation(Rf.t[:, 4 * hq:4 * hq + 4, :], puv, AF.Copy), R=[pu], W=[Rf.q[hq]])
                    bbc = bt.t[:].unsqueeze(2).to_broadcast([128, 16, 128])
                    k.op("dve", lambda e: e.tensor_tensor(Rf.t[:, :, 0:128], Rf.t[:, :, 0:128], bbc, ALU.mult), R=Rf.q + [bt], W=Rf.q)
                    k.op("pool", lambda e: e.tensor_tensor(Wb_.t[:], Rf.t[:, :, 128:256], bbc, ALU.mult), R=Rf.q + [bt], W=[Wb_])
                    for i2 in range(2):
                        pt = psr.next()
                        ptv = pt.t[:, 0:512].bitcast(BF16).rearrange("p (a b) -> p a b", a=8)
                        for hl in range(8):
                            h = 8 * i2 + hl
                            k.op("pe", lambda e: e.transpose(ptv[:, hl, :], Wb_.t[:, h, :], identb.t[:]), R=[Wb_, identb], W=[pt])
                        evac(WT.t[:, 8 * i2:8 * i2 + 8, :], ptv, R=[pt], W=[WT])
                    k.op("pool", lambda e: e.tensor_tensor(KG.t[:], kt.t[:], erest.unsqueeze(2).to_broadcast([128, 16, 128]), ALU.mult), R=[kt, ecum], W=[KG])
                    osb = osr.next()
                    p12 = []
                    for hq in range(4):
                        pp = psr.next()
                        ppv = pp.t[:].rearrange("p (x a b) -> p x a b", x=2, a=4)
                        for hl in range(4):
                            h = 4 * hq + hl
                            k.op("pe", lambda e: e.matmul(ppv[:, 0, hl, :], WT.t[:, h, :], Sb.t[:, h, :], start=True, stop=True), R=[WT, Sb.q[hq]], W=[pp])
                            k.op("pe", lambda e: e.matmul(ppv[:, 1, hl, :], QK.t[:, h, 1, :], Sb.t[:, h, :], start=True, stop=True), R=[QK, Sb.q[hq]], W=[pp])
                        k.op("dve", lambda e: e.tensor_tensor(vnb.t[:, 4 * hq:4 * hq + 4, :], Rf.t[:, 4 * hq:4 * hq + 4, 0:128], ppv[:, 0, :, :], ALU.subtract), R=[Rf.q[hq], pp], W=[vnb.q[hq]])
                        k.op("dve", lambda e: e.tensor_tensor(osb.t[:, 4 * hq:4 * hq + 4, :], ppv[:, 1, :, :], egam[:, 4 * hq:4 * hq + 4].unsqueeze(2).to_broadcast([128, 4, 128]), ALU.mult), R=[pp, ecum], W=[osb.q[hq]])
                        p12.append(pp)
                    for hq in range(4):
                        pp = psr.next()
                        ppv = pp.t[:].rearrange("p (x a b) -> p x a b", x=2, a=4)
                        for hl in range(4):
                            h = 4 * hq + hl
                            k.op("pe", lambda e: e.matmul(ppv[:, 0, hl, :], AQ.t[:, h, :], vnb.t[:, h, :], start=True, stop=True), R=[AQ.q[hq], vnb.q[hq]], W=[pp])
                            k.op("pe", lambda e: e.matmul(ppv[:, 1, hl, :], KG.t[:, h, :], vnb.t[:, h, :], start=True, stop=True), R=[KG, vnb.q[hq]], W=[pp])
                        k.op("dve", lambda e: e.tensor_tensor(osb.t[:, 4 * hq:4 * hq + 4, :], osb.t[:, 4 * hq:4 * hq + 4, :], ppv[:, 0, :, :], ALU.add), R=[osb.q[hq], pp], W=[osb.q[hq]])
                        k.op("pool", lambda e: e.tensor_tensor(Sf.t[:, 4 * hq:4 * hq + 4, :], Sf.t[:, 4 * hq:4 * hq + 4, :], etot[:, 4 * hq:4 * hq + 4].unsqueeze(2).to_broadcast([128, 4, 128]), ALU.mult), R=[Sf.q[hq], ecum], W=[Sf.q[hq]])
                        k.op("dve", lambda e: e.tensor_tensor(Sf.t[:, 4 * hq:4 * hq + 4, :], Sf.t[:, 4 * hq:4 * hq + 4, :], ppv[:, 1, :, :], ALU.add), R=[Sf.q[hq], pp], W=[Sf.q[hq]])
                        k.op("act", lambda e: e.activation(Sb.t[:, 4 * hq:4 * hq + 4, :], Sf.t[:, 4 * hq:4 * hq + 4, :], AF.Copy), R=[Sf.q[hq]], W=[Sb.q[hq]])
                    return osb

                dn_reset()
                nxt = dn_load(0)
                for tau in range(NTILE):
                    cur = nxt
                    if tau + 1 < NTILE:
                        nxt = dn_load(tau + 1)
                    if tau == M1:
                        dn_link()
                    osb = dn_step(tau, 0, cur)
                    k.dma("sp", ofs[tau * 128:(tau + 1) * 128, :], osb.t[:].rearrange("p a b -> p (a b)"), R=osb.q)
                k.barrier()
                if phase_done():
                    return

                ofr = Ring([sb(es, "d_of%d" % i, [128, 16, 128], F32) for i in range(1)])
                zcr = Ring([sb(es, "d_zc%d" % i, [128, D], F32) for i in range(1)])
                osq = GU
                oss = sb(es, "d_oss", [128, 16], F32)
                ogr = Ring([sb(es, "d_og%d" % i, [128, D], BF16) for i in range(2)])

                dn_reset()
                nxt = dn_load(NTILE - 1)
                for tau in range(NTILE - 1, -1, -1):
                    ld = nxt
                    if tau - 1 >= 0:
                        nxt = dn_load(tau - 1)
                    of_t = ofr.next()
                    k.dma("sp", of_t.t[:].rearrange("p a b -> p (a b)"), ofs[tau * 128:(tau + 1) * 128, :], W=[of_t])
                    zc = zcr.next()
                    k.dma("sp", zc.t[:], zt[tau * 128:(tau + 1) * 128, 0:2048], W=[zc])
                    if tau == LA:
                        dn_link()
                    osb = dn_step(tau, 1, ld)
                    k.op("dve", lambda e: e.tensor_tensor(osb.t[:], osb.t[:], of_t.t[:], ALU.add), R=osb.q + [of_t], W=osb.q)
                    k.op("pool", lambda e: e.tensor_tensor(osq.t[:], osb.t[:], osb.t[:], ALU.mult), R=osb.q, W=[osq])
                    k.op("dve", lambda e: e.tensor_reduce(oss.t[:], osq.t[:], AX.X, ALU.add), R=[osq], W=[oss])
                    rstd_from_ss(oss, 1.0 / 128.0)
                    k.op("dve", lambda e: e.tensor_tensor(osb.t[:], osb.t[:], oss.t[:].unsqueeze(2).to_broadcast([128, 16, 128]), ALU.mult), R=osb.q + [oss], W=osb.q)
                    k.op("pool", lambda e: e.tensor_tensor(osb.t[:], osb.t[:], hnb.t[:, 0:1, :].to_broadcast([128, 16, 128]), ALU.mult), R=osb.q + [hnb], W=osb.q)
                    k.op("act", lambda e: e.activation(zc.t[:], zc.t[:], AF.Silu), R=[zc], W=[zc])
                    og = ogr.next()
                    k.op("dve", lambda e: e.tensor_tensor(og.t[:], osb.t[:].rearrange("p a b -> p (a b)"), zc.t[:], ALU.mult), R=osb.q + [zc], W=[og])
                    k.dma("sp", ogs[tau * 128:(tau + 1) * 128, :], og.t[:], R=[og])
                k.barrier()
            if phase_done():
                return

        try:
          for layer in range(4):
            j = layer // 2
            hsrc = hin if layer == 0 else hs
            if layer % 2 == 0:
                for ph in (lambda: phase_in(hsrc, norm_even[j:j + 1, :], wie[j], EVEN_IN, 0), lambda: phase_even(j), lambda: phase_out(hsrc, woe[j], False)):
                    if not stopped[0]:
                        ph()
            else:
                for ph in (lambda: phase_in(hsrc, norm_odd[j:j + 1, :], wio[j], ODD_IN, 6144), lambda: phase_odd(j), lambda: phase_out(hsrc, woo[j], layer == 3)):
                    if not stopped[0]:
                        ph()
        except _Stop:
            pass
        k.barrier()
        if debug:
            for nm in debug:
                src = scr_all[nm]
                dst = nc.dram_tensor("dbg_" + nm, list(src.shape), src.dtype, kind="ExternalOutput").ap()
                n0 = src.shape[0]
                stp = max(1, n0 // 8)
                for r in range(0, n0, stp):
                    k.dma("sp", dst[r:r + stp], src[r:r + stp])
            k.barrier()
    build.ninstr = k.nins
    build.nop = k.nop_
    return nc


def _core_inputs(xs, meta, S_seg, is_prompt):
    NT_SEG = S_seg // 128
    NTILE = 2 * (NT_SEG + 1)
    T = NTILE * 128
    hin = np.zeros((T, D), np.float32)
    pos = np.zeros((T,), np.float32)
    m0 = 0
    m1 = (NT_SEG + 1) * 128
    r0 = 128
    r1 = (NT_SEG + 2) * 128
    hin[m0 + 112:m0 + 128] = meta
    pos[m0 + 112:m0 + 128] = np.arange(16)
    vm = np.zeros((128, 2), np.float32)
    vm[112:, 0] = 1.0
    if is_prompt:
        x = xs[0]
        hin[r0:r0 + S_seg] = x[:S_seg]
        hin[r1:r1 + S_seg] = x[S_seg:]
        pos[r0:r0 + S_seg] = 16 + np.arange(S_seg)
        pos[r1:r1 + S_seg] = 16 + S_seg + np.arange(S_seg)
        link = 1.0
    else:
        hin[r0:r0 + S_seg] = xs[0]
        hin[r1:r1 + S_seg] = xs[1]
        hin[m1 + 112:m1 + 128] = meta
        pos[m1 + 112:m1 + 128] = np.arange(16)
        pos[r0:r0 + S_seg] = 16 + np.arange(S_seg)
        pos[r1:r1 + S_seg] = 16 + np.arange(S_seg)
        vm[112:, 1] = 1.0
        link = 0.0
    inv = (1.0 / (np.float32(10000.0) ** (np.arange(0, 64, 2, dtype=np.float32) / np.float32(64)))).astype(np.float32)
    ang = pos[:, None].astype(np.float32) * inv[None]
    return {
        "hin": hin,
        "cosr": np.cos(ang).astype(np.float32),
        "sinr": np.sin(ang).astype(np.float32),
        "linkc": np.full((128, 1), link, np.float32),
        "vmask": vm,
    }


def _consts():
    s = np.arange(128)[:, None]
    i = np.arange(128)[None, :]
    tri = np.stack([(s <= i), (s < i), (s >= i), (s > i), (s // 32 == i // 32)]).astype(np.float32)
    return {"ident": np.eye(128, dtype=np.float32), "tri": tri}


_NC_CACHE = {}


def kernel(**inputs):
    xp = np.asarray(inputs["x_prompt"], np.float32)
    xsm = np.asarray(inputs["x_sample"], np.float32)
    nb_p, seq, _ = xp.shape
    nb_s, dseq, _ = xsm.shape
    assert seq == 2 * dseq and nb_s % 2 == 0
    S_seg = dseq
    NT_SEG = S_seg // 128
    meta = np.asarray(inputs["meta_tokens"], np.float32)
    shared = _consts()
    for name in ["norm_even", "w_in_even", "a_sink", "b_gate_up_fwd", "b_gate_bias_fwd", "b_gate_up_bwd",
                 "b_gate_bias_bwd", "b_head_norm", "w_out_even", "norm_odd", "w_in_odd", "c_conv", "c_a_log_fwd",
                 "c_dt_bias_fwd", "c_a_log_bwd", "c_dt_bias_bwd", "c_head_norm", "w_out_odd"]:
        shared[name] = np.ascontiguousarray(np.asarray(inputs[name], np.float32))
    shared["norm_final"] = np.ascontiguousarray(np.asarray(inputs["norm_final"], np.float32).reshape(1, D))
    in_maps = []
    for p in range(nb_p):
        m = dict(shared)
        m.update(_core_inputs([xp[p]], meta, S_seg, True))
        in_maps.append(m)
    for s in range(nb_s // 2):
        m = dict(shared)
        m.update(_core_inputs([xsm[2 * s], xsm[2 * s + 1]], meta, S_seg, False))
        in_maps.append(m)
    ncores = len(in_maps)
    if NT_SEG not in _NC_CACHE:
        _NC_CACHE[NT_SEG] = build(NT_SEG)
    nc = _NC_CACHE[NT_SEG]
    res = run_bass_kernel_spmd(nc, in_maps, core_ids=list(range(ncores)))
    outs = [np.asarray(r["y"], np.float32) for r in res.results]
    y_prompt = np.stack([outs[p].reshape(seq, D) for p in range(nb_p)], axis=0)
    y_sample = np.stack([outs[nb_p + s // 2].reshape(2, dseq, D)[s % 2] for s in range(nb_s)], axis=0)
    return (y_prompt, y_sample)
```

```python
import numpy as np
from contextlib import ExitStack
import concourse.bass as bass
import concourse.mybir as mybir
from concourse.bass_utils import run_bass_kernel_spmd

F32 = mybir.dt.float32
BF16 = mybir.dt.bfloat16
AF = mybir.ActivationFunctionType
ALU = mybir.AluOpType
AX = mybir.AxisListType

D = 2048
EPS = 1e-6
N_META = 16
EVEN_IN = 5408
ODD_IN = 8256
G = 6
NDMA = 48


class Buf:
    __slots__ = ("t", "w", "r", "ex", "q")

    def __init__(self, t, ex=False):
        self.t = t
        self.w = None
        self.r = []
        self.ex = ex
        self.q = None


class Ring:
    def __init__(self, bufs):
        self.b = bufs
        self.i = 0

    def next(self):
        b = self.b[self.i % len(self.b)]
        self.i += 1
        return b


class K:
    def __init__(self, nc, es):
        self.nc = nc
        self.es = es
        self.eng = {"pe": nc.tensor, "dve": nc.vector, "act": nc.scalar, "pool": nc.gpsimd, "sp": nc.sync}
        self.sem = {n: es.enter_context(nc.semaphore("s_" + n)) for n in ["pe", "dve", "act", "pool"]}
        self.cnt = {n: 0 for n in self.sem}
        self.waited = {}
        self.dslots = [[es.enter_context(nc.semaphore("d%d" % i)), 0] for i in range(NDMA)]
        self.ndma = 0
        self.nins = 0
        import os
        self.limit = int(os.environ.get("KLIMIT", "0")) or None
        self.nop_ = 0
        self.dbgops = set(int(x) for x in os.environ.get("KDBG", "").split(",") if x)

    def _wait(self, on, deps):
        e = self.eng[on]
        for d in deps:
            if d is None:
                continue
            key, val = d
            if key == on and on == "pe":
                continue
            if self.waited.get((on, key), 0) >= val:
                continue
            sem = self.sem[key] if isinstance(key, str) else self.dslots[key][0]
            if self.nop_ in self.dbgops:
                print("DBGWAIT op", self.nop_, "on", on, "waits", key, val, "cnt", dict(self.cnt))
            e.wait_ge(sem, val)
            self.nins += 1
            self.waited[(on, key)] = val

    def _deps(self, R, W):
        deps = []
        for b in R:
            deps.append(b.w)
            if b.ex:
                deps.extend(b.r)
        for b in W:
            deps.append(b.w)
            deps.extend(b.r)
        return deps

    def _commit(self, tok, R, W):
        for b in R:
            b.r = [t for t in b.r if t[0] != tok[0]] + [tok]
        for b in W:
            b.w = tok
            b.r = []

    def op(self, on, fn, R=(), W=()):
        self.nop_ += 1
        if self.limit is not None and self.nop_ > self.limit:
            return None
        self._wait(on, self._deps(R, W))
        ins = fn(self.eng[on])
        self.cnt[on] += 1
        ins.then_inc(self.sem[on], 1)
        self.nins += 1
        tok = (on, self.cnt[on])
        self._commit(tok, R, W)
        return tok

    def dma(self, on, out, in_, R=(), W=(), **kw):
        self.nop_ += 1
        if self.limit is not None and self.nop_ > self.limit and not str(getattr(out.tensor, "name", "")).startswith("dbg_"):
            return None
        i = self.ndma % NDMA
        self.ndma += 1
        slot = self.dslots[i]
        self._wait(on, self._deps(R, W) + [(i, slot[1])])
        slot[1] += 16
        self.eng[on].dma_start(out=out, in_=in_, **kw).then_inc(slot[0], 16)
        self.nins += 1
        tok = (i, slot[1])
        self._commit(tok, R, W)
        return tok

    def barrier(self):
        deps = [(n, c) for n, c in self.cnt.items() if c > 0]
        deps += [(i, s[1]) for i, s in enumerate(self.dslots) if s[1] > 0]
        for on in self.eng:
            self._wait(on, deps)


class _Stop(Exception):
    pass


def build(NT_SEG, debug=False, stop_at=None):
    NTILE = 2 * (NT_SEG + 1)
    T = NTILE * 128
    assert NTILE % G == 0
    NSUP = NTILE // G
    M0, M1 = 0, NT_SEG + 1
    LA, LB = NT_SEG, NT_SEG + 2
    assert LA // G == LB // G
    NREAL = 2 * NT_SEG

    nc = bass.Bass("TRN2", target_bir_lowering=False)

    def din(name, shape, dt=F32):
        return nc.dram_tensor(name, list(shape), dt, kind="ExternalInput").ap()

    def dscr(name, shape, dt):
        t = nc.dram_tensor(name, list(shape), dt, kind="Internal").ap()
        scr_all[name] = t
        return t

    scr_all = {}

    phase_no = [0]

    def phase_done():
        phase_no[0] += 1
        if stop_at is not None and phase_no[0] >= stop_at:
            stopped[0] = True
        return stopped[0]

    stopped = [False]

    hin = din("hin", [T, D])
    cosr = din("cosr", [T, 32])
    sinr = din("sinr", [T, 32])
    linkc_d = din("linkc", [128, 1])
    vmask_d = din("vmask", [128, 2])
    ident_d = din("ident", [128, 128])
    tri_d = din("tri", [5, 128, 128])
    norm_even = din("norm_even", [2, D])
    w_in_even = din("w_in_even", [2, D, EVEN_IN])
    a_sink = din("a_sink", [2, 16])
    gu_f = din("b_gate_up_fwd", [2, 16, 512])
    gb_f = din("b_gate_bias_fwd", [2, 512])
    gu_b = din("b_gate_up_bwd", [2, 16, 512])
    gb_b = din("b_gate_bias_bwd", [2, 512])
    b_hnorm = din("b_head_norm", [2, 256])
    w_out_even = din("w_out_even", [2, D, D])
    norm_odd = din("norm_odd", [2, D])
    w_in_odd = din("w_in_odd", [2, D, ODD_IN])
    c_conv = din("c_conv", [2, 5, 6144])
    alog_f = din("c_a_log_fwd", [2, 16])
    dtb_f = din("c_dt_bias_fwd", [2, 16])
    alog_b = din("c_a_log_bwd", [2, 16])
    dtb_b = din("c_dt_bias_bwd", [2, 16])
    c_hnorm = din("c_head_norm", [2, 128])
    w_out_odd = din("w_out_odd", [2, D, D])
    norm_final = din("norm_final", [1, D])
    y = nc.dram_tensor("y", [NREAL * 128, D], F32, kind="ExternalOutput").ap()

    hs = dscr("hs", [T, D], F32)
    zt = dscr("zt", [T, EVEN_IN], F32)
    xcT = dscr("xcT", [6144, T], BF16)
    qTs = dscr("qTs", [NTILE, 128, 16, 128], BF16)
    kTs = dscr("kTs", [NTILE, 128, 16, 128], BF16)
    ktok = dscr("ktok", [T, D], BF16)
    vtok = dscr("vtok", [T, D], BF16)
    ofs = dscr("ofs", [T, D], F32)
    ogs = dscr("ogs", [T, D], BF16)
    wie = dscr("wie", [2, D, EVEN_IN], BF16)
    woe = dscr("woe", [2, D, D], BF16)
    wio = dscr("wio", [2, D, ODD_IN], BF16)
    woo = dscr("woo", [2, D, D], BF16)

    es0 = ExitStack()
    with es0:
        k = K(nc, es0)

        uid = [0]

        def uname(name):
            uid[0] += 1
            return "%s_u%d" % (name, uid[0])

        def sb(es, name, shape, dt):
            return Buf(es.enter_context(nc.sbuf_tensor(uname("sb_" + name), list(shape), dt)))

        def psb(es, name, shape, dt=F32):
            return Buf(es.enter_context(nc.psum_tensor(uname("ps_" + name), list(shape), dt)), ex=True)

        identb = sb(es0, "identb", [128, 128], BF16)
        tri = sb(es0, "tri", [128, 5, 128], F32)
        onesf = sb(es0, "onesf", [128, 128], F32)
        trib = sb(es0, "trib", [128, 5, 128], BF16)
        onesb = sb(es0, "onesb", [128, 128], BF16)
        linkc = sb(es0, "linkc", [128, 1], F32)
        vmask = sb(es0, "vmask", [128, 2], F32)
        amask = sb(es0, "amask", [128, 5, 128], BF16)
        k.dma("pool", identb.t[:], ident_d, W=[identb])
        k.dma("sp", tri.t[:], tri_d.rearrange("a p c -> p a c"), W=[tri])
        k.dma("sp", linkc.t[:], linkc_d, W=[linkc])
        k.dma("sp", vmask.t[:], vmask_d, W=[vmask])
        k.op("dve", lambda e: e.memset(onesf.t[:], 1.0), W=[onesf])
        k.op("dve", lambda e: e.memset(onesb.t[:], 1.0), W=[onesb])
        k.op("dve", lambda e: e.tensor_copy(trib.t[:], tri.t[:]), R=[tri], W=[trib])
        U_LE, U_LT, U_GE, U_GT, BDM = 0, 1, 2, 3, 4
        k.op("dve", lambda e: e.tensor_copy(amask.t[:, 0, :], tri.t[:, U_GE, :]), R=[tri], W=[amask])
        k.op("dve", lambda e: e.tensor_copy(amask.t[:, 1, :], tri.t[:, U_LE, :]), R=[tri], W=[amask])
        k.op("dve", lambda e: e.tensor_scalar(amask.t[:, 2, :], tri.t[:, U_GE, :], linkc.t[:, 0:1], None, ALU.mult), R=[tri, linkc], W=[amask])
        k.op("dve", lambda e: e.tensor_scalar(amask.t[:, 3, :], tri.t[:, U_LE, :], linkc.t[:, 0:1], None, ALU.mult), R=[tri, linkc], W=[amask])
        k.op("dve", lambda e: e.tensor_scalar(amask.t[:, 4, :], onesf.t[:], linkc.t[:, 0:1], None, ALU.mult), R=[onesf, linkc], W=[amask])
        AM_PREV, AM_NEXT, AM_PREVL, AM_NEXTL, AM_FULLL = 0, 1, 2, 3, 4

        conv_list = [(j, dst, src) for j in range(2) for (dst, src) in ((wie, w_in_even), (woe, w_out_even), (wio, w_in_odd), (woo, w_out_odd))]
        for ci, (j, dst, src) in enumerate(conv_list):
            for r in range(0, D, 128):
                k.dma("pool", dst[j, r:r + 128, :], src[j, r:r + 128, :])
            if ci == 0:
                k.barrier()

        evq = [0]

        def evac(dst_ap, src_ap, R, W, engines=("dve", "act")):
            on = engines[evq[0] % len(engines)]
            evq[0] += 1
            if on == "act":
                return k.op("act", lambda e: e.activation(dst_ap, src_ap, AF.Copy), R=R, W=W)
            return k.op(on, lambda e: e.tensor_copy(dst_ap, src_ap), R=R, W=W)

        def rstd_from_ss(rs, scale):
            k.op("dve", lambda e: e.tensor_scalar(rs.t[:], rs.t[:], scale, EPS, ALU.mult, ALU.add), R=[rs], W=[rs])
            k.op("act", lambda e: e.activation(rs.t[:], rs.t[:], AF.Sqrt), R=[rs], W=[rs])
            k.op("dve", lambda e: e.reciprocal(rs.t[:], rs.t[:]), R=[rs], W=[rs])

        def phase_in(hsrc, gamma_row, Wb, E, feat_cols):
            with ExitStack() as es:
                gam = sb(es, "in_gam", [128, D], F32)
                k.dma("sp", gam.t[:], gamma_row.partition_broadcast(128), W=[gam])
                hring = Ring([sb(es, "in_h%d" % i, [128, D], F32) for i in range(2)])
                sqj = sb(es, "in_sq", [128, D], BF16)
                ssr = Ring([sb(es, "in_ss%d" % i, [128, 1], F32) for i in range(2)])
                ubr = Ring([sb(es, "in_ub%d" % i, [128, D], BF16) for i in range(2)])
                uTr = Ring([sb(es, "in_uT%d" % i, [128, 16, G * 128], BF16) for i in range(2)])
                wring = Ring([sb(es, "in_w%d" % i, [128, 16, 512], BF16) for i in range(3)])
                groups = []
                c0_ = 0
                while c0_ < E:
                    groups.append((c0_, min(512, E - c0_)))
                    c0_ += 512

                def load_w(gi):
                    c0, cw = groups[gi]
                    wt = wring.next()
                    k.dma("sp", wt.t[:, :, 0:cw], Wb[:, c0:c0 + cw].rearrange("(kc p) c -> p kc c", p=128), W=[wt])
                    return wt

                pending = [load_w(0)]
                stage = Ring([sb(es, "in_st%d" % i, [128, 512], F32) for i in range(4)])
                stageb = Ring([sb(es, "in_sb%d" % i, [128, 512], BF16) for i in range(4)])
                psr = Ring([psb(es, "in_ps%d" % i, [128, 512]) for i in range(8)])
                for sp in range(NSUP):
                    uT = uTr.next()
                    for tl in range(G):
                        tau = sp * G + tl
                        hb = hring.next()
                        k.dma("sp", hb.t[:], hsrc[tau * 128:(tau + 1) * 128, :], W=[hb])
                        ss = ssr.next()
                        k.op("dve", lambda e: e.memset(ss.t[:], 0.0), W=[ss])
                        k.op("act", lambda e: e.activation(sqj.t[:], hb.t[:], AF.Square, accum_out=ss.t[:, 0:1]), R=[hb], W=[sqj, ss])
                        rstd_from_ss(ss, 1.0 / D)
                        ub = ubr.next()
                        k.op("dve", lambda e: e.scalar_tensor_tensor(ub.t[:], hb.t[:], ss.t[:, 0:1], gam.t[:], ALU.mult, ALU.mult), R=[hb, ss, gam], W=[ub])
                        for q in range(2):
                            pt = psr.next()
                            ptv = pt.t[:].bitcast(BF16).rearrange("p (a b) -> p a b", a=8)
                            for j in range(8):
                                kc = q * 8 + j
                                k.op("pe", lambda e: e.transpose(ptv[:, j, :], ub.t[:, kc * 128:(kc + 1) * 128], identb.t[:]), R=[ub, identb], W=[pt])
                            evac(uT.t[:, q * 8:(q + 1) * 8, tl * 128:(tl + 1) * 128], ptv, R=[pt], W=[uT])
                    for gi, (c0, cw) in enumerate(groups):
                        wt = pending[0]
                        if gi + 1 < len(groups):
                            pending[0] = load_w(gi + 1)
                        elif sp + 1 < NSUP:
                            pending[0] = load_w(0)
                        if c0 < feat_cols:
                            for j in range(cw // 128):
                                for half in range(2):
                                    ps = psr.next()
                                    nn = G * 64
                                    for kc in range(16):
                                        k.op("pe", lambda e: e.matmul(ps.t[:, 0:nn], wt.t[:, kc, j * 128:(j + 1) * 128], uT.t[:, kc, half * nn:(half + 1) * nn], start=(kc == 0), stop=(kc == 15)), R=[wt, uT], W=[ps])
                                    st = stageb.next()
                                    evac(st.t[:, 0:nn], ps.t[:, 0:nn], R=[ps], W=[st])
                                    col = sp * G * 128 + half * nn
                                    k.dma("sp", xcT[c0 + j * 128:c0 + (j + 1) * 128, col:col + nn], st.t[:, 0:nn], R=[st])
                        else:
                            for tl in range(G):
                                tau = sp * G + tl
                                ps = psr.next()
                                for kc in range(16):
                                    k.op("pe", lambda e: e.matmul(ps.t[:, 0:cw], uT.t[:, kc, tl * 128:(tl + 1) * 128], wt.t[:, kc, 0:cw], start=(kc == 0), stop=(kc == 15)), R=[wt, uT], W=[ps])
                                st = stage.next()
                                evac(st.t[:, 0:cw], ps.t[:, 0:cw], R=[ps], W=[st])
                                k.dma("sp", zt[tau * 128:(tau + 1) * 128, c0 - feat_cols:c0 - feat_cols + cw], st.t[:, 0:cw], R=[st])
                k.barrier()
            if phase_done():
                return

        def phase_out(hsrc, Wb, final):
            with ExitStack() as es:
                wout = sb(es, "o_w", [128, 16, D], BF16)
                for q in range(4):
                    k.dma("sp", wout.t[:, q * 4:(q + 1) * 4, :], Wb[q * 512:(q + 1) * 512, :].rearrange("(kc p) c -> p kc c", p=128), W=[wout])
                gfin = None
                if final:
                    gfin = sb(es, "o_gf", [128, D], F32)
                    k.dma("sp", gfin.t[:], norm_final.partition_broadcast(128), W=[gfin])
                    sqj = sb(es, "o_sq", [128, D], BF16)
                    ssr = Ring([sb(es, "o_ss%d" % i, [128, 1], F32) for i in range(2)])
                    yr = Ring([sb(es, "o_y%d" % i, [128, D], F32) for i in range(2)])
                ogr = Ring([sb(es, "o_og%d" % i, [128, D], BF16) for i in range(2)])
                hor = Ring([sb(es, "o_ho%d" % i, [128, D], F32) for i in range(2)])
                hnr = Ring([sb(es, "o_hn%d" % i, [128, D], F32) for i in range(2)])
                oTr = Ring([sb(es, "o_oT%d" % i, [128, 16, 128], BF16) for i in range(2)])
                psr = Ring([psb(es, "o_ps%d" % i, [128, 512]) for i in range(8)])
                def out_load(tau):
                    og = ogr.next()
                    k.dma("sp", og.t[:], ogs[tau * 128:(tau + 1) * 128, :], W=[og])
                    ho = hor.next()
                    k.dma("sp", ho.t[:], hsrc[tau * 128:(tau + 1) * 128, :], W=[ho])
                    return og, ho

                nxt_o = out_load(0)
                for tau in range(NTILE):
                    og, ho = nxt_o
                    if tau + 1 < NTILE:
                        nxt_o = out_load(tau + 1)
                    oT = oTr.next()
                    for q in range(2):
                        pt = psr.next()
                        ptv = pt.t[:].bitcast(BF16).rearrange("p (a b) -> p a b", a=8)
                        for j in range(8):
                            kc = q * 8 + j
                            k.op("pe", lambda e: e.transpose(ptv[:, j, :], og.t[:, kc * 128:(kc + 1) * 128], identb.t[:]), R=[og, identb], W=[pt])
                        evac(oT.t[:, q * 8:(q + 1) * 8, :], ptv, R=[pt], W=[oT])
                    hn = hnr.next()
                    for cg in range(4):
                        ps = psr.next()
                        for kc in range(16):
                            k.op("pe", lambda e: e.matmul(ps.t[:], oT.t[:, kc, :], wout.t[:, kc, cg * 512:(cg + 1) * 512], start=(kc == 0), stop=(kc == 15)), R=[oT, wout], W=[ps])
                        k.op("dve", lambda e: e.tensor_tensor(hn.t[:, cg * 512:(cg + 1) * 512], ps.t[:], ho.t[:, cg * 512:(cg + 1) * 512], ALU.add), R=[ps, ho], W=[hn])
                    if not final:
                        k.dma("sp", hs[tau * 128:(tau + 1) * 128, :], hn.t[:], R=[hn])
                    elif tau not in (M0, M1):
                        ss = ssr.next()
                        k.op("dve", lambda e: e.memset(ss.t[:], 0.0), W=[ss])
                        k.op("act", lambda e: e.activation(sqj.t[:], hn.t[:], AF.Square, accum_out=ss.t[:, 0:1]), R=[hn], W=[sqj, ss])
                        rstd_from_ss(ss, 1.0 / D)
                        yb = yr.next()
                        k.op("dve", lambda e: e.scalar_tensor_tensor(yb.t[:], hn.t[:], ss.t[:, 0:1], gfin.t[:], ALU.mult, ALU.mult), R=[hn, ss, gfin], W=[yb])
                        ry = (tau - 1) if tau <= NT_SEG else (tau - 2)
                        k.dma("sp", y[ry * 128:(ry + 1) * 128, :], yb.t[:], R=[yb])
                k.barrier()
            if phase_done():
                return

        def rope(dst4, src4, cs, nh, tA, tB, R, W):
            c = cs.t[:, 0:1, :].to_broadcast([128, nh, 32])
            s = cs.t[:, 1:2, :].to_broadcast([128, nh, 32])
            k.op("dve", lambda e: e.tensor_tensor(tA.t[:, :, 0, :], src4[:, :, 0, :], c, ALU.mult), R=R + [cs], W=[tA])
            k.op("dve", lambda e: e.tensor_tensor(tA.t[:, :, 1, :], src4[:, :, 1, :], s, ALU.mult), R=R + [cs], W=[tA])
            k.op("pool", lambda e: e.tensor_tensor(tB.t[:, :, 0, :], src4[:, :, 1, :], c, ALU.mult), R=R + [cs], W=[tB])
            k.op("pool", lambda e: e.tensor_tensor(tB.t[:, :, 1, :], src4[:, :, 0, :], s, ALU.mult), R=R + [cs], W=[tB])
            k.op("dve", lambda e: e.tensor_tensor(dst4[:, :, 0, :], tA.t[:, :, 0, :], tA.t[:, :, 1, :], ALU.subtract), R=[tA], W=W)
            k.op("pool", lambda e: e.tensor_tensor(dst4[:, :, 1, :], tB.t[:, :, 0, :], tB.t[:, :, 1, :], ALU.add), R=[tB], W=W)

        def phase_even(j):
            with ExitStack() as es:
                KTall = es.enter_context(nc.sbuf_tensor(uname("sb_e_KT"), [128, NTILE, 2, 128], BF16))
                VAall = es.enter_context(nc.sbuf_tensor(uname("sb_e_VA"), [128, NTILE, 2, 65], BF16))
                KTb = [Buf(KTall) for _ in range(NTILE)]
                VAb = [Buf(VAall) for _ in range(NTILE)]
                esink = sb(es, "e_esink", [128, 16], F32)
                k.dma("sp", esink.t[:], a_sink[j:j + 1, :].partition_broadcast(128), W=[esink])
                k.op("act", lambda e: e.activation(esink.t[:], esink.t[:], AF.Exp), R=[esink], W=[esink])
                gub = [sb(es, "e_gu%d" % d, [16, 512], BF16) for d in range(2)]
                gbias = [sb(es, "e_gb%d" % d, [128, 512], F32) for d in range(2)]
                k.dma("pool", gub[0].t[:], gu_f[j], W=[gub[0]])
                k.dma("pool", gub[1].t[:], gu_b[j], W=[gub[1]])
                k.dma("sp", gbias[0].t[:], gb_f[j:j + 1, :].partition_broadcast(128), W=[gbias[0]])
                k.dma("sp", gbias[1].t[:], gb_b[j:j + 1, :].partition_broadcast(128), W=[gbias[1]])
                hnb = sb(es, "e_hn", [128, 1, 256], F32)
                k.dma("sp", hnb.t[:, 0, :], b_hnorm[j:j + 1, :].partition_broadcast(128), W=[hnb])
                psr = Ring([psb(es, "e_ps%d" % i, [128, 1024]) for i in range(4)])
                csr = Ring([sb(es, "e_cs%d" % i, [128, 2, 32], F32) for i in range(2)])

                def load_cs(tau):
                    cs = csr.next()
                    k.dma("sp", cs.t[:, 0, :], cosr[tau * 128:(tau + 1) * 128, :], W=[cs])
                    k.dma("sp", cs.t[:, 1, :], sinr[tau * 128:(tau + 1) * 128, :], W=[cs])
                    return cs

                with ExitStack() as e1:
                    kvr = Ring([sb(e1, "e1_kv%d" % i, [128, 256], F32) for i in range(2)])
                    tA = sb(e1, "e1_tA", [128, 2, 2, 32], F32)
                    tB = sb(e1, "e1_tB", [128, 2, 2, 32], F32)
                    krr = Ring([sb(e1, "e1_kr%d" % i, [128, 2, 2, 64], BF16) for i in range(2)])
                    for tau in range(NTILE):
                        kv = kvr.next()
                        k.dma("sp", kv.t[:], zt[tau * 128:(tau + 1) * 128, 1024:1280], W=[kv])
                        cs = load_cs(tau)
                        kr = krr.next()
                        src4 = kv.t[:, 0:128].rearrange("p (h a b) -> p h a b", h=2, a=2)
                        dst4 = kr.t[:, :, 0, :].rearrange("p h (a b) -> p h a b", a=2)
                        rope(dst4, src4, cs, 2, tA, tB, [kv], [kr])
                        k.op("dve", lambda e: e.tensor_copy(kr.t[:, :, 1, :], kr.t[:, :, 0, :]), R=[kr], W=[kr])
                        pt = psr.next()
                        ptv = pt.t[:, 0:128].bitcast(BF16).rearrange("p (a b) -> p a b", a=2)
                        for g in range(2):
                            k.op("pe", lambda e: e.transpose(ptv[:, g, :], kr.t[:, g, :, :].rearrange("p a b -> p (a b)"), identb.t[:]), R=[kr, identb], W=[pt])
                        evac(KTall[:, tau, :, :], ptv, R=[pt], W=[KTb[tau]])
                        k.op("dve", lambda e: e.tensor_copy(VAall[:, tau, :, 0:64], kv.t[:, 128:256].rearrange("p (g d) -> p g d", g=2)), R=[kv], W=[VAb[tau]])
                        if tau == M0 or tau == M1:
                            mi = 0 if tau == M0 else 1
                            k.op("dve", lambda e: e.tensor_copy(VAall[:, tau, :, 64:65], vmask.t[:, mi:mi + 1].unsqueeze(1).to_broadcast([128, 2, 1])), R=[vmask], W=[VAb[tau]])
                        else:
                            k.op("dve", lambda e: e.memset(VAall[:, tau, :, 64:65], 1.0), W=[VAb[tau]])
                    k.barrier()

                gl = ExitStack()
                es.enter_context(gl)
                qkr = Ring([sb(gl, "g_qk%d" % i, [128, 1024], F32) for i in range(2)])
                vr = Ring([sb(gl, "g_v%d" % i, [128, 1024], F32) for i in range(2)])
                lr = Ring([sb(gl, "g_l%d" % i, [128, 16], F32) for i in range(2)])
                l16 = sb(gl, "g_l16", [128, 16], BF16)
                lT = sb(gl, "g_lT", [16, 128], BF16)
                gt = sb(gl, "g_gt", [128, 512], F32)
                gg = sb(gl, "g_gg", [128, 512], F32)
                ghl = sb(gl, "g_ghl", [128, 2, 512], BF16)
                eb = sb(gl, "g_eb", [128, 512], F32)
                enb = sb(gl, "g_enb", [128, 512], F32)
                ek = sb(gl, "g_ek", [128, 512], F32)
                dec = sb(gl, "g_dec", [128, 4], F32)
                qd = sb(gl, "g_qd", [128, 512], BF16)
                kd = sb(gl, "g_kd", [128, 512], BF16)
                kdec = sb(gl, "g_kdec", [128, 512], BF16)
                v16 = sb(gl, "g_v16", [128, 1024], BF16)
                qkT = sb(gl, "g_qkT", [128, 8, 128], BF16)
                att = sb(gl, "g_att", [128, 4, 128], BF16)
                Sf = sb(gl, "g_S", [128, 1024], F32)
                Sb = sb(gl, "g_Sb", [128, 1024], BF16)
                ofr = Ring([sb(gl, "g_of%d" % i, [128, 1024], F32) for i in range(2)])

                def gla_reset():
                    k.op("dve", lambda e: e.memset(Sf.t[:], 0.0), W=[Sf])
                    k.op("dve", lambda e: e.memset(Sb.t[:], 0.0), W=[Sb])

                def gla_link():
                    k.op("dve", lambda e: e.tensor_scalar(Sf.t[:], Sf.t[:], linkc.t[:, 0:1], None, ALU.mult), R=[Sf, linkc], W=[Sf])
                    k.op("act", lambda e: e.activation(Sb.t[:], Sf.t[:], AF.Copy), R=[Sf], W=[Sb])

                def gla_load(tau, d):
                    qk = qkr.next()
                    k.dma("sp", qk.t[:], zt[tau * 128:(tau + 1) * 128, 2304:3328], W=[qk])
                    vv = vr.next()
                    k.dma("sp", vv.t[:], zt[tau * 128:(tau + 1) * 128, 3328:4352], W=[vv])
                    ll = lr.next()
                    k.dma("sp", ll.t[:], zt[tau * 128:(tau + 1) * 128, 5376 + 16 * d:5392 + 16 * d], W=[ll])
                    return qk, vv, ll

                def gla_step(tau, d, loaded):
                    qk, vv, ll = loaded
                    TA = U_LE if d == 0 else U_GE
                    TB = U_GT if d == 0 else U_LT
                    k.op("dve", lambda e: e.tensor_copy(l16.t[:], ll.t[:]), R=[ll], W=[l16])
                    p0 = psr.next()
                    p0b = p0.t[0:16, 0:64].bitcast(BF16)
                    k.op("pe", lambda e: e.transpose(p0b, l16.t[:], identb.t[:]), R=[l16, identb], W=[p0])
                    k.op("dve", lambda e: e.tensor_copy(lT.t[:], p0b), R=[p0], W=[lT])
                    p1 = psr.next()
                    k.op("pe", lambda e: e.matmul(p1.t[:, 0:512], lT.t[:], gub[d].t[:], start=True, stop=True), R=[lT, gub[d]], W=[p1])
                    k.op("dve", lambda e: e.tensor_tensor(gt.t[:], p1.t[:, 0:512], gbias[d].t[:], ALU.add), R=[p1, gbias[d]], W=[gt])
                    k.op("act", lambda e: e.activation(gt.t[:], gt.t[:], AF.Exp, scale=-1.0), R=[gt], W=[gt])
                    k.op("act", lambda e: e.activation(gt.t[:], gt.t[:], AF.Ln, bias=1.0), R=[gt], W=[gt])
                    if tau in (M0, M1):
                        mi = 0 if tau == M0 else 1
                        k.op("dve", lambda e: e.tensor_scalar(gg.t[:], gt.t[:], -1.0 / 16.0, vmask.t[:, mi:mi + 1], ALU.mult, ALU.mult), R=[gt, vmask], W=[gg])
                    else:
                        k.op("dve", lambda e: e.tensor_scalar(gg.t[:], gt.t[:], -1.0 / 16.0, None, ALU.mult), R=[gt], W=[gg])
                    k.op("dve", lambda e: e.tensor_copy(ghl.t[:, 0, :], gg.t[:]), R=[gg], W=[ghl])
                    k.op("dve", lambda e: e.tensor_tensor(ghl.t[:, 1, :], gg.t[:], ghl.t[:, 0, :], ALU.subtract), R=[gg, ghl], W=[ghl])
                    p2 = psr.next()
                    for hl in range(2):
                        k.op("pe", lambda e: e.matmul(p2.t[:, 0:512], trib.t[:, TA, :], ghl.t[:, hl, :], start=(hl == 0), stop=(hl == 1)), R=[trib, ghl], W=[p2])
                    for hl in range(2):
                        k.op("pe", lambda e: e.matmul(p2.t[:, 512:1024], trib.t[:, TB, :], ghl.t[:, hl, :], start=(hl == 0), stop=(hl == 1)), R=[trib, ghl], W=[p2])
                    p3 = psr.next()
                    for h in range(4):
                        for hl in range(2):
                            k.op("pe", lambda e: e.matmul(p3.t[:, h:h + 1], ghl.t[:, hl, h * 128:(h + 1) * 128], onesb.t[:, 0:1], start=(hl == 0), stop=(hl == 1)), R=[ghl, onesb], W=[p3])
                    k.op("act", lambda e: e.activation(eb.t[:], p2.t[:, 0:512], AF.Exp), R=[p2], W=[eb])
                    k.op("act", lambda e: e.activation(enb.t[:], p2.t[:, 0:512], AF.Exp, scale=-1.0), R=[p2], W=[enb])
                    k.op("act", lambda e: e.activation(ek.t[:], p2.t[:, 512:1024], AF.Exp), R=[p2], W=[ek])
                    k.op("act", lambda e: e.activation(dec.t[:], p3.t[:, 0:4], AF.Exp), R=[p3], W=[dec])
                    k.op("dve", lambda e: e.scalar_tensor_tensor(qd.t[:], qk.t[:, 0:512], 128.0 ** -0.5, eb.t[:], ALU.mult, ALU.mult), R=[qk, eb], W=[qd])
                    k.op("pool", lambda e: e.tensor_tensor(kd.t[:], qk.t[:, 512:1024], enb.t[:], ALU.mult), R=[qk, enb], W=[kd])
                    k.op("pool", lambda e: e.tensor_tensor(kdec.t[:], qk.t[:, 512:1024], ek.t[:], ALU.mult), R=[qk, ek], W=[kdec])
                    k.op("act", lambda e: e.activation(v16.t[:], vv.t[:], AF.Copy), R=[vv], W=[v16])
                    p4 = psr.next()
                    p4v = p4.t[:, 0:512].bitcast(BF16).rearrange("p (a b) -> p a b", a=8)
                    for h in range(4):
                        k.op("pe", lambda e: e.transpose(p4v[:, h, :], qd.t[:, h * 128:(h + 1) * 128], identb.t[:]), R=[qd, identb], W=[p4])
                        k.op("pe", lambda e: e.transpose(p4v[:, 4 + h, :], kd.t[:, h * 128:(h + 1) * 128], identb.t[:]), R=[kd, identb], W=[p4])
                    k.op("dve", lambda e: e.tensor_copy(qkT.t[:], p4v), R=[p4], W=[qkT])
                    p5 = psr.next()
                    p5v = p5.t[:, 0:512].rearrange("p (a b) -> p a b", a=4)
                    for h in range(4):
                        k.op("pe", lambda e: e.matmul(p5v[:, h, :], qkT.t[:, 4 + h, :], qkT.t[:, h, :], start=True, stop=True), R=[qkT], W=[p5])
                    k.op("dve", lambda e: e.tensor_tensor(att.t[:], p5v, tri.t[:, TA:TA + 1, :].to_broadcast([128, 4, 128]), ALU.mult), R=[p5, tri], W=[att])
                    po = psr.next()
                    for h in range(4):
                        k.op("pe", lambda e: e.matmul(po.t[:, h * 256:(h + 1) * 256], att.t[:, h, :], v16.t[:, h * 256:(h + 1) * 256], start=True, stop=False), R=[att, v16], W=[po])
                        k.op("pe", lambda e: e.matmul(po.t[:, h * 256:(h + 1) * 256], qkT.t[:, h, :], Sb.t[:, h * 256:(h + 1) * 256], start=False, stop=True), R=[qkT, Sb], W=[po])
                    pS = psr.next()
                    for h in range(4):
                        k.op("pe", lambda e: e.matmul(pS.t[:, h * 256:(h + 1) * 256], kdec.t[:, h * 128:(h + 1) * 128], v16.t[:, h * 256:(h + 1) * 256], start=True, stop=True), R=[kdec, v16], W=[pS])
                    for h in range(4):
                        k.op("dve", lambda e: e.scalar_tensor_tensor(Sf.t[:, h * 256:(h + 1) * 256], Sf.t[:, h * 256:(h + 1) * 256], dec.t[:, h:h + 1], pS.t[:, h * 256:(h + 1) * 256], ALU.mult, ALU.add), R=[Sf, dec, pS], W=[Sf])
                    k.op("act", lambda e: e.activation(Sb.t[:], Sf.t[:], AF.Copy), R=[Sf], W=[Sb])
                    return po

                gla_reset()
                nxt = gla_load(0, 0)
                for tau in range(NTILE):
                    cur = nxt
                    if tau + 1 < NTILE:
                        nxt = gla_load(tau + 1, 0)
                    if tau == M1:
                        gla_link()
                    po = gla_step(tau, 0, cur)
                    ob = ofr.next()
                    k.op("act", lambda e: e.activation(ob.t[:], po.t[:], AF.Copy), R=[po], W=[ob])
                    k.dma("sp", ofs[tau * 128:(tau + 1) * 128, 0:1024], ob.t[:], R=[ob])
                k.barrier()
                if phase_done():
                    return

                with ExitStack() as e3:
                    zbr = Ring([sb(e3, "e3_zb%d" % i, [128, 1024], F32) for i in range(2)])
                    osum = sb(e3, "e3_osum", [128, 4, 256], F32)
                    osq = sb(e3, "e3_osq", [128, 4, 256], F32)
                    oss = sb(e3, "e3_oss", [128, 4], F32)
                    ogr = Ring([sb(e3, "e3_og%d" % i, [128, D], BF16) for i in range(2)])
                    qar = Ring([sb(e3, "e3_qa%d" % i, [128, 1024], F32) for i in range(2)])
                    zar = Ring([sb(e3, "e3_za%d" % i, [128, 1024], F32) for i in range(2)])
                    tA = sb(e3, "e3_tA", [128, 16, 2, 32], F32)
                    tB = sb(e3, "e3_tB", [128, 16, 2, 32], F32)
                    qr = sb(e3, "e3_qr", [128, 1024], BF16)
                    QT = sb(e3, "e3_QT", [128, 8, 128], BF16)
                    PTr = Ring([sb(e3, "e3_PT%d" % i, [128, 2, 4, 128], BF16) for i in range(6)])
                    oall = sb(e3, "e3_oall", [128, 16, 65], F32)
                    den = sb(e3, "e3_den", [128, 16], F32)
                    onr = sb(e3, "e3_on", [128, 16, 64], F32)

                    def e3_load(tau):
                        ld = gla_load(tau, 1)
                        zb = zbr.next()
                        k.dma("sp", zb.t[:], zt[tau * 128:(tau + 1) * 128, 4352:5376], W=[zb])
                        of_t = ofr.next()
                        k.dma("sp", of_t.t[:], ofs[tau * 128:(tau + 1) * 128, 0:1024], W=[of_t])
                        qa = qar.next()
                        k.dma("sp", qa.t[:], zt[tau * 128:(tau + 1) * 128, 0:1024], W=[qa])
                        za = zar.next()
                        k.dma("sp", za.t[:], zt[tau * 128:(tau + 1) * 128, 1280:2304], W=[za])
                        cs = load_cs(tau)
                        return ld, zb, of_t, qa, za, cs

                    def keylist(tau):
                        if tau == M0 or tau == M1:
                            return [(tau, None), (tau + 1, AM_NEXT)]
                        ks = []
                        seg1 = tau > M1
                        if seg1:
                            ks.append((M0, AM_FULLL))
                            ks.append((M1, None))
                        else:
                            ks.append((M0, None))
                        first = (tau == 1) or (tau == LB)
                        last = (tau == LA) or (tau == NTILE - 1)
                        if not first:
                            ks.append((tau - 1, AM_PREV))
                        elif tau == LB:
                            ks.append((LA, AM_PREVL))
                        ks.append((tau, None))
                        if not last:
                            ks.append((tau + 1, AM_NEXT))
                        elif tau == LA:
                            ks.append((LB, AM_NEXTL))
                        return ks

                    gla_reset()
                    nxt = e3_load(NTILE - 1)
                    for tau in range(NTILE - 1, -1, -1):
                        ld, zb, of_t, qa, za, cs = nxt
                        if tau - 1 >= 0:
                            nxt = e3_load(tau - 1)
                        if tau == LA:
                            gla_link()
                        og = ogr.next()
                        po = gla_step(tau, 1, ld)
                        k.op("dve", lambda e: e.tensor_tensor(osum.t[:], po.t[:].rearrange("p (a b) -> p a b", a=4), of_t.t[:].rearrange("p (a b) -> p a b", a=4), ALU.add), R=[po, of_t], W=[osum])
                        k.op("pool", lambda e: e.tensor_tensor(osq.t[:], osum.t[:], osum.t[:], ALU.mult), R=[osum], W=[osq])
                        k.op("dve", lambda e: e.tensor_reduce(oss.t[:], osq.t[:], AX.X, ALU.add), R=[osq], W=[oss])
                        rstd_from_ss(oss, 1.0 / 256.0)
                        k.op("dve", lambda e: e.tensor_tensor(osum.t[:], osum.t[:], oss.t[:].unsqueeze(2).to_broadcast([128, 4, 256]), ALU.mult), R=[osum, oss], W=[osum])
                        k.op("pool", lambda e: e.tensor_tensor(osum.t[:], osum.t[:], hnb.t[:, 0:1, :].to_broadcast([128, 4, 256]), ALU.mult), R=[osum, hnb], W=[osum])
                        k.op("act", lambda e: e.activation(zb.t[:], zb.t[:], AF.Silu), R=[zb], W=[zb])
                        k.op("dve", lambda e: e.tensor_tensor(og.t[:, 1024:2048], osum.t[:].rearrange("p a b -> p (a b)"), zb.t[:], ALU.mult), R=[osum, zb], W=[og])
                        src4 = qa.t[:].rearrange("p (h a b) -> p h a b", h=16, a=2)
                        dst4 = qr.t[:].rearrange("p (h a b) -> p h a b", h=16, a=2)
                        rope(dst4, src4, cs, 16, tA, tB, [qa], [qr])
                        pq = psr.next()
                        pqv = pq.t[:, 0:512].bitcast(BF16).rearrange("p (a b) -> p a b", a=8)
                        for jj in range(8):
                            k.op("pe", lambda e: e.transpose(pqv[:, jj, :], qr.t[:, jj * 128:(jj + 1) * 128], identb.t[:]), R=[qr, identb], W=[pq])
                        k.op("dve", lambda e: e.tensor_copy(QT.t[:], pqv), R=[pq], W=[QT])
                        keys = keylist(tau)
                        for g in range(2):
                            pts = []
                            for (c, mk) in keys:
                                pst = psr.next()
                                for par in range(2):
                                    k.op("pe", lambda e: e.matmul(pst.t[:, par * 512:(par + 1) * 512], KTall[par * 64:(par + 1) * 64, c, g, :], QT.t[par * 64:(par + 1) * 64, 4 * g:4 * g + 4, :], start=True, stop=True), R=[KTb[c], QT], W=[pst])
                                pt = PTr.next()
                                k.op("act", lambda e: e.activation(pt.t[:].rearrange("p a b c -> p (a b c)"), pst.t[:], AF.Exp, scale=0.125), R=[pst], W=[pt])
                                if mk is not None:
                                    k.op("pool", lambda e: e.tensor_tensor(pt.t[:].rearrange("p a b c -> p (a b) c"), pt.t[:].rearrange("p a b c -> p (a b) c"), amask.t[:, mk:mk + 1, :].to_broadcast([128, 8, 128]), ALU.mult), R=[pt, amask], W=[pt])
                                pts.append((pt, c))
                            pso = psr.next()
                            psov = pso.t[:].rearrange("p (a b) -> p a b", a=8)
                            for jj in range(4):
                                for par in range(2):
                                    hh = 2 * jj + par
                                    for ci, (pt, c) in enumerate(pts):
                                        k.op("pe", lambda e: e.matmul(psov[:, hh, 0:65], pt.t[:, par, jj, :], VAall[:, c, g, :], start=(ci == 0), stop=(ci == len(pts) - 1)), R=[pt, VAb[c]], W=[pso])
                            k.op("act", lambda e: e.activation(oall.t[:, 8 * g:8 * g + 8, :], psov[:, :, 0:65], AF.Copy), R=[pso], W=[oall])
                        k.op("dve", lambda e: e.tensor_tensor(den.t[:], oall.t[:, :, 64], esink.t[:], ALU.add), R=[oall, esink], W=[den])
                        k.op("dve", lambda e: e.reciprocal(den.t[:], den.t[:]), R=[den], W=[den])
                        k.op("dve", lambda e: e.tensor_tensor(onr.t[:], oall.t[:, :, 0:64], den.t[:].unsqueeze(2).to_broadcast([128, 16, 64]), ALU.mult), R=[oall, den], W=[onr])
                        k.op("act", lambda e: e.activation(za.t[:], za.t[:], AF.Silu), R=[za], W=[za])
                        k.op("dve", lambda e: e.tensor_tensor(og.t[:, 0:1024], onr.t[:].rearrange("p a b -> p (a b)"), za.t[:], ALU.mult), R=[onr, za], W=[og])
                        k.dma("sp", ogs[tau * 128:(tau + 1) * 128, :], og.t[:], R=[og])
                k.barrier()
            if phase_done():
                return

        def phase_odd(j):
            with ExitStack() as es:
                NW = G * 128
                cw = sb(es, "o1_cw", [128, 48, 5], F32)
                with nc.allow_non_contiguous_dma("conv weights, tiny"):
                    for kk in range(5):
                        k.dma("sp", cw.t[:, :, kk], c_conv[j, kk, :].rearrange("(c p) -> p c", p=128), W=[cw])
                xr = Ring([sb(es, "o1_x%d" % i, [128, NW + 4], BF16) for i in range(4)])
                dgr = Ring([sb(es, "o1_dg%d" % i, [128, 5, 128], BF16) for i in range(2)])
                acr = Ring([sb(es, "o1_ac%d" % i, [128, NW], F32) for i in range(3)])
                xl = sb(es, "o1_xl", [128, 4], BF16)
                sq = sb(es, "o1_sq", [128, NW], BF16)
                rn = sb(es, "o1_rn", [128, NW], F32)
                xnr = Ring([sb(es, "o1_xn%d" % i, [128, G, 128], BF16) for i in range(2)])
                tmr = Ring([sb(es, "o1_tm%d" % i, [128, G, 128], BF16) for i in range(2)])
                psr = Ring([psb(es, "o1_ps%d" % i, [128, 1024]) for i in range(4)])
                def o1_load(sp, cc):
                    col0 = sp * NW
                    xb = xr.next()
                    lo = col0 - 2
                    hi = col0 + NW + 2
                    if sp == 0:
                        k.op("pool", lambda e: e.memset(xb.t[:, 0:2], 0.0), W=[xb])
                        lo = col0
                    if sp == NSUP - 1:
                        k.op("pool", lambda e: e.memset(xb.t[:, NW + 2:NW + 4], 0.0), W=[xb])
                        hi = col0 + NW
                    k.dma("sp", xb.t[:, lo - (col0 - 2):hi - (col0 - 2)], xcT[cc * 128:(cc + 1) * 128, lo:hi], W=[xb])
                    return xb

                seq = [(sp, cc) for sp in range(NSUP) for cc in range(48)]
                xq = [o1_load(*seq[0]), o1_load(*seq[1])]

                def stage_a(si, sp, cc):
                    col0 = sp * NW
                    xb = xq.pop(0)
                    if si + 2 < len(seq):
                        xq.append(o1_load(*seq[si + 2]))
                    ac = acr.next()
                    dg = dgr.next()
                    for kk in range(5):
                        k.op("dve", lambda e: e.tensor_scalar(dg.t[:, kk, :], identb.t[:], cw.t[:, cc, kk:kk + 1], None, ALU.mult), R=[identb, cw], W=[dg])
                    nh = NW // 2
                    fixes = []
                    if sp == LA // G:
                        cA = (LA + 1) * 128 - 1 - col0
                        cB = LB * 128 - col0
                        k.op("dve", lambda e: e.tensor_scalar(xl.t[:, 0:2], xb.t[:, cA + 1:cA + 3], linkc.t[:, 0:1], None, ALU.mult), R=[xb, linkc], W=[xl])
                        k.op("dve", lambda e: e.tensor_scalar(xl.t[:, 2:4], xb.t[:, cB + 2:cB + 4], linkc.t[:, 0:1], None, ALU.mult), R=[xb, linkc], W=[xl])
                        fixes = [(cA, 2, 3), (cA, 3, 4), (cA - 1, 2, 4), (cB, 1, 1), (cB, 0, 0), (cB + 1, 1, 0)]
                    pcv = psr.next()
                    for half in range(2):
                        for kk in range(5):
                            if kk == 4:
                                for (col, xi, wk) in fixes:
                                    if col // nh == half:
                                        pc_ = half * 512 + col % nh
                                        k.op("pe", lambda e: e.matmul(pcv.t[:, pc_:pc_ + 1], dg.t[:, wk, :], xl.t[:, xi:xi + 1], start=False, stop=False), R=[dg, xl], W=[pcv])
                            k.op("pe", lambda e: e.matmul(pcv.t[:, half * 512:half * 512 + nh], dg.t[:, kk, :], xb.t[:, half * nh + kk:half * nh + kk + nh], start=(kk == 0), stop=(kk == 4)), R=[dg, xb], W=[pcv])
                    k.op("act", lambda e: e.activation(ac.t[:].rearrange("p (a b) -> p a b", a=2), pcv.t[:].rearrange("p (a b) -> p a b", a=2)[:, :, 0:nh], AF.Silu), R=[pcv], W=[ac])
                    return ac

                def stage_b(sp, cc, ac):
                    xn = xnr.next()
                    xnf = xn.t[:].rearrange("p a b -> p (a b)")
                    if cc < 32:
                        head = cc % 16
                        k.op("pool", lambda e: e.tensor_tensor(sq.t[:], ac.t[:], ac.t[:], ALU.mult), R=[ac], W=[sq])
                        ps = psr.next()
                        nn = NW // 2
                        for half in range(2):
                            k.op("pe", lambda e: e.matmul(ps.t[:, half * 512:half * 512 + nn], onesb.t[:], sq.t[:, half * nn:(half + 1) * nn], start=True, stop=True), R=[onesb, sq], W=[ps])
                        rnv = rn.t[:].rearrange("p (a b) -> p a b", a=2)
                        psv = ps.t[:].rearrange("p (a b) -> p a b", a=2)[:, :, 0:nn]
                        k.op("act", lambda e: e.activation(rnv, psv, AF.Sqrt, bias=EPS), R=[ps], W=[rn])
                        k.op("dve", lambda e: e.reciprocal(rn.t[:], rn.t[:]), R=[rn], W=[rn])
                        scl = (128.0 ** -0.5) if cc < 16 else 1.0
                        k.op("dve", lambda e: e.scalar_tensor_tensor(xnf, ac.t[:], scl, rn.t[:], ALU.mult, ALU.mult), R=[ac, rn], W=[xn])
                        dst = qTs if cc < 16 else kTs
                        k.dma("sp", dst[sp * G:(sp + 1) * G, :, head, :].rearrange("t p c -> p t c"), xn.t[:], R=[xn])
                    else:
                        head = cc - 32
                        k.op("dve", lambda e: e.tensor_copy(xnf, ac.t[:]), R=[ac], W=[xn])
                    if cc >= 16:
                        ps = psr.next()
                        psv = ps.t[:, 0:G * 64].bitcast(BF16).rearrange("p (a b) -> p a b", a=G)
                        for tl in range(G):
                            k.op("pe", lambda e: e.transpose(psv[:, tl, :], xn.t[:, tl, :], identb.t[:]), R=[xn, identb], W=[ps])
                        tm = tmr.next()
                        evac(tm.t[:], psv, R=[ps], W=[tm])
                        dst = ktok if cc < 32 else vtok
                        k.dma("sp", dst[sp * NW:(sp + 1) * NW, head * 128:(head + 1) * 128].rearrange("(t p) c -> p t c", p=128), tm.t[:], R=[tm])

                ac_prev = stage_a(0, *seq[0])
                for si in range(len(seq)):
                    ac_cur = ac_prev
                    if si + 1 < len(seq):
                        ac_prev = stage_a(si + 1, *seq[si + 1])
                    stage_b(seq[si][0], seq[si][1], ac_cur)
                k.barrier()
            if phase_done():
                return

            with ExitStack() as es:
                cst = sb(es, "d_cst", [128, 4, 16], F32)
                k.dma("sp", cst.t[:, 0, :], alog_f[j:j + 1, :].partition_broadcast(128), W=[cst])
                k.dma("sp", cst.t[:, 1, :], dtb_f[j:j + 1, :].partition_broadcast(128), W=[cst])
                k.dma("sp", cst.t[:, 2, :], alog_b[j:j + 1, :].partition_broadcast(128), W=[cst])
                k.dma("sp", cst.t[:, 3, :], dtb_b[j:j + 1, :].partition_broadcast(128), W=[cst])
                for a in (0, 2):
                    k.op("act", lambda e: e.activation(cst.t[:, a, :], cst.t[:, a, :], AF.Exp), R=[cst], W=[cst])
                    k.op("dve", lambda e: e.tensor_scalar(cst.t[:, a, :], cst.t[:, a, :], -1.0, None, ALU.mult), R=[cst], W=[cst])
                hnb = sb(es, "d_hn", [128, 1, 128], F32)
                k.dma("sp", hnb.t[:, 0, :], c_hnorm[j:j + 1, :].partition_broadcast(128), W=[hnb])
                psr = Ring([psb(es, "d_ps%d" % i, [128, 1024]) for i in range(4)])
                QKr = Ring([sb(es, "d_QK%d" % i, [128, 16, 2, 128], BF16) for i in range(2)])
                ktr = Ring([sb(es, "d_kt%d" % i, [128, 16, 128], BF16) for i in range(2)])
                Rbr = Ring([sb(es, "d_Rb%d" % i, [128, 16, 256], BF16) for i in range(2)])
                zzr = Ring([sb(es, "d_zz%d" % i, [128, 64], F32) for i in range(2)])
                gt = sb(es, "d_gt", [128, 16], F32)
                gg = sb(es, "d_gg", [128, 16], F32)
                ghl = sb(es, "d_ghl", [128, 2, 16], BF16)
                GUl = sb(es, "d_GUl", [128, 16, 128], BF16)
                GUh = sb(es, "d_GUh", [128, 16, 128], BF16)
                bt = sb(es, "d_bt", [128, 16], F32)
                nbt = sb(es, "d_nbt", [128, 16], F32)
                ecum = sb(es, "d_ecum", [128, 48], F32)
                GU = sb(es, "d_GU", [128, 16, 128], F32)
                EX = sb(es, "d_EX", [128, 16, 128], F32)
                Ys = [[sb(es, "d_Y%d%d" % (a, b), [128, 16, 128], BF16) for b in range(2)] for a in range(2)]
                AQ = sb(es, "d_AQ", [128, 16, 128], BF16)
                Ao = sb(es, "d_Ao", [128, 16, 128], BF16)
                Qm = sb(es, "d_Qm", [128, 16, 128], BF16)
                Yx = sb(es, "d_Yx", [128, 16, 128], BF16)
                Ub = sb(es, "d_Ub", [128, 16, 256], BF16)
                Rf = sb(es, "d_Rf", [128, 16, 256], F32)
                Wb_ = sb(es, "d_Wb", [128, 16, 128], BF16)
                WT = sb(es, "d_WT", [128, 16, 128], BF16)
                KG = sb(es, "d_KG", [128, 16, 128], BF16)
                vnb = sb(es, "d_vnb", [128, 16, 128], BF16)
                osr = Ring([sb(es, "d_os%d" % i, [128, 16, 128], F32) for i in range(2)])
                Sf = sb(es, "d_S", [128, 16, 128], F32)
                Sb = sb(es, "d_Sb", [128, 16, 128], BF16)
                for b_ in [Ys[0][0], Ys[0][1], Ys[1][0], Ys[1][1], Yx, Qm, Ub, Rf, vnb, Sf, Sb, AQ, Ao] + osr.b:
                    b_.q = [Buf(b_.t) for _ in range(4)]

                def dn_reset():
                    k.op("dve", lambda e: e.memset(Sf.t[:], 0.0), W=Sf.q)
                    k.op("dve", lambda e: e.memset(Sb.t[:], 0.0), W=Sb.q)

                def dn_link():
                    k.op("dve", lambda e: e.tensor_scalar(Sf.t[:], Sf.t[:], linkc.t[:, 0:1], None, ALU.mult), R=Sf.q + [linkc], W=Sf.q)
                    k.op("act", lambda e: e.activation(Sb.t[:], Sf.t[:], AF.Copy), R=Sf.q, W=Sb.q)

                def dn_load(tau):
                    QK = QKr.next()
                    k.dma("sp", QK.t[:, :, 0, :], kTs[tau], W=[QK])
                    k.dma("sp", QK.t[:, :, 1, :], qTs[tau], W=[QK])
                    kt = ktr.next()
                    k.dma("sp", kt.t[:], ktok[tau * 128:(tau + 1) * 128, :].rearrange("p (h c) -> p h c", h=16), W=[kt])
                    Rb = Rbr.next()
                    k.dma("sp", Rb.t[:, :, 0:128], vtok[tau * 128:(tau + 1) * 128, :].rearrange("p (h c) -> p h c", h=16), W=[Rb])
                    zz = zzr.next()
                    k.dma("sp", zz.t[:], zt[tau * 128:(tau + 1) * 128, 2048:2112], W=[zz])
                    return QK, kt, Rb, zz

                def dn_step(tau, d, loaded):
                    QK, kt, Rb, zz = loaded
                    TA = U_LE if d == 0 else U_GE
                    TB = U_GT if d == 0 else U_LT
                    TS = U_LT if d == 0 else U_GT
                    a_ap = zz.t[:, 32 * d:32 * d + 16]
                    b_ap = zz.t[:, 32 * d + 16:32 * d + 32]
                    isM = tau in (M0, M1)
                    mi = 0 if tau == M0 else 1
                    k.op("dve", lambda e: e.tensor_tensor(gt.t[:], a_ap, cst.t[:, 2 * d + 1, :], ALU.add), R=[zz, cst], W=[gt])
                    k.op("act", lambda e: e.activation(gt.t[:], gt.t[:], AF.Exp), R=[gt], W=[gt])
                    k.op("act", lambda e: e.activation(gt.t[:], gt.t[:], AF.Ln, bias=1.0), R=[gt], W=[gt])
                    k.op("dve", lambda e: e.tensor_tensor(gg.t[:], gt.t[:], cst.t[:, 2 * d, :], ALU.mult), R=[gt, cst], W=[gg])
                    k.op("act", lambda e: e.activation(bt.t[:], b_ap, AF.Exp, scale=-1.0), R=[zz], W=[bt])
                    k.op("dve", lambda e: e.tensor_scalar(bt.t[:], bt.t[:], 1.0, None, ALU.add), R=[bt], W=[bt])
                    k.op("dve", lambda e: e.reciprocal(bt.t[:], bt.t[:]), R=[bt], W=[bt])
                    if isM:
                        k.op("dve", lambda e: e.tensor_scalar(gg.t[:], gg.t[:], vmask.t[:, mi:mi + 1], None, ALU.mult), R=[gg, vmask], W=[gg])
                        k.op("dve", lambda e: e.tensor_scalar(bt.t[:], bt.t[:], vmask.t[:, mi:mi + 1], None, ALU.mult), R=[bt, vmask], W=[bt])
                    k.op("dve", lambda e: e.tensor_scalar(nbt.t[:], bt.t[:], -1.0, None, ALU.mult), R=[bt], W=[nbt])
                    pc = psr.next()
                    k.op("dve", lambda e: e.tensor_copy(ghl.t[:, 0, :], gg.t[:]), R=[gg], W=[ghl])
                    k.op("dve", lambda e: e.tensor_tensor(ghl.t[:, 1, :], gg.t[:], ghl.t[:, 0, :], ALU.subtract), R=[gg, ghl], W=[ghl])
                    for hl in range(2):
                        k.op("pe", lambda e: e.matmul(pc.t[:, 0:16], trib.t[:, TA, :], ghl.t[:, hl, :], start=(hl == 0), stop=(hl == 1)), R=[trib, ghl], W=[pc])
                    for hl in range(2):
                        k.op("pe", lambda e: e.matmul(pc.t[:, 16:32], trib.t[:, TB, :], ghl.t[:, hl, :], start=(hl == 0), stop=(hl == 1)), R=[trib, ghl], W=[pc])
                    for hl in range(2):
                        k.op("pe", lambda e: e.matmul(pc.t[:, 32:48], onesb.t[:], ghl.t[:, hl, :], start=(hl == 0), stop=(hl == 1)), R=[onesb, ghl], W=[pc])
                    k.op("act", lambda e: e.activation(ecum.t[:], pc.t[:, 0:48], AF.Exp), R=[pc], W=[ecum])
                    egam = ecum.t[:, 0:16]
                    erest = ecum.t[:, 16:32]
                    etot = ecum.t[:, 32:48]
                    k.op("pool", lambda e: e.tensor_tensor(GUh.t[:], trib.t[:, TA:TA + 1, :].to_broadcast([128, 16, 128]), ghl.t[:, 0, :].unsqueeze(2).to_broadcast([128, 16, 128]), ALU.mult), R=[trib, ghl], W=[GUh])
                    k.op("pool", lambda e: e.tensor_tensor(GUl.t[:], trib.t[:, TA:TA + 1, :].to_broadcast([128, 16, 128]), ghl.t[:, 1, :].unsqueeze(2).to_broadcast([128, 16, 128]), ALU.mult), R=[trib, ghl], W=[GUl])
                    pe_ = [psr.next(), psr.next()]
                    for hq in range(4):
                        pp = pe_[hq // 2]
                        k.op("pe", lambda e: e.matmul(pp.t[:, (hq % 2) * 512:(hq % 2) * 512 + 512], trib.t[:, TB, :], GUh.t[:, 4 * hq:4 * hq + 4, :], start=True, stop=False), R=[trib, GUh], W=[pp])
                        k.op("pe", lambda e: e.matmul(pp.t[:, (hq % 2) * 512:(hq % 2) * 512 + 512], trib.t[:, TB, :], GUl.t[:, 4 * hq:4 * hq + 4, :], start=False, stop=True), R=[trib, GUl], W=[pp])
                    for i2 in range(2):
                        k.op("act", lambda e: e.activation(EX.t[:, 8 * i2:8 * i2 + 8, :].rearrange("p a b -> p (a b)"), pe_[i2].t[:], AF.Exp), R=[pe_[i2]], W=[EX])
                    k.op("dve", lambda e: e.tensor_tensor(GU.t[:], EX.t[:], tri.t[:, TS:TS + 1, :].to_broadcast([128, 16, 128]), ALU.mult), R=[EX, tri], W=[GU])
                    k.op("pool", lambda e: e.tensor_tensor(GU.t[:], GU.t[:], nbt.t[:].unsqueeze(2).to_broadcast([128, 16, 128]), ALU.mult), R=[GU, nbt], W=[GU])
                    k.op("dve", lambda e: e.tensor_tensor(EX.t[:], EX.t[:], tri.t[:, TA:TA + 1, :].to_broadcast([128, 16, 128]), ALU.mult), R=[EX, tri], W=[EX])
                    YT0, Y0 = Ys[0]
                    for hq in range(4):
                        pk = psr.next()
                        pkv = pk.t[:].rearrange("p (a b c) -> p a b c", a=4, b=2)
                        for hl in range(4):
                            h = 4 * hq + hl
                            k.op("pe", lambda e: e.matmul(pk.t[:, hl * 256:(hl + 1) * 256], QK.t[:, h, 0, :], QK.t[:, h, :, :].rearrange("p a b -> p (a b)"), start=True, stop=True), R=[QK], W=[pk])
                        k.op("dve", lambda e: e.tensor_tensor(YT0.t[:, 4 * hq:4 * hq + 4, :], pkv[:, :, 0, :], GU.t[:, 4 * hq:4 * hq + 4, :], ALU.mult), R=[pk, GU], W=[YT0.q[hq]])
                        k.op("dve", lambda e: e.tensor_tensor(AQ.t[:, 4 * hq:4 * hq + 4, :], pkv[:, :, 1, :], EX.t[:, 4 * hq:4 * hq + 4, :], ALU.mult), R=[pk, EX], W=[AQ.q[hq]])
                    k.op("pool", lambda e: e.tensor_tensor(Ao.t[:], YT0.t[:], trib.t[:, BDM:BDM + 1, :].to_broadcast([128, 16, 128]), ALU.mult), R=YT0.q + [trib], W=Ao.q)
                    k.op("dve", lambda e: e.tensor_tensor(YT0.t[:], YT0.t[:], Ao.t[:], ALU.subtract), R=YT0.q + Ao.q, W=YT0.q)
                    AdT, AoT = Ao, YT0
                    YTc, Yc = Ys[1]
                    for i2 in range(2):
                        pt = psr.next()
                        ptv = pt.t[:, 0:512].bitcast(BF16).rearrange("p (a b) -> p a b", a=8)
                        for hl in range(8):
                            h = 8 * i2 + hl
                            k.op("pe", lambda e: e.transpose(ptv[:, hl, :], AdT.t[:, h, :], identb.t[:]), R=[AdT.q[2 * i2], AdT.q[2 * i2 + 1], identb], W=[pt])
                        evac(Yc.t[:, 8 * i2:8 * i2 + 8, :], ptv, R=[pt], W=[Yc.q[2 * i2], Yc.q[2 * i2 + 1]])
                    k.op("pool", lambda e: e.tensor_copy(YTc.t[:], AdT.t[:]), R=AdT.q, W=YTc.q)
                    k.op("dve", lambda e: e.tensor_tensor(Qm.t[:], AdT.t[:], identb.t[:].unsqueeze(1).to_broadcast([128, 16, 128]), ALU.add), R=AdT.q + [identb], W=Qm.q)
                    k.op("act", lambda e: e.activation(Rf.t[:, :, 0:128], Rb.t[:, :, 0:128], AF.Copy), R=[Rb], W=Rf.q)
                    k.op("dve", lambda e: e.tensor_tensor(Rf.t[:, :, 128:256], kt.t[:], egam.unsqueeze(2).to_broadcast([128, 16, 128]), ALU.mult), R=[kt, ecum], W=Rf.q)
                    k.op("pool", lambda e: e.tensor_copy(Rb.t[:, :, 128:256], Rf.t[:, :, 128:256]), R=Rf.q, W=[Rb])
                    cur = 1
                    ysets = [(Yx, Ys[0][1]), Ys[1]]
                    for lv in range(1, 5):
                        YT, Y = ysets[cur]
                        YTn, Yn = ysets[1 - cur]
                        for hq in range(4):
                            py = psr.next()
                            pyv = py.t[:].rearrange("p (a b c) -> p a b c", a=4, b=2)
                            for hl in range(4):
                                h = 4 * hq + hl
                                k.op("pe", lambda e: e.matmul(pyv[:, hl, 0, :], Y.t[:, h, :], YT.t[:, h, :], start=True, stop=True), R=[Y.q[hq], YT.q[hq]], W=[py])
                                k.op("pe", lambda e: e.matmul(pyv[:, hl, 1, :], YT.t[:, h, :], Y.t[:, h, :], start=True, stop=True), R=[Y.q[hq], YT.q[hq]], W=[py])
                            k.op("act", lambda e: e.activation(YTn.t[:, 4 * hq:4 * hq + 4, :], pyv[:, :, 0, :], AF.Copy), R=[py], W=[YTn.q[hq]])
                            k.op("dve", lambda e: e.tensor_copy(Yn.t[:, 4 * hq:4 * hq + 4, :], pyv[:, :, 1, :]), R=[py], W=[Yn.q[hq]])
                        for hq in range(4):
                            pq_ = psr.next()
                            pqv = pq_.t[:, 0:512].rearrange("p (a b) -> p a b", a=4)
                            for hl in range(4):
                                h = 4 * hq + hl
                                k.op("pe", lambda e: e.matmul(pqv[:, hl, :], Yn.t[:, h, :], Qm.t[:, h, :], start=True, stop=True), R=[Yn.q[hq], Qm.q[hq]], W=[pq_])
                            k.op("dve", lambda e: e.tensor_tensor(Qm.t[:, 4 * hq:4 * hq + 4, :], Qm.t[:, 4 * hq:4 * hq + 4, :], pqv, ALU.add), R=[Qm.q[hq], pq_], W=[Qm.q[hq]])
                        cur = 1 - cur
                    for it in range(4):
                        for hq in range(4):
                            if it == 0:
                                zsrc = Rb
                            else:
                                pz = psr.next()
                                pzv = pz.t[:].rearrange("p (a b) -> p a b", a=4)
                                for hl in range(4):
                                    h = 4 * hq + hl
                                    k.op("pe", lambda e: e.matmul(pzv[:, hl, :], AoT.t[:, h, :], Ub.t[:, h, :], start=True, stop=True), R=[AoT.q[hq], Ub.q[hq]], W=[pz])
                                k.op("dve", lambda e: e.tensor_tensor(Ub.t[:, 4 * hq:4 * hq + 4, :], Rf.t[:, 4 * hq:4 * hq + 4, :], pzv, ALU.add), R=[Rf.q[hq], pz], W=[Ub.q[hq]])
                                zsrc = Ub
                            pu = psr.next()
                            puv = pu.t[:].rearrange("p (a b) -> p a b", a=4)
                            for hl in range(4):
                                h = 4 * hq + hl
                                k.op("pe", lambda e: e.matmul(puv[:, hl, :], Qm.t[:, h, :], zsrc.t[:, h, :], start=True, stop=True), R=[Qm.q[hq], (zsrc.q[hq] if zsrc.q else zsrc)], W=[pu])
                            if it < 3:
                                k.op("act", lambda e: e.activation(Ub.t[:, 4 * hq:4 * hq + 4, :], puv, AF.Copy), R=[pu], W=[Ub.q[hq]])
                            else:
                                k.op("act", lambda e: e.activation(Rf.t[:, 4 * hq:4 * hq + 4, :], puv, AF.Copy), R=[pu], W=[Rf.q[hq]])
                    bbc = bt.t[:].unsqueeze(2).to_broadcast([128, 16, 128])
                    k.op("dve", lambda e: e.tensor_tensor(Rf.t[:, :, 0:128], Rf.t[:, :, 0:128], bbc, ALU.mult), R=Rf.q + [bt], W=Rf.q)
                    k.op("pool", lambda e: e.tensor_tensor(Wb_.t[:], Rf.t[:, :, 128:256], bbc, ALU.mult), R=Rf.q + [bt], W=[Wb_])
                    for i2 in range(2):
                        pt = psr.next()
                        ptv = pt.t[:, 0:512].bitcast(BF16).rearrange("p (a b) -> p a b", a=8)
                        for hl in range(8):
                            h = 8 * i2 + hl
                            k.op("pe", lambda e: e.transpose(ptv[:, hl, :], Wb_.t[:, h, :], identb.t[:]), R=[Wb_, identb], W=[pt])
                        evac(WT.t[:, 8 * i2:8 * i2 + 8, :], ptv, R=[pt], W=[WT])
                    k.op("pool", lambda e: e.tensor_tensor(KG.t[:], kt.t[:], erest.unsqueeze(2).to_broadcast([128, 16, 128]), ALU.mult), R=[kt, ecum], W=[KG])
                    osb = osr.next()
                    p12 = []
                    for hq in range(4):
                        pp = psr.next()
                        ppv = pp.t[:].rearrange("p (x a b) -> p x a b", x=2, a=4)
                        for hl in range(4):
                            h = 4 * hq + hl
                            k.op("pe", lambda e: e.matmul(ppv[:, 0, hl, :], WT.t[:, h, :], Sb.t[:, h, :], start=True, stop=True), R=[WT, Sb.q[hq]], W=[pp])
                            k.op("pe", lambda e: e.matmul(ppv[:, 1, hl, :], QK.t[:, h, 1, :], Sb.t[:, h, :], start=True, stop=True), R=[QK, Sb.q[hq]], W=[pp])
                        k.op("dve", lambda e: e.tensor_tensor(vnb.t[:, 4 * hq:4 * hq + 4, :], Rf.t[:, 4 * hq:4 * hq + 4, 0:128], ppv[:, 0, :, :], ALU.subtract), R=[Rf.q[hq], pp], W=[vnb.q[hq]])
                        k.op("dve", lambda e: e.tensor_tensor(osb.t[:, 4 * hq:4 * hq + 4, :], ppv[:, 1, :, :], egam[:, 4 * hq:4 * hq + 4].unsqueeze(2).to_broadcast([128, 4, 128]), ALU.mult), R=[pp, ecum], W=[osb.q[hq]])
                        p12.append(pp)
                    for hq in range(4):
                        pp = psr.next()
                        ppv = pp.t[:].rearrange("p (x a b) -> p x a b", x=2, a=4)
                        for hl in range(4):
                            h = 4 * hq + hl
                            k.op("pe", lambda e: e.matmul(ppv[:, 0, hl, :], AQ.t[:, h, :], vnb.t[:, h, :], start=True, stop=True), R=[AQ.q[hq], vnb.q[hq]], W=[pp])
                            k.op("pe", lambda e: e.matmul(ppv[:, 1, hl, :], KG.t[:, h, :], vnb.t[:, h, :], start=True, stop=True), R=[KG, vnb.q[hq]], W=[pp])
                        k.op("dve", lambda e: e.tensor_tensor(osb.t[:, 4 * hq:4 * hq + 4, :], osb.t[:, 4 * hq:4 * hq + 4, :], ppv[:, 0, :, :], ALU.add), R=[osb.q[hq], pp], W=[osb.q[hq]])
                        k.op("pool", lambda e: e.tensor_tensor(Sf.t[:, 4 * hq:4 * hq + 4, :], Sf.t[:, 4 * hq:4 * hq + 4, :], etot[:, 4 * hq:4 * hq + 4].unsqueeze(2).to_broadcast([128, 4, 128]), ALU.mult), R=[Sf.q[hq], ecum], W=[Sf.q[hq]])
                        k.op("dve", lambda e: e.tensor_tensor(Sf.t[:, 4 * hq:4 * hq + 4, :], Sf.t[:, 4 * hq:4 * hq + 4, :], ppv[:, 1, :, :], ALU.add), R=[Sf.q[hq], pp], W=[Sf.q[hq]])
                        k.op("act", lambda e: e.activation(Sb.t[:, 4 * hq:4 * hq + 4, :], Sf.t[:, 4 * hq:4 * hq + 4, :], AF.Copy), R=[Sf.q[hq]], W=[Sb.q[hq]])
                    return osb

                dn_reset()
                nxt = dn_load(0)
                for tau in range(NTILE):
                    cur = nxt
                    if tau + 1 < NTILE:
                        nxt = dn_load(tau + 1)
                    if tau == M1:
                        dn_link()
                    osb = dn_step(tau, 0, cur)
                    k.dma("sp", ofs[tau * 128:(tau + 1) * 128, :], osb.t[:].rearrange("p a b -> p (a b)"), R=osb.q)
                k.barrier()
                if phase_done():
                    return

                ofr = Ring([sb(es, "d_of%d" % i, [128, 16, 128], F32) for i in range(1)])
                zcr = Ring([sb(es, "d_zc%d" % i, [128, D], F32) for i in range(1)])
                osq = GU
                oss = sb(es, "d_oss", [128, 16], F32)
                ogr = Ring([sb(es, "d_og%d" % i, [128, D], BF16) for i in range(2)])

                dn_reset()
                nxt = dn_load(NTILE - 1)
                for tau in range(NTILE - 1, -1, -1):
                    ld = nxt
                    if tau - 1 >= 0:
                        nxt = dn_load(tau - 1)
                    of_t = ofr.next()
                    k.dma("sp", of_t.t[:].rearrange("p a b -> p (a b)"), ofs[tau * 128:(tau + 1) * 128, :], W=[of_t])
                    zc = zcr.next()
                    k.dma("sp", zc.t[:], zt[tau * 128:(tau + 1) * 128, 0:2048], W=[zc])
                    if tau == LA:
                        dn_link()
                    osb = dn_step(tau, 1, ld)
                    k.op("dve", lambda e: e.tensor_tensor(osb.t[:], osb.t[:], of_t.t[:], ALU.add), R=osb.q + [of_t], W=osb.q)
                    k.op("pool", lambda e: e.tensor_tensor(osq.t[:], osb.t[:], osb.t[:], ALU.mult), R=osb.q, W=[osq])
                    k.op("dve", lambda e: e.tensor_reduce(oss.t[:], osq.t[:], AX.X, ALU.add), R=[osq], W=[oss])
                    rstd_from_ss(oss, 1.0 / 128.0)
                    k.op("dve", lambda e: e.tensor_tensor(osb.t[:], osb.t[:], oss.t[:].unsqueeze(2).to_broadcast([128, 16, 128]), ALU.mult), R=osb.q + [oss], W=osb.q)
                    k.op("pool", lambda e: e.tensor_tensor(osb.t[:], osb.t[:], hnb.t[:, 0:1, :].to_broadcast([128, 16, 128]), ALU.mult), R=osb.q + [hnb], W=osb.q)
                    k.op("act", lambda e: e.activation(zc.t[:], zc.t[:], AF.Silu), R=[zc], W=[zc])
                    og = ogr.next()
                    k.op("dve", lambda e: e.tensor_tensor(og.t[:], osb.t[:].rearrange("p a b -> p (a b)"), zc.t[:], ALU.mult), R=osb.q + [zc], W=[og])
                    k.dma("sp", ogs[tau * 128:(tau + 1) * 128, :], og.t[:], R=[og])
                k.barrier()
            if phase_done():
                return

        try:
          for layer in range(4):
            j = layer // 2
            hsrc = hin if layer == 0 else hs
            if layer % 2 == 0:
                for ph in (lambda: phase_in(hsrc, norm_even[j:j + 1, :], wie[j], EVEN_IN, 0), lambda: phase_even(j), lambda: phase_out(hsrc, woe[j], False)):
                    if not stopped[0]:
                        ph()
            else:
                for ph in (lambda: phase_in(hsrc, norm_odd[j:j + 1, :], wio[j], ODD_IN, 6144), lambda: phase_odd(j), lambda: phase_out(hsrc, woo[j], layer == 3)):
                    if not stopped[0]:
                        ph()
        except _Stop:
            pass
        k.barrier()
        if debug:
            for nm in debug:
                src = scr_all[nm]
                dst = nc.dram_tensor("dbg_" + nm, list(src.shape), src.dtype, kind="ExternalOutput").ap()
                n0 = src.shape[0]
                stp = max(1, n0 // 8)
                for r in range(0, n0, stp):
                    k.dma("sp", dst[r:r + stp], src[r:r + stp])
            k.barrier()
    build.ninstr = k.nins
    build.nop = k.nop_
    return nc


def _core_inputs(xs, meta, S_seg, is_prompt):
    NT_SEG = S_seg // 128
    NTILE = 2 * (NT_SEG + 1)
    T = NTILE * 128
    hin = np.zeros((T, D), np.float32)
    pos = np.zeros((T,), np.float32)
    m0 = 0
    m1 = (NT_SEG + 1) * 128
    r0 = 128
    r1 = (NT_SEG + 2) * 128
    hin[m0 + 112:m0 + 128] = meta
    pos[m0 + 112:m0 + 128] = np.arange(16)
    vm = np.zeros((128, 2), np.float32)
    vm[112:, 0] = 1.0
    if is_prompt:
        x = xs[0]
        hin[r0:r0 + S_seg] = x[:S_seg]
        hin[r1:r1 + S_seg] = x[S_seg:]
        pos[r0:r0 + S_seg] = 16 + np.arange(S_seg)
        pos[r1:r1 + S_seg] = 16 + S_seg + np.arange(S_seg)
        link = 1.0
    else:
        hin[r0:r0 + S_seg] = xs[0]
        hin[r1:r1 + S_seg] = xs[1]
        hin[m1 + 112:m1 + 128] = meta
        pos[m1 + 112:m1 + 128] = np.arange(16)
        pos[r0:r0 + S_seg] = 16 + np.arange(S_seg)
        pos[r1:r1 + S_seg] = 16 + np.arange(S_seg)
        vm[112:, 1] = 1.0
        link = 0.0
    inv = (1.0 / (np.float32(10000.0) ** (np.arange(0, 64, 2, dtype=np.float32) / np.float32(64)))).astype(np.float32)
    ang = pos[:, None].astype(np.float32) * inv[None]
    return {
        "hin": hin,
        "cosr": np.cos(ang).astype(np.float32),
        "sinr": np.sin(ang).astype(np.float32),
        "linkc": np.full((128, 1), link, np.float32),
        "vmask": vm,
    }


def _consts():
    s = np.arange(128)[:, None]
    i = np.arange(128)[None, :]
    tri = np.stack([(s <= i), (s < i), (s >= i), (s > i), (s // 32 == i // 32)]).astype(np.float32)
    return {"ident": np.eye(128, dtype=np.float32), "tri": tri}


_NC_CACHE = {}


def kernel(**inputs):
    xp = np.asarray(inputs["x_prompt"], np.float32)
    xsm = np.asarray(inputs["x_sample"], np.float32)
    nb_p, seq, _ = xp.shape
    nb_s, dseq, _ = xsm.shape
    assert seq == 2 * dseq and nb_s % 2 == 0
    S_seg = dseq
    NT_SEG = S_seg // 128
    meta = np.asarray(inputs["meta_tokens"], np.float32)
    shared = _consts()
    for name in ["norm_even", "w_in_even", "a_sink", "b_gate_up_fwd", "b_gate_bias_fwd", "b_gate_up_bwd",
                 "b_gate_bias_bwd", "b_head_norm", "w_out_even", "norm_odd", "w_in_odd", "c_conv", "c_a_log_fwd",
                 "c_dt_bias_fwd", "c_a_log_bwd", "c_dt_bias_bwd", "c_head_norm", "w_out_odd"]:
        shared[name] = np.ascontiguousarray(np.asarray(inputs[name], np.float32))
    shared["norm_final"] = np.ascontiguousarray(np.asarray(inputs["norm_final"], np.float32).reshape(1, D))
    in_maps = []
    for p in range(nb_p):
        m = dict(shared)
        m.update(_core_inputs([xp[p]], meta, S_seg, True))
        in_maps.append(m)
    for s in range(nb_s // 2):
        m = dict(shared)
        m.update(_core_inputs([xsm[2 * s], xsm[2 * s + 1]], meta, S_seg, False))
        in_maps.append(m)
    ncores = len(in_maps)
    if NT_SEG not in _NC_CACHE:
        _NC_CACHE[NT_SEG] = build(NT_SEG)
    nc = _NC_CACHE[NT_SEG]
    res = run_bass_kernel_spmd(nc, in_maps, core_ids=list(range(ncores)))
    outs = [np.asarray(r["y"], np.float32) for r in res.results]
    y_prompt = np.stack([outs[p].reshape(seq, D) for p in range(nb_p)], axis=0)
    y_sample = np.stack([outs[nb_p + s // 2].reshape(2, dseq, D)[s % 2] for s in range(nb_s)], axis=0)
    return (y_prompt, y_sample)
```

```python
import numpy as np
from contextlib import ExitStack
import concourse.bass as bass
import concourse.mybir as mybir
from concourse.bass_utils import run_bass_kernel_spmd

F32 = mybir.dt.float32
BF16 = mybir.dt.bfloat16
AF = mybir.ActivationFunctionType
ALU = mybir.AluOpType
AX = mybir.AxisListType

D = 2048
EPS = 1e-6
N_META = 16
EVEN_IN = 5408
ODD_IN = 8256
G = 6
NDMA = 48


class Buf:
    __slots__ = ("t", "w", "r", "ex", "q")

    def __init__(self, t, ex=False):
        self.t = t
        self.w = None
        self.r = []
        self.ex = ex
        self.q = None


class Ring:
    def __init__(self, bufs):
        self.b = bufs
        self.i = 0

    def next(self):
        b = self.b[self.i % len(self.b)]
        self.i += 1
        return b


class K:
    def __init__(self, nc, es):
        self.nc = nc
        self.es = es
        self.eng = {"pe": nc.tensor, "dve": nc.vector, "act": nc.scalar, "pool": nc.gpsimd, "sp": nc.sync}
        self.sem = {n: es.enter_context(nc.semaphore("s_" + n)) for n in ["pe", "dve", "act", "pool"]}
        self.cnt = {n: 0 for n in self.sem}
        self.waited = {}
        self.dslots = [[es.enter_context(nc.semaphore("d%d" % i)), 0] for i in range(NDMA)]
        self.ndma = 0
        self.nins = 0
        import os
        self.limit = int(os.environ.get("KLIMIT", "0")) or None
        self.nop_ = 0
        self.dbgops = set(int(x) for x in os.environ.get("KDBG", "").split(",") if x)

    def _wait(self, on, deps):
        e = self.eng[on]
        for d in deps:
            if d is None:
                continue
            key, val = d
            if key == on and on == "pe":
                continue
            if self.waited.get((on, key), 0) >= val:
                continue
            sem = self.sem[key] if isinstance(key, str) else self.dslots[key][0]
            if self.nop_ in self.dbgops:
                print("DBGWAIT op", self.nop_, "on", on, "waits", key, val, "cnt", dict(self.cnt))
            e.wait_ge(sem, val)
            self.nins += 1
            self.waited[(on, key)] = val

    def _deps(self, R, W):
        deps = []
        for b in R:
            deps.append(b.w)
            if b.ex:
                deps.extend(b.r)
        for b in W:
            deps.append(b.w)
            deps.extend(b.r)
        return deps

    def _commit(self, tok, R, W):
        for b in R:
            b.r = [t for t in b.r if t[0] != tok[0]] + [tok]
        for b in W:
            b.w = tok
            b.r = []

    def op(self, on, fn, R=(), W=()):
        self.nop_ += 1
        if self.limit is not None and self.nop_ > self.limit:
            return None
        self._wait(on, self._deps(R, W))
        ins = fn(self.eng[on])
        self.cnt[on] += 1
        ins.then_inc(self.sem[on], 1)
        self.nins += 1
        tok = (on, self.cnt[on])
        self._commit(tok, R, W)
        return tok

    def dma(self, on, out, in_, R=(), W=(), **kw):
        self.nop_ += 1
        if self.limit is not None and self.nop_ > self.limit and not str(getattr(out.tensor, "name", "")).startswith("dbg_"):
            return None
        i = self.ndma % NDMA
        self.ndma += 1
        slot = self.dslots[i]
        self._wait(on, self._deps(R, W) + [(i, slot[1])])
        slot[1] += 16
        self.eng[on].dma_start(out=out, in_=in_, **kw).then_inc(slot[0], 16)
        self.nins += 1
        tok = (i, slot[1])
        self._commit(tok, R, W)
        return tok

    def barrier(self):
        deps = [(n, c) for n, c in self.cnt.items() if c > 0]
        deps += [(i, s[1]) for i, s in enumerate(self.dslots) if s[1] > 0]
        for on in self.eng:
            self._wait(on, deps)


class _Stop(Exception):
    pass


def build(NT_SEG, debug=False, stop_at=None):
    NTILE = 2 * (NT_SEG + 1)
    T = NTILE * 128
    assert NTILE % G == 0
    NSUP = NTILE // G
    M0, M1 = 0, NT_SEG + 1
    LA, LB = NT_SEG, NT_SEG + 2
    assert LA // G == LB // G
    NREAL = 2 * NT_SEG

    nc = bass.Bass("TRN2", target_bir_lowering=False)

    def din(name, shape, dt=F32):
        return nc.dram_tensor(name, list(shape), dt, kind="ExternalInput").ap()

    def dscr(name, shape, dt):
        t = nc.dram_tensor(name, list(shape), dt, kind="Internal").ap()
        scr_all[name] = t
        return t

    scr_all = {}

    phase_no = [0]

    def phase_done():
        phase_no[0] += 1
        if stop_at is not None and phase_no[0] >= stop_at:
            stopped[0] = True
        return stopped[0]

    stopped = [False]

    hin = din("hin", [T, D])
    cosr = din("cosr", [T, 32])
    sinr = din("sinr", [T, 32])
    linkc_d = din("linkc", [128, 1])
    vmask_d = din("vmask", [128, 2])
    ident_d = din("ident", [128, 128])
    tri_d = din("tri", [5, 128, 128])
    norm_even = din("norm_even", [2, D])
    w_in_even = din("w_in_even", [2, D, EVEN_IN])
    a_sink = din("a_sink", [2, 16])
    gu_f = din("b_gate_up_fwd", [2, 16, 512])
    gb_f = din("b_gate_bias_fwd", [2, 512])
    gu_b = din("b_gate_up_bwd", [2, 16, 512])
    gb_b = din("b_gate_bias_bwd", [2, 512])
    b_hnorm = din("b_head_norm", [2, 256])
    w_out_even = din("w_out_even", [2, D, D])
    norm_odd = din("norm_odd", [2, D])
    w_in_odd = din("w_in_odd", [2, D, ODD_IN])
    c_conv = din("c_conv", [2, 5, 6144])
    alog_f = din("c_a_log_fwd", [2, 16])
    dtb_f = din("c_dt_bias_fwd", [2, 16])
    alog_b = din("c_a_log_bwd", [2, 16])
    dtb_b = din("c_dt_bias_bwd", [2, 16])
    c_hnorm = din("c_head_norm", [2, 128])
    w_out_odd = din("w_out_odd", [2, D, D])
    norm_final = din("norm_final", [1, D])
    y = nc.dram_tensor("y", [NREAL * 128, D], F32, kind="ExternalOutput").ap()

    hs = dscr("hs", [T, D], F32)
    zt = dscr("zt", [T, EVEN_IN], F32)
    xcT = dscr("xcT", [6144, T], BF16)
    qTs = dscr("qTs", [NTILE, 128, 16, 128], BF16)
    kTs = dscr("kTs", [NTILE, 128, 16, 128], BF16)
    ktok = dscr("ktok", [T, D], BF16)
    vtok = dscr("vtok", [T, D], BF16)
    ofs = dscr("ofs", [T, D], F32)
    ogs = dscr("ogs", [T, D], BF16)
    wie = dscr("wie", [2, D, EVEN_IN], BF16)
    woe = dscr("woe", [2, D, D], BF16)
    wio = dscr("wio", [2, D, ODD_IN], BF16)
    woo = dscr("woo", [2, D, D], BF16)

    es0 = ExitStack()
    with es0:
        k = K(nc, es0)

        uid = [0]

        def uname(name):
            uid[0] += 1
            return "%s_u%d" % (name, uid[0])

        def sb(es, name, shape, dt):
            return Buf(es.enter_context(nc.sbuf_tensor(uname("sb_" + name), list(shape), dt)))

        def psb(es, name, shape, dt=F32):
            return Buf(es.enter_context(nc.psum_tensor(uname("ps_" + name), list(shape), dt)), ex=True)

        identb = sb(es0, "identb", [128, 128], BF16)
        tri = sb(es0, "tri", [128, 5, 128], F32)
        onesf = sb(es0, "onesf", [128, 128], F32)
        trib = sb(es0, "trib", [128, 5, 128], BF16)
        onesb = sb(es0, "onesb", [128, 128], BF16)
        linkc = sb(es0, "linkc", [128, 1], F32)
        vmask = sb(es0, "vmask", [128, 2], F32)
        amask = sb(es0, "amask", [128, 5, 128], BF16)
        k.dma("pool", identb.t[:], ident_d, W=[identb])
        k.dma("sp", tri.t[:], tri_d.rearrange("a p c -> p a c"), W=[tri])
        k.dma("sp", linkc.t[:], linkc_d, W=[linkc])
        k.dma("sp", vmask.t[:], vmask_d, W=[vmask])
        k.op("dve", lambda e: e.memset(onesf.t[:], 1.0), W=[onesf])
        k.op("dve", lambda e: e.memset(onesb.t[:], 1.0), W=[onesb])
        k.op("dve", lambda e: e.tensor_copy(trib.t[:], tri.t[:]), R=[tri], W=[trib])
        U_LE, U_LT, U_GE, U_GT, BDM = 0, 1, 2, 3, 4
        k.op("dve", lambda e: e.tensor_copy(amask.t[:, 0, :], tri.t[:, U_GE, :]), R=[tri], W=[amask])
        k.op("dve", lambda e: e.tensor_copy(amask.t[:, 1, :], tri.t[:, U_LE, :]), R=[tri], W=[amask])
        k.op("dve", lambda e: e.tensor_scalar(amask.t[:, 2, :], tri.t[:, U_GE, :], linkc.t[:, 0:1], None, ALU.mult), R=[tri, linkc], W=[amask])
        k.op("dve", lambda e: e.tensor_scalar(amask.t[:, 3, :], tri.t[:, U_LE, :], linkc.t[:, 0:1], None, ALU.mult), R=[tri, linkc], W=[amask])
        k.op("dve", lambda e: e.tensor_scalar(amask.t[:, 4, :], onesf.t[:], linkc.t[:, 0:1], None, ALU.mult), R=[onesf, linkc], W=[amask])
        AM_PREV, AM_NEXT, AM_PREVL, AM_NEXTL, AM_FULLL = 0, 1, 2, 3, 4

        conv_list = [(j, dst, src) for j in range(2) for (dst, src) in ((wie, w_in_even), (woe, w_out_even), (wio, w_in_odd), (woo, w_out_odd))]
        for ci, (j, dst, src) in enumerate(conv_list):
            for r in range(0, D, 128):
                k.dma("pool", dst[j, r:r + 128, :], src[j, r:r + 128, :])
            if ci == 0:
                k.barrier()

        evq = [0]

        def evac(dst_ap, src_ap, R, W, engines=("dve", "act")):
            on = engines[evq[0] % len(engines)]
            evq[0] += 1
            if on == "act":
                return k.op("act", lambda e: e.activation(dst_ap, src_ap, AF.Copy), R=R, W=W)
            return k.op(on, lambda e: e.tensor_copy(dst_ap, src_ap), R=R, W=W)

        def rstd_from_ss(rs, scale):
            k.op("dve", lambda e: e.tensor_scalar(rs.t[:], rs.t[:], scale, EPS, ALU.mult, ALU.add), R=[rs], W=[rs])
            k.op("act", lambda e: e.activation(rs.t[:], rs.t[:], AF.Sqrt), R=[rs], W=[rs])
            k.op("dve", lambda e: e.reciprocal(rs.t[:], rs.t[:]), R=[rs], W=[rs])

        def phase_in(hsrc, gamma_row, Wb, E, feat_cols):
            with ExitStack() as es:
                gam = sb(es, "in_gam", [128, D], F32)
                k.dma("sp", gam.t[:], gamma_row.partition_broadcast(128), W=[gam])
                hring = Ring([sb(es, "in_h%d" % i, [128, D], F32) for i in range(2)])
                sqj = sb(es, "in_sq", [128, D], BF16)
                ssr = Ring([sb(es, "in_ss%d" % i, [128, 1], F32) for i in range(2)])
                ubr = Ring([sb(es, "in_ub%d" % i, [128, D], BF16) for i in range(2)])
                uTr = Ring([sb(es, "in_uT%d" % i, [128, 16, G * 128], BF16) for i in range(2)])
                wring = Ring([sb(es, "in_w%d" % i, [128, 16, 512], BF16) for i in range(3)])
                groups = []
                c0_ = 0
                while c0_ < E:
                    groups.append((c0_, min(512, E - c0_)))
                    c0_ += 512

                def load_w(gi):
                    c0, cw = groups[gi]
                    wt = wring.next()
                    k.dma("sp", wt.t[:, :, 0:cw], Wb[:, c0:c0 + cw].rearrange("(kc p) c -> p kc c", p=128), W=[wt])
                    return wt

                pending = [load_w(0)]
                stage = Ring([sb(es, "in_st%d" % i, [128, 512], F32) for i in range(4)])
                stageb = Ring([sb(es, "in_sb%d" % i, [128, 512], BF16) for i in range(4)])
                psr = Ring([psb(es, "in_ps%d" % i, [128, 512]) for i in range(8)])
                for sp in range(NSUP):
                    uT = uTr.next()
                    for tl in range(G):
                        tau = sp * G + tl
                        hb = hring.next()
                        k.dma("sp", hb.t[:], hsrc[tau * 128:(tau + 1) * 128, :], W=[hb])
                        ss = ssr.next()
                        k.op("dve", lambda e: e.memset(ss.t[:], 0.0), W=[ss])
                        k.op("act", lambda e: e.activation(sqj.t[:], hb.t[:], AF.Square, accum_out=ss.t[:, 0:1]), R=[hb], W=[sqj, ss])
                        rstd_from_ss(ss, 1.0 / D)
                        ub = ubr.next()
                        k.op("dve", lambda e: e.scalar_tensor_tensor(ub.t[:], hb.t[:], ss.t[:, 0:1], gam.t[:], ALU.mult, ALU.mult), R=[hb, ss, gam], W=[ub])
                        for q in range(2):
                            pt = psr.next()
                            ptv = pt.t[:].bitcast(BF16).rearrange("p (a b) -> p a b", a=8)
                            for j in range(8):
                                kc = q * 8 + j
                                k.op("pe", lambda e: e.transpose(ptv[:, j, :], ub.t[:, kc * 128:(kc + 1) * 128], identb.t[:]), R=[ub, identb], W=[pt])
                            evac(uT.t[:, q * 8:(q + 1) * 8, tl * 128:(tl + 1) * 128], ptv, R=[pt], W=[uT])
                    for gi, (c0, cw) in enumerate(groups):
                        wt = pending[0]
                        if gi + 1 < len(groups):
                            pending[0] = load_w(gi + 1)
                        elif sp + 1 < NSUP:
                            pending[0] = load_w(0)
                        if c0 < feat_cols:
                            for j in range(cw // 128):
                                for half in range(2):
                                    ps = psr.next()
                                    nn = G * 64
                                    for kc in range(16):
                                        k.op("pe", lambda e: e.matmul(ps.t[:, 0:nn], wt.t[:, kc, j * 128:(j + 1) * 128], uT.t[:, kc, half * nn:(half + 1) * nn], start=(kc == 0), stop=(kc == 15)), R=[wt, uT], W=[ps])
                                    st = stageb.next()
                                    evac(st.t[:, 0:nn], ps.t[:, 0:nn], R=[ps], W=[st])
                                    col = sp * G * 128 + half * nn
                                    k.dma("sp", xcT[c0 + j * 128:c0 + (j + 1) * 128, col:col + nn], st.t[:, 0:nn], R=[st])
                        else:
                            for tl in range(G):
                                tau = sp * G + tl
                                ps = psr.next()
                                for kc in range(16):
                                    k.op("pe", lambda e: e.matmul(ps.t[:, 0:cw], uT.t[:, kc, tl * 128:(tl + 1) * 128], wt.t[:, kc, 0:cw], start=(kc == 0), stop=(kc == 15)), R=[wt, uT], W=[ps])
                                st = stage.next()
                                evac(st.t[:, 0:cw], ps.t[:, 0:cw], R=[ps], W=[st])
                                k.dma("sp", zt[tau * 128:(tau + 1) * 128, c0 - feat_cols:c0 - feat_cols + cw], st.t[:, 0:cw], R=[st])
                k.barrier()
            if phase_done():
                return

        def phase_out(hsrc, Wb, final):
            with ExitStack() as es:
                wout = sb(es, "o_w", [128, 16, D], BF16)
                for q in range(4):
                    k.dma("sp", wout.t[:, q * 4:(q + 1) * 4, :], Wb[q * 512:(q + 1) * 512, :].rearrange("(kc p) c -> p kc c", p=128), W=[wout])
                gfin = None
                if final:
                    gfin = sb(es, "o_gf", [128, D], F32)
                    k.dma("sp", gfin.t[:], norm_final.partition_broadcast(128), W=[gfin])
                    sqj = sb(es, "o_sq", [128, D], BF16)
                    ssr = Ring([sb(es, "o_ss%d" % i, [128, 1], F32) for i in range(2)])
                    yr = Ring([sb(es, "o_y%d" % i, [128, D], F32) for i in range(2)])
                ogr = Ring([sb(es, "o_og%d" % i, [128, D], BF16) for i in range(2)])
                hor = Ring([sb(es, "o_ho%d" % i, [128, D], F32) for i in range(2)])
                hnr = Ring([sb(es, "o_hn%d" % i, [128, D], F32) for i in range(2)])
                oTr = Ring([sb(es, "o_oT%d" % i, [128, 16, 128], BF16) for i in range(2)])
                psr = Ring([psb(es, "o_ps%d" % i, [128, 512]) for i in range(8)])
                def out_load(tau):
                    og = ogr.next()
                    k.dma("sp", og.t[:], ogs[tau * 128:(tau + 1) * 128, :], W=[og])
                    ho = hor.next()
                    k.dma("sp", ho.t[:], hsrc[tau * 128:(tau + 1) * 128, :], W=[ho])
                    return og, ho

                nxt_o = out_load(0)
                for tau in range(NTILE):
                    og, ho = nxt_o
                    if tau + 1 < NTILE:
                        nxt_o = out_load(tau + 1)
                    oT = oTr.next()
                    for q in range(2):
                        pt = psr.next()
                        ptv = pt.t[:].bitcast(BF16).rearrange("p (a b) -> p a b", a=8)
                        for j in range(8):
                            kc = q * 8 + j
                            k.op("pe", lambda e: e.transpose(ptv[:, j, :], og.t[:, kc * 128:(kc + 1) * 128], identb.t[:]), R=[og, identb], W=[pt])
                        evac(oT.t[:, q * 8:(q + 1) * 8, :], ptv, R=[pt], W=[oT])
                    hn = hnr.next()
                    for cg in range(4):
                        ps = psr.next()
                        for kc in range(16):
                            k.op("pe", lambda e: e.matmul(ps.t[:], oT.t[:, kc, :], wout.t[:, kc, cg * 512:(cg + 1) * 512], start=(kc == 0), stop=(kc == 15)), R=[oT, wout], W=[ps])
                        k.op("dve", lambda e: e.tensor_tensor(hn.t[:, cg * 512:(cg + 1) * 512], ps.t[:], ho.t[:, cg * 512:(cg + 1) * 512], ALU.add), R=[ps, ho], W=[hn])
                    if not final:
                        k.dma("sp", hs[tau * 128:(tau + 1) * 128, :], hn.t[:], R=[hn])
                    elif tau not in (M0, M1):
                        ss = ssr.next()
                        k.op("dve", lambda e: e.memset(ss.t[:], 0.0), W=[ss])
                        k.op("act", lambda e: e.activation(sqj.t[:], hn.t[:], AF.Square, accum_out=ss.t[:, 0:1]), R=[hn], W=[sqj, ss])
                        rstd_from_ss(ss, 1.0 / D)
                        yb = yr.next()
                        k.op("dve", lambda e: e.scalar_tensor_tensor(yb.t[:], hn.t[:], ss.t[:, 0:1], gfin.t[:], ALU.mult, ALU.mult), R=[hn, ss, gfin], W=[yb])
                        ry = (tau - 1) if tau <= NT_SEG else (tau - 2)
                        k.dma("sp", y[ry * 128:(ry + 1) * 128, :], yb.t[:], R=[yb])
                k.barrier()
            if phase_done():
                return

        def rope(dst4, src4, cs, nh, tA, tB, R, W):
            c = cs.t[:, 0:1, :].to_broadcast([128, nh, 32])
            s = cs.t[:, 1:2, :].to_broadcast([128, nh, 32])
            k.op("dve", lambda e: e.tensor_tensor(tA.t[:, :, 0, :], src4[:, :, 0, :], c, ALU.mult), R=R + [cs], W=[tA])
            k.op("dve", lambda e: e.tensor_tensor(tA.t[:, :, 1, :], src4[:, :, 1, :], s, ALU.mult), R=R + [cs], W=[tA])
            k.op("pool", lambda e: e.tensor_tensor(tB.t[:, :, 0, :], src4[:, :, 1, :], c, ALU.mult), R=R + [cs], W=[tB])
            k.op("pool", lambda e: e.tensor_tensor(tB.t[:, :, 1, :], src4[:, :, 0, :], s, ALU.mult), R=R + [cs], W=[tB])
            k.op("dve", lambda e: e.tensor_tensor(dst4[:, :, 0, :], tA.t[:, :, 0, :], tA.t[:, :, 1, :], ALU.subtract), R=[tA], W=W)
            k.op("pool", lambda e: e.tensor_tensor(dst4[:, :, 1, :], tB.t[:, :, 0, :], tB.t[:, :, 1, :], ALU.add), R=[tB], W=W)

        def phase_even(j):
            with ExitStack() as es:
                KTall = es.enter_context(nc.sbuf_tensor(uname("sb_e_KT"), [128, NTILE, 2, 128], BF16))
                VAall = es.enter_context(nc.sbuf_tensor(uname("sb_e_VA"), [128, NTILE, 2, 65], BF16))
                KTb = [Buf(KTall) for _ in range(NTILE)]
                VAb = [Buf(VAall) for _ in range(NTILE)]
                esink = sb(es, "e_esink", [128, 16], F32)
                k.dma("sp", esink.t[:], a_sink[j:j + 1, :].partition_broadcast(128), W=[esink])
                k.op("act", lambda e: e.activation(esink.t[:], esink.t[:], AF.Exp), R=[esink], W=[esink])
                gub = [sb(es, "e_gu%d" % d, [16, 512], BF16) for d in range(2)]
                gbias = [sb(es, "e_gb%d" % d, [128, 512], F32) for d in range(2)]
                k.dma("pool", gub[0].t[:], gu_f[j], W=[gub[0]])
                k.dma("pool", gub[1].t[:], gu_b[j], W=[gub[1]])
                k.dma("sp", gbias[0].t[:], gb_f[j:j + 1, :].partition_broadcast(128), W=[gbias[0]])
                k.dma("sp", gbias[1].t[:], gb_b[j:j + 1, :].partition_broadcast(128), W=[gbias[1]])
                hnb = sb(es, "e_hn", [128, 1, 256], F32)
                k.dma("sp", hnb.t[:, 0, :], b_hnorm[j:j + 1, :].partition_broadcast(128), W=[hnb])
                psr = Ring([psb(es, "e_ps%d" % i, [128, 1024]) for i in range(4)])
                csr = Ring([sb(es, "e_cs%d" % i, [128, 2, 32], F32) for i in range(2)])

                def load_cs(tau):
                    cs = csr.next()
                    k.dma("sp", cs.t[:, 0, :], cosr[tau * 128:(tau + 1) * 128, :], W=[cs])
                    k.dma("sp", cs.t[:, 1, :], sinr[tau * 128:(tau + 1) * 128, :], W=[cs])
                    return cs

                with ExitStack() as e1:
                    kvr = Ring([sb(e1, "e1_kv%d" % i, [128, 256], F32) for i in range(2)])
                    tA = sb(e1, "e1_tA", [128, 2, 2, 32], F32)
                    tB = sb(e1, "e1_tB", [128, 2, 2, 32], F32)
                    krr = Ring([sb(e1, "e1_kr%d" % i, [128, 2, 2, 64], BF16) for i in range(2)])
                    for tau in range(NTILE):
                        kv = kvr.next()
                        k.dma("sp", kv.t[:], zt[tau * 128:(tau + 1) * 128, 1024:1280], W=[kv])
                        cs = load_cs(tau)
                        kr = krr.next()
                        src4 = kv.t[:, 0:128].rearrange("p (h a b) -> p h a b", h=2, a=2)
                        dst4 = kr.t[:, :, 0, :].rearrange("p h (a b) -> p h a b", a=2)
                        rope(dst4, src4, cs, 2, tA, tB, [kv], [kr])
                        k.op("dve", lambda e: e.tensor_copy(kr.t[:, :, 1, :], kr.t[:, :, 0, :]), R=[kr], W=[kr])
                        pt = psr.next()
                        ptv = pt.t[:, 0:128].bitcast(BF16).rearrange("p (a b) -> p a b", a=2)
                        for g in range(2):
                            k.op("pe", lambda e: e.transpose(ptv[:, g, :], kr.t[:, g, :, :].rearrange("p a b -> p (a b)"), identb.t[:]), R=[kr, identb], W=[pt])
                        evac(KTall[:, tau, :, :], ptv, R=[pt], W=[KTb[tau]])
                        k.op("dve", lambda e: e.tensor_copy(VAall[:, tau, :, 0:64], kv.t[:, 128:256].rearrange("p (g d) -> p g d", g=2)), R=[kv], W=[VAb[tau]])
                        if tau == M0 or tau == M1:
                            mi = 0 if tau == M0 else 1
                            k.op("dve", lambda e: e.tensor_copy(VAall[:, tau, :, 64:65], vmask.t[:, mi:mi + 1].unsqueeze(1).to_broadcast([128, 2, 1])), R=[vmask], W=[VAb[tau]])
                        else:
                            k.op("dve", lambda e: e.memset(VAall[:, tau, :, 64:65], 1.0), W=[VAb[tau]])
                    k.barrier()

                gl = ExitStack()
                es.enter_context(gl)
                qkr = Ring([sb(gl, "g_qk%d" % i, [128, 1024], F32) for i in range(2)])
                vr = Ring([sb(gl, "g_v%d" % i, [128, 1024], F32) for i in range(2)])
                lr = Ring([sb(gl, "g_l%d" % i, [128, 16], F32) for i in range(2)])
                l16 = sb(gl, "g_l16", [128, 16], BF16)
                lT = sb(gl, "g_lT", [16, 128], BF16)
                gt = sb(gl, "g_gt", [128, 512], F32)
                gg = sb(gl, "g_gg", [128, 512], F32)
                ghl = sb(gl, "g_ghl", [128, 2, 512], BF16)
                eb = sb(gl, "g_eb", [128, 512], F32)
                enb = sb(gl, "g_enb", [128, 512], F32)
                ek = sb(gl, "g_ek", [128, 512], F32)
                dec = sb(gl, "g_dec", [128, 4], F32)
                qd = sb(gl, "g_qd", [128, 512], BF16)
                kd = sb(gl, "g_kd", [128, 512], BF16)
                kdec = sb(gl, "g_kdec", [128, 512], BF16)
                v16 = sb(gl, "g_v16", [128, 1024], BF16)
                qkT = sb(gl, "g_qkT", [128, 8, 128], BF16)
                att = sb(gl, "g_att", [128, 4, 128], BF16)
                Sf = sb(gl, "g_S", [128, 1024], F32)
                Sb = sb(gl, "g_Sb", [128, 1024], BF16)
                ofr = Ring([sb(gl, "g_of%d" % i, [128, 1024], F32) for i in range(2)])

                def gla_reset():
                    k.op("dve", lambda e: e.memset(Sf.t[:], 0.0), W=[Sf])
                    k.op("dve", lambda e: e.memset(Sb.t[:], 0.0), W=[Sb])

                def gla_link():
                    k.op("dve", lambda e: e.tensor_scalar(Sf.t[:], Sf.t[:], linkc.t[:, 0:1], None, ALU.mult), R=[Sf, linkc], W=[Sf])
                    k.op("act", lambda e: e.activation(Sb.t[:], Sf.t[:], AF.Copy), R=[Sf], W=[Sb])

                def gla_load(tau, d):
                    qk = qkr.next()
                    k.dma("sp", qk.t[:], zt[tau * 128:(tau + 1) * 128, 2304:3328], W=[qk])
                    vv = vr.next()
                    k.dma("sp", vv.t[:], zt[tau * 128:(tau + 1) * 128, 3328:4352], W=[vv])
                    ll = lr.next()
                    k.dma("sp", ll.t[:], zt[tau * 128:(tau + 1) * 128, 5376 + 16 * d:5392 + 16 * d], W=[ll])
                    return qk, vv, ll

                def gla_step(tau, d, loaded):
                    qk, vv, ll = loaded
                    TA = U_LE if d == 0 else U_GE
                    TB = U_GT if d == 0 else U_LT
                    k.op("dve", lambda e: e.tensor_copy(l16.t[:], ll.t[:]), R=[ll], W=[l16])
                    p0 = psr.next()
                    p0b = p0.t[0:16, 0:64].bitcast(BF16)
                    k.op("pe", lambda e: e.transpose(p0b, l16.t[:], identb.t[:]), R=[l16, identb], W=[p0])
                    k.op("dve", lambda e: e.tensor_copy(lT.t[:], p0b), R=[p0], W=[lT])
                    p1 = psr.next()
                    k.op("pe", lambda e: e.matmul(p1.t[:, 0:512], lT.t[:], gub[d].t[:], start=True, stop=True), R=[lT, gub[d]], W=[p1])
                    k.op("dve", lambda e: e.tensor_tensor(gt.t[:], p1.t[:, 0:512], gbias[d].t[:], ALU.add), R=[p1, gbias[d]], W=[gt])
                    k.op("act", lambda e: e.activation(gt.t[:], gt.t[:], AF.Exp, scale=-1.0), R=[gt], W=[gt])
                    k.op("act", lambda e: e.activation(gt.t[:], gt.t[:], AF.Ln, bias=1.0), R=[gt], W=[gt])
                    if tau in (M0, M1):
                        mi = 0 if tau == M0 else 1
                        k.op("dve", lambda e: e.tensor_scalar(gg.t[:], gt.t[:], -1.0 / 16.0, vmask.t[:, mi:mi + 1], ALU.mult, ALU.mult), R=[gt, vmask], W=[gg])
                    else:
                        k.op("dve", lambda e: e.tensor_scalar(gg.t[:], gt.t[:], -1.0 / 16.0, None, ALU.mult), R=[gt], W=[gg])
                    k.op("dve", lambda e: e.tensor_copy(ghl.t[:, 0, :], gg.t[:]), R=[gg], W=[ghl])
                    k.op("dve", lambda e: e.tensor_tensor(ghl.t[:, 1, :], gg.t[:], ghl.t[:, 0, :], ALU.subtract), R=[gg, ghl], W=[ghl])
                    p2 = psr.next()
                    for hl in range(2):
                        k.op("pe", lambda e: e.matmul(p2.t[:, 0:512], trib.t[:, TA, :], ghl.t[:, hl, :], start=(hl == 0), stop=(hl == 1)), R=[trib, ghl], W=[p2])
                    for hl in range(2):
                        k.op("pe", lambda e: e.matmul(p2.t[:, 512:1024], trib.t[:, TB, :], ghl.t[:, hl, :], start=(hl == 0), stop=(hl == 1)), R=[trib, ghl], W=[p2])
                    p3 = psr.next()
                    for h in range(4):
                        for hl in range(2):
                            k.op("pe", lambda e: e.matmul(p3.t[:, h:h + 1], ghl.t[:, hl, h * 128:(h + 1) * 128], onesb.t[:, 0:1], start=(hl == 0), stop=(hl == 1)), R=[ghl, onesb], W=[p3])
                    k.op("act", lambda e: e.activation(eb.t[:], p2.t[:, 0:512], AF.Exp), R=[p2], W=[eb])
                    k.op("act", lambda e: e.activation(enb.t[:], p2.t[:, 0:512], AF.Exp, scale=-1.0), R=[p2], W=[enb])
                    k.op("act", lambda e: e.activation(ek.t[:], p2.t[:, 512:1024], AF.Exp), R=[p2], W=[ek])
                    k.op("act", lambda e: e.activation(dec.t[:], p3.t[:, 0:4], AF.Exp), R=[p3], W=[dec])
                    k.op("dve", lambda e: e.scalar_tensor_tensor(qd.t[:], qk.t[:, 0:512], 128.0 ** -0.5, eb.t[:], ALU.mult, ALU.mult), R=[qk, eb], W=[qd])
                    k.op("pool", lambda e: e.tensor_tensor(kd.t[:], qk.t[:, 512:1024], enb.t[:], ALU.mult), R=[qk, enb], W=[kd])
                    k.op("pool", lambda e: e.tensor_tensor(kdec.t[:], qk.t[:, 512:1024], ek.t[:], ALU.mult), R=[qk, ek], W=[kdec])
                    k.op("act", lambda e: e.activation(v16.t[:], vv.t[:], AF.Copy), R=[vv], W=[v16])
                    p4 = psr.next()
                    p4v = p4.t[:, 0:512].bitcast(BF16).rearrange("p (a b) -> p a b", a=8)
                    for h in range(4):
                        k.op("pe", lambda e: e.transpose(p4v[:, h, :], qd.t[:, h * 128:(h + 1) * 128], identb.t[:]), R=[qd, identb], W=[p4])
                        k.op("pe", lambda e: e.transpose(p4v[:, 4 + h, :], kd.t[:, h * 128:(h + 1) * 128], identb.t[:]), R=[kd, identb], W=[p4])
                    k.op("dve", lambda e: e.tensor_copy(qkT.t[:], p4v), R=[p4], W=[qkT])
                    p5 = psr.next()
                    p5v = p5.t[:, 0:512].rearrange("p (a b) -> p a b", a=4)
                    for h in range(4):
                        k.op("pe", lambda e: e.matmul(p5v[:, h, :], qkT.t[:, 4 + h, :], qkT.t[:, h, :], start=True, stop=True), R=[qkT], W=[p5])
                    k.op("dve", lambda e: e.tensor_tensor(att.t[:], p5v, tri.t[:, TA:TA + 1, :].to_broadcast([128, 4, 128]), ALU.mult), R=[p5, tri], W=[att])
                    po = psr.next()
                    for h in range(4):
                        k.op("pe", lambda e: e.matmul(po.t[:, h * 256:(h + 1) * 256], att.t[:, h, :], v16.t[:, h * 256:(h + 1) * 256], start=True, stop=False), R=[att, v16], W=[po])
                        k.op("pe", lambda e: e.matmul(po.t[:, h * 256:(h + 1) * 256], qkT.t[:, h, :], Sb.t[:, h * 256:(h + 1) * 256], start=False, stop=True), R=[qkT, Sb], W=[po])
                    pS = psr.next()
                    for h in range(4):
                        k.op("pe", lambda e: e.matmul(pS.t[:, h * 256:(h + 1) * 256], kdec.t[:, h * 128:(h + 1) * 128], v16.t[:, h * 256:(h + 1) * 256], start=True, stop=True), R=[kdec, v16], W=[pS])
                    for h in range(4):
                        k.op("dve", lambda e: e.scalar_tensor_tensor(Sf.t[:, h * 256:(h + 1) * 256], Sf.t[:, h * 256:(h + 1) * 256], dec.t[:, h:h + 1], pS.t[:, h * 256:(h + 1) * 256], ALU.mult, ALU.add), R=[Sf, dec, pS], W=[Sf])
                    k.op("act", lambda e: e.activation(Sb.t[:], Sf.t[:], AF.Copy), R=[Sf], W=[Sb])
                    return po

                gla_reset()
                nxt = gla_load(0, 0)
                for tau in range(NTILE):
                    cur = nxt
                    if tau + 1 < NTILE:
                        nxt = gla_load(tau + 1, 0)
                    if tau == M1:
                        gla_link()
                    po = gla_step(tau, 0, cur)
                    ob = ofr.next()
                    k.op("act", lambda e: e.activation(ob.t[:], po.t[:], AF.Copy), R=[po], W=[ob])
                    k.dma("sp", ofs[tau * 128:(tau + 1) * 128, 0:1024], ob.t[:], R=[ob])
                k.barrier()
                if phase_done():
                    return

                with ExitStack() as e3:
                    zbr = Ring([sb(e3, "e3_zb%d" % i, [128, 1024], F32) for i in range(2)])
                    osum = sb(e3, "e3_osum", [128, 4, 256], F32)
                    osq = sb(e3, "e3_osq", [128, 4, 256], F32)
                    oss = sb(e3, "e3_oss", [128, 4], F32)
                    ogr = Ring([sb(e3, "e3_og%d" % i, [128, D], BF16) for i in range(2)])
                    qar = Ring([sb(e3, "e3_qa%d" % i, [128, 1024], F32) for i in range(2)])
                    zar = Ring([sb(e3, "e3_za%d" % i, [128, 1024], F32) for i in range(2)])
                    tA = sb(e3, "e3_tA", [128, 16, 2, 32], F32)
                    tB = sb(e3, "e3_tB", [128, 16, 2, 32], F32)
                    qr = sb(e3, "e3_qr", [128, 1024], BF16)
                    QT = sb(e3, "e3_QT", [128, 8, 128], BF16)
                    PTr = Ring([sb(e3, "e3_PT%d" % i, [128, 2, 4, 128], BF16) for i in range(6)])
                    oall = sb(e3, "e3_oall", [128, 16, 65], F32)
                    den = sb(e3, "e3_den", [128, 16], F32)
                    onr = sb(e3, "e3_on", [128, 16, 64], F32)

                    def e3_load(tau):
                        ld = gla_load(tau, 1)
                        zb = zbr.next()
                        k.dma("sp", zb.t[:], zt[tau * 128:(tau + 1) * 128, 4352:5376], W=[zb])
                        of_t = ofr.next()
                        k.dma("sp", of_t.t[:], ofs[tau * 128:(tau + 1) * 128, 0:1024], W=[of_t])
                        qa = qar.next()
                        k.dma("sp", qa.t[:], zt[tau * 128:(tau + 1) * 128, 0:1024], W=[qa])
                        za = zar.next()
                        k.dma("sp", za.t[:], zt[tau * 128:(tau + 1) * 128, 1280:2304], W=[za])
                        cs = load_cs(tau)
                        return ld, zb, of_t, qa, za, cs

                    def keylist(tau):
                        if tau == M0 or tau == M1:
                            return [(tau, None), (tau + 1, AM_NEXT)]
                        ks = []
                        seg1 = tau > M1
                        if seg1:
                            ks.append((M0, AM_FULLL))
                            ks.append((M1, None))
                        else:
                            ks.append((M0, None))
                        first = (tau == 1) or (tau == LB)
                        last = (tau == LA) or (tau == NTILE - 1)
                        if not first:
                            ks.append((tau - 1, AM_PREV))
                        elif tau == LB:
                            ks.append((LA, AM_PREVL))
                        ks.append((tau, None))
                        if not last:
                            ks.append((tau + 1, AM_NEXT))
                        elif tau == LA:
                            ks.append((LB, AM_NEXTL))
                        return ks

                    gla_reset()
                    nxt = e3_load(NTILE - 1)
                    for tau in range(NTILE - 1, -1, -1):
                        ld, zb, of_t, qa, za, cs = nxt
                        if tau - 1 >= 0:
                            nxt = e3_load(tau - 1)
                        if tau == LA:
                            gla_link()
                        og = ogr.next()
                        po = gla_step(tau, 1, ld)
                        k.op("dve", lambda e: e.tensor_tensor(osum.t[:], po.t[:].rearrange("p (a b) -> p a b", a=4), of_t.t[:].rearrange("p (a b) -> p a b", a=4), ALU.add), R=[po, of_t], W=[osum])
                        k.op("pool", lambda e: e.tensor_tensor(osq.t[:], osum.t[:], osum.t[:], ALU.mult), R=[osum], W=[osq])
                        k.op("dve", lambda e: e.tensor_reduce(oss.t[:], osq.t[:], AX.X, ALU.add), R=[osq], W=[oss])
                        rstd_from_ss(oss, 1.0 / 256.0)
                        k.op("dve", lambda e: e.tensor_tensor(osum.t[:], osum.t[:], oss.t[:].unsqueeze(2).to_broadcast([128, 4, 256]), ALU.mult), R=[osum, oss], W=[osum])
                        k.op("pool", lambda e: e.tensor_tensor(osum.t[:], osum.t[:], hnb.t[:, 0:1, :].to_broadcast([128, 4, 256]), ALU.mult), R=[osum, hnb], W=[osum])
                        k.op("act", lambda e: e.activation(zb.t[:], zb.t[:], AF.Silu), R=[zb], W=[zb])
                        k.op("dve", lambda e: e.tensor_tensor(og.t[:, 1024:2048], osum.t[:].rearrange("p a b -> p (a b)"), zb.t[:], ALU.mult), R=[osum, zb], W=[og])
                        src4 = qa.t[:].rearrange("p (h a b) -> p h a b", h=16, a=2)
                        dst4 = qr.t[:].rearrange("p (h a b) -> p h a b", h=16, a=2)
                        rope(dst4, src4, cs, 16, tA, tB, [qa], [qr])
                        pq = psr.next()
                        pqv = pq.t[:, 0:512].bitcast(BF16).rearrange("p (a b) -> p a b", a=8)
                        for jj in range(8):
                            k.op("pe", lambda e: e.transpose(pqv[:, jj, :], qr.t[:, jj * 128:(jj + 1) * 128], identb.t[:]), R=[qr, identb], W=[pq])
                        k.op("dve", lambda e: e.tensor_copy(QT.t[:], pqv), R=[pq], W=[QT])
                        keys = keylist(tau)
                        for g in range(2):
                            pts = []
                            for (c, mk) in keys:
                                pst = psr.next()
                                for par in range(2):
                                    k.op("pe", lambda e: e.matmul(pst.t[:, par * 512:(par + 1) * 512], KTall[par * 64:(par + 1) * 64, c, g, :], QT.t[par * 64:(par + 1) * 64, 4 * g:4 * g + 4, :], start=True, stop=True), R=[KTb[c], QT], W=[pst])
                                pt = PTr.next()
                                k.op("act", lambda e: e.activation(pt.t[:].rearrange("p a b c -> p (a b c)"), pst.t[:], AF.Exp, scale=0.125), R=[pst], W=[pt])
                                if mk is not None:
                                    k.op("pool", lambda e: e.tensor_tensor(pt.t[:].rearrange("p a b c -> p (a b) c"), pt.t[:].rearrange("p a b c -> p (a b) c"), amask.t[:, mk:mk + 1, :].to_broadcast([128, 8, 128]), ALU.mult), R=[pt, amask], W=[pt])
                                pts.append((pt, c))
                            pso = psr.next()
                            psov = pso.t[:].rearrange("p (a b) -> p a b", a=8)
                            for jj in range(4):
                                for par in range(2):
                                    hh = 2 * jj + par
                                    for ci, (pt, c) in enumerate(pts):
                                        k.op("pe", lambda e: e.matmul(psov[:, hh, 0:65], pt.t[:, par, jj, :], VAall[:, c, g, :], start=(ci == 0), stop=(ci == len(pts) - 1)), R=[pt, VAb[c]], W=[pso])
                            k.op("act", lambda e: e.activation(oall.t[:, 8 * g:8 * g + 8, :], psov[:, :, 0:65], AF.Copy), R=[pso], W=[oall])
                        k.op("dve", lambda e: e.tensor_tensor(den.t[:], oall.t[:, :, 64], esink.t[:], ALU.add), R=[oall, esink], W=[den])
                        k.op("dve", lambda e: e.reciprocal(den.t[:], den.t[:]), R=[den], W=[den])
                        k.op("dve", lambda e: e.tensor_tensor(onr.t[:], oall.t[:, :, 0:64], den.t[:].unsqueeze(2).to_broadcast([128, 16, 64]), ALU.mult), R=[oall, den], W=[onr])
                        k.op("act", lambda e: e.activation(za.t[:], za.t[:], AF.Silu), R=[za], W=[za])
                        k.op("dve", lambda e: e.tensor_tensor(og.t[:, 0:1024], onr.t[:].rearrange("p a b -> p (a b)"), za.t[:], ALU.mult), R=[onr, za], W=[og])
                        k.dma("sp", ogs[tau * 128:(tau + 1) * 128, :], og.t[:], R=[og])
                k.barrier()
            if phase_done():
                return

        def phase_odd(j):
            with ExitStack() as es:
                NW = G * 128
                cw = sb(es, "o1_cw", [128, 48, 5], F32)
                with nc.allow_non_contiguous_dma("conv weights, tiny"):
                    for kk in range(5):
                        k.dma("sp", cw.t[:, :, kk], c_conv[j, kk, :].rearrange("(c p) -> p c", p=128), W=[cw])
                xr = Ring([sb(es, "o1_x%d" % i, [128, NW + 4], BF16) for i in range(4)])
                dgr = Ring([sb(es, "o1_dg%d" % i, [128, 5, 128], BF16) for i in range(2)])
                acr = Ring([sb(es, "o1_ac%d" % i, [128, NW], F32) for i in range(3)])
                xl = sb(es, "o1_xl", [128, 4], BF16)
                sqr = Ring([sb(es, "o1_sq%d" % i, [128, NW], BF16) for i in range(2)])
                rnr = Ring([sb(es, "o1_rn%d" % i, [128, NW], F32) for i in range(2)])
                xnr = Ring([sb(es, "o1_xn%d" % i, [128, G, 128], BF16) for i in range(4)])
                tmr = Ring([sb(es, "o1_tm%d" % i, [128, G, 128], BF16) for i in range(2)])
                psr = Ring([psb(es, "o1_ps%d" % i, [128, 1024]) for i in range(4)])
                def o1_load(sp, cc):
                    col0 = sp * NW
                    xb = xr.next()
                    lo = col0 - 2
                    hi = col0 + NW + 2
                    if sp == 0:
                        k.op("pool", lambda e: e.memset(xb.t[:, 0:2], 0.0), W=[xb])
                        lo = col0
                    if sp == NSUP - 1:
                        k.op("pool", lambda e: e.memset(xb.t[:, NW + 2:NW + 4], 0.0), W=[xb])
                        hi = col0 + NW
                    k.dma("sp", xb.t[:, lo - (col0 - 2):hi - (col0 - 2)], xcT[cc * 128:(cc + 1) * 128, lo:hi], W=[xb])
                    return xb

                seq = [(sp, cc) for sp in range(NSUP) for cc in range(48)]
                xq = [o1_load(*seq[0]), o1_load(*seq[1])]

                def stage_a(si, sp, cc):
                    col0 = sp * NW
                    xb = xq.pop(0)
                    if si + 2 < len(seq):
                        xq.append(o1_load(*seq[si + 2]))
                    ac = acr.next()
                    dg = dgr.next()
                    for kk in range(5):
                        k.op("dve", lambda e: e.tensor_scalar(dg.t[:, kk, :], identb.t[:], cw.t[:, cc, kk:kk + 1], None, ALU.mult), R=[identb, cw], W=[dg])
                    nh = NW // 2
                    fixes = []
                    if sp == LA // G:
                        cA = (LA + 1) * 128 - 1 - col0
                        cB = LB * 128 - col0
                        k.op("dve", lambda e: e.tensor_scalar(xl.t[:, 0:2], xb.t[:, cA + 1:cA + 3], linkc.t[:, 0:1], None, ALU.mult), R=[xb, linkc], W=[xl])
                        k.op("dve", lambda e: e.tensor_scalar(xl.t[:, 2:4], xb.t[:, cB + 2:cB + 4], linkc.t[:, 0:1], None, ALU.mult), R=[xb, linkc], W=[xl])
                        fixes = [(cA, 2, 3), (cA, 3, 4), (cA - 1, 2, 4), (cB, 1, 1), (cB, 0, 0), (cB + 1, 1, 0)]
                    pcv = psr.next()
                    for half in range(2):
                        for kk in range(5):
                            if kk == 4:
                                for (col, xi, wk) in fixes:
                                    if col // nh == half:
                                        pc_ = half * 512 + col % nh
                                        k.op("pe", lambda e: e.matmul(pcv.t[:, pc_:pc_ + 1], dg.t[:, wk, :], xl.t[:, xi:xi + 1], start=False, stop=False), R=[dg, xl], W=[pcv])
                            k.op("pe", lambda e: e.matmul(pcv.t[:, half * 512:half * 512 + nh], dg.t[:, kk, :], xb.t[:, half * nh + kk:half * nh + kk + nh], start=(kk == 0), stop=(kk == 4)), R=[dg, xb], W=[pcv])
                    k.op("act", lambda e: e.activation(ac.t[:].rearrange("p (a b) -> p a b", a=2), pcv.t[:].rearrange("p (a b) -> p a b", a=2)[:, :, 0:nh], AF.Silu), R=[pcv], W=[ac])
                    return ac

                def stage_b(sp, cc, ac):
                    xn = xnr.next()
                    xnf = xn.t[:].rearrange("p a b -> p (a b)")
                    if cc < 32:
                        head = cc % 16
                        sq = sqr.next()
                        rn = rnr.next()
                        k.op("pool", lambda e: e.tensor_tensor(sq.t[:], ac.t[:], ac.t[:], ALU.mult), R=[ac], W=[sq])
                        ps = psr.next()
                        nn = NW // 2
                        for half in range(2):
                            k.op("pe", lambda e: e.matmul(ps.t[:, half * 512:half * 512 + nn], onesb.t[:], sq.t[:, half * nn:(half + 1) * nn], start=True, stop=True), R=[onesb, sq], W=[ps])
                        rnv = rn.t[:].rearrange("p (a b) -> p a b", a=2)
                        psv = ps.t[:].rearrange("p (a b) -> p a b", a=2)[:, :, 0:nn]
                        k.op("act", lambda e: e.activation(rnv, psv, AF.Sqrt, bias=EPS), R=[ps], W=[rn])
                        k.op("dve", lambda e: e.reciprocal(rn.t[:], rn.t[:]), R=[rn], W=[rn])
                        scl = (128.0 ** -0.5) if cc < 16 else 1.0
                        k.op("dve", lambda e: e.scalar_tensor_tensor(xnf, ac.t[:], scl, rn.t[:], ALU.mult, ALU.mult), R=[ac, rn], W=[xn])
                        dst = qTs if cc < 16 else kTs
                        k.dma("sp", dst[sp * G:(sp + 1) * G, :, head, :].rearrange("t p c -> p t c"), xn.t[:], R=[xn])
                    else:
                        head = cc - 32
                        k.op("dve", lambda e: e.tensor_copy(xnf, ac.t[:]), R=[ac], W=[xn])
                    return xn

                def stage_c(sp, cc, xn):
                    head = (cc % 16) if cc < 32 else (cc - 32)
                    if cc >= 16:
                        ps = psr.next()
                        psv = ps.t[:, 0:G * 64].bitcast(BF16).rearrange("p (a b) -> p a b", a=G)
                        for tl in range(G):
                            k.op("pe", lambda e: e.transpose(psv[:, tl, :], xn.t[:, tl, :], identb.t[:]), R=[xn, identb], W=[ps])
                        tm = tmr.next()
                        evac(tm.t[:], psv, R=[ps], W=[tm])
                        dst = ktok if cc < 32 else vtok
                        k.dma("sp", dst[sp * NW:(sp + 1) * NW, head * 128:(head + 1) * 128].rearrange("(t p) c -> p t c", p=128), tm.t[:], R=[tm])

                ac_prev = stage_a(0, *seq[0])
                xn_prev = None
                for si in range(len(seq)):
                    ac_cur = ac_prev
                    if si + 1 < len(seq):
                        ac_prev = stage_a(si + 1, *seq[si + 1])
                    xn_cur = stage_b(seq[si][0], seq[si][1], ac_cur)
                    if xn_prev is not None:
                        stage_c(seq[si - 1][0], seq[si - 1][1], xn_prev)
                    xn_prev = xn_cur
                stage_c(seq[-1][0], seq[-1][1], xn_prev)
                k.barrier()
            if phase_done():
                return

            with ExitStack() as es:
                cst = sb(es, "d_cst", [128, 4, 16], F32)
                k.dma("sp", cst.t[:, 0, :], alog_f[j:j + 1, :].partition_broadcast(128), W=[cst])
                k.dma("sp", cst.t[:, 1, :], dtb_f[j:j + 1, :].partition_broadcast(128), W=[cst])
                k.dma("sp", cst.t[:, 2, :], alog_b[j:j + 1, :].partition_broadcast(128), W=[cst])
                k.dma("sp", cst.t[:, 3, :], dtb_b[j:j + 1, :].partition_broadcast(128), W=[cst])
                for a in (0, 2):
                    k.op("act", lambda e: e.activation(cst.t[:, a, :], cst.t[:, a, :], AF.Exp), R=[cst], W=[cst])
                    k.op("dve", lambda e: e.tensor_scalar(cst.t[:, a, :], cst.t[:, a, :], -1.0, None, ALU.mult), R=[cst], W=[cst])
                hnb = sb(es, "d_hn", [128, 1, 128], F32)
                k.dma("sp", hnb.t[:, 0, :], c_hnorm[j:j + 1, :].partition_broadcast(128), W=[hnb])
                psr = Ring([psb(es, "d_ps%d" % i, [128, 1024]) for i in range(4)])
                QKr = Ring([sb(es, "d_QK%d" % i, [128, 16, 2, 128], BF16) for i in range(2)])
                ktr = Ring([sb(es, "d_kt%d" % i, [128, 16, 128], BF16) for i in range(2)])
                Rbr = Ring([sb(es, "d_Rb%d" % i, [128, 16, 256], BF16) for i in range(2)])
                zzr = Ring([sb(es, "d_zz%d" % i, [128, 64], F32) for i in range(2)])
                gt = sb(es, "d_gt", [128, 16], F32)
                gg = sb(es, "d_gg", [128, 16], F32)
                ghl = sb(es, "d_ghl", [128, 2, 16], BF16)
                GUl = sb(es, "d_GUl", [128, 16, 128], BF16)
                GUh = sb(es, "d_GUh", [128, 16, 128], BF16)
                bt = sb(es, "d_bt", [128, 16], F32)
                nbt = sb(es, "d_nbt", [128, 16], F32)
                ecum = sb(es, "d_ecum", [128, 48], F32)
                GU = sb(es, "d_GU", [128, 16, 128], F32)
                EX = sb(es, "d_EX", [128, 16, 128], F32)
                Ys = [[sb(es, "d_Y%d%d" % (a, b), [128, 16, 128], BF16) for b in range(2)] for a in range(2)]
                AQ = sb(es, "d_AQ", [128, 16, 128], BF16)
                Ao = sb(es, "d_Ao", [128, 16, 128], BF16)
                Qm = sb(es, "d_Qm", [128, 16, 128], BF16)
                Yx = sb(es, "d_Yx", [128, 16, 128], BF16)
                Ub = sb(es, "d_Ub", [128, 16, 256], BF16)
                Rf = sb(es, "d_Rf", [128, 16, 256], F32)
                Wb_ = sb(es, "d_Wb", [128, 16, 128], BF16)
                WT = sb(es, "d_WT", [128, 16, 128], BF16)
                KG = sb(es, "d_KG", [128, 16, 128], BF16)
                vnb = sb(es, "d_vnb", [128, 16, 128], BF16)
                osr = Ring([sb(es, "d_os%d" % i, [128, 16, 128], F32) for i in range(2)])
                Sf = sb(es, "d_S", [128, 16, 128], F32)
                Sb = sb(es, "d_Sb", [128, 16, 128], BF16)
                for b_ in [Ys[0][0], Ys[0][1], Ys[1][0], Ys[1][1], Yx, Qm, Ub, Rf, vnb, Sf, Sb, AQ, Ao] + osr.b:
                    b_.q = [Buf(b_.t) for _ in range(4)]

                def dn_reset():
                    k.op("dve", lambda e: e.memset(Sf.t[:], 0.0), W=Sf.q)
                    k.op("dve", lambda e: e.memset(Sb.t[:], 0.0), W=Sb.q)

                def dn_link():
                    k.op("dve", lambda e: e.tensor_scalar(Sf.t[:], Sf.t[:], linkc.t[:, 0:1], None, ALU.mult), R=Sf.q + [linkc], W=Sf.q)
                    k.op("act", lambda e: e.activation(Sb.t[:], Sf.t[:], AF.Copy), R=Sf.q, W=Sb.q)

                def dn_load(tau):
                    QK = QKr.next()
                    k.dma("sp", QK.t[:, :, 0, :], kTs[tau], W=[QK])
                    k.dma("sp", QK.t[:, :, 1, :], qTs[tau], W=[QK])
                    kt = ktr.next()
                    k.dma("sp", kt.t[:], ktok[tau * 128:(tau + 1) * 128, :].rearrange("p (h c) -> p h c", h=16), W=[kt])
                    Rb = Rbr.next()
                    k.dma("sp", Rb.t[:, :, 0:128], vtok[tau * 128:(tau + 1) * 128, :].rearrange("p (h c) -> p h c", h=16), W=[Rb])
                    zz = zzr.next()
                    k.dma("sp", zz.t[:], zt[tau * 128:(tau + 1) * 128, 2048:2112], W=[zz])
                    return QK, kt, Rb, zz

                def dn_step(tau, d, loaded):
                    QK, kt, Rb, zz = loaded
                    TA = U_LE if d == 0 else U_GE
                    TB = U_GT if d == 0 else U_LT
                    TS = U_LT if d == 0 else U_GT
                    a_ap = zz.t[:, 32 * d:32 * d + 16]
                    b_ap = zz.t[:, 32 * d + 16:32 * d + 32]
                    isM = tau in (M0, M1)
                    mi = 0 if tau == M0 else 1
                    k.op("dve", lambda e: e.tensor_tensor(gt.t[:], a_ap, cst.t[:, 2 * d + 1, :], ALU.add), R=[zz, cst], W=[gt])
                    k.op("act", lambda e: e.activation(gt.t[:], gt.t[:], AF.Exp), R=[gt], W=[gt])
                    k.op("act", lambda e: e.activation(gt.t[:], gt.t[:], AF.Ln, bias=1.0), R=[gt], W=[gt])
                    k.op("dve", lambda e: e.tensor_tensor(gg.t[:], gt.t[:], cst.t[:, 2 * d, :], ALU.mult), R=[gt, cst], W=[gg])
                    k.op("act", lambda e: e.activation(bt.t[:], b_ap, AF.Exp, scale=-1.0), R=[zz], W=[bt])
                    k.op("dve", lambda e: e.tensor_scalar(bt.t[:], bt.t[:], 1.0, None, ALU.add), R=[bt], W=[bt])
                    k.op("dve", lambda e: e.reciprocal(bt.t[:], bt.t[:]), R=[bt], W=[bt])
                    if isM:
                        k.op("dve", lambda e: e.tensor_scalar(gg.t[:], gg.t[:], vmask.t[:, mi:mi + 1], None, ALU.mult), R=[gg, vmask], W=[gg])
                        k.op("dve", lambda e: e.tensor_scalar(bt.t[:], bt.t[:], vmask.t[:, mi:mi + 1], None, ALU.mult), R=[bt, vmask], W=[bt])
                    k.op("dve", lambda e: e.tensor_scalar(nbt.t[:], bt.t[:], -1.0, None, ALU.mult), R=[bt], W=[nbt])
                    pc = psr.next()
                    k.op("dve", lambda e: e.tensor_copy(ghl.t[:, 0, :], gg.t[:]), R=[gg], W=[ghl])
                    k.op("dve", lambda e: e.tensor_tensor(ghl.t[:, 1, :], gg.t[:], ghl.t[:, 0, :], ALU.subtract), R=[gg, ghl], W=[ghl])
                    for hl in range(2):
                        k.op("pe", lambda e: e.matmul(pc.t[:, 0:16], trib.t[:, TA, :], ghl.t[:, hl, :], start=(hl == 0), stop=(hl == 1)), R=[trib, ghl], W=[pc])
                    for hl in range(2):
                        k.op("pe", lambda e: e.matmul(pc.t[:, 16:32], trib.t[:, TB, :], ghl.t[:, hl, :], start=(hl == 0), stop=(hl == 1)), R=[trib, ghl], W=[pc])
                    for hl in range(2):
                        k.op("pe", lambda e: e.matmul(pc.t[:, 32:48], onesb.t[:], ghl.t[:, hl, :], start=(hl == 0), stop=(hl == 1)), R=[onesb, ghl], W=[pc])
                    k.op("act", lambda e: e.activation(ecum.t[:], pc.t[:, 0:48], AF.Exp), R=[pc], W=[ecum])
                    egam = ecum.t[:, 0:16]
                    erest = ecum.t[:, 16:32]
                    etot = ecum.t[:, 32:48]
                    k.op("pool", lambda e: e.tensor_tensor(GUh.t[:], trib.t[:, TA:TA + 1, :].to_broadcast([128, 16, 128]), ghl.t[:, 0, :].unsqueeze(2).to_broadcast([128, 16, 128]), ALU.mult), R=[trib, ghl], W=[GUh])
                    k.op("pool", lambda e: e.tensor_tensor(GUl.t[:], trib.t[:, TA:TA + 1, :].to_broadcast([128, 16, 128]), ghl.t[:, 1, :].unsqueeze(2).to_broadcast([128, 16, 128]), ALU.mult), R=[trib, ghl], W=[GUl])
                    pe_ = [psr.next(), psr.next()]
                    for hq in range(4):
                        pp = pe_[hq // 2]
                        k.op("pe", lambda e: e.matmul(pp.t[:, (hq % 2) * 512:(hq % 2) * 512 + 512], trib.t[:, TB, :], GUh.t[:, 4 * hq:4 * hq + 4, :], start=True, stop=False), R=[trib, GUh], W=[pp])
                        k.op("pe", lambda e: e.matmul(pp.t[:, (hq % 2) * 512:(hq % 2) * 512 + 512], trib.t[:, TB, :], GUl.t[:, 4 * hq:4 * hq + 4, :], start=False, stop=True), R=[trib, GUl], W=[pp])
                    for i2 in range(2):
                        k.op("act", lambda e: e.activation(EX.t[:, 8 * i2:8 * i2 + 8, :].rearrange("p a b -> p (a b)"), pe_[i2].t[:], AF.Exp), R=[pe_[i2]], W=[EX])
                    k.op("dve", lambda e: e.tensor_tensor(GU.t[:], EX.t[:], tri.t[:, TS:TS + 1, :].to_broadcast([128, 16, 128]), ALU.mult), R=[EX, tri], W=[GU])
                    k.op("pool", lambda e: e.tensor_tensor(GU.t[:], GU.t[:], nbt.t[:].unsqueeze(2).to_broadcast([128, 16, 128]), ALU.mult), R=[GU, nbt], W=[GU])
                    k.op("dve", lambda e: e.tensor_tensor(EX.t[:], EX.t[:], tri.t[:, TA:TA + 1, :].to_broadcast([128, 16, 128]), ALU.mult), R=[EX, tri], W=[EX])
                    YT0, Y0 = Ys[0]
                    for hq in range(4):
                        pk = psr.next()
                        pkv = pk.t[:].rearrange("p (a b c) -> p a b c", a=4, b=2)
                        for hl in range(4):
                            h = 4 * hq + hl
                            k.op("pe", lambda e: e.matmul(pk.t[:, hl * 256:(hl + 1) * 256], QK.t[:, h, 0, :], QK.t[:, h, :, :].rearrange("p a b -> p (a b)"), start=True, stop=True), R=[QK], W=[pk])
                        k.op("dve", lambda e: e.tensor_tensor(YT0.t[:, 4 * hq:4 * hq + 4, :], pkv[:, :, 0, :], GU.t[:, 4 * hq:4 * hq + 4, :], ALU.mult), R=[pk, GU], W=[YT0.q[hq]])
                        k.op("dve", lambda e: e.tensor_tensor(AQ.t[:, 4 * hq:4 * hq + 4, :], pkv[:, :, 1, :], EX.t[:, 4 * hq:4 * hq + 4, :], ALU.mult), R=[pk, EX], W=[AQ.q[hq]])
                    k.op("pool", lambda e: e.tensor_tensor(Ao.t[:], YT0.t[:], trib.t[:, BDM:BDM + 1, :].to_broadcast([128, 16, 128]), ALU.mult), R=YT0.q + [trib], W=Ao.q)
                    k.op("dve", lambda e: e.tensor_tensor(YT0.t[:], YT0.t[:], Ao.t[:], ALU.subtract), R=YT0.q + Ao.q, W=YT0.q)
                    AdT, AoT = Ao, YT0
                    YTc, Yc = Ys[1]
                    for i2 in range(2):
                        pt = psr.next()
                        ptv = pt.t[:, 0:512].bitcast(BF16).rearrange("p (a b) -> p a b", a=8)
                        for hl in range(8):
                            h = 8 * i2 + hl
                            k.op("pe", lambda e: e.transpose(ptv[:, hl, :], AdT.t[:, h, :], identb.t[:]), R=[AdT.q[2 * i2], AdT.q[2 * i2 + 1], identb], W=[pt])
                        evac(Yc.t[:, 8 * i2:8 * i2 + 8, :], ptv, R=[pt], W=[Yc.q[2 * i2], Yc.q[2 * i2 + 1]])
                    k.op("pool", lambda e: e.tensor_copy(YTc.t[:], AdT.t[:]), R=AdT.q, W=YTc.q)
                    k.op("dve", lambda e: e.tensor_tensor(Qm.t[:], AdT.t[:], identb.t[:].unsqueeze(1).to_broadcast([128, 16, 128]), ALU.add), R=AdT.q + [identb], W=Qm.q)
                    k.op("act", lambda e: e.activation(Rf.t[:, :, 0:128], Rb.t[:, :, 0:128], AF.Copy), R=[Rb], W=Rf.q)
                    k.op("dve", lambda e: e.tensor_tensor(Rf.t[:, :, 128:256], kt.t[:], egam.unsqueeze(2).to_broadcast([128, 16, 128]), ALU.mult), R=[kt, ecum], W=Rf.q)
                    k.op("pool", lambda e: e.tensor_copy(Rb.t[:, :, 128:256], Rf.t[:, :, 128:256]), R=Rf.q, W=[Rb])
                    cur = 1
                    ysets = [(Yx, Ys[0][1]), Ys[1]]
                    for lv in range(1, 5):
                        YT, Y = ysets[cur]
                        YTn, Yn = ysets[1 - cur]
                        for hq in range(4):
                            py = psr.next()
                            pyv = py.t[:].rearrange("p (a b c) -> p a b c", a=4, b=2)
                            for hl in range(4):
                                h = 4 * hq + hl
                                k.op("pe", lambda e: e.matmul(pyv[:, hl, 0, :], Y.t[:, h, :], YT.t[:, h, :], start=True, stop=True), R=[Y.q[hq], YT.q[hq]], W=[py])
                                k.op("pe", lambda e: e.matmul(pyv[:, hl, 1, :], YT.t[:, h, :], Y.t[:, h, :], start=True, stop=True), R=[Y.q[hq], YT.q[hq]], W=[py])
                            k.op("act", lambda e: e.activation(YTn.t[:, 4 * hq:4 * hq + 4, :], pyv[:, :, 0, :], AF.Copy), R=[py], W=[YTn.q[hq]])
                            k.op("dve", lambda e: e.tensor_copy(Yn.t[:, 4 * hq:4 * hq + 4, :], pyv[:, :, 1, :]), R=[py], W=[Yn.q[hq]])
                        for hq in range(4):
                            pq_ = psr.next()
                            pqv = pq_.t[:, 0:512].rearrange("p (a b) -> p a b", a=4)
                            for hl in range(4):
                                h = 4 * hq + hl
                                k.op("pe", lambda e: e.matmul(pqv[:, hl, :], Yn.t[:, h, :], Qm.t[:, h, :], start=True, stop=True), R=[Yn.q[hq], Qm.q[hq]], W=[pq_])
                            k.op("dve", lambda e: e.tensor_tensor(Qm.t[:, 4 * hq:4 * hq + 4, :], Qm.t[:, 4 * hq:4 * hq + 4, :], pqv, ALU.add), R=[Qm.q[hq], pq_], W=[Qm.q[hq]])
                        cur = 1 - cur
                    for it in range(4):
                        for hq in range(4):
                            if it == 0:
                                zsrc = Rb
                            else:
                                pz = psr.next()
                                pzv = pz.t[:].rearrange("p (a b) -> p a b", a=4)
                                for hl in range(4):
                                    h = 4 * hq + hl
                                    k.op("pe", lambda e: e.matmul(pzv[:, hl, :], AoT.t[:, h, :], Ub.t[:, h, :], start=True, stop=True), R=[AoT.q[hq], Ub.q[hq]], W=[pz])
                                k.op("dve", lambda e: e.tensor_tensor(Ub.t[:, 4 * hq:4 * hq + 4, :], Rf.t[:, 4 * hq:4 * hq + 4, :], pzv, ALU.add), R=[Rf.q[hq], pz], W=[Ub.q[hq]])
                                zsrc = Ub
                            pu = psr.next()
                            puv = pu.t[:].rearrange("p (a b) -> p a b", a=4)
                            for hl in range(4):
                                h = 4 * hq + hl
                                k.op("pe", lambda e: e.matmul(puv[:, hl, :], Qm.t[:, h, :], zsrc.t[:, h, :], start=True, stop=True), R=[Qm.q[hq], (zsrc.q[hq] if zsrc.q else zsrc)], W=[pu])
                            if it < 3:
                                k.op("act", lambda e: e.activation(Ub.t[:, 4 * hq:4 * hq + 4, :], puv, AF.Copy), R=[pu], W=[Ub.q[hq]])
                            else:
                                k.op("act", lambda e: e.activation(Rf.t[:, 4 * hq:4 * hq + 4, :], puv, AF.Copy), R=[pu], W=[Rf.q[hq]])
                    bbc = bt.t[:].unsqueeze(2).to_broadcast([128, 16, 128])
                    k.op("dve", lambda e: e.tensor_tensor(Rf.t[:, :, 0:128], Rf.t[:, :, 0:128], bbc, ALU.mult), R=Rf.q + [bt], W=Rf.q)
                    k.op("pool", lambda e: e.tensor_tensor(Wb_.t[:], Rf.t[:, :, 128:256], bbc, ALU.mult), R=Rf.q + [bt], W=[Wb_])
                    for i2 in range(2):
                        pt = psr.next()
                        ptv = pt.t[:, 0:512].bitcast(BF16).rearrange("p (a b) -> p a b", a=8)
                        for hl in range(8):
                            h = 8 * i2 + hl
                            k.op("pe", lambda e: e.transpose(ptv[:, hl, :], Wb_.t[:, h, :], identb.t[:]), R=[Wb_, identb], W=[pt])
                        evac(WT.t[:, 8 * i2:8 * i2 + 8, :], ptv, R=[pt], W=[WT])
                    k.op("pool", lambda e: e.tensor_tensor(KG.t[:], kt.t[:], erest.unsqueeze(2).to_broadcast([128, 16, 128]), ALU.mult), R=[kt, ecum], W=[KG])
                    osb = osr.next()
                    p12 = []
                    for hq in range(4):
                        pp = psr.next()
                        ppv = pp.t[:].rearrange("p (x a b) -> p x a b", x=2, a=4)
                        for hl in range(4):
                            h = 4 * hq + hl
                            k.op("pe", lambda e: e.matmul(ppv[:, 0, hl, :], WT.t[:, h, :], Sb.t[:, h, :], start=True, stop=True), R=[WT, Sb.q[hq]], W=[pp])
                            k.op("pe", lambda e: e.matmul(ppv[:, 1, hl, :], QK.t[:, h, 1, :], Sb.t[:, h, :], start=True, stop=True), R=[QK, Sb.q[hq]], W=[pp])
                        k.op("dve", lambda e: e.tensor_tensor(vnb.t[:, 4 * hq:4 * hq + 4, :], Rf.t[:, 4 * hq:4 * hq + 4, 0:128], ppv[:, 0, :, :], ALU.subtract), R=[Rf.q[hq], pp], W=[vnb.q[hq]])
                        k.op("dve", lambda e: e.tensor_tensor(osb.t[:, 4 * hq:4 * hq + 4, :], ppv[:, 1, :, :], egam[:, 4 * hq:4 * hq + 4].unsqueeze(2).to_broadcast([128, 4, 128]), ALU.mult), R=[pp, ecum], W=[osb.q[hq]])
                        p12.append(pp)
                    for hq in range(4):
                        pp = psr.next()
                        ppv = pp.t[:].rearrange("p (x a b) -> p x a b", x=2, a=4)
                        for hl in range(4):
                            h = 4 * hq + hl
                            k.op("pe", lambda e: e.matmul(ppv[:, 0, hl, :], AQ.t[:, h, :], vnb.t[:, h, :], start=True, stop=True), R=[AQ.q[hq], vnb.q[hq]], W=[pp])
                            k.op("pe", lambda e: e.matmul(ppv[:, 1, hl, :], KG.t[:, h, :], vnb.t[:, h, :], start=True, stop=True), R=[KG, vnb.q[hq]], W=[pp])
                        k.op("dve", lambda e: e.tensor_tensor(osb.t[:, 4 * hq:4 * hq + 4, :], osb.t[:, 4 * hq:4 * hq + 4, :], ppv[:, 0, :, :], ALU.add), R=[osb.q[hq], pp], W=[osb.q[hq]])
                        k.op("pool", lambda e: e.tensor_tensor(Sf.t[:, 4 * hq:4 * hq + 4, :], Sf.t[:, 4 * hq:4 * hq + 4, :], etot[:, 4 * hq:4 * hq + 4].unsqueeze(2).to_broadcast([128, 4, 128]), ALU.mult), R=[Sf.q[hq], ecum], W=[Sf.q[hq]])
                        k.op("dve", lambda e: e.tensor_tensor(Sf.t[:, 4 * hq:4 * hq + 4, :], Sf.t[:, 4 * hq:4 * hq + 4, :], ppv[:, 1, :, :], ALU.add), R=[Sf.q[hq], pp], W=[Sf.q[hq]])
                        k.op("act", lambda e: e.activation(Sb.t[:, 4 * hq:4 * hq + 4, :], Sf.t[:, 4 * hq:4 * hq + 4, :], AF.Copy), R=[Sf.q[hq]], W=[Sb.q[hq]])
                    return osb

                dn_reset()
                nxt = dn_load(0)
                for tau in range(NTILE):
                    cur = nxt
                    if tau + 1 < NTILE:
                        nxt = dn_load(tau + 1)
                    if tau == M1:
                        dn_link()
                    osb = dn_step(tau, 0, cur)
                    k.dma("sp", ofs[tau * 128:(tau + 1) * 128, :], osb.t[:].rearrange("p a b -> p (a b)"), R=osb.q)
                k.barrier()
                if phase_done():
                    return

                ofr = Ring([sb(es, "d_of%d" % i, [128, 16, 128], F32) for i in range(1)])
                zcr = Ring([sb(es, "d_zc%d" % i, [128, D], F32) for i in range(1)])
                osq = GU
                oss = sb(es, "d_oss", [128, 16], F32)
                ogr = Ring([sb(es, "d_og%d" % i, [128, D], BF16) for i in range(2)])

                dn_reset()
                nxt = dn_load(NTILE - 1)
                for tau in range(NTILE - 1, -1, -1):
                    ld = nxt
                    if tau - 1 >= 0:
                        nxt = dn_load(tau - 1)
                    of_t = ofr.next()
                    k.dma("sp", of_t.t[:].rearrange("p a b -> p (a b)"), ofs[tau * 128:(tau + 1) * 128, :], W=[of_t])
                    zc = zcr.next()
                    k.dma("sp", zc.t[:], zt[tau * 128:(tau + 1) * 128, 0:2048], W=[zc])
                    if tau == LA:
                        dn_link()
                    osb = dn_step(tau, 1, ld)
                    k.op("dve", lambda e: e.tensor_tensor(osb.t[:], osb.t[:], of_t.t[:], ALU.add), R=osb.q + [of_t], W=osb.q)
                    k.op("pool", lambda e: e.tensor_tensor(osq.t[:], osb.t[:], osb.t[:], ALU.mult), R=osb.q, W=[osq])
                    k.op("dve", lambda e: e.tensor_reduce(oss.t[:], osq.t[:], AX.X, ALU.add), R=[osq], W=[oss])
                    rstd_from_ss(oss, 1.0 / 128.0)
                    k.op("dve", lambda e: e.tensor_tensor(osb.t[:], osb.t[:], oss.t[:].unsqueeze(2).to_broadcast([128, 16, 128]), ALU.mult), R=osb.q + [oss], W=osb.q)
                    k.op("pool", lambda e: e.tensor_tensor(osb.t[:], osb.t[:], hnb.t[:, 0:1, :].to_broadcast([128, 16, 128]), ALU.mult), R=osb.q + [hnb], W=osb.q)
                    k.op("act", lambda e: e.activation(zc.t[:], zc.t[:], AF.Silu), R=[zc], W=[zc])
                    og = ogr.next()
                    k.op("dve", lambda e: e.tensor_tensor(og.t[:], osb.t[:].rearrange("p a b -> p (a b)"), zc.t[:], ALU.mult), R=osb.q + [zc], W=[og])
                    k.dma("sp", ogs[tau * 128:(tau + 1) * 128, :], og.t[:], R=[og])
                k.barrier()
            if phase_done():
                return

        try:
          for layer in range(4):
            j = layer // 2
            hsrc = hin if layer == 0 else hs
            if layer % 2 == 0:
                for ph in (lambda: phase_in(hsrc, norm_even[j:j + 1, :], wie[j], EVEN_IN, 0), lambda: phase_even(j), lambda: phase_out(hsrc, woe[j], False)):
                    if not stopped[0]:
                        ph()
            else:
                for ph in (lambda: phase_in(hsrc, norm_odd[j:j + 1, :], wio[j], ODD_IN, 6144), lambda: phase_odd(j), lambda: phase_out(hsrc, woo[j], layer == 3)):
                    if not stopped[0]:
                        ph()
        except _Stop:
            pass
        k.barrier()
        if debug:
            for nm in debug:
                src = scr_all[nm]
                dst = nc.dram_tensor("dbg_" + nm, list(src.shape), src.dtype, kind="ExternalOutput").ap()
                n0 = src.shape[0]
                stp = max(1, n0 // 8)
                for r in range(0, n0, stp):
                    k.dma("sp", dst[r:r + stp], src[r:r + stp])
            k.barrier()
    build.ninstr = k.nins
    build.nop = k.nop_
    return nc


def _core_inputs(xs, meta, S_seg, is_prompt):
    NT_SEG = S_seg // 128
    NTILE = 2 * (NT_SEG + 1)
    T = NTILE * 128
    hin = np.zeros((T, D), np.float32)
    pos = np.zeros((T,), np.float32)
    m0 = 0
    m1 = (NT_SEG + 1) * 128
    r0 = 128
    r1 = (NT_SEG + 2) * 128
    hin[m0 + 112:m0 + 128] = meta
    pos[m0 + 112:m0 + 128] = np.arange(16)
    vm = np.zeros((128, 2), np.float32)
    vm[112:, 0] = 1.0
    if is_prompt:
        x = xs[0]
        hin[r0:r0 + S_seg] = x[:S_seg]
        hin[r1:r1 + S_seg] = x[S_seg:]
        pos[r0:r0 + S_seg] = 16 + np.arange(S_seg)
        pos[r1:r1 + S_seg] = 16 + S_seg + np.arange(S_seg)
        link = 1.0
    else:
        hin[r0:r0 + S_seg] = xs[0]
        hin[r1:r1 + S_seg] = xs[1]
        hin[m1 + 112:m1 + 128] = meta
        pos[m1 + 112:m1 + 128] = np.arange(16)
        pos[r0:r0 + S_seg] = 16 + np.arange(S_seg)
        pos[r1:r1 + S_seg] = 16 + np.arange(S_seg)
        vm[112:, 1] = 1.0
        link = 0.0
    inv = (1.0 / (np.float32(10000.0) ** (np.arange(0, 64, 2, dtype=np.float32) / np.float32(64)))).astype(np.float32)
    ang = pos[:, None].astype(np.float32) * inv[None]
    return {
        "hin": hin,
        "cosr": np.cos(ang).astype(np.float32),
        "sinr": np.sin(ang).astype(np.float32),
        "linkc": np.full((128, 1), link, np.float32),
        "vmask": vm,
    }


def _consts():
    s = np.arange(128)[:, None]
    i = np.arange(128)[None, :]
    tri = np.stack([(s <= i), (s < i), (s >= i), (s > i), (s // 32 == i // 32)]).astype(np.float32)
    return {"ident": np.eye(128, dtype=np.float32), "tri": tri}


_NC_CACHE = {}


def kernel(**inputs):
    xp = np.asarray(inputs["x_prompt"], np.float32)
    xsm = np.asarray(inputs["x_sample"], np.float32)
    nb_p, seq, _ = xp.shape
    nb_s, dseq, _ = xsm.shape
    assert seq == 2 * dseq and nb_s % 2 == 0
    S_seg = dseq
    NT_SEG = S_seg // 128
    meta = np.asarray(inputs["meta_tokens"], np.float32)
    shared = _consts()
    for name in ["norm_even", "w_in_even", "a_sink", "b_gate_up_fwd", "b_gate_bias_fwd", "b_gate_up_bwd",
                 "b_gate_bias_bwd", "b_head_norm", "w_out_even", "norm_odd", "w_in_odd", "c_conv", "c_a_log_fwd",
                 "c_dt_bias_fwd", "c_a_log_bwd", "c_dt_bias_bwd", "c_head_norm", "w_out_odd"]:
        shared[name] = np.ascontiguousarray(np.asarray(inputs[name], np.float32))
    shared["norm_final"] = np.ascontiguousarray(np.asarray(inputs["norm_final"], np.float32).reshape(1, D))
    in_maps = []
    for p in range(nb_p):
        m = dict(shared)
        m.update(_core_inputs([xp[p]], meta, S_seg, True))
        in_maps.append(m)
    for s in range(nb_s // 2):
        m = dict(shared)
        m.update(_core_inputs([xsm[2 * s], xsm[2 * s + 1]], meta, S_seg, False))
        in_maps.append(m)
    ncores = len(in_maps)
    if NT_SEG not in _NC_CACHE:
        _NC_CACHE[NT_SEG] = build(NT_SEG)
    nc = _NC_CACHE[NT_SEG]
    res = run_bass_kernel_spmd(nc, in_maps, core_ids=list(range(ncores)))
    outs = [np.asarray(r["y"], np.float32) for r in res.results]
    y_prompt = np.stack([outs[p].reshape(seq, D) for p in range(nb_p)], axis=0)
    y_sample = np.stack([outs[nb_p + s // 2].reshape(2, dseq, D)[s % 2] for s in range(nb_s)], axis=0)
    return (y_prompt, y_sample)
```

```python
import numpy as np
from contextlib import ExitStack
import concourse.bass as bass
import concourse.mybir as mybir
from concourse.bass_utils import run_bass_kernel_spmd

F32 = mybir.dt.float32
BF16 = mybir.dt.bfloat16
AF = mybir.ActivationFunctionType
ALU = mybir.AluOpType
AX = mybir.AxisListType

D = 2048
EPS = 1e-6
N_META = 16
EVEN_IN = 5408
ODD_IN = 8256
G = 6
NDMA = 48


class Buf:
    __slots__ = ("t", "w", "r", "ex", "q")

    def __init__(self, t, ex=False):
        self.t = t
        self.w = None
        self.r = []
        self.ex = ex
        self.q = None


class Ring:
    def __init__(self, bufs):
        self.b = bufs
        self.i = 0

    def next(self):
        b = self.b[self.i % len(self.b)]
        self.i += 1
        return b


class K:
    def __init__(self, nc, es):
        self.nc = nc
        self.es = es
        self.eng = {"pe": nc.tensor, "dve": nc.vector, "act": nc.scalar, "pool": nc.gpsimd, "sp": nc.sync}
        self.sem = {n: es.enter_context(nc.semaphore("s_" + n)) for n in ["pe", "dve", "act", "pool"]}
        self.cnt = {n: 0 for n in self.sem}
        self.waited = {}
        self.dslots = [[es.enter_context(nc.semaphore("d%d" % i)), 0] for i in range(NDMA)]
        self.ndma = 0
        self.nins = 0
        import os
        self.limit = int(os.environ.get("KLIMIT", "0")) or None
        self.nop_ = 0
        self.dbgops = set(int(x) for x in os.environ.get("KDBG", "").split(",") if x)

    def _wait(self, on, deps):
        e = self.eng[on]
        for d in deps:
            if d is None:
                continue
            key, val = d
            if key == on and on == "pe":
                continue
            if self.waited.get((on, key), 0) >= val:
                continue
            sem = self.sem[key] if isinstance(key, str) else self.dslots[key][0]
            if self.nop_ in self.dbgops:
                print("DBGWAIT op", self.nop_, "on", on, "waits", key, val, "cnt", dict(self.cnt))
            e.wait_ge(sem, val)
            self.nins += 1
            self.waited[(on, key)] = val

    def _deps(self, R, W):
        deps = []
        for b in R:
            deps.append(b.w)
            if b.ex:
                deps.extend(b.r)
        for b in W:
            deps.append(b.w)
            deps.extend(b.r)
        return deps

    def _commit(self, tok, R, W):
        for b in R:
            b.r = [t for t in b.r if t[0] != tok[0]] + [tok]
        for b in W:
            b.w = tok
            b.r = []

    def op(self, on, fn, R=(), W=()):
        self.nop_ += 1
        if self.limit is not None and self.nop_ > self.limit:
            return None
        self._wait(on, self._deps(R, W))
        ins = fn(self.eng[on])
        self.cnt[on] += 1
        ins.then_inc(self.sem[on], 1)
        self.nins += 1
        tok = (on, self.cnt[on])
        self._commit(tok, R, W)
        return tok

    def dma(self, on, out, in_, R=(), W=(), **kw):
        self.nop_ += 1
        if self.limit is not None and self.nop_ > self.limit and not str(getattr(out.tensor, "name", "")).startswith("dbg_"):
            return None
        i = self.ndma % NDMA
        self.ndma += 1
        slot = self.dslots[i]
        self._wait(on, self._deps(R, W) + [(i, slot[1])])
        slot[1] += 16
        self.eng[on].dma_start(out=out, in_=in_, **kw).then_inc(slot[0], 16)
        self.nins += 1
        tok = (i, slot[1])
        self._commit(tok, R, W)
        return tok

    def barrier(self):
        deps = [(n, c) for n, c in self.cnt.items() if c > 0]
        deps += [(i, s[1]) for i, s in enumerate(self.dslots) if s[1] > 0]
        for on in self.eng:
            self._wait(on, deps)


class _Stop(Exception):
    pass


def build(NT_SEG, debug=False, stop_at=None):
    NTILE = 2 * (NT_SEG + 1)
    T = NTILE * 128
    assert NTILE % G == 0
    NSUP = NTILE // G
    M0, M1 = 0, NT_SEG + 1
    LA, LB = NT_SEG, NT_SEG + 2
    assert LA // G == LB // G
    NREAL = 2 * NT_SEG

    nc = bass.Bass("TRN2", target_bir_lowering=False)

    def din(name, shape, dt=F32):
        return nc.dram_tensor(name, list(shape), dt, kind="ExternalInput").ap()

    def dscr(name, shape, dt):
        t = nc.dram_tensor(name, list(shape), dt, kind="Internal").ap()
        scr_all[name] = t
        return t

    scr_all = {}

    phase_no = [0]

    def phase_done():
        phase_no[0] += 1
        if stop_at is not None and phase_no[0] >= stop_at:
            stopped[0] = True
        return stopped[0]

    stopped = [False]

    hin = din("hin", [T, D])
    cosr = din("cosr", [T, 32])
    sinr = din("sinr", [T, 32])
    linkc_d = din("linkc", [128, 1])
    vmask_d = din("vmask", [128, 2])
    ident_d = din("ident", [128, 128])
    tri_d = din("tri", [5, 128, 128])
    norm_even = din("norm_even", [2, D])
    w_in_even = din("w_in_even", [2, D, EVEN_IN])
    a_sink = din("a_sink", [2, 16])
    gu_f = din("b_gate_up_fwd", [2, 16, 512])
    gb_f = din("b_gate_bias_fwd", [2, 512])
    gu_b = din("b_gate_up_bwd", [2, 16, 512])
    gb_b = din("b_gate_bias_bwd", [2, 512])
    b_hnorm = din("b_head_norm", [2, 256])
    w_out_even = din("w_out_even", [2, D, D])
    norm_odd = din("norm_odd", [2, D])
    w_in_odd = din("w_in_odd", [2, D, ODD_IN])
    c_conv = din("c_conv", [2, 5, 6144])
    alog_f = din("c_a_log_fwd", [2, 16])
    dtb_f = din("c_dt_bias_fwd", [2, 16])
    alog_b = din("c_a_log_bwd", [2, 16])
    dtb_b = din("c_dt_bias_bwd", [2, 16])
    c_hnorm = din("c_head_norm", [2, 128])
    w_out_odd = din("w_out_odd", [2, D, D])
    norm_final = din("norm_final", [1, D])
    y = nc.dram_tensor("y", [NREAL * 128, D], F32, kind="ExternalOutput").ap()

    hs = dscr("hs", [T, D], F32)
    zt = dscr("zt", [T, EVEN_IN], F32)
    xcT = dscr("xcT", [6144, T], BF16)
    qTs = dscr("qTs", [NTILE, 128, 16, 128], BF16)
    kTs = dscr("kTs", [NTILE, 128, 16, 128], BF16)
    ktok = dscr("ktok", [T, D], BF16)
    vtok = dscr("vtok", [T, D], BF16)
    ofs = dscr("ofs", [T, D], F32)
    ogs = dscr("ogs", [T, D], BF16)
    wie = dscr("wie", [2, D, EVEN_IN], BF16)
    woe = dscr("woe", [2, D, D], BF16)
    wio = dscr("wio", [2, D, ODD_IN], BF16)
    woo = dscr("woo", [2, D, D], BF16)

    es0 = ExitStack()
    with es0:
        k = K(nc, es0)

        uid = [0]

        def uname(name):
            uid[0] += 1
            return "%s_u%d" % (name, uid[0])

        def sb(es, name, shape, dt):
            return Buf(es.enter_context(nc.sbuf_tensor(uname("sb_" + name), list(shape), dt)))

        def psb(es, name, shape, dt=F32):
            return Buf(es.enter_context(nc.psum_tensor(uname("ps_" + name), list(shape), dt)), ex=True)

        identb = sb(es0, "identb", [128, 128], BF16)
        tri = sb(es0, "tri", [128, 5, 128], F32)
        onesf = sb(es0, "onesf", [128, 128], F32)
        trib = sb(es0, "trib", [128, 5, 128], BF16)
        onesb = sb(es0, "onesb", [128, 128], BF16)
        linkc = sb(es0, "linkc", [128, 1], F32)
        vmask = sb(es0, "vmask", [128, 2], F32)
        amask = sb(es0, "amask", [128, 5, 128], BF16)
        k.dma("pool", identb.t[:], ident_d, W=[identb])
        k.dma("sp", tri.t[:], tri_d.rearrange("a p c -> p a c"), W=[tri])
        k.dma("sp", linkc.t[:], linkc_d, W=[linkc])
        k.dma("sp", vmask.t[:], vmask_d, W=[vmask])
        k.op("dve", lambda e: e.memset(onesf.t[:], 1.0), W=[onesf])
        k.op("dve", lambda e: e.memset(onesb.t[:], 1.0), W=[onesb])
        k.op("dve", lambda e: e.tensor_copy(trib.t[:], tri.t[:]), R=[tri], W=[trib])
        U_LE, U_LT, U_GE, U_GT, BDM = 0, 1, 2, 3, 4
        k.op("dve", lambda e: e.tensor_copy(amask.t[:, 0, :], tri.t[:, U_GE, :]), R=[tri], W=[amask])
        k.op("dve", lambda e: e.tensor_copy(amask.t[:, 1, :], tri.t[:, U_LE, :]), R=[tri], W=[amask])
        k.op("dve", lambda e: e.tensor_scalar(amask.t[:, 2, :], tri.t[:, U_GE, :], linkc.t[:, 0:1], None, ALU.mult), R=[tri, linkc], W=[amask])
        k.op("dve", lambda e: e.tensor_scalar(amask.t[:, 3, :], tri.t[:, U_LE, :], linkc.t[:, 0:1], None, ALU.mult), R=[tri, linkc], W=[amask])
        k.op("dve", lambda e: e.tensor_scalar(amask.t[:, 4, :], onesf.t[:], linkc.t[:, 0:1], None, ALU.mult), R=[onesf, linkc], W=[amask])
        AM_PREV, AM_NEXT, AM_PREVL, AM_NEXTL, AM_FULLL = 0, 1, 2, 3, 4

        conv_list = [(j, dst, src) for j in range(2) for (dst, src) in ((wie, w_in_even), (woe, w_out_even), (wio, w_in_odd), (woo, w_out_odd))]
        for ci, (j, dst, src) in enumerate(conv_list):
            for r in range(0, D, 128):
                k.dma("pool", dst[j, r:r + 128, :], src[j, r:r + 128, :])
            if ci == 0:
                k.barrier()

        evq = [0]

        def evac(dst_ap, src_ap, R, W, engines=("dve", "act")):
            on = engines[evq[0] % len(engines)]
            evq[0] += 1
            if on == "act":
                return k.op("act", lambda e: e.activation(dst_ap, src_ap, AF.Copy), R=R, W=W)
            return k.op(on, lambda e: e.tensor_copy(dst_ap, src_ap), R=R, W=W)

        def rstd_from_ss(rs, scale):
            k.op("dve", lambda e: e.tensor_scalar(rs.t[:], rs.t[:], scale, EPS, ALU.mult, ALU.add), R=[rs], W=[rs])
            k.op("act", lambda e: e.activation(rs.t[:], rs.t[:], AF.Sqrt), R=[rs], W=[rs])
            k.op("dve", lambda e: e.reciprocal(rs.t[:], rs.t[:]), R=[rs], W=[rs])

        def phase_in(hsrc, gamma_row, Wb, E, feat_cols):
            with ExitStack() as es:
                gam = sb(es, "in_gam", [128, D], F32)
                k.dma("sp", gam.t[:], gamma_row.partition_broadcast(128), W=[gam])
                hring = Ring([sb(es, "in_h%d" % i, [128, D], F32) for i in range(2)])
                sqj = sb(es, "in_sq", [128, D], BF16)
                ssr = Ring([sb(es, "in_ss%d" % i, [128, 1], F32) for i in range(2)])
                ubr = Ring([sb(es, "in_ub%d" % i, [128, D], BF16) for i in range(2)])
                uTr = Ring([sb(es, "in_uT%d" % i, [128, 16, G * 128], BF16) for i in range(2)])
                wring = Ring([sb(es, "in_w%d" % i, [128, 16, 512], BF16) for i in range(3)])
                groups = []
                c0_ = 0
                while c0_ < E:
                    groups.append((c0_, min(512, E - c0_)))
                    c0_ += 512

                def load_w(gi):
                    c0, cw = groups[gi]
                    wt = wring.next()
                    k.dma("sp", wt.t[:, :, 0:cw], Wb[:, c0:c0 + cw].rearrange("(kc p) c -> p kc c", p=128), W=[wt])
                    return wt

                pending = [load_w(0)]
                stage = Ring([sb(es, "in_st%d" % i, [128, 512], F32) for i in range(4)])
                stageb = Ring([sb(es, "in_sb%d" % i, [128, 512], BF16) for i in range(4)])
                psr = Ring([psb(es, "in_ps%d" % i, [128, 512]) for i in range(8)])
                for sp in range(NSUP):
                    uT = uTr.next()
                    for tl in range(G):
                        tau = sp * G + tl
                        hb = hring.next()
                        k.dma("sp", hb.t[:], hsrc[tau * 128:(tau + 1) * 128, :], W=[hb])
                        ss = ssr.next()
                        k.op("dve", lambda e: e.memset(ss.t[:], 0.0), W=[ss])
                        k.op("act", lambda e: e.activation(sqj.t[:], hb.t[:], AF.Square, accum_out=ss.t[:, 0:1]), R=[hb], W=[sqj, ss])
                        rstd_from_ss(ss, 1.0 / D)
                        ub = ubr.next()
                        k.op("dve", lambda e: e.scalar_tensor_tensor(ub.t[:], hb.t[:], ss.t[:, 0:1], gam.t[:], ALU.mult, ALU.mult), R=[hb, ss, gam], W=[ub])
                        for q in range(2):
                            pt = psr.next()
                            ptv = pt.t[:].bitcast(BF16).rearrange("p (a b) -> p a b", a=8)
                            for j in range(8):
                                kc = q * 8 + j
                                k.op("pe", lambda e: e.transpose(ptv[:, j, :], ub.t[:, kc * 128:(kc + 1) * 128], identb.t[:]), R=[ub, identb], W=[pt])
                            evac(uT.t[:, q * 8:(q + 1) * 8, tl * 128:(tl + 1) * 128], ptv, R=[pt], W=[uT])
                    for gi, (c0, cw) in enumerate(groups):
                        wt = pending[0]
                        if gi + 1 < len(groups):
                            pending[0] = load_w(gi + 1)
                        elif sp + 1 < NSUP:
                            pending[0] = load_w(0)
                        if c0 < feat_cols:
                            for j in range(cw // 128):
                                for half in range(2):
                                    ps = psr.next()
                                    nn = G * 64
                                    for kc in range(16):
                                        k.op("pe", lambda e: e.matmul(ps.t[:, 0:nn], wt.t[:, kc, j * 128:(j + 1) * 128], uT.t[:, kc, half * nn:(half + 1) * nn], start=(kc == 0), stop=(kc == 15)), R=[wt, uT], W=[ps])
                                    st = stageb.next()
                                    evac(st.t[:, 0:nn], ps.t[:, 0:nn], R=[ps], W=[st])
                                    col = sp * G * 128 + half * nn
                                    k.dma("sp", xcT[c0 + j * 128:c0 + (j + 1) * 128, col:col + nn], st.t[:, 0:nn], R=[st])
                        else:
                            for tl in range(G):
                                tau = sp * G + tl
                                ps = psr.next()
                                for kc in range(16):
                                    k.op("pe", lambda e: e.matmul(ps.t[:, 0:cw], uT.t[:, kc, tl * 128:(tl + 1) * 128], wt.t[:, kc, 0:cw], start=(kc == 0), stop=(kc == 15)), R=[wt, uT], W=[ps])
                                st = stage.next()
                                evac(st.t[:, 0:cw], ps.t[:, 0:cw], R=[ps], W=[st])
                                k.dma("sp", zt[tau * 128:(tau + 1) * 128, c0 - feat_cols:c0 - feat_cols + cw], st.t[:, 0:cw], R=[st])
                k.barrier()
            if phase_done():
                return

        def phase_out(hsrc, Wb, final):
            with ExitStack() as es:
                wout = sb(es, "o_w", [128, 16, D], BF16)
                for q in range(4):
                    k.dma("sp", wout.t[:, q * 4:(q + 1) * 4, :], Wb[q * 512:(q + 1) * 512, :].rearrange("(kc p) c -> p kc c", p=128), W=[wout])
                gfin = None
                if final:
                    gfin = sb(es, "o_gf", [128, D], F32)
                    k.dma("sp", gfin.t[:], norm_final.partition_broadcast(128), W=[gfin])
                    sqj = sb(es, "o_sq", [128, D], BF16)
                    ssr = Ring([sb(es, "o_ss%d" % i, [128, 1], F32) for i in range(2)])
                    yr = Ring([sb(es, "o_y%d" % i, [128, D], F32) for i in range(2)])
                ogr = Ring([sb(es, "o_og%d" % i, [128, D], BF16) for i in range(2)])
                hor = Ring([sb(es, "o_ho%d" % i, [128, D], F32) for i in range(2)])
                hnr = Ring([sb(es, "o_hn%d" % i, [128, D], F32) for i in range(2)])
                oTr = Ring([sb(es, "o_oT%d" % i, [128, 16, 128], BF16) for i in range(2)])
                psr = Ring([psb(es, "o_ps%d" % i, [128, 512]) for i in range(8)])
                def out_load(tau):
                    og = ogr.next()
                    k.dma("sp", og.t[:], ogs[tau * 128:(tau + 1) * 128, :], W=[og])
                    ho = hor.next()
                    k.dma("sp", ho.t[:], hsrc[tau * 128:(tau + 1) * 128, :], W=[ho])
                    return og, ho

                nxt_o = out_load(0)
                for tau in range(NTILE):
                    og, ho = nxt_o
                    if tau + 1 < NTILE:
                        nxt_o = out_load(tau + 1)
                    oT = oTr.next()
                    for q in range(2):
                        pt = psr.next()
                        ptv = pt.t[:].bitcast(BF16).rearrange("p (a b) -> p a b", a=8)
                        for j in range(8):
                            kc = q * 8 + j
                            k.op("pe", lambda e: e.transpose(ptv[:, j, :], og.t[:, kc * 128:(kc + 1) * 128], identb.t[:]), R=[og, identb], W=[pt])
                        evac(oT.t[:, q * 8:(q + 1) * 8, :], ptv, R=[pt], W=[oT])
                    hn = hnr.next()
                    for cg in range(4):
                        ps = psr.next()
                        for kc in range(16):
                            k.op("pe", lambda e: e.matmul(ps.t[:], oT.t[:, kc, :], wout.t[:, kc, cg * 512:(cg + 1) * 512], start=(kc == 0), stop=(kc == 15)), R=[oT, wout], W=[ps])
                        k.op("dve", lambda e: e.tensor_tensor(hn.t[:, cg * 512:(cg + 1) * 512], ps.t[:], ho.t[:, cg * 512:(cg + 1) * 512], ALU.add), R=[ps, ho], W=[hn])
                    if not final:
                        k.dma("sp", hs[tau * 128:(tau + 1) * 128, :], hn.t[:], R=[hn])
                    elif tau not in (M0, M1):
                        ss = ssr.next()
                        k.op("dve", lambda e: e.memset(ss.t[:], 0.0), W=[ss])
                        k.op("act", lambda e: e.activation(sqj.t[:], hn.t[:], AF.Square, accum_out=ss.t[:, 0:1]), R=[hn], W=[sqj, ss])
                        rstd_from_ss(ss, 1.0 / D)
                        yb = yr.next()
                        k.op("dve", lambda e: e.scalar_tensor_tensor(yb.t[:], hn.t[:], ss.t[:, 0:1], gfin.t[:], ALU.mult, ALU.mult), R=[hn, ss, gfin], W=[yb])
                        ry = (tau - 1) if tau <= NT_SEG else (tau - 2)
                        k.dma("sp", y[ry * 128:(ry + 1) * 128, :], yb.t[:], R=[yb])
                k.barrier()
            if phase_done():
                return

        def rope(dst4, src4, cs, nh, tA, tB, R, W):
            c = cs.t[:, 0:1, :].to_broadcast([128, nh, 32])
            s = cs.t[:, 1:2, :].to_broadcast([128, nh, 32])
            k.op("dve", lambda e: e.tensor_tensor(tA.t[:, :, 0, :], src4[:, :, 0, :], c, ALU.mult), R=R + [cs], W=[tA])
            k.op("dve", lambda e: e.tensor_tensor(tA.t[:, :, 1, :], src4[:, :, 1, :], s, ALU.mult), R=R + [cs], W=[tA])
            k.op("pool", lambda e: e.tensor_tensor(tB.t[:, :, 0, :], src4[:, :, 1, :], c, ALU.mult), R=R + [cs], W=[tB])
            k.op("pool", lambda e: e.tensor_tensor(tB.t[:, :, 1, :], src4[:, :, 0, :], s, ALU.mult), R=R + [cs], W=[tB])
            k.op("dve", lambda e: e.tensor_tensor(dst4[:, :, 0, :], tA.t[:, :, 0, :], tA.t[:, :, 1, :], ALU.subtract), R=[tA], W=W)
            k.op("pool", lambda e: e.tensor_tensor(dst4[:, :, 1, :], tB.t[:, :, 0, :], tB.t[:, :, 1, :], ALU.add), R=[tB], W=W)

        def phase_even(j):
            with ExitStack() as es:
                KTall = es.enter_context(nc.sbuf_tensor(uname("sb_e_KT"), [128, NTILE, 2, 128], BF16))
                VAall = es.enter_context(nc.sbuf_tensor(uname("sb_e_VA"), [128, NTILE, 2, 65], BF16))
                KTb = [Buf(KTall) for _ in range(NTILE)]
                VAb = [Buf(VAall) for _ in range(NTILE)]
                esink = sb(es, "e_esink", [128, 16], F32)
                k.dma("sp", esink.t[:], a_sink[j:j + 1, :].partition_broadcast(128), W=[esink])
                k.op("act", lambda e: e.activation(esink.t[:], esink.t[:], AF.Exp), R=[esink], W=[esink])
                gub = [sb(es, "e_gu%d" % d, [16, 512], BF16) for d in range(2)]
                gbias = [sb(es, "e_gb%d" % d, [128, 512], F32) for d in range(2)]
                k.dma("pool", gub[0].t[:], gu_f[j], W=[gub[0]])
                k.dma("pool", gub[1].t[:], gu_b[j], W=[gub[1]])
                k.dma("sp", gbias[0].t[:], gb_f[j:j + 1, :].partition_broadcast(128), W=[gbias[0]])
                k.dma("sp", gbias[1].t[:], gb_b[j:j + 1, :].partition_broadcast(128), W=[gbias[1]])
                hnb = sb(es, "e_hn", [128, 1, 256], F32)
                k.dma("sp", hnb.t[:, 0, :], b_hnorm[j:j + 1, :].partition_broadcast(128), W=[hnb])
                psr = Ring([psb(es, "e_ps%d" % i, [128, 1024]) for i in range(4)])
                csr = Ring([sb(es, "e_cs%d" % i, [128, 2, 32], F32) for i in range(2)])

                def load_cs(tau):
                    cs = csr.next()
                    k.dma("sp", cs.t[:, 0, :], cosr[tau * 128:(tau + 1) * 128, :], W=[cs])
                    k.dma("sp", cs.t[:, 1, :], sinr[tau * 128:(tau + 1) * 128, :], W=[cs])
                    return cs

                with ExitStack() as e1:
                    kvr = Ring([sb(e1, "e1_kv%d" % i, [128, 256], F32) for i in range(2)])
                    tA = sb(e1, "e1_tA", [128, 2, 2, 32], F32)
                    tB = sb(e1, "e1_tB", [128, 2, 2, 32], F32)
                    krr = Ring([sb(e1, "e1_kr%d" % i, [128, 2, 2, 64], BF16) for i in range(2)])
                    for tau in range(NTILE):
                        kv = kvr.next()
                        k.dma("sp", kv.t[:], zt[tau * 128:(tau + 1) * 128, 1024:1280], W=[kv])
                        cs = load_cs(tau)
                        kr = krr.next()
                        src4 = kv.t[:, 0:128].rearrange("p (h a b) -> p h a b", h=2, a=2)
                        dst4 = kr.t[:, :, 0, :].rearrange("p h (a b) -> p h a b", a=2)
                        rope(dst4, src4, cs, 2, tA, tB, [kv], [kr])
                        k.op("dve", lambda e: e.tensor_copy(kr.t[:, :, 1, :], kr.t[:, :, 0, :]), R=[kr], W=[kr])
                        pt = psr.next()
                        ptv = pt.t[:, 0:128].bitcast(BF16).rearrange("p (a b) -> p a b", a=2)
                        for g in range(2):
                            k.op("pe", lambda e: e.transpose(ptv[:, g, :], kr.t[:, g, :, :].rearrange("p a b -> p (a b)"), identb.t[:]), R=[kr, identb], W=[pt])
                        evac(KTall[:, tau, :, :], ptv, R=[pt], W=[KTb[tau]])
                        k.op("dve", lambda e: e.tensor_copy(VAall[:, tau, :, 0:64], kv.t[:, 128:256].rearrange("p (g d) -> p g d", g=2)), R=[kv], W=[VAb[tau]])
                        if tau == M0 or tau == M1:
                            mi = 0 if tau == M0 else 1
                            k.op("dve", lambda e: e.tensor_copy(VAall[:, tau, :, 64:65], vmask.t[:, mi:mi + 1].unsqueeze(1).to_broadcast([128, 2, 1])), R=[vmask], W=[VAb[tau]])
                        else:
                            k.op("dve", lambda e: e.memset(VAall[:, tau, :, 64:65], 1.0), W=[VAb[tau]])
                    k.barrier()

                gl = ExitStack()
                es.enter_context(gl)
                qkr = Ring([sb(gl, "g_qk%d" % i, [128, 1024], F32) for i in range(2)])
                vr = Ring([sb(gl, "g_v%d" % i, [128, 1024], F32) for i in range(2)])
                lr = Ring([sb(gl, "g_l%d" % i, [128, 16], F32) for i in range(2)])
                l16 = sb(gl, "g_l16", [128, 16], BF16)
                lT = sb(gl, "g_lT", [16, 128], BF16)
                gt = sb(gl, "g_gt", [128, 512], F32)
                gg = sb(gl, "g_gg", [128, 512], F32)
                ghl = sb(gl, "g_ghl", [128, 2, 512], BF16)
                eb = sb(gl, "g_eb", [128, 512], F32)
                enb = sb(gl, "g_enb", [128, 512], F32)
                ek = sb(gl, "g_ek", [128, 512], F32)
                dec = sb(gl, "g_dec", [128, 4], F32)
                qd = sb(gl, "g_qd", [128, 512], BF16)
                kd = sb(gl, "g_kd", [128, 512], BF16)
                kdec = sb(gl, "g_kdec", [128, 512], BF16)
                v16 = sb(gl, "g_v16", [128, 1024], BF16)
                qkT = sb(gl, "g_qkT", [128, 8, 128], BF16)
                att = sb(gl, "g_att", [128, 4, 128], BF16)
                Sf = sb(gl, "g_S", [128, 1024], F32)
                Sb = sb(gl, "g_Sb", [128, 1024], BF16)
                ofr = Ring([sb(gl, "g_of%d" % i, [128, 1024], F32) for i in range(2)])

                def gla_reset():
                    k.op("dve", lambda e: e.memset(Sf.t[:], 0.0), W=[Sf])
                    k.op("dve", lambda e: e.memset(Sb.t[:], 0.0), W=[Sb])

                def gla_link():
                    k.op("dve", lambda e: e.tensor_scalar(Sf.t[:], Sf.t[:], linkc.t[:, 0:1], None, ALU.mult), R=[Sf, linkc], W=[Sf])
                    k.op("act", lambda e: e.activation(Sb.t[:], Sf.t[:], AF.Copy), R=[Sf], W=[Sb])

                def gla_load(tau, d):
                    qk = qkr.next()
                    k.dma("sp", qk.t[:], zt[tau * 128:(tau + 1) * 128, 2304:3328], W=[qk])
                    vv = vr.next()
                    k.dma("sp", vv.t[:], zt[tau * 128:(tau + 1) * 128, 3328:4352], W=[vv])
                    ll = lr.next()
                    k.dma("sp", ll.t[:], zt[tau * 128:(tau + 1) * 128, 5376 + 16 * d:5392 + 16 * d], W=[ll])
                    return qk, vv, ll

                def gla_step(tau, d, loaded):
                    qk, vv, ll = loaded
                    TA = U_LE if d == 0 else U_GE
                    TB = U_GT if d == 0 else U_LT
                    k.op("dve", lambda e: e.tensor_copy(l16.t[:], ll.t[:]), R=[ll], W=[l16])
                    p0 = psr.next()
                    p0b = p0.t[0:16, 0:64].bitcast(BF16)
                    k.op("pe", lambda e: e.transpose(p0b, l16.t[:], identb.t[:]), R=[l16, identb], W=[p0])
                    k.op("dve", lambda e: e.tensor_copy(lT.t[:], p0b), R=[p0], W=[lT])
                    p1 = psr.next()
                    k.op("pe", lambda e: e.matmul(p1.t[:, 0:512], lT.t[:], gub[d].t[:], start=True, stop=True), R=[lT, gub[d]], W=[p1])
                    k.op("dve", lambda e: e.tensor_tensor(gt.t[:], p1.t[:, 0:512], gbias[d].t[:], ALU.add), R=[p1, gbias[d]], W=[gt])
                    k.op("act", lambda e: e.activation(gt.t[:], gt.t[:], AF.Exp, scale=-1.0), R=[gt], W=[gt])
                    k.op("act", lambda e: e.activation(gt.t[:], gt.t[:], AF.Ln, bias=1.0), R=[gt], W=[gt])
                    if tau in (M0, M1):
                        mi = 0 if tau == M0 else 1
                        k.op("dve", lambda e: e.tensor_scalar(gg.t[:], gt.t[:], -1.0 / 16.0, vmask.t[:, mi:mi + 1], ALU.mult, ALU.mult), R=[gt, vmask], W=[gg])
                    else:
                        k.op("dve", lambda e: e.tensor_scalar(gg.t[:], gt.t[:], -1.0 / 16.0, None, ALU.mult), R=[gt], W=[gg])
                    k.op("dve", lambda e: e.tensor_copy(ghl.t[:, 0, :], gg.t[:]), R=[gg], W=[ghl])
                    k.op("dve", lambda e: e.tensor_tensor(ghl.t[:, 1, :], gg.t[:], ghl.t[:, 0, :], ALU.subtract), R=[gg, ghl], W=[ghl])
                    p2 = psr.next()
                    for hl in range(2):
                        k.op("pe", lambda e: e.matmul(p2.t[:, 0:512], trib.t[:, TA, :], ghl.t[:, hl, :], start=(hl == 0), stop=(hl == 1)), R=[trib, ghl], W=[p2])
                    for hl in range(2):
                        k.op("pe", lambda e: e.matmul(p2.t[:, 512:1024], trib.t[:, TB, :], ghl.t[:, hl, :], start=(hl == 0), stop=(hl == 1)), R=[trib, ghl], W=[p2])
                    p3 = psr.next()
                    for h in range(4):
                        for hl in range(2):
                            k.op("pe", lambda e: e.matmul(p3.t[:, h:h + 1], ghl.t[:, hl, h * 128:(h + 1) * 128], onesb.t[:, 0:1], start=(hl == 0), stop=(hl == 1)), R=[ghl, onesb], W=[p3])
                    k.op("act", lambda e: e.activation(eb.t[:], p2.t[:, 0:512], AF.Exp), R=[p2], W=[eb])
                    k.op("act", lambda e: e.activation(enb.t[:], p2.t[:, 0:512], AF.Exp, scale=-1.0), R=[p2], W=[enb])
                    k.op("act", lambda e: e.activation(ek.t[:], p2.t[:, 512:1024], AF.Exp), R=[p2], W=[ek])
                    k.op("act", lambda e: e.activation(dec.t[:], p3.t[:, 0:4], AF.Exp), R=[p3], W=[dec])
                    k.op("dve", lambda e: e.scalar_tensor_tensor(qd.t[:], qk.t[:, 0:512], 128.0 ** -0.5, eb.t[:], ALU.mult, ALU.mult), R=[qk, eb], W=[qd])
                    k.op("pool", lambda e: e.tensor_tensor(kd.t[:], qk.t[:, 512:1024], enb.t[:], ALU.mult), R=[qk, enb], W=[kd])
                    k.op("pool", lambda e: e.tensor_tensor(kdec.t[:], qk.t[:, 512:1024], ek.t[:], ALU.mult), R=[qk, ek], W=[kdec])
                    k.op("act", lambda e: e.activation(v16.t[:], vv.t[:], AF.Copy), R=[vv], W=[v16])
                    p4 = psr.next()
                    p4v = p4.t[:, 0:512].bitcast(BF16).rearrange("p (a b) -> p a b", a=8)
                    for h in range(4):
                        k.op("pe", lambda e: e.transpose(p4v[:, h, :], qd.t[:, h * 128:(h + 1) * 128], identb.t[:]), R=[qd, identb], W=[p4])
                        k.op("pe", lambda e: e.transpose(p4v[:, 4 + h, :], kd.t[:, h * 128:(h + 1) * 128], identb.t[:]), R=[kd, identb], W=[p4])
                    k.op("dve", lambda e: e.tensor_copy(qkT.t[:], p4v), R=[p4], W=[qkT])
                    p5 = psr.next()
                    p5v = p5.t[:, 0:512].rearrange("p (a b) -> p a b", a=4)
                    for h in range(4):
                        k.op("pe", lambda e: e.matmul(p5v[:, h, :], qkT.t[:, 4 + h, :], qkT.t[:, h, :], start=True, stop=True), R=[qkT], W=[p5])
                    k.op("dve", lambda e: e.tensor_tensor(att.t[:], p5v, tri.t[:, TA:TA + 1, :].to_broadcast([128, 4, 128]), ALU.mult), R=[p5, tri], W=[att])
                    po = psr.next()
                    for h in range(4):
                        k.op("pe", lambda e: e.matmul(po.t[:, h * 256:(h + 1) * 256], att.t[:, h, :], v16.t[:, h * 256:(h + 1) * 256], start=True, stop=False), R=[att, v16], W=[po])
                        k.op("pe", lambda e: e.matmul(po.t[:, h * 256:(h + 1) * 256], qkT.t[:, h, :], Sb.t[:, h * 256:(h + 1) * 256], start=False, stop=True), R=[qkT, Sb], W=[po])
                    pS = psr.next()
                    for h in range(4):
                        k.op("pe", lambda e: e.matmul(pS.t[:, h * 256:(h + 1) * 256], kdec.t[:, h * 128:(h + 1) * 128], v16.t[:, h * 256:(h + 1) * 256], start=True, stop=True), R=[kdec, v16], W=[pS])
                    for h in range(4):
                        k.op("dve", lambda e: e.scalar_tensor_tensor(Sf.t[:, h * 256:(h + 1) * 256], Sf.t[:, h * 256:(h + 1) * 256], dec.t[:, h:h + 1], pS.t[:, h * 256:(h + 1) * 256], ALU.mult, ALU.add), R=[Sf, dec, pS], W=[Sf])
                    k.op("act", lambda e: e.activation(Sb.t[:], Sf.t[:], AF.Copy), R=[Sf], W=[Sb])
                    return po

                gla_reset()
                nxt = gla_load(0, 0)
                for tau in range(NTILE):
                    cur = nxt
                    if tau + 1 < NTILE:
                        nxt = gla_load(tau + 1, 0)
                    if tau == M1:
                        gla_link()
                    po = gla_step(tau, 0, cur)
                    ob = ofr.next()
                    k.op("act", lambda e: e.activation(ob.t[:], po.t[:], AF.Copy), R=[po], W=[ob])
                    k.dma("sp", ofs[tau * 128:(tau + 1) * 128, 0:1024], ob.t[:], R=[ob])
                k.barrier()
                if phase_done():
                    return

                with ExitStack() as e3:
                    zbr = Ring([sb(e3, "e3_zb%d" % i, [128, 1024], F32) for i in range(2)])
                    osum = sb(e3, "e3_osum", [128, 4, 256], F32)
                    osq = sb(e3, "e3_osq", [128, 4, 256], F32)
                    oss = sb(e3, "e3_oss", [128, 4], F32)
                    ogr = Ring([sb(e3, "e3_og%d" % i, [128, D], BF16) for i in range(2)])
                    qar = Ring([sb(e3, "e3_qa%d" % i, [128, 1024], F32) for i in range(2)])
                    zar = Ring([sb(e3, "e3_za%d" % i, [128, 1024], F32) for i in range(2)])
                    tA = sb(e3, "e3_tA", [128, 16, 2, 32], F32)
                    tB = sb(e3, "e3_tB", [128, 16, 2, 32], F32)
                    qr = sb(e3, "e3_qr", [128, 1024], BF16)
                    QT = sb(e3, "e3_QT", [128, 8, 128], BF16)
                    PTr = Ring([sb(e3, "e3_PT%d" % i, [128, 2, 4, 128], BF16) for i in range(6)])
                    oall = sb(e3, "e3_oall", [128, 16, 65], F32)
                    den = sb(e3, "e3_den", [128, 16], F32)
                    onr = sb(e3, "e3_on", [128, 16, 64], F32)

                    def e3_load(tau):
                        ld = gla_load(tau, 1)
                        zb = zbr.next()
                        k.dma("sp", zb.t[:], zt[tau * 128:(tau + 1) * 128, 4352:5376], W=[zb])
                        of_t = ofr.next()
                        k.dma("sp", of_t.t[:], ofs[tau * 128:(tau + 1) * 128, 0:1024], W=[of_t])
                        qa = qar.next()
                        k.dma("sp", qa.t[:], zt[tau * 128:(tau + 1) * 128, 0:1024], W=[qa])
                        za = zar.next()
                        k.dma("sp", za.t[:], zt[tau * 128:(tau + 1) * 128, 1280:2304], W=[za])
                        cs = load_cs(tau)
                        return ld, zb, of_t, qa, za, cs

                    def keylist(tau):
                        if tau == M0 or tau == M1:
                            return [(tau, None), (tau + 1, AM_NEXT)]
                        ks = []
                        seg1 = tau > M1
                        if seg1:
                            ks.append((M0, AM_FULLL))
                            ks.append((M1, None))
                        else:
                            ks.append((M0, None))
                        first = (tau == 1) or (tau == LB)
                        last = (tau == LA) or (tau == NTILE - 1)
                        if not first:
                            ks.append((tau - 1, AM_PREV))
                        elif tau == LB:
                            ks.append((LA, AM_PREVL))
                        ks.append((tau, None))
                        if not last:
                            ks.append((tau + 1, AM_NEXT))
                        elif tau == LA:
                            ks.append((LB, AM_NEXTL))
                        return ks

                    gla_reset()
                    nxt = e3_load(NTILE - 1)
                    for tau in range(NTILE - 1, -1, -1):
                        ld, zb, of_t, qa, za, cs = nxt
                        if tau - 1 >= 0:
                            nxt = e3_load(tau - 1)
                        if tau == LA:
                            gla_link()
                        og = ogr.next()
                        po = gla_step(tau, 1, ld)
                        k.op("dve", lambda e: e.tensor_tensor(osum.t[:], po.t[:].rearrange("p (a b) -> p a b", a=4), of_t.t[:].rearrange("p (a b) -> p a b", a=4), ALU.add), R=[po, of_t], W=[osum])
                        k.op("pool", lambda e: e.tensor_tensor(osq.t[:], osum.t[:], osum.t[:], ALU.mult), R=[osum], W=[osq])
                        k.op("dve", lambda e: e.tensor_reduce(oss.t[:], osq.t[:], AX.X, ALU.add), R=[osq], W=[oss])
                        rstd_from_ss(oss, 1.0 / 256.0)
                        k.op("dve", lambda e: e.tensor_tensor(osum.t[:], osum.t[:], oss.t[:].unsqueeze(2).to_broadcast([128, 4, 256]), ALU.mult), R=[osum, oss], W=[osum])
                        k.op("pool", lambda e: e.tensor_tensor(osum.t[:], osum.t[:], hnb.t[:, 0:1, :].to_broadcast([128, 4, 256]), ALU.mult), R=[osum, hnb], W=[osum])
                        k.op("act", lambda e: e.activation(zb.t[:], zb.t[:], AF.Silu), R=[zb], W=[zb])
                        k.op("dve", lambda e: e.tensor_tensor(og.t[:, 1024:2048], osum.t[:].rearrange("p a b -> p (a b)"), zb.t[:], ALU.mult), R=[osum, zb], W=[og])
                        src4 = qa.t[:].rearrange("p (h a b) -> p h a b", h=16, a=2)
                        dst4 = qr.t[:].rearrange("p (h a b) -> p h a b", h=16, a=2)
                        rope(dst4, src4, cs, 16, tA, tB, [qa], [qr])
                        pq = psr.next()
                        pqv = pq.t[:, 0:512].bitcast(BF16).rearrange("p (a b) -> p a b", a=8)
                        for jj in range(8):
                            k.op("pe", lambda e: e.transpose(pqv[:, jj, :], qr.t[:, jj * 128:(jj + 1) * 128], identb.t[:]), R=[qr, identb], W=[pq])
                        k.op("dve", lambda e: e.tensor_copy(QT.t[:], pqv), R=[pq], W=[QT])
                        keys = keylist(tau)
                        for g in range(2):
                            pts = []
                            for (c, mk) in keys:
                                pst = psr.next()
                                for par in range(2):
                                    k.op("pe", lambda e: e.matmul(pst.t[:, par * 512:(par + 1) * 512], KTall[par * 64:(par + 1) * 64, c, g, :], QT.t[par * 64:(par + 1) * 64, 4 * g:4 * g + 4, :], start=True, stop=True), R=[KTb[c], QT], W=[pst])
                                pt = PTr.next()
                                k.op("act", lambda e: e.activation(pt.t[:].rearrange("p a b c -> p (a b c)"), pst.t[:], AF.Exp, scale=0.125), R=[pst], W=[pt])
                                if mk is not None:
                                    k.op("pool", lambda e: e.tensor_tensor(pt.t[:].rearrange("p a b c -> p (a b) c"), pt.t[:].rearrange("p a b c -> p (a b) c"), amask.t[:, mk:mk + 1, :].to_broadcast([128, 8, 128]), ALU.mult), R=[pt, amask], W=[pt])
                                pts.append((pt, c))
                            pso = psr.next()
                            psov = pso.t[:].rearrange("p (a b) -> p a b", a=8)
                            for jj in range(4):
                                for par in range(2):
                                    hh = 2 * jj + par
                                    for ci, (pt, c) in enumerate(pts):
                                        k.op("pe", lambda e: e.matmul(psov[:, hh, 0:65], pt.t[:, par, jj, :], VAall[:, c, g, :], start=(ci == 0), stop=(ci == len(pts) - 1)), R=[pt, VAb[c]], W=[pso])
                            k.op("act", lambda e: e.activation(oall.t[:, 8 * g:8 * g + 8, :], psov[:, :, 0:65], AF.Copy), R=[pso], W=[oall])
                        k.op("dve", lambda e: e.tensor_tensor(den.t[:], oall.t[:, :, 64], esink.t[:], ALU.add), R=[oall, esink], W=[den])
                        k.op("dve", lambda e: e.reciprocal(den.t[:], den.t[:]), R=[den], W=[den])
                        k.op("dve", lambda e: e.tensor_tensor(onr.t[:], oall.t[:, :, 0:64], den.t[:].unsqueeze(2).to_broadcast([128, 16, 64]), ALU.mult), R=[oall, den], W=[onr])
                        k.op("act", lambda e: e.activation(za.t[:], za.t[:], AF.Silu), R=[za], W=[za])
                        k.op("dve", lambda e: e.tensor_tensor(og.t[:, 0:1024], onr.t[:].rearrange("p a b -> p (a b)"), za.t[:], ALU.mult), R=[onr, za], W=[og])
                        k.dma("sp", ogs[tau * 128:(tau + 1) * 128, :], og.t[:], R=[og])
                k.barrier()
            if phase_done():
                return

        def phase_odd(j):
            with ExitStack() as es:
                NW = G * 128
                cw = sb(es, "o1_cw", [128, 48, 5], F32)
                with nc.allow_non_contiguous_dma("conv weights, tiny"):
                    for kk in range(5):
                        k.dma("sp", cw.t[:, :, kk], c_conv[j, kk, :].rearrange("(c p) -> p c", p=128), W=[cw])
                xr = Ring([sb(es, "o1_x%d" % i, [128, NW + 4], BF16) for i in range(4)])
                dgall = sb(es, "o1_dgall", [128, 48, 5, 128], BF16)
                fd = Buf(dgall.t)
                fp = Buf(dgall.t)
                for cc_ in range(48):
                    for kk in range(5):
                        on_ = "dve" if (cc_ + kk) % 2 == 0 else "pool"
                        lastq = (cc_ == 47 and kk >= 3)
                        k.op(on_, lambda e: e.tensor_scalar(dgall.t[:, cc_, kk, :], identb.t[:], cw.t[:, cc_, kk:kk + 1], None, ALU.mult), R=[identb, cw], W=([fd if on_ == "dve" else fp] if lastq else []))
                acr = Ring([sb(es, "o1_ac%d" % i, [128, NW], F32) for i in range(3)])
                xl = sb(es, "o1_xl", [128, 4], BF16)
                sqr = Ring([sb(es, "o1_sq%d" % i, [128, NW], BF16) for i in range(2)])
                rnr = Ring([sb(es, "o1_rn%d" % i, [128, NW], F32) for i in range(2)])
                xnr = Ring([sb(es, "o1_xn%d" % i, [128, G, 128], BF16) for i in range(4)])
                tmr = Ring([sb(es, "o1_tm%d" % i, [128, G, 128], BF16) for i in range(2)])
                psr = Ring([psb(es, "o1_ps%d" % i, [128, 1024]) for i in range(4)])
                def o1_load(sp, cc):
                    col0 = sp * NW
                    xb = xr.next()
                    lo = col0 - 2
                    hi = col0 + NW + 2
                    if sp == 0:
                        k.op("pool", lambda e: e.memset(xb.t[:, 0:2], 0.0), W=[xb])
                        lo = col0
                    if sp == NSUP - 1:
                        k.op("pool", lambda e: e.memset(xb.t[:, NW + 2:NW + 4], 0.0), W=[xb])
                        hi = col0 + NW
                    k.dma("sp", xb.t[:, lo - (col0 - 2):hi - (col0 - 2)], xcT[cc * 128:(cc + 1) * 128, lo:hi], W=[xb])
                    return xb

                seq = [(sp, cc) for sp in range(NSUP) for cc in range(48)]
                xq = [o1_load(*seq[0]), o1_load(*seq[1])]

                def stage_a(si, sp, cc):
                    col0 = sp * NW
                    xb = xq.pop(0)
                    if si + 2 < len(seq):
                        xq.append(o1_load(*seq[si + 2]))
                    ac = acr.next()
                    dg = dgall
                    nh = NW // 2
                    fixes = []
                    if sp == LA // G:
                        cA = (LA + 1) * 128 - 1 - col0
                        cB = LB * 128 - col0
                        k.op("dve", lambda e: e.tensor_scalar(xl.t[:, 0:2], xb.t[:, cA + 1:cA + 3], linkc.t[:, 0:1], None, ALU.mult), R=[xb, linkc], W=[xl])
                        k.op("dve", lambda e: e.tensor_scalar(xl.t[:, 2:4], xb.t[:, cB + 2:cB + 4], linkc.t[:, 0:1], None, ALU.mult), R=[xb, linkc], W=[xl])
                        fixes = [(cA, 2, 3), (cA, 3, 4), (cA - 1, 2, 4), (cB, 1, 1), (cB, 0, 0), (cB + 1, 1, 0)]
                    pcv = psr.next()
                    for half in range(2):
                        for kk in range(5):
                            if kk == 4:
                                for (col, xi, wk) in fixes:
                                    if col // nh == half:
                                        pc_ = half * 512 + col % nh
                                        k.op("pe", lambda e: e.matmul(pcv.t[:, pc_:pc_ + 1], dg.t[:, cc, wk, :], xl.t[:, xi:xi + 1], start=False, stop=False), R=[fd, fp, xl], W=[pcv])
                            k.op("pe", lambda e: e.matmul(pcv.t[:, half * 512:half * 512 + nh], dg.t[:, cc, kk, :], xb.t[:, half * nh + kk:half * nh + kk + nh], start=(kk == 0), stop=(kk == 4)), R=[fd, fp, xb], W=[pcv])
                    k.op("act", lambda e: e.activation(ac.t[:].rearrange("p (a b) -> p a b", a=2), pcv.t[:].rearrange("p (a b) -> p a b", a=2)[:, :, 0:nh], AF.Silu), R=[pcv], W=[ac])
                    return ac

                def stage_b(sp, cc, ac):
                    xn = xnr.next()
                    xnf = xn.t[:].rearrange("p a b -> p (a b)")
                    if cc < 32:
                        head = cc % 16
                        sq = sqr.next()
                        rn = rnr.next()
                        k.op("pool", lambda e: e.tensor_tensor(sq.t[:], ac.t[:], ac.t[:], ALU.mult), R=[ac], W=[sq])
                        ps = psr.next()
                        nn = NW // 2
                        for half in range(2):
                            k.op("pe", lambda e: e.matmul(ps.t[:, half * 512:half * 512 + nn], onesb.t[:], sq.t[:, half * nn:(half + 1) * nn], start=True, stop=True), R=[onesb, sq], W=[ps])
                        rnv = rn.t[:].rearrange("p (a b) -> p a b", a=2)
                        psv = ps.t[:].rearrange("p (a b) -> p a b", a=2)[:, :, 0:nn]
                        k.op("act", lambda e: e.activation(rnv, psv, AF.Ln, bias=EPS), R=[ps], W=[rn])
                        k.op("act", lambda e: e.activation(rn.t[:], rn.t[:], AF.Exp, scale=-0.5), R=[rn], W=[rn])
                        scl = (128.0 ** -0.5) if cc < 16 else 1.0
                        k.op("dve", lambda e: e.scalar_tensor_tensor(xnf, ac.t[:], scl, rn.t[:], ALU.mult, ALU.mult), R=[ac, rn], W=[xn])
                        dst = qTs if cc < 16 else kTs
                        k.dma("sp", dst[sp * G:(sp + 1) * G, :, head, :].rearrange("t p c -> p t c"), xn.t[:], R=[xn])
                    else:
                        head = cc - 32
                        k.op("dve", lambda e: e.tensor_copy(xnf, ac.t[:]), R=[ac], W=[xn])
                    return xn

                def stage_c(sp, cc, xn):
                    head = (cc % 16) if cc < 32 else (cc - 32)
                    if cc >= 16:
                        ps = psr.next()
                        psv = ps.t[:, 0:G * 64].bitcast(BF16).rearrange("p (a b) -> p a b", a=G)
                        for tl in range(G):
                            k.op("pe", lambda e: e.transpose(psv[:, tl, :], xn.t[:, tl, :], identb.t[:]), R=[xn, identb], W=[ps])
                        tm = tmr.next()
                        evac(tm.t[:], psv, R=[ps], W=[tm])
                        dst = ktok if cc < 32 else vtok
                        k.dma("sp", dst[sp * NW:(sp + 1) * NW, head * 128:(head + 1) * 128].rearrange("(t p) c -> p t c", p=128), tm.t[:], R=[tm])

                ac_prev = stage_a(0, *seq[0])
                xn_prev = None
                for si in range(len(seq)):
                    ac_cur = ac_prev
                    if si + 1 < len(seq):
                        ac_prev = stage_a(si + 1, *seq[si + 1])
                    xn_cur = stage_b(seq[si][0], seq[si][1], ac_cur)
                    if xn_prev is not None:
                        stage_c(seq[si - 1][0], seq[si - 1][1], xn_prev)
                    xn_prev = xn_cur
                stage_c(seq[-1][0], seq[-1][1], xn_prev)
                k.barrier()
            if phase_done():
                return

            with ExitStack() as es:
                cst = sb(es, "d_cst", [128, 4, 16], F32)
                k.dma("sp", cst.t[:, 0, :], alog_f[j:j + 1, :].partition_broadcast(128), W=[cst])
                k.dma("sp", cst.t[:, 1, :], dtb_f[j:j + 1, :].partition_broadcast(128), W=[cst])
                k.dma("sp", cst.t[:, 2, :], alog_b[j:j + 1, :].partition_broadcast(128), W=[cst])
                k.dma("sp", cst.t[:, 3, :], dtb_b[j:j + 1, :].partition_broadcast(128), W=[cst])
                for a in (0, 2):
                    k.op("act", lambda e: e.activation(cst.t[:, a, :], cst.t[:, a, :], AF.Exp), R=[cst], W=[cst])
                    k.op("dve", lambda e: e.tensor_scalar(cst.t[:, a, :], cst.t[:, a, :], -1.0, None, ALU.mult), R=[cst], W=[cst])
                hnb = sb(es, "d_hn", [128, 1, 128], F32)
                k.dma("sp", hnb.t[:, 0, :], c_hnorm[j:j + 1, :].partition_broadcast(128), W=[hnb])
                psr = Ring([psb(es, "d_ps%d" % i, [128, 1024]) for i in range(4)])
                QKr = Ring([sb(es, "d_QK%d" % i, [128, 16, 2, 128], BF16) for i in range(2)])
                ktr = Ring([sb(es, "d_kt%d" % i, [128, 16, 128], BF16) for i in range(2)])
                Rbr = Ring([sb(es, "d_Rb%d" % i, [128, 16, 256], BF16) for i in range(2)])
                zzr = Ring([sb(es, "d_zz%d" % i, [128, 64], F32) for i in range(2)])
                gt = sb(es, "d_gt", [128, 16], F32)
                gg = sb(es, "d_gg", [128, 16], F32)
                ghl = sb(es, "d_ghl", [128, 2, 16], BF16)
                GUl = sb(es, "d_GUl", [128, 16, 128], BF16)
                GUh = sb(es, "d_GUh", [128, 16, 128], BF16)
                bt = sb(es, "d_bt", [128, 16], F32)
                nbt = sb(es, "d_nbt", [128, 16], F32)
                ecum = sb(es, "d_ecum", [128, 48], F32)
                GU = sb(es, "d_GU", [128, 16, 128], F32)
                EX = sb(es, "d_EX", [128, 16, 128], F32)
                Ys = [[sb(es, "d_Y%d%d" % (a, b), [128, 16, 128], BF16) for b in range(2)] for a in range(2)]
                AQ = sb(es, "d_AQ", [128, 16, 128], BF16)
                Ao = sb(es, "d_Ao", [128, 16, 128], BF16)
                Qm = sb(es, "d_Qm", [128, 16, 128], BF16)
                Yx = sb(es, "d_Yx", [128, 16, 128], BF16)
                Ub = sb(es, "d_Ub", [128, 16, 256], BF16)
                Rf = sb(es, "d_Rf", [128, 16, 256], F32)
                Wb_ = sb(es, "d_Wb", [128, 16, 128], BF16)
                WT = sb(es, "d_WT", [128, 16, 128], BF16)
                KG = sb(es, "d_KG", [128, 16, 128], BF16)
                vnb = sb(es, "d_vnb", [128, 16, 128], BF16)
                osr = Ring([sb(es, "d_os%d" % i, [128, 16, 128], F32) for i in range(2)])
                Sf = sb(es, "d_S", [128, 16, 128], F32)
                Sb = sb(es, "d_Sb", [128, 16, 128], BF16)
                for b_ in [Ys[0][0], Ys[0][1], Ys[1][0], Ys[1][1], Yx, Qm, Ub, Rf, vnb, Sf, Sb, AQ, Ao] + osr.b:
                    b_.q = [Buf(b_.t) for _ in range(4)]

                def dn_reset():
                    k.op("dve", lambda e: e.memset(Sf.t[:], 0.0), W=Sf.q)
                    k.op("dve", lambda e: e.memset(Sb.t[:], 0.0), W=Sb.q)

                def dn_link():
                    k.op("dve", lambda e: e.tensor_scalar(Sf.t[:], Sf.t[:], linkc.t[:, 0:1], None, ALU.mult), R=Sf.q + [linkc], W=Sf.q)
                    k.op("act", lambda e: e.activation(Sb.t[:], Sf.t[:], AF.Copy), R=Sf.q, W=Sb.q)

                def dn_load(tau):
                    QK = QKr.next()
                    k.dma("sp", QK.t[:, :, 0, :], kTs[tau], W=[QK])
                    k.dma("sp", QK.t[:, :, 1, :], qTs[tau], W=[QK])
                    kt = ktr.next()
                    k.dma("sp", kt.t[:], ktok[tau * 128:(tau + 1) * 128, :].rearrange("p (h c) -> p h c", h=16), W=[kt])
                    Rb = Rbr.next()
                    k.dma("sp", Rb.t[:, :, 0:128], vtok[tau * 128:(tau + 1) * 128, :].rearrange("p (h c) -> p h c", h=16), W=[Rb])
                    zz = zzr.next()
                    k.dma("sp", zz.t[:], zt[tau * 128:(tau + 1) * 128, 2048:2112], W=[zz])
                    return QK, kt, Rb, zz

                def dn_step(tau, d, loaded):
                    QK, kt, Rb, zz = loaded
                    TA = U_LE if d == 0 else U_GE
                    TB = U_GT if d == 0 else U_LT
                    TS = U_LT if d == 0 else U_GT
                    a_ap = zz.t[:, 32 * d:32 * d + 16]
                    b_ap = zz.t[:, 32 * d + 16:32 * d + 32]
                    isM = tau in (M0, M1)
                    mi = 0 if tau == M0 else 1
                    k.op("dve", lambda e: e.tensor_tensor(gt.t[:], a_ap, cst.t[:, 2 * d + 1, :], ALU.add), R=[zz, cst], W=[gt])
                    k.op("act", lambda e: e.activation(gt.t[:], gt.t[:], AF.Exp), R=[gt], W=[gt])
                    k.op("act", lambda e: e.activation(gt.t[:], gt.t[:], AF.Ln, bias=1.0), R=[gt], W=[gt])
                    k.op("dve", lambda e: e.tensor_tensor(gg.t[:], gt.t[:], cst.t[:, 2 * d, :], ALU.mult), R=[gt, cst], W=[gg])
                    k.op("act", lambda e: e.activation(bt.t[:], b_ap, AF.Exp, scale=-1.0), R=[zz], W=[bt])
                    k.op("dve", lambda e: e.tensor_scalar(bt.t[:], bt.t[:], 1.0, None, ALU.add), R=[bt], W=[bt])
                    k.op("dve", lambda e: e.reciprocal(bt.t[:], bt.t[:]), R=[bt], W=[bt])
                    if isM:
                        k.op("dve", lambda e: e.tensor_scalar(gg.t[:], gg.t[:], vmask.t[:, mi:mi + 1], None, ALU.mult), R=[gg, vmask], W=[gg])
                        k.op("dve", lambda e: e.tensor_scalar(bt.t[:], bt.t[:], vmask.t[:, mi:mi + 1], None, ALU.mult), R=[bt, vmask], W=[bt])
                    k.op("dve", lambda e: e.tensor_scalar(nbt.t[:], bt.t[:], -1.0, None, ALU.mult), R=[bt], W=[nbt])
                    pc = psr.next()
                    k.op("dve", lambda e: e.tensor_copy(ghl.t[:, 0, :], gg.t[:]), R=[gg], W=[ghl])
                    k.op("dve", lambda e: e.tensor_tensor(ghl.t[:, 1, :], gg.t[:], ghl.t[:, 0, :], ALU.subtract), R=[gg, ghl], W=[ghl])
                    for hl in range(2):
                        k.op("pe", lambda e: e.matmul(pc.t[:, 0:16], trib.t[:, TA, :], ghl.t[:, hl, :], start=(hl == 0), stop=(hl == 1)), R=[trib, ghl], W=[pc])
                    for hl in range(2):
                        k.op("pe", lambda e: e.matmul(pc.t[:, 16:32], trib.t[:, TB, :], ghl.t[:, hl, :], start=(hl == 0), stop=(hl == 1)), R=[trib, ghl], W=[pc])
                    for hl in range(2):
                        k.op("pe", lambda e: e.matmul(pc.t[:, 32:48], onesb.t[:], ghl.t[:, hl, :], start=(hl == 0), stop=(hl == 1)), R=[onesb, ghl], W=[pc])
                    k.op("act", lambda e: e.activation(ecum.t[:], pc.t[:, 0:48], AF.Exp), R=[pc], W=[ecum])
                    egam = ecum.t[:, 0:16]
                    erest = ecum.t[:, 16:32]
                    etot = ecum.t[:, 32:48]
                    k.op("pool", lambda e: e.tensor_tensor(GUh.t[:], trib.t[:, TA:TA + 1, :].to_broadcast([128, 16, 128]), ghl.t[:, 0, :].unsqueeze(2).to_broadcast([128, 16, 128]), ALU.mult), R=[trib, ghl], W=[GUh])
                    k.op("pool", lambda e: e.tensor_tensor(GUl.t[:], trib.t[:, TA:TA + 1, :].to_broadcast([128, 16, 128]), ghl.t[:, 1, :].unsqueeze(2).to_broadcast([128, 16, 128]), ALU.mult), R=[trib, ghl], W=[GUl])
                    pe_ = [psr.next(), psr.next()]
                    for hq in range(4):
                        pp = pe_[hq // 2]
                        k.op("pe", lambda e: e.matmul(pp.t[:, (hq % 2) * 512:(hq % 2) * 512 + 512], trib.t[:, TB, :], GUh.t[:, 4 * hq:4 * hq + 4, :], start=True, stop=False), R=[trib, GUh], W=[pp])
                        k.op("pe", lambda e: e.matmul(pp.t[:, (hq % 2) * 512:(hq % 2) * 512 + 512], trib.t[:, TB, :], GUl.t[:, 4 * hq:4 * hq + 4, :], start=False, stop=True), R=[trib, GUl], W=[pp])
                    for i2 in range(2):
                        k.op("act", lambda e: e.activation(EX.t[:, 8 * i2:8 * i2 + 8, :].rearrange("p a b -> p (a b)"), pe_[i2].t[:], AF.Exp), R=[pe_[i2]], W=[EX])
                    k.op("dve", lambda e: e.tensor_tensor(GU.t[:], EX.t[:], tri.t[:, TS:TS + 1, :].to_broadcast([128, 16, 128]), ALU.mult), R=[EX, tri], W=[GU])
                    k.op("pool", lambda e: e.tensor_tensor(GU.t[:], GU.t[:], nbt.t[:].unsqueeze(2).to_broadcast([128, 16, 128]), ALU.mult), R=[GU, nbt], W=[GU])
                    k.op("dve", lambda e: e.tensor_tensor(EX.t[:], EX.t[:], tri.t[:, TA:TA + 1, :].to_broadcast([128, 16, 128]), ALU.mult), R=[EX, tri], W=[EX])
                    YT0, Y0 = Ys[0]
                    for hq in range(4):
                        pk = psr.next()
                        pkv = pk.t[:].rearrange("p (a b c) -> p a b c", a=4, b=2)
                        for hl in range(4):
                            h = 4 * hq + hl
                            k.op("pe", lambda e: e.matmul(pk.t[:, hl * 256:(hl + 1) * 256], QK.t[:, h, 0, :], QK.t[:, h, :, :].rearrange("p a b -> p (a b)"), start=True, stop=True), R=[QK], W=[pk])
                        k.op("dve", lambda e: e.tensor_tensor(YT0.t[:, 4 * hq:4 * hq + 4, :], pkv[:, :, 0, :], GU.t[:, 4 * hq:4 * hq + 4, :], ALU.mult), R=[pk, GU], W=[YT0.q[hq]])
                        k.op("dve", lambda e: e.tensor_tensor(AQ.t[:, 4 * hq:4 * hq + 4, :], pkv[:, :, 1, :], EX.t[:, 4 * hq:4 * hq + 4, :], ALU.mult), R=[pk, EX], W=[AQ.q[hq]])
                    k.op("pool", lambda e: e.tensor_tensor(Ao.t[:], YT0.t[:], trib.t[:, BDM:BDM + 1, :].to_broadcast([128, 16, 128]), ALU.mult), R=YT0.q + [trib], W=Ao.q)
                    k.op("dve", lambda e: e.tensor_tensor(YT0.t[:], YT0.t[:], Ao.t[:], ALU.subtract), R=YT0.q + Ao.q, W=YT0.q)
                    AdT, AoT = Ao, YT0
                    YTc, Yc = Ys[1]
                    for i2 in range(2):
                        pt = psr.next()
                        ptv = pt.t[:, 0:512].bitcast(BF16).rearrange("p (a b) -> p a b", a=8)
                        for hl in range(8):
                            h = 8 * i2 + hl
                            k.op("pe", lambda e: e.transpose(ptv[:, hl, :], AdT.t[:, h, :], identb.t[:]), R=[AdT.q[2 * i2], AdT.q[2 * i2 + 1], identb], W=[pt])
                        evac(Yc.t[:, 8 * i2:8 * i2 + 8, :], ptv, R=[pt], W=[Yc.q[2 * i2], Yc.q[2 * i2 + 1]])
                    k.op("pool", lambda e: e.tensor_copy(YTc.t[:], AdT.t[:]), R=AdT.q, W=YTc.q)
                    k.op("dve", lambda e: e.tensor_tensor(Qm.t[:], AdT.t[:], identb.t[:].unsqueeze(1).to_broadcast([128, 16, 128]), ALU.add), R=AdT.q + [identb], W=Qm.q)
                    k.op("act", lambda e: e.activation(Rf.t[:, :, 0:128], Rb.t[:, :, 0:128], AF.Copy), R=[Rb], W=Rf.q)
                    k.op("dve", lambda e: e.tensor_tensor(Rf.t[:, :, 128:256], kt.t[:], egam.unsqueeze(2).to_broadcast([128, 16, 128]), ALU.mult), R=[kt, ecum], W=Rf.q)
                    k.op("pool", lambda e: e.tensor_copy(Rb.t[:, :, 128:256], Rf.t[:, :, 128:256]), R=Rf.q, W=[Rb])
                    cur = 1
                    ysets = [(Yx, Ys[0][1]), Ys[1]]
                    for lv in range(1, 5):
                        YT, Y = ysets[cur]
                        YTn, Yn = ysets[1 - cur]
                        for hq in range(4):
                            py = psr.next()
                            pyv = py.t[:].rearrange("p (a b c) -> p a b c", a=4, b=2)
                            for hl in range(4):
                                h = 4 * hq + hl
                                k.op("pe", lambda e: e.matmul(pyv[:, hl, 0, :], Y.t[:, h, :], YT.t[:, h, :], start=True, stop=True), R=[Y.q[hq], YT.q[hq]], W=[py])
                                k.op("pe", lambda e: e.matmul(pyv[:, hl, 1, :], YT.t[:, h, :], Y.t[:, h, :], start=True, stop=True), R=[Y.q[hq], YT.q[hq]], W=[py])
                            k.op("act", lambda e: e.activation(YTn.t[:, 4 * hq:4 * hq + 4, :], pyv[:, :, 0, :], AF.Copy), R=[py], W=[YTn.q[hq]])
                            k.op("dve", lambda e: e.tensor_copy(Yn.t[:, 4 * hq:4 * hq + 4, :], pyv[:, :, 1, :]), R=[py], W=[Yn.q[hq]])
                        for hq in range(4):
                            pq_ = psr.next()
                            pqv = pq_.t[:, 0:512].rearrange("p (a b) -> p a b", a=4)
                            for hl in range(4):
                                h = 4 * hq + hl
                                k.op("pe", lambda e: e.matmul(pqv[:, hl, :], Yn.t[:, h, :], Qm.t[:, h, :], start=True, stop=True), R=[Yn.q[hq], Qm.q[hq]], W=[pq_])
                            k.op("dve", lambda e: e.tensor_tensor(Qm.t[:, 4 * hq:4 * hq + 4, :], Qm.t[:, 4 * hq:4 * hq + 4, :], pqv, ALU.add), R=[Qm.q[hq], pq_], W=[Qm.q[hq]])
                        cur = 1 - cur
                    for it in range(4):
                        for hq in range(4):
                            if it == 0:
                                zsrc = Rb
                            else:
                                pz = psr.next()
                                pzv = pz.t[:].rearrange("p (a b) -> p a b", a=4)
                                for hl in range(4):
                                    h = 4 * hq + hl
                                    k.op("pe", lambda e: e.matmul(pzv[:, hl, :], AoT.t[:, h, :], Ub.t[:, h, :], start=True, stop=True), R=[AoT.q[hq], Ub.q[hq]], W=[pz])
                                k.op("dve", lambda e: e.tensor_tensor(Ub.t[:, 4 * hq:4 * hq + 4, :], Rf.t[:, 4 * hq:4 * hq + 4, :], pzv, ALU.add), R=[Rf.q[hq], pz], W=[Ub.q[hq]])
                                zsrc = Ub
                            pu = psr.next()
                            puv = pu.t[:].rearrange("p (a b) -> p a b", a=4)
                            for hl in range(4):
                                h = 4 * hq + hl
                                k.op("pe", lambda e: e.matmul(puv[:, hl, :], Qm.t[:, h, :], zsrc.t[:, h, :], start=True, stop=True), R=[Qm.q[hq], (zsrc.q[hq] if zsrc.q else zsrc)], W=[pu])
                            if it < 3:
                                k.op("act", lambda e: e.activation(Ub.t[:, 4 * hq:4 * hq + 4, :], puv, AF.Copy), R=[pu], W=[Ub.q[hq]])
                            else:
                                k.op("act", lambda e: e.activation(Rf.t[:, 4 * hq:4 * hq + 4, :], puv, AF.Copy), R=[pu], W=[Rf.q[hq]])
                    bbc = bt.t[:].unsqueeze(2).to_broadcast([128, 16, 128])
                    k.op("dve", lambda e: e.tensor_tensor(Rf.t[:, :, 0:128], Rf.t[:, :, 0:128], bbc, ALU.mult), R=Rf.q + [bt], W=Rf.q)
                    k.op("pool", lambda e: e.tensor_tensor(Wb_.t[:], Rf.t[:, :, 128:256], bbc, ALU.mult), R=Rf.q + [bt], W=[Wb_])
                    for i2 in range(2):
                        pt = psr.next()
                        ptv = pt.t[:, 0:512].bitcast(BF16).rearrange("p (a b) -> p a b", a=8)
                        for hl in range(8):
                            h = 8 * i2 + hl
                            k.op("pe", lambda e: e.transpose(ptv[:, hl, :], Wb_.t[:, h, :], identb.t[:]), R=[Wb_, identb], W=[pt])
                        evac(WT.t[:, 8 * i2:8 * i2 + 8, :], ptv, R=[pt], W=[WT])
                    k.op("pool", lambda e: e.tensor_tensor(KG.t[:], kt.t[:], erest.unsqueeze(2).to_broadcast([128, 16, 128]), ALU.mult), R=[kt, ecum], W=[KG])
                    osb = osr.next()
                    p12 = []
                    for hq in range(4):
                        pp = psr.next()
                        ppv = pp.t[:].rearrange("p (x a b) -> p x a b", x=2, a=4)
                        for hl in range(4):
                            h = 4 * hq + hl
                            k.op("pe", lambda e: e.matmul(ppv[:, 0, hl, :], WT.t[:, h, :], Sb.t[:, h, :], start=True, stop=True), R=[WT, Sb.q[hq]], W=[pp])
                            k.op("pe", lambda e: e.matmul(ppv[:, 1, hl, :], QK.t[:, h, 1, :], Sb.t[:, h, :], start=True, stop=True), R=[QK, Sb.q[hq]], W=[pp])
                        k.op("dve", lambda e: e.tensor_tensor(vnb.t[:, 4 * hq:4 * hq + 4, :], Rf.t[:, 4 * hq:4 * hq + 4, 0:128], ppv[:, 0, :, :], ALU.subtract), R=[Rf.q[hq], pp], W=[vnb.q[hq]])
                        k.op("dve", lambda e: e.tensor_tensor(osb.t[:, 4 * hq:4 * hq + 4, :], ppv[:, 1, :, :], egam[:, 4 * hq:4 * hq + 4].unsqueeze(2).to_broadcast([128, 4, 128]), ALU.mult), R=[pp, ecum], W=[osb.q[hq]])
                        p12.append(pp)
                    for hq in range(4):
                        pp = psr.next()
                        ppv = pp.t[:].rearrange("p (x a b) -> p x a b", x=2, a=4)
                        for hl in range(4):
                            h = 4 * hq + hl
                            k.op("pe", lambda e: e.matmul(ppv[:, 0, hl, :], AQ.t[:, h, :], vnb.t[:, h, :], start=True, stop=True), R=[AQ.q[hq], vnb.q[hq]], W=[pp])
                            k.op("pe", lambda e: e.matmul(ppv[:, 1, hl, :], KG.t[:, h, :], vnb.t[:, h, :], start=True, stop=True), R=[KG, vnb.q[hq]], W=[pp])
                        k.op("dve", lambda e: e.tensor_tensor(osb.t[:, 4 * hq:4 * hq + 4, :], osb.t[:, 4 * hq:4 * hq + 4, :], ppv[:, 0, :, :], ALU.add), R=[osb.q[hq], pp], W=[osb.q[hq]])
                        k.op("pool", lambda e: e.tensor_tensor(Sf.t[:, 4 * hq:4 * hq + 4, :], Sf.t[:, 4 * hq:4 * hq + 4, :], etot[:, 4 * hq:4 * hq + 4].unsqueeze(2).to_broadcast([128, 4, 128]), ALU.mult), R=[Sf.q[hq], ecum], W=[Sf.q[hq]])
                        k.op("dve", lambda e: e.tensor_tensor(Sf.t[:, 4 * hq:4 * hq + 4, :], Sf.t[:, 4 * hq:4 * hq + 4, :], ppv[:, 1, :, :], ALU.add), R=[Sf.q[hq], pp], W=[Sf.q[hq]])
                        k.op("act", lambda e: e.activation(Sb.t[:, 4 * hq:4 * hq + 4, :], Sf.t[:, 4 * hq:4 * hq + 4, :], AF.Copy), R=[Sf.q[hq]], W=[Sb.q[hq]])
                    return osb

                dn_reset()
                nxt = dn_load(0)
                for tau in range(NTILE):
                    cur = nxt
                    if tau + 1 < NTILE:
                        nxt = dn_load(tau + 1)
                    if tau == M1:
                        dn_link()
                    osb = dn_step(tau, 0, cur)
                    k.dma("sp", ofs[tau * 128:(tau + 1) * 128, :], osb.t[:].rearrange("p a b -> p (a b)"), R=osb.q)
                k.barrier()
                if phase_done():
                    return

                ofr = Ring([sb(es, "d_of%d" % i, [128, 16, 128], F32) for i in range(1)])
                zcr = Ring([sb(es, "d_zc%d" % i, [128, D], F32) for i in range(1)])
                osq = GU
                oss = sb(es, "d_oss", [128, 16], F32)
                ogr = Ring([sb(es, "d_og%d" % i, [128, D], BF16) for i in range(2)])

                dn_reset()
                nxt = dn_load(NTILE - 1)
                for tau in range(NTILE - 1, -1, -1):
                    ld = nxt
                    if tau - 1 >= 0:
                        nxt = dn_load(tau - 1)
                    of_t = ofr.next()
                    k.dma("sp", of_t.t[:].rearrange("p a b -> p (a b)"), ofs[tau * 128:(tau + 1) * 128, :], W=[of_t])
                    zc = zcr.next()
                    k.dma("sp", zc.t[:], zt[tau * 128:(tau + 1) * 128, 0:2048], W=[zc])
                    if tau == LA:
                        dn_link()
                    osb = dn_step(tau, 1, ld)
                    k.op("dve", lambda e: e.tensor_tensor(osb.t[:], osb.t[:], of_t.t[:], ALU.add), R=osb.q + [of_t], W=osb.q)
                    k.op("pool", lambda e: e.tensor_tensor(osq.t[:], osb.t[:], osb.t[:], ALU.mult), R=osb.q, W=[osq])
                    k.op("dve", lambda e: e.tensor_reduce(oss.t[:], osq.t[:], AX.X, ALU.add), R=[osq], W=[oss])
                    rstd_from_ss(oss, 1.0 / 128.0)
                    k.op("dve", lambda e: e.tensor_tensor(osb.t[:], osb.t[:], oss.t[:].unsqueeze(2).to_broadcast([128, 16, 128]), ALU.mult), R=osb.q + [oss], W=osb.q)
                    k.op("pool", lambda e: e.tensor_tensor(osb.t[:], osb.t[:], hnb.t[:, 0:1, :].to_broadcast([128, 16, 128]), ALU.mult), R=osb.q + [hnb], W=osb.q)
                    k.op("act", lambda e: e.activation(zc.t[:], zc.t[:], AF.Silu), R=[zc], W=[zc])
                    og = ogr.next()
                    k.op("dve", lambda e: e.tensor_tensor(og.t[:], osb.t[:].rearrange("p a b -> p (a b)"), zc.t[:], ALU.mult), R=osb.q + [zc], W=[og])
                    k.dma("sp", ogs[tau * 128:(tau + 1) * 128, :], og.t[:], R=[og])
                k.barrier()
            if phase_done():
                return

        try:
          for layer in range(4):
            j = layer // 2
            hsrc = hin if layer == 0 else hs
            if layer % 2 == 0:
                for ph in (lambda: phase_in(hsrc, norm_even[j:j + 1, :], wie[j], EVEN_IN, 0), lambda: phase_even(j), lambda: phase_out(hsrc, woe[j], False)):
                    if not stopped[0]:
                        ph()
            else:
                for ph in (lambda: phase_in(hsrc, norm_odd[j:j + 1, :], wio[j], ODD_IN, 6144), lambda: phase_odd(j), lambda: phase_out(hsrc, woo[j], layer == 3)):
                    if not stopped[0]:
                        ph()
        except _Stop:
            pass
        k.barrier()
        if debug:
            for nm in debug:
                src = scr_all[nm]
                dst = nc.dram_tensor("dbg_" + nm, list(src.shape), src.dtype, kind="ExternalOutput").ap()
                n0 = src.shape[0]
                stp = max(1, n0 // 8)
                for r in range(0, n0, stp):
                    k.dma("sp", dst[r:r + stp], src[r:r + stp])
            k.barrier()
    build.ninstr = k.nins
    build.nop = k.nop_
    return nc


def _core_inputs(xs, meta, S_seg, is_prompt):
    NT_SEG = S_seg // 128
    NTILE = 2 * (NT_SEG + 1)
    T = NTILE * 128
    hin = np.zeros((T, D), np.float32)
    pos = np.zeros((T,), np.float32)
    m0 = 0
    m1 = (NT_SEG + 1) * 128
    r0 = 128
    r1 = (NT_SEG + 2) * 128
    hin[m0 + 112:m0 + 128] = meta
    pos[m0 + 112:m0 + 128] = np.arange(16)
    vm = np.zeros((128, 2), np.float32)
    vm[112:, 0] = 1.0
    if is_prompt:
        x = xs[0]
        hin[r0:r0 + S_seg] = x[:S_seg]
        hin[r1:r1 + S_seg] = x[S_seg:]
        pos[r0:r0 + S_seg] = 16 + np.arange(S_seg)
        pos[r1:r1 + S_seg] = 16 + S_seg + np.arange(S_seg)
        link = 1.0
    else:
        hin[r0:r0 + S_seg] = xs[0]
        hin[r1:r1 + S_seg] = xs[1]
        hin[m1 + 112:m1 + 128] = meta
        pos[m1 + 112:m1 + 128] = np.arange(16)
        pos[r0:r0 + S_seg] = 16 + np.arange(S_seg)
        pos[r1:r1 + S_seg] = 16 + np.arange(S_seg)
        vm[112:, 1] = 1.0
        link = 0.0
    inv = (1.0 / (np.float32(10000.0) ** (np.arange(0, 64, 2, dtype=np.float32) / np.float32(64)))).astype(np.float32)
    ang = pos[:, None].astype(np.float32) * inv[None]
    return {
        "hin": hin,
        "cosr": np.cos(ang).astype(np.float32),
        "sinr": np.sin(ang).astype(np.float32),
        "linkc": np.full((128, 1), link, np.float32),
        "vmask": vm,
    }


def _consts():
    s = np.arange(128)[:, None]
    i = np.arange(128)[None, :]
    tri = np.stack([(s <= i), (s < i), (s >= i), (s > i), (s // 32 == i // 32)]).astype(np.float32)
    return {"ident": np.eye(128, dtype=np.float32), "tri": tri}


_NC_CACHE = {}


def kernel(**inputs):
    xp = np.asarray(inputs["x_prompt"], np.float32)
    xsm = np.asarray(inputs["x_sample"], np.float32)
    nb_p, seq, _ = xp.shape
    nb_s, dseq, _ = xsm.shape
    assert seq == 2 * dseq and nb_s % 2 == 0
    S_seg = dseq
    NT_SEG = S_seg // 128
    meta = np.asarray(inputs["meta_tokens"], np.float32)
    shared = _consts()
    for name in ["norm_even", "w_in_even", "a_sink", "b_gate_up_fwd", "b_gate_bias_fwd", "b_gate_up_bwd",
                 "b_gate_bias_bwd", "b_head_norm", "w_out_even", "norm_odd", "w_in_odd", "c_conv", "c_a_log_fwd",
                 "c_dt_bias_fwd", "c_a_log_bwd", "c_dt_bias_bwd", "c_head_norm", "w_out_odd"]:
        shared[name] = np.ascontiguousarray(np.asarray(inputs[name], np.float32))
    shared["norm_final"] = np.ascontiguousarray(np.asarray(inputs["norm_final"], np.float32).reshape(1, D))
    in_maps = []
    for p in range(nb_p):
        m = dict(shared)
        m.update(_core_inputs([xp[p]], meta, S_seg, True))
        in_maps.append(m)
    for s in range(nb_s // 2):
        m = dict(shared)
        m.update(_core_inputs([xsm[2 * s], xsm[2 * s + 1]], meta, S_seg, False))
        in_maps.append(m)
    ncores = len(in_maps)
    if NT_SEG not in _NC_CACHE:
        _NC_CACHE[NT_SEG] = build(NT_SEG)
    nc = _NC_CACHE[NT_SEG]
    res = run_bass_kernel_spmd(nc, in_maps, core_ids=list(range(ncores)))
    outs = [np.asarray(r["y"], np.float32) for r in res.results]
    y_prompt = np.stack([outs[p].reshape(seq, D) for p in range(nb_p)], axis=0)
    y_sample = np.stack([outs[nb_p + s // 2].reshape(2, dseq, D)[s % 2] for s in range(nb_s)], axis=0)
    return (y_prompt, y_sample)
```

```python
import numpy as np
from contextlib import ExitStack
import concourse.bass as bass
import concourse.mybir as mybir
from concourse.bass_utils import run_bass_kernel_spmd

F32 = mybir.dt.float32
BF16 = mybir.dt.bfloat16
AF = mybir.ActivationFunctionType
ALU = mybir.AluOpType
AX = mybir.AxisListType

D = 2048
EPS = 1e-6
N_META = 16
EVEN_IN = 5408
ODD_IN = 8256
G = 6
NDMA = 48


class Buf:
    __slots__ = ("t", "w", "r", "ex", "q")

    def __init__(self, t, ex=False):
        self.t = t
        self.w = None
        self.r = []
        self.ex = ex
        self.q = None


class Ring:
    def __init__(self, bufs):
        self.b = bufs
        self.i = 0

    def next(self):
        b = self.b[self.i % len(self.b)]
        self.i += 1
        return b


class K:
    def __init__(self, nc, es):
        self.nc = nc
        self.es = es
        self.eng = {"pe": nc.tensor, "dve": nc.vector, "act": nc.scalar, "pool": nc.gpsimd, "sp": nc.sync}
        self.sem = {n: es.enter_context(nc.semaphore("s_" + n)) for n in ["pe", "dve", "act", "pool"]}
        self.cnt = {n: 0 for n in self.sem}
        self.waited = {}
        self.dslots = [[es.enter_context(nc.semaphore("d%d" % i)), 0] for i in range(NDMA)]
        self.ndma = 0
        self.nins = 0
        import os
        self.limit = int(os.environ.get("KLIMIT", "0")) or None
        self.nop_ = 0
        self.dbgops = set(int(x) for x in os.environ.get("KDBG", "").split(",") if x)

    def _wait(self, on, deps):
        e = self.eng[on]
        for d in deps:
            if d is None:
                continue
            key, val = d
            if key == on and on == "pe":
                continue
            if self.waited.get((on, key), 0) >= val:
                continue
            sem = self.sem[key] if isinstance(key, str) else self.dslots[key][0]
            if self.nop_ in self.dbgops:
                print("DBGWAIT op", self.nop_, "on", on, "waits", key, val, "cnt", dict(self.cnt))
            e.wait_ge(sem, val)
            self.nins += 1
            self.waited[(on, key)] = val

    def _deps(self, R, W):
        deps = []
        for b in R:
            deps.append(b.w)
            if b.ex:
                deps.extend(b.r)
        for b in W:
            deps.append(b.w)
            deps.extend(b.r)
        return deps

    def _commit(self, tok, R, W):
        for b in R:
            b.r = [t for t in b.r if t[0] != tok[0]] + [tok]
        for b in W:
            b.w = tok
            b.r = []

    def op(self, on, fn, R=(), W=()):
        self.nop_ += 1
        if self.limit is not None and self.nop_ > self.limit:
            return None
        self._wait(on, self._deps(R, W))
        ins = fn(self.eng[on])
        self.cnt[on] += 1
        ins.then_inc(self.sem[on], 1)
        self.nins += 1
        tok = (on, self.cnt[on])
        self._commit(tok, R, W)
        return tok

    def dma(self, on, out, in_, R=(), W=(), **kw):
        self.nop_ += 1
        if self.limit is not None and self.nop_ > self.limit and not str(getattr(out.tensor, "name", "")).startswith("dbg_"):
            return None
        i = self.ndma % NDMA
        self.ndma += 1
        slot = self.dslots[i]
        self._wait(on, self._deps(R, W) + [(i, slot[1])])
        slot[1] += 16
        self.eng[on].dma_start(out=out, in_=in_, **kw).then_inc(slot[0], 16)
        self.nins += 1
        tok = (i, slot[1])
        self._commit(tok, R, W)
        return tok

    def barrier(self):
        deps = [(n, c) for n, c in self.cnt.items() if c > 0]
        deps += [(i, s[1]) for i, s in enumerate(self.dslots) if s[1] > 0]
        for on in self.eng:
            self._wait(on, deps)


class _Stop(Exception):
    pass


def build(NT_SEG, debug=False, stop_at=None):
    NTILE = 2 * (NT_SEG + 1)
    T = NTILE * 128
    assert NTILE % G == 0
    NSUP = NTILE // G
    M0, M1 = 0, NT_SEG + 1
    LA, LB = NT_SEG, NT_SEG + 2
    assert LA // G == LB // G
    NREAL = 2 * NT_SEG

    nc = bass.Bass("TRN2", target_bir_lowering=False)

    def din(name, shape, dt=F32):
        return nc.dram_tensor(name, list(shape), dt, kind="ExternalInput").ap()

    def dscr(name, shape, dt):
        t = nc.dram_tensor(name, list(shape), dt, kind="Internal").ap()
        scr_all[name] = t
        return t

    scr_all = {}

    phase_no = [0]

    def phase_done():
        phase_no[0] += 1
        if stop_at is not None and phase_no[0] >= stop_at:
            stopped[0] = True
        return stopped[0]

    stopped = [False]

    hin = din("hin", [T, D])
    cosr = din("cosr", [T, 32])
    sinr = din("sinr", [T, 32])
    linkc_d = din("linkc", [128, 1])
    vmask_d = din("vmask", [128, 2])
    ident_d = din("ident", [128, 128])
    tri_d = din("tri", [5, 128, 128])
    norm_even = din("norm_even", [2, D])
    w_in_even = din("w_in_even", [2, D, EVEN_IN])
    a_sink = din("a_sink", [2, 16])
    gu_f = din("b_gate_up_fwd", [2, 16, 512])
    gb_f = din("b_gate_bias_fwd", [2, 512])
    gu_b = din("b_gate_up_bwd", [2, 16, 512])
    gb_b = din("b_gate_bias_bwd", [2, 512])
    b_hnorm = din("b_head_norm", [2, 256])
    w_out_even = din("w_out_even", [2, D, D])
    norm_odd = din("norm_odd", [2, D])
    w_in_odd = din("w_in_odd", [2, D, ODD_IN])
    c_conv = din("c_conv", [2, 5, 6144])
    alog_f = din("c_a_log_fwd", [2, 16])
    dtb_f = din("c_dt_bias_fwd", [2, 16])
    alog_b = din("c_a_log_bwd", [2, 16])
    dtb_b = din("c_dt_bias_bwd", [2, 16])
    c_hnorm = din("c_head_norm", [2, 128])
    w_out_odd = din("w_out_odd", [2, D, D])
    norm_final = din("norm_final", [1, D])
    y = nc.dram_tensor("y", [NREAL * 128, D], F32, kind="ExternalOutput").ap()

    hs = dscr("hs", [T, D], F32)
    zt = dscr("zt", [T, EVEN_IN], F32)
    xcT = dscr("xcT", [6144, T], BF16)
    qTs = dscr("qTs", [NTILE, 128, 16, 128], BF16)
    kTs = dscr("kTs", [NTILE, 128, 16, 128], BF16)
    ktok = dscr("ktok", [T, D], BF16)
    vtok = dscr("vtok", [T, D], BF16)
    ofs = dscr("ofs", [T, D], F32)
    ogs = dscr("ogs", [T, D], BF16)
    wie = dscr("wie", [2, D, EVEN_IN], BF16)
    woe = dscr("woe", [2, D, D], BF16)
    wio = dscr("wio", [2, D, ODD_IN], BF16)
    woo = dscr("woo", [2, D, D], BF16)

    es0 = ExitStack()
    with es0:
        k = K(nc, es0)

        uid = [0]

        def uname(name):
            uid[0] += 1
            return "%s_u%d" % (name, uid[0])

        def sb(es, name, shape, dt):
            return Buf(es.enter_context(nc.sbuf_tensor(uname("sb_" + name), list(shape), dt)))

        def psb(es, name, shape, dt=F32):
            return Buf(es.enter_context(nc.psum_tensor(uname("ps_" + name), list(shape), dt)), ex=True)

        identb = sb(es0, "identb", [128, 128], BF16)
        tri = sb(es0, "tri", [128, 5, 128], F32)
        onesf = sb(es0, "onesf", [128, 128], F32)
        trib = sb(es0, "trib", [128, 5, 128], BF16)
        onesb = sb(es0, "onesb", [128, 128], BF16)
        linkc = sb(es0, "linkc", [128, 1], F32)
        vmask = sb(es0, "vmask", [128, 2], F32)
        amask = sb(es0, "amask", [128, 5, 128], BF16)
        k.dma("pool", identb.t[:], ident_d, W=[identb])
        k.dma("sp", tri.t[:], tri_d.rearrange("a p c -> p a c"), W=[tri])
        k.dma("sp", linkc.t[:], linkc_d, W=[linkc])
        k.dma("sp", vmask.t[:], vmask_d, W=[vmask])
        k.op("dve", lambda e: e.memset(onesf.t[:], 1.0), W=[onesf])
        k.op("dve", lambda e: e.memset(onesb.t[:], 1.0), W=[onesb])
        k.op("dve", lambda e: e.tensor_copy(trib.t[:], tri.t[:]), R=[tri], W=[trib])
        U_LE, U_LT, U_GE, U_GT, BDM = 0, 1, 2, 3, 4
        k.op("dve", lambda e: e.tensor_copy(amask.t[:, 0, :], tri.t[:, U_GE, :]), R=[tri], W=[amask])
        k.op("dve", lambda e: e.tensor_copy(amask.t[:, 1, :], tri.t[:, U_LE, :]), R=[tri], W=[amask])
        k.op("dve", lambda e: e.tensor_scalar(amask.t[:, 2, :], tri.t[:, U_GE, :], linkc.t[:, 0:1], None, ALU.mult), R=[tri, linkc], W=[amask])
        k.op("dve", lambda e: e.tensor_scalar(amask.t[:, 3, :], tri.t[:, U_LE, :], linkc.t[:, 0:1], None, ALU.mult), R=[tri, linkc], W=[amask])
        k.op("dve", lambda e: e.tensor_scalar(amask.t[:, 4, :], onesf.t[:], linkc.t[:, 0:1], None, ALU.mult), R=[onesf, linkc], W=[amask])
        AM_PREV, AM_NEXT, AM_PREVL, AM_NEXTL, AM_FULLL = 0, 1, 2, 3, 4

        conv_list = [(j, dst, src) for j in range(2) for (dst, src) in ((wie, w_in_even), (woe, w_out_even), (wio, w_in_odd), (woo, w_out_odd))]
        for ci, (j, dst, src) in enumerate(conv_list):
            for r in range(0, D, 128):
                k.dma("pool", dst[j, r:r + 128, :], src[j, r:r + 128, :])
            if ci == 0:
                k.barrier()

        evq = [0]

        def evac(dst_ap, src_ap, R, W, engines=("dve", "act")):
            on = engines[evq[0] % len(engines)]
            evq[0] += 1
            if on == "act":
                return k.op("act", lambda e: e.activation(dst_ap, src_ap, AF.Copy), R=R, W=W)
            return k.op(on, lambda e: e.tensor_copy(dst_ap, src_ap), R=R, W=W)

        def rstd_from_ss(rs, scale):
            k.op("dve", lambda e: e.tensor_scalar(rs.t[:], rs.t[:], scale, EPS, ALU.mult, ALU.add), R=[rs], W=[rs])
            k.op("act", lambda e: e.activation(rs.t[:], rs.t[:], AF.Sqrt), R=[rs], W=[rs])
            k.op("dve", lambda e: e.reciprocal(rs.t[:], rs.t[:]), R=[rs], W=[rs])

        def phase_in(hsrc, gamma_row, Wb, E, feat_cols):
            with ExitStack() as es:
                gam = sb(es, "in_gam", [128, D], F32)
                k.dma("sp", gam.t[:], gamma_row.partition_broadcast(128), W=[gam])
                hring = Ring([sb(es, "in_h%d" % i, [128, D], F32) for i in range(2)])
                sqj = sb(es, "in_sq", [128, D], BF16)
                ssr = Ring([sb(es, "in_ss%d" % i, [128, 1], F32) for i in range(2)])
                ubr = Ring([sb(es, "in_ub%d" % i, [128, D], BF16) for i in range(2)])
                uTr = Ring([sb(es, "in_uT%d" % i, [128, 16, G * 128], BF16) for i in range(2)])
                wring = Ring([sb(es, "in_w%d" % i, [128, 16, 512], BF16) for i in range(3)])
                groups = []
                c0_ = 0
                while c0_ < E:
                    groups.append((c0_, min(512, E - c0_)))
                    c0_ += 512

                def load_w(gi):
                    c0, cw = groups[gi]
                    wt = wring.next()
                    k.dma("sp", wt.t[:, :, 0:cw], Wb[:, c0:c0 + cw].rearrange("(kc p) c -> p kc c", p=128), W=[wt])
                    return wt

                pending = [load_w(0)]
                stage = Ring([sb(es, "in_st%d" % i, [128, 512], F32) for i in range(4)])
                stageb = Ring([sb(es, "in_sb%d" % i, [128, 512], BF16) for i in range(4)])
                psr = Ring([psb(es, "in_ps%d" % i, [128, 512]) for i in range(8)])
                for sp in range(NSUP):
                    uT = uTr.next()
                    for tl in range(G):
                        tau = sp * G + tl
                        hb = hring.next()
                        k.dma("sp", hb.t[:], hsrc[tau * 128:(tau + 1) * 128, :], W=[hb])
                        ss = ssr.next()
                        k.op("dve", lambda e: e.memset(ss.t[:], 0.0), W=[ss])
                        k.op("act", lambda e: e.activation(sqj.t[:], hb.t[:], AF.Square, accum_out=ss.t[:, 0:1]), R=[hb], W=[sqj, ss])
                        rstd_from_ss(ss, 1.0 / D)
                        ub = ubr.next()
                        k.op("dve", lambda e: e.scalar_tensor_tensor(ub.t[:], hb.t[:], ss.t[:, 0:1], gam.t[:], ALU.mult, ALU.mult), R=[hb, ss, gam], W=[ub])
                        for q in range(2):
                            pt = psr.next()
                            ptv = pt.t[:].bitcast(BF16).rearrange("p (a b) -> p a b", a=8)
                            for j in range(8):
                                kc = q * 8 + j
                                k.op("pe", lambda e: e.transpose(ptv[:, j, :], ub.t[:, kc * 128:(kc + 1) * 128], identb.t[:]), R=[ub, identb], W=[pt])
                            evac(uT.t[:, q * 8:(q + 1) * 8, tl * 128:(tl + 1) * 128], ptv, R=[pt], W=[uT])
                    for gi, (c0, cw) in enumerate(groups):
                        wt = pending[0]
                        if gi + 1 < len(groups):
                            pending[0] = load_w(gi + 1)
                        elif sp + 1 < NSUP:
                            pending[0] = load_w(0)
                        if c0 < feat_cols:
                            for j in range(cw // 128):
                                for half in range(2):
                                    ps = psr.next()
                                    nn = G * 64
                                    for kc in range(16):
                                        k.op("pe", lambda e: e.matmul(ps.t[:, 0:nn], wt.t[:, kc, j * 128:(j + 1) * 128], uT.t[:, kc, half * nn:(half + 1) * nn], start=(kc == 0), stop=(kc == 15)), R=[wt, uT], W=[ps])
                                    st = stageb.next()
                                    evac(st.t[:, 0:nn], ps.t[:, 0:nn], R=[ps], W=[st])
                                    col = sp * G * 128 + half * nn
                                    k.dma("sp", xcT[c0 + j * 128:c0 + (j + 1) * 128, col:col + nn], st.t[:, 0:nn], R=[st])
                        else:
                            for tl in range(G):
                                tau = sp * G + tl
                                ps = psr.next()
                                for kc in range(16):
                                    k.op("pe", lambda e: e.matmul(ps.t[:, 0:cw], uT.t[:, kc, tl * 128:(tl + 1) * 128], wt.t[:, kc, 0:cw], start=(kc == 0), stop=(kc == 15)), R=[wt, uT], W=[ps])
                                st = stage.next()
                                evac(st.t[:, 0:cw], ps.t[:, 0:cw], R=[ps], W=[st])
                                k.dma("sp", zt[tau * 128:(tau + 1) * 128, c0 - feat_cols:c0 - feat_cols + cw], st.t[:, 0:cw], R=[st])
                k.barrier()
            if phase_done():
                return

        def phase_out(hsrc, Wb, final):
            with ExitStack() as es:
                wout = sb(es, "o_w", [128, 16, D], BF16)
                for q in range(4):
                    k.dma("sp", wout.t[:, q * 4:(q + 1) * 4, :], Wb[q * 512:(q + 1) * 512, :].rearrange("(kc p) c -> p kc c", p=128), W=[wout])
                gfin = None
                if final:
                    gfin = sb(es, "o_gf", [128, D], F32)
                    k.dma("sp", gfin.t[:], norm_final.partition_broadcast(128), W=[gfin])
                    sqj = sb(es, "o_sq", [128, D], BF16)
                    ssr = Ring([sb(es, "o_ss%d" % i, [128, 1], F32) for i in range(2)])
                    yr = Ring([sb(es, "o_y%d" % i, [128, D], F32) for i in range(2)])
                ogr = Ring([sb(es, "o_og%d" % i, [128, D], BF16) for i in range(2)])
                hor = Ring([sb(es, "o_ho%d" % i, [128, D], F32) for i in range(2)])
                hnr = Ring([sb(es, "o_hn%d" % i, [128, D], F32) for i in range(2)])
                oTr = Ring([sb(es, "o_oT%d" % i, [128, 16, 128], BF16) for i in range(2)])
                psr = Ring([psb(es, "o_ps%d" % i, [128, 512]) for i in range(8)])
                def out_load(tau):
                    og = ogr.next()
                    k.dma("sp", og.t[:], ogs[tau * 128:(tau + 1) * 128, :], W=[og])
                    ho = hor.next()
                    k.dma("sp", ho.t[:], hsrc[tau * 128:(tau + 1) * 128, :], W=[ho])
                    return og, ho

                nxt_o = out_load(0)
                for tau in range(NTILE):
                    og, ho = nxt_o
                    if tau + 1 < NTILE:
                        nxt_o = out_load(tau + 1)
                    oT = oTr.next()
                    for q in range(2):
                        pt = psr.next()
                        ptv = pt.t[:].bitcast(BF16).rearrange("p (a b) -> p a b", a=8)
                        for j in range(8):
                            kc = q * 8 + j
                            k.op("pe", lambda e: e.transpose(ptv[:, j, :], og.t[:, kc * 128:(kc + 1) * 128], identb.t[:]), R=[og, identb], W=[pt])
                        evac(oT.t[:, q * 8:(q + 1) * 8, :], ptv, R=[pt], W=[oT])
                    hn = hnr.next()
                    for cg in range(4):
                        ps = psr.next()
                        for kc in range(16):
                            k.op("pe", lambda e: e.matmul(ps.t[:], oT.t[:, kc, :], wout.t[:, kc, cg * 512:(cg + 1) * 512], start=(kc == 0), stop=(kc == 15)), R=[oT, wout], W=[ps])
                        k.op("dve", lambda e: e.tensor_tensor(hn.t[:, cg * 512:(cg + 1) * 512], ps.t[:], ho.t[:, cg * 512:(cg + 1) * 512], ALU.add), R=[ps, ho], W=[hn])
                    if not final:
                        k.dma("sp", hs[tau * 128:(tau + 1) * 128, :], hn.t[:], R=[hn])
                    elif tau not in (M0, M1):
                        ss = ssr.next()
                        k.op("dve", lambda e: e.memset(ss.t[:], 0.0), W=[ss])
                        k.op("act", lambda e: e.activation(sqj.t[:], hn.t[:], AF.Square, accum_out=ss.t[:, 0:1]), R=[hn], W=[sqj, ss])
                        rstd_from_ss(ss, 1.0 / D)
                        yb = yr.next()
                        k.op("dve", lambda e: e.scalar_tensor_tensor(yb.t[:], hn.t[:], ss.t[:, 0:1], gfin.t[:], ALU.mult, ALU.mult), R=[hn, ss, gfin], W=[yb])
                        ry = (tau - 1) if tau <= NT_SEG else (tau - 2)
                        k.dma("sp", y[ry * 128:(ry + 1) * 128, :], yb.t[:], R=[yb])
                k.barrier()
            if phase_done():
                return

        def rope(dst4, src4, cs, nh, tA, tB, R, W):
            c = cs.t[:, 0:1, :].to_broadcast([128, nh, 32])
            s = cs.t[:, 1:2, :].to_broadcast([128, nh, 32])
            k.op("dve", lambda e: e.tensor_tensor(tA.t[:, :, 0, :], src4[:, :, 0, :], c, ALU.mult), R=R + [cs], W=[tA])
            k.op("dve", lambda e: e.tensor_tensor(tA.t[:, :, 1, :], src4[:, :, 1, :], s, ALU.mult), R=R + [cs], W=[tA])
            k.op("pool", lambda e: e.tensor_tensor(tB.t[:, :, 0, :], src4[:, :, 1, :], c, ALU.mult), R=R + [cs], W=[tB])
            k.op("pool", lambda e: e.tensor_tensor(tB.t[:, :, 1, :], src4[:, :, 0, :], s, ALU.mult), R=R + [cs], W=[tB])
            k.op("dve", lambda e: e.tensor_tensor(dst4[:, :, 0, :], tA.t[:, :, 0, :], tA.t[:, :, 1, :], ALU.subtract), R=[tA], W=W)
            k.op("pool", lambda e: e.tensor_tensor(dst4[:, :, 1, :], tB.t[:, :, 0, :], tB.t[:, :, 1, :], ALU.add), R=[tB], W=W)

        def phase_even(j):
            with ExitStack() as es:
                KTall = es.enter_context(nc.sbuf_tensor(uname("sb_e_KT"), [128, NTILE, 2, 128], BF16))
                VAall = es.enter_context(nc.sbuf_tensor(uname("sb_e_VA"), [128, NTILE, 2, 65], BF16))
                KTb = [Buf(KTall) for _ in range(NTILE)]
                VAb = [Buf(VAall) for _ in range(NTILE)]
                esink = sb(es, "e_esink", [128, 16], F32)
                k.dma("sp", esink.t[:], a_sink[j:j + 1, :].partition_broadcast(128), W=[esink])
                k.op("act", lambda e: e.activation(esink.t[:], esink.t[:], AF.Exp), R=[esink], W=[esink])
                gub = [sb(es, "e_gu%d" % d, [16, 512], BF16) for d in range(2)]
                gbias = [sb(es, "e_gb%d" % d, [128, 512], F32) for d in range(2)]
                k.dma("pool", gub[0].t[:], gu_f[j], W=[gub[0]])
                k.dma("pool", gub[1].t[:], gu_b[j], W=[gub[1]])
                k.dma("sp", gbias[0].t[:], gb_f[j:j + 1, :].partition_broadcast(128), W=[gbias[0]])
                k.dma("sp", gbias[1].t[:], gb_b[j:j + 1, :].partition_broadcast(128), W=[gbias[1]])
                hnb = sb(es, "e_hn", [128, 1, 256], F32)
                k.dma("sp", hnb.t[:, 0, :], b_hnorm[j:j + 1, :].partition_broadcast(128), W=[hnb])
                psr = Ring([psb(es, "e_ps%d" % i, [128, 1024]) for i in range(4)])
                csr = Ring([sb(es, "e_cs%d" % i, [128, 2, 32], F32) for i in range(2)])

                def load_cs(tau):
                    cs = csr.next()
                    k.dma("sp", cs.t[:, 0, :], cosr[tau * 128:(tau + 1) * 128, :], W=[cs])
                    k.dma("sp", cs.t[:, 1, :], sinr[tau * 128:(tau + 1) * 128, :], W=[cs])
                    return cs

                with ExitStack() as e1:
                    kvr = Ring([sb(e1, "e1_kv%d" % i, [128, 256], F32) for i in range(2)])
                    tA = sb(e1, "e1_tA", [128, 2, 2, 32], F32)
                    tB = sb(e1, "e1_tB", [128, 2, 2, 32], F32)
                    krr = Ring([sb(e1, "e1_kr%d" % i, [128, 2, 2, 64], BF16) for i in range(2)])
                    for tau in range(NTILE):
                        kv = kvr.next()
                        k.dma("sp", kv.t[:], zt[tau * 128:(tau + 1) * 128, 1024:1280], W=[kv])
                        cs = load_cs(tau)
                        kr = krr.next()
                        src4 = kv.t[:, 0:128].rearrange("p (h a b) -> p h a b", h=2, a=2)
                        dst4 = kr.t[:, :, 0, :].rearrange("p h (a b) -> p h a b", a=2)
                        rope(dst4, src4, cs, 2, tA, tB, [kv], [kr])
                        k.op("dve", lambda e: e.tensor_copy(kr.t[:, :, 1, :], kr.t[:, :, 0, :]), R=[kr], W=[kr])
                        pt = psr.next()
                        ptv = pt.t[:, 0:128].bitcast(BF16).rearrange("p (a b) -> p a b", a=2)
                        for g in range(2):
                            k.op("pe", lambda e: e.transpose(ptv[:, g, :], kr.t[:, g, :, :].rearrange("p a b -> p (a b)"), identb.t[:]), R=[kr, identb], W=[pt])
                        evac(KTall[:, tau, :, :], ptv, R=[pt], W=[KTb[tau]])
                        k.op("dve", lambda e: e.tensor_copy(VAall[:, tau, :, 0:64], kv.t[:, 128:256].rearrange("p (g d) -> p g d", g=2)), R=[kv], W=[VAb[tau]])
                        if tau == M0 or tau == M1:
                            mi = 0 if tau == M0 else 1
                            k.op("dve", lambda e: e.tensor_copy(VAall[:, tau, :, 64:65], vmask.t[:, mi:mi + 1].unsqueeze(1).to_broadcast([128, 2, 1])), R=[vmask], W=[VAb[tau]])
                        else:
                            k.op("dve", lambda e: e.memset(VAall[:, tau, :, 64:65], 1.0), W=[VAb[tau]])
                    k.barrier()

                gl = ExitStack()
                es.enter_context(gl)
                qkr = Ring([sb(gl, "g_qk%d" % i, [128, 1024], F32) for i in range(2)])
                vr = Ring([sb(gl, "g_v%d" % i, [128, 1024], F32) for i in range(2)])
                lr = Ring([sb(gl, "g_l%d" % i, [128, 16], F32) for i in range(2)])
                l16 = sb(gl, "g_l16", [128, 16], BF16)
                lT = sb(gl, "g_lT", [16, 128], BF16)
                gt = sb(gl, "g_gt", [128, 512], F32)
                gg = sb(gl, "g_gg", [128, 512], F32)
                ghl = sb(gl, "g_ghl", [128, 2, 512], BF16)
                eb = sb(gl, "g_eb", [128, 512], F32)
                enb = sb(gl, "g_enb", [128, 512], F32)
                ek = sb(gl, "g_ek", [128, 512], F32)
                dec = sb(gl, "g_dec", [128, 4], F32)
                qd = sb(gl, "g_qd", [128, 512], BF16)
                kd = sb(gl, "g_kd", [128, 512], BF16)
                kdec = sb(gl, "g_kdec", [128, 512], BF16)
                v16 = sb(gl, "g_v16", [128, 1024], BF16)
                qkT = sb(gl, "g_qkT", [128, 8, 128], BF16)
                att = sb(gl, "g_att", [128, 4, 128], BF16)
                Sf = sb(gl, "g_S", [128, 1024], F32)
                Sb = sb(gl, "g_Sb", [128, 1024], BF16)
                ofr = Ring([sb(gl, "g_of%d" % i, [128, 1024], F32) for i in range(2)])

                def gla_reset():
                    k.op("dve", lambda e: e.memset(Sf.t[:], 0.0), W=[Sf])
                    k.op("dve", lambda e: e.memset(Sb.t[:], 0.0), W=[Sb])

                def gla_link():
                    k.op("dve", lambda e: e.tensor_scalar(Sf.t[:], Sf.t[:], linkc.t[:, 0:1], None, ALU.mult), R=[Sf, linkc], W=[Sf])
                    k.op("act", lambda e: e.activation(Sb.t[:], Sf.t[:], AF.Copy), R=[Sf], W=[Sb])

                def gla_load(tau, d):
                    qk = qkr.next()
                    k.dma("sp", qk.t[:], zt[tau * 128:(tau + 1) * 128, 2304:3328], W=[qk])
                    vv = vr.next()
                    k.dma("sp", vv.t[:], zt[tau * 128:(tau + 1) * 128, 3328:4352], W=[vv])
                    ll = lr.next()
                    k.dma("sp", ll.t[:], zt[tau * 128:(tau + 1) * 128, 5376 + 16 * d:5392 + 16 * d], W=[ll])
                    return qk, vv, ll

                def gla_step(tau, d, loaded):
                    qk, vv, ll = loaded
                    TA = U_LE if d == 0 else U_GE
                    TB = U_GT if d == 0 else U_LT
                    k.op("dve", lambda e: e.tensor_copy(l16.t[:], ll.t[:]), R=[ll], W=[l16])
                    p0 = psr.next()
                    p0b = p0.t[0:16, 0:64].bitcast(BF16)
                    k.op("pe", lambda e: e.transpose(p0b, l16.t[:], identb.t[:]), R=[l16, identb], W=[p0])
                    k.op("dve", lambda e: e.tensor_copy(lT.t[:], p0b), R=[p0], W=[lT])
                    p1 = psr.next()
                    k.op("pe", lambda e: e.matmul(p1.t[:, 0:512], lT.t[:], gub[d].t[:], start=True, stop=True), R=[lT, gub[d]], W=[p1])
                    k.op("dve", lambda e: e.tensor_tensor(gt.t[:], p1.t[:, 0:512], gbias[d].t[:], ALU.add), R=[p1, gbias[d]], W=[gt])
                    k.op("act", lambda e: e.activation(gt.t[:], gt.t[:], AF.Exp, scale=-1.0), R=[gt], W=[gt])
                    k.op("act", lambda e: e.activation(gt.t[:], gt.t[:], AF.Ln, bias=1.0), R=[gt], W=[gt])
                    if tau in (M0, M1):
                        mi = 0 if tau == M0 else 1
                        k.op("dve", lambda e: e.tensor_scalar(gg.t[:], gt.t[:], -1.0 / 16.0, vmask.t[:, mi:mi + 1], ALU.mult, ALU.mult), R=[gt, vmask], W=[gg])
                    else:
                        k.op("dve", lambda e: e.tensor_scalar(gg.t[:], gt.t[:], -1.0 / 16.0, None, ALU.mult), R=[gt], W=[gg])
                    k.op("dve", lambda e: e.tensor_copy(ghl.t[:, 0, :], gg.t[:]), R=[gg], W=[ghl])
                    k.op("dve", lambda e: e.tensor_tensor(ghl.t[:, 1, :], gg.t[:], ghl.t[:, 0, :], ALU.subtract), R=[gg, ghl], W=[ghl])
                    p2 = psr.next()
                    for hl in range(2):
                        k.op("pe", lambda e: e.matmul(p2.t[:, 0:512], trib.t[:, TA, :], ghl.t[:, hl, :], start=(hl == 0), stop=(hl == 1)), R=[trib, ghl], W=[p2])
                    for hl in range(2):
                        k.op("pe", lambda e: e.matmul(p2.t[:, 512:1024], trib.t[:, TB, :], ghl.t[:, hl, :], start=(hl == 0), stop=(hl == 1)), R=[trib, ghl], W=[p2])
                    p3 = psr.next()
                    for h in range(4):
                        for hl in range(2):
                            k.op("pe", lambda e: e.matmul(p3.t[:, h:h + 1], ghl.t[:, hl, h * 128:(h + 1) * 128], onesb.t[:, 0:1], start=(hl == 0), stop=(hl == 1)), R=[ghl, onesb], W=[p3])
                    k.op("act", lambda e: e.activation(eb.t[:], p2.t[:, 0:512], AF.Exp), R=[p2], W=[eb])
                    k.op("act", lambda e: e.activation(enb.t[:], p2.t[:, 0:512], AF.Exp, scale=-1.0), R=[p2], W=[enb])
                    k.op("act", lambda e: e.activation(ek.t[:], p2.t[:, 512:1024], AF.Exp), R=[p2], W=[ek])
                    k.op("act", lambda e: e.activation(dec.t[:], p3.t[:, 0:4], AF.Exp), R=[p3], W=[dec])
                    k.op("dve", lambda e: e.scalar_tensor_tensor(qd.t[:], qk.t[:, 0:512], 128.0 ** -0.5, eb.t[:], ALU.mult, ALU.mult), R=[qk, eb], W=[qd])
                    k.op("pool", lambda e: e.tensor_tensor(kd.t[:], qk.t[:, 512:1024], enb.t[:], ALU.mult), R=[qk, enb], W=[kd])
                    k.op("pool", lambda e: e.tensor_tensor(kdec.t[:], qk.t[:, 512:1024], ek.t[:], ALU.mult), R=[qk, ek], W=[kdec])
                    k.op("act", lambda e: e.activation(v16.t[:], vv.t[:], AF.Copy), R=[vv], W=[v16])
                    p4 = psr.next()
                    p4v = p4.t[:, 0:512].bitcast(BF16).rearrange("p (a b) -> p a b", a=8)
                    for h in range(4):
                        k.op("pe", lambda e: e.transpose(p4v[:, h, :], qd.t[:, h * 128:(h + 1) * 128], identb.t[:]), R=[qd, identb], W=[p4])
                        k.op("pe", lambda e: e.transpose(p4v[:, 4 + h, :], kd.t[:, h * 128:(h + 1) * 128], identb.t[:]), R=[kd, identb], W=[p4])
                    k.op("dve", lambda e: e.tensor_copy(qkT.t[:], p4v), R=[p4], W=[qkT])
                    p5 = psr.next()
                    p5v = p5.t[:, 0:512].rearrange("p (a b) -> p a b", a=4)
                    for h in range(4):
                        k.op("pe", lambda e: e.matmul(p5v[:, h, :], qkT.t[:, 4 + h, :], qkT.t[:, h, :], start=True, stop=True), R=[qkT], W=[p5])
                    k.op("dve", lambda e: e.tensor_tensor(att.t[:], p5v, tri.t[:, TA:TA + 1, :].to_broadcast([128, 4, 128]), ALU.mult), R=[p5, tri], W=[att])
                    po = psr.next()
                    for h in range(4):
                        k.op("pe", lambda e: e.matmul(po.t[:, h * 256:(h + 1) * 256], att.t[:, h, :], v16.t[:, h * 256:(h + 1) * 256], start=True, stop=False), R=[att, v16], W=[po])
                        k.op("pe", lambda e: e.matmul(po.t[:, h * 256:(h + 1) * 256], qkT.t[:, h, :], Sb.t[:, h * 256:(h + 1) * 256], start=False, stop=True), R=[qkT, Sb], W=[po])
                    pS = psr.next()
                    for h in range(4):
                        k.op("pe", lambda e: e.matmul(pS.t[:, h * 256:(h + 1) * 256], kdec.t[:, h * 128:(h + 1) * 128], v16.t[:, h * 256:(h + 1) * 256], start=True, stop=True), R=[kdec, v16], W=[pS])
                    for h in range(4):
                        k.op("dve", lambda e: e.scalar_tensor_tensor(Sf.t[:, h * 256:(h + 1) * 256], Sf.t[:, h * 256:(h + 1) * 256], dec.t[:, h:h + 1], pS.t[:, h * 256:(h + 1) * 256], ALU.mult, ALU.add), R=[Sf, dec, pS], W=[Sf])
                    k.op("act", lambda e: e.activation(Sb.t[:], Sf.t[:], AF.Copy), R=[Sf], W=[Sb])
                    return po

                gla_reset()
                nxt = gla_load(0, 0)
                for tau in range(NTILE):
                    cur = nxt
                    if tau + 1 < NTILE:
                        nxt = gla_load(tau + 1, 0)
                    if tau == M1:
                        gla_link()
                    po = gla_step(tau, 0, cur)
                    ob = ofr.next()
                    k.op("act", lambda e: e.activation(ob.t[:], po.t[:], AF.Copy), R=[po], W=[ob])
                    k.dma("sp", ofs[tau * 128:(tau + 1) * 128, 0:1024], ob.t[:], R=[ob])
                k.barrier()
                if phase_done():
                    return

                with ExitStack() as e3:
                    zbr = Ring([sb(e3, "e3_zb%d" % i, [128, 1024], F32) for i in range(2)])
                    osum = sb(e3, "e3_osum", [128, 4, 256], F32)
                    osq = sb(e3, "e3_osq", [128, 4, 256], F32)
                    oss = sb(e3, "e3_oss", [128, 4], F32)
                    ogr = Ring([sb(e3, "e3_og%d" % i, [128, D], BF16) for i in range(2)])
                    qar = Ring([sb(e3, "e3_qa%d" % i, [128, 1024], F32) for i in range(2)])
                    zar = Ring([sb(e3, "e3_za%d" % i, [128, 1024], F32) for i in range(2)])
                    tA = sb(e3, "e3_tA", [128, 16, 2, 32], F32)
                    tB = sb(e3, "e3_tB", [128, 16, 2, 32], F32)
                    qr = sb(e3, "e3_qr", [128, 1024], BF16)
                    QT = sb(e3, "e3_QT", [128, 8, 128], BF16)
                    PTr = Ring([sb(e3, "e3_PT%d" % i, [128, 2, 4, 128], BF16) for i in range(6)])
                    oall = sb(e3, "e3_oall", [128, 16, 65], F32)
                    den = sb(e3, "e3_den", [128, 16], F32)
                    onr = sb(e3, "e3_on", [128, 16, 64], F32)

                    def e3_load(tau):
                        ld = gla_load(tau, 1)
                        zb = zbr.next()
                        k.dma("sp", zb.t[:], zt[tau * 128:(tau + 1) * 128, 4352:5376], W=[zb])
                        of_t = ofr.next()
                        k.dma("sp", of_t.t[:], ofs[tau * 128:(tau + 1) * 128, 0:1024], W=[of_t])
                        qa = qar.next()
                        k.dma("sp", qa.t[:], zt[tau * 128:(tau + 1) * 128, 0:1024], W=[qa])
                        za = zar.next()
                        k.dma("sp", za.t[:], zt[tau * 128:(tau + 1) * 128, 1280:2304], W=[za])
                        cs = load_cs(tau)
                        return ld, zb, of_t, qa, za, cs

                    def keylist(tau):
                        if tau == M0 or tau == M1:
                            return [(tau, None), (tau + 1, AM_NEXT)]
                        ks = []
                        seg1 = tau > M1
                        if seg1:
                            ks.append((M0, AM_FULLL))
                            ks.append((M1, None))
                        else:
                            ks.append((M0, None))
                        first = (tau == 1) or (tau == LB)
                        last = (tau == LA) or (tau == NTILE - 1)
                        if not first:
                            ks.append((tau - 1, AM_PREV))
                        elif tau == LB:
                            ks.append((LA, AM_PREVL))
                        ks.append((tau, None))
                        if not last:
                            ks.append((tau + 1, AM_NEXT))
                        elif tau == LA:
                            ks.append((LB, AM_NEXTL))
                        return ks

                    gla_reset()
                    nxt = e3_load(NTILE - 1)
                    for tau in range(NTILE - 1, -1, -1):
                        ld, zb, of_t, qa, za, cs = nxt
                        if tau - 1 >= 0:
                            nxt = e3_load(tau - 1)
                        if tau == LA:
                            gla_link()
                        og = ogr.next()
                        po = gla_step(tau, 1, ld)
                        k.op("dve", lambda e: e.tensor_tensor(osum.t[:], po.t[:].rearrange("p (a b) -> p a b", a=4), of_t.t[:].rearrange("p (a b) -> p a b", a=4), ALU.add), R=[po, of_t], W=[osum])
                        k.op("pool", lambda e: e.tensor_tensor(osq.t[:], osum.t[:], osum.t[:], ALU.mult), R=[osum], W=[osq])
                        k.op("dve", lambda e: e.tensor_reduce(oss.t[:], osq.t[:], AX.X, ALU.add), R=[osq], W=[oss])
                        rstd_from_ss(oss, 1.0 / 256.0)
                        k.op("dve", lambda e: e.tensor_tensor(osum.t[:], osum.t[:], oss.t[:].unsqueeze(2).to_broadcast([128, 4, 256]), ALU.mult), R=[osum, oss], W=[osum])
                        k.op("pool", lambda e: e.tensor_tensor(osum.t[:], osum.t[:], hnb.t[:, 0:1, :].to_broadcast([128, 4, 256]), ALU.mult), R=[osum, hnb], W=[osum])
                        k.op("act", lambda e: e.activation(zb.t[:], zb.t[:], AF.Silu), R=[zb], W=[zb])
                        k.op("dve", lambda e: e.tensor_tensor(og.t[:, 1024:2048], osum.t[:].rearrange("p a b -> p (a b)"), zb.t[:], ALU.mult), R=[osum, zb], W=[og])
                        src4 = qa.t[:].rearrange("p (h a b) -> p h a b", h=16, a=2)
                        dst4 = qr.t[:].rearrange("p (h a b) -> p h a b", h=16, a=2)
                        rope(dst4, src4, cs, 16, tA, tB, [qa], [qr])
                        pq = psr.next()
                        pqv = pq.t[:, 0:512].bitcast(BF16).rearrange("p (a b) -> p a b", a=8)
                        for jj in range(8):
                            k.op("pe", lambda e: e.transpose(pqv[:, jj, :], qr.t[:, jj * 128:(jj + 1) * 128], identb.t[:]), R=[qr, identb], W=[pq])
                        k.op("dve", lambda e: e.tensor_copy(QT.t[:], pqv), R=[pq], W=[QT])
                        keys = keylist(tau)
                        for g in range(2):
                            pts = []
                            for (c, mk) in keys:
                                pst = psr.next()
                                for par in range(2):
                                    k.op("pe", lambda e: e.matmul(pst.t[:, par * 512:(par + 1) * 512], KTall[par * 64:(par + 1) * 64, c, g, :], QT.t[par * 64:(par + 1) * 64, 4 * g:4 * g + 4, :], start=True, stop=True), R=[KTb[c], QT], W=[pst])
                                pt = PTr.next()
                                k.op("act", lambda e: e.activation(pt.t[:].rearrange("p a b c -> p (a b c)"), pst.t[:], AF.Exp, scale=0.125), R=[pst], W=[pt])
                                if mk is not None:
                                    k.op("pool", lambda e: e.tensor_tensor(pt.t[:].rearrange("p a b c -> p (a b) c"), pt.t[:].rearrange("p a b c -> p (a b) c"), amask.t[:, mk:mk + 1, :].to_broadcast([128, 8, 128]), ALU.mult), R=[pt, amask], W=[pt])
                                pts.append((pt, c))
                            pso = psr.next()
                            psov = pso.t[:].rearrange("p (a b) -> p a b", a=8)
                            for jj in range(4):
                                for par in range(2):
                                    hh = 2 * jj + par
                                    for ci, (pt, c) in enumerate(pts):
                                        k.op("pe", lambda e: e.matmul(psov[:, hh, 0:65], pt.t[:, par, jj, :], VAall[:, c, g, :], start=(ci == 0), stop=(ci == len(pts) - 1)), R=[pt, VAb[c]], W=[pso])
                            k.op("act", lambda e: e.activation(oall.t[:, 8 * g:8 * g + 8, :], psov[:, :, 0:65], AF.Copy), R=[pso], W=[oall])
                        k.op("dve", lambda e: e.tensor_tensor(den.t[:], oall.t[:, :, 64], esink.t[:], ALU.add), R=[oall, esink], W=[den])
                        k.op("dve", lambda e: e.reciprocal(den.t[:], den.t[:]), R=[den], W=[den])
                        k.op("dve", lambda e: e.tensor_tensor(onr.t[:], oall.t[:, :, 0:64], den.t[:].unsqueeze(2).to_broadcast([128, 16, 64]), ALU.mult), R=[oall, den], W=[onr])
                        k.op("act", lambda e: e.activation(za.t[:], za.t[:], AF.Silu), R=[za], W=[za])
                        k.op("dve", lambda e: e.tensor_tensor(og.t[:, 0:1024], onr.t[:].rearrange("p a b -> p (a b)"), za.t[:], ALU.mult), R=[onr, za], W=[og])
                        k.dma("sp", ogs[tau * 128:(tau + 1) * 128, :], og.t[:], R=[og])
                k.barrier()
            if phase_done():
                return

        def phase_odd(j):
            with ExitStack() as es:
                NW = G * 128
                cw = sb(es, "o1_cw", [128, 48, 5], F32)
                with nc.allow_non_contiguous_dma("conv weights, tiny"):
                    for kk in range(5):
                        k.dma("sp", cw.t[:, :, kk], c_conv[j, kk, :].rearrange("(c p) -> p c", p=128), W=[cw])
                xr = Ring([sb(es, "o1_x%d" % i, [128, NW + 4], BF16) for i in range(4)])
                dgall = sb(es, "o1_dgall", [128, 48, 5, 128], BF16)
                fd = Buf(dgall.t)
                fp = Buf(dgall.t)
                for cc_ in range(48):
                    for kk in range(5):
                        on_ = "dve" if (cc_ + kk) % 2 == 0 else "pool"
                        lastq = (cc_ == 47 and kk >= 3)
                        k.op(on_, lambda e: e.tensor_scalar(dgall.t[:, cc_, kk, :], identb.t[:], cw.t[:, cc_, kk:kk + 1], None, ALU.mult), R=[identb, cw], W=([fd if on_ == "dve" else fp] if lastq else []))
                acr = Ring([sb(es, "o1_ac%d" % i, [128, NW], F32) for i in range(3)])
                xl = sb(es, "o1_xl", [128, 4], BF16)
                sqr = Ring([sb(es, "o1_sq%d" % i, [128, NW], BF16) for i in range(2)])
                rnr = Ring([sb(es, "o1_rn%d" % i, [128, NW], F32) for i in range(2)])
                xnr = Ring([sb(es, "o1_xn%d" % i, [128, G, 128], BF16) for i in range(4)])
                tmr = Ring([sb(es, "o1_tm%d" % i, [128, G, 128], BF16) for i in range(2)])
                psr = Ring([psb(es, "o1_ps%d" % i, [128, 1024]) for i in range(4)])
                def o1_load(sp, cc):
                    col0 = sp * NW
                    xb = xr.next()
                    lo = col0 - 2
                    hi = col0 + NW + 2
                    if sp == 0:
                        k.op("pool", lambda e: e.memset(xb.t[:, 0:2], 0.0), W=[xb])
                        lo = col0
                    if sp == NSUP - 1:
                        k.op("pool", lambda e: e.memset(xb.t[:, NW + 2:NW + 4], 0.0), W=[xb])
                        hi = col0 + NW
                    k.dma("sp", xb.t[:, lo - (col0 - 2):hi - (col0 - 2)], xcT[cc * 128:(cc + 1) * 128, lo:hi], W=[xb])
                    return xb

                seq = [(sp, cc) for sp in range(NSUP) for cc in range(48)]
                xq = [o1_load(*seq[0]), o1_load(*seq[1])]

                def stage_a(si, sp, cc):
                    col0 = sp * NW
                    xb = xq.pop(0)
                    if si + 2 < len(seq):
                        xq.append(o1_load(*seq[si + 2]))
                    ac = acr.next()
                    dg = dgall
                    nh = NW // 2
                    fixes = []
                    if sp == LA // G:
                        cA = (LA + 1) * 128 - 1 - col0
                        cB = LB * 128 - col0
                        k.op("dve", lambda e: e.tensor_scalar(xl.t[:, 0:2], xb.t[:, cA + 1:cA + 3], linkc.t[:, 0:1], None, ALU.mult), R=[xb, linkc], W=[xl])
                        k.op("dve", lambda e: e.tensor_scalar(xl.t[:, 2:4], xb.t[:, cB + 2:cB + 4], linkc.t[:, 0:1], None, ALU.mult), R=[xb, linkc], W=[xl])
                        fixes = [(cA, 2, 3), (cA, 3, 4), (cA - 1, 2, 4), (cB, 1, 1), (cB, 0, 0), (cB + 1, 1, 0)]
                    pcv = psr.next()
                    for half in range(2):
                        for kk in range(5):
                            if kk == 4:
                                for (col, xi, wk) in fixes:
                                    if col // nh == half:
                                        pc_ = half * 512 + col % nh
                                        k.op("pe", lambda e: e.matmul(pcv.t[:, pc_:pc_ + 1], dg.t[:, cc, wk, :], xl.t[:, xi:xi + 1], start=False, stop=False), R=[fd, fp, xl], W=[pcv])
                            k.op("pe", lambda e: e.matmul(pcv.t[:, half * 512:half * 512 + nh], dg.t[:, cc, kk, :], xb.t[:, half * nh + kk:half * nh + kk + nh], start=(kk == 0), stop=(kk == 4)), R=[fd, fp, xb], W=[pcv])
                    k.op("act", lambda e: e.activation(ac.t[:].rearrange("p (a b) -> p a b", a=2), pcv.t[:].rearrange("p (a b) -> p a b", a=2)[:, :, 0:nh], AF.Silu), R=[pcv], W=[ac])
                    return ac

                def stage_b(sp, cc, ac):
                    xn = xnr.next()
                    xnf = xn.t[:].rearrange("p a b -> p (a b)")
                    if cc < 32:
                        head = cc % 16
                        sq = sqr.next()
                        rn = rnr.next()
                        k.op("pool", lambda e: e.tensor_tensor(sq.t[:], ac.t[:], ac.t[:], ALU.mult), R=[ac], W=[sq])
                        ps = psr.next()
                        nn = NW // 2
                        for half in range(2):
                            k.op("pe", lambda e: e.matmul(ps.t[:, half * 512:half * 512 + nn], onesb.t[:], sq.t[:, half * nn:(half + 1) * nn], start=True, stop=True), R=[onesb, sq], W=[ps])
                        rnv = rn.t[:].rearrange("p (a b) -> p a b", a=2)
                        psv = ps.t[:].rearrange("p (a b) -> p a b", a=2)[:, :, 0:nn]
                        k.op("act", lambda e: e.activation(rnv, psv, AF.Ln, bias=EPS), R=[ps], W=[rn])
                        k.op("act", lambda e: e.activation(rn.t[:], rn.t[:], AF.Exp, scale=-0.5), R=[rn], W=[rn])
                        scl = (128.0 ** -0.5) if cc < 16 else 1.0
                        k.op("dve", lambda e: e.scalar_tensor_tensor(xnf, ac.t[:], scl, rn.t[:], ALU.mult, ALU.mult), R=[ac, rn], W=[xn])
                        dst = qTs if cc < 16 else kTs
                        k.dma("sp", dst[sp * G:(sp + 1) * G, :, head, :].rearrange("t p c -> p t c"), xn.t[:], R=[xn])
                    else:
                        head = cc - 32
                        k.op("dve", lambda e: e.tensor_copy(xnf, ac.t[:]), R=[ac], W=[xn])
                    return xn

                def stage_c(sp, cc, xn):
                    head = (cc % 16) if cc < 32 else (cc - 32)
                    if cc >= 16:
                        ps = psr.next()
                        psv = ps.t[:, 0:G * 64].bitcast(BF16).rearrange("p (a b) -> p a b", a=G)
                        for tl in range(G):
                            k.op("pe", lambda e: e.transpose(psv[:, tl, :], xn.t[:, tl, :], identb.t[:]), R=[xn, identb], W=[ps])
                        tm = tmr.next()
                        evac(tm.t[:], psv, R=[ps], W=[tm])
                        dst = ktok if cc < 32 else vtok
                        k.dma("sp", dst[sp * NW:(sp + 1) * NW, head * 128:(head + 1) * 128].rearrange("(t p) c -> p t c", p=128), tm.t[:], R=[tm])

                ac_prev = stage_a(0, *seq[0])
                xn_prev = None
                for si in range(len(seq)):
                    ac_cur = ac_prev
                    if si + 1 < len(seq):
                        ac_prev = stage_a(si + 1, *seq[si + 1])
                    xn_cur = stage_b(seq[si][0], seq[si][1], ac_cur)
                    if xn_prev is not None:
                        stage_c(seq[si - 1][0], seq[si - 1][1], xn_prev)
                    xn_prev = xn_cur
                stage_c(seq[-1][0], seq[-1][1], xn_prev)
                k.barrier()
            if phase_done():
                return

            with ExitStack() as es:
                cst = sb(es, "d_cst", [128, 4, 16], F32)
                k.dma("sp", cst.t[:, 0, :], alog_f[j:j + 1, :].partition_broadcast(128), W=[cst])
                k.dma("sp", cst.t[:, 1, :], dtb_f[j:j + 1, :].partition_broadcast(128), W=[cst])
                k.dma("sp", cst.t[:, 2, :], alog_b[j:j + 1, :].partition_broadcast(128), W=[cst])
                k.dma("sp", cst.t[:, 3, :], dtb_b[j:j + 1, :].partition_broadcast(128), W=[cst])
                for a in (0, 2):
                    k.op("act", lambda e: e.activation(cst.t[:, a, :], cst.t[:, a, :], AF.Exp), R=[cst], W=[cst])
                    k.op("dve", lambda e: e.tensor_scalar(cst.t[:, a, :], cst.t[:, a, :], -1.0, None, ALU.mult), R=[cst], W=[cst])
                hnb = sb(es, "d_hn", [128, 1, 128], F32)
                k.dma("sp", hnb.t[:, 0, :], c_hnorm[j:j + 1, :].partition_broadcast(128), W=[hnb])
                psr = Ring([psb(es, "d_ps%d" % i, [128, 1024]) for i in range(4)])
                QKr = Ring([sb(es, "d_QK%d" % i, [128, 16, 2, 128], BF16) for i in range(2)])
                ktr = Ring([sb(es, "d_kt%d" % i, [128, 16, 128], BF16) for i in range(2)])
                Rbr = Ring([sb(es, "d_Rb%d" % i, [128, 16, 256], BF16) for i in range(2)])
                zzr = Ring([sb(es, "d_zz%d" % i, [128, 64], F32) for i in range(2)])
                gt = sb(es, "d_gt", [128, 16], F32)
                gg = sb(es, "d_gg", [128, 16], F32)
                ghl = sb(es, "d_ghl", [128, 2, 16], BF16)
                GUl = sb(es, "d_GUl", [128, 16, 128], BF16)
                GUh = sb(es, "d_GUh", [128, 16, 128], BF16)
                bt = sb(es, "d_bt", [128, 16], F32)
                nbt = sb(es, "d_nbt", [128, 16], F32)
                ecum = sb(es, "d_ecum", [128, 48], F32)
                GU = sb(es, "d_GU", [128, 16, 128], F32)
                EX = sb(es, "d_EX", [128, 16, 128], F32)
                Ys = [[sb(es, "d_Y%d%d" % (a, b), [128, 16, 128], BF16) for b in range(2)] for a in range(2)]
                AQ = sb(es, "d_AQ", [128, 16, 128], BF16)
                Ao = sb(es, "d_Ao", [128, 16, 128], BF16)
                Qm = sb(es, "d_Qm", [128, 16, 128], BF16)
                Yx = sb(es, "d_Yx", [128, 16, 128], BF16)
                Ub = sb(es, "d_Ub", [128, 16, 256], BF16)
                Rf = sb(es, "d_Rf", [128, 16, 256], F32)
                Wb_ = sb(es, "d_Wb", [128, 16, 128], BF16)
                WT = sb(es, "d_WT", [128, 16, 128], BF16)
                KG = sb(es, "d_KG", [128, 16, 128], BF16)
                vnb = sb(es, "d_vnb", [128, 16, 128], BF16)
                osr = Ring([sb(es, "d_os%d" % i, [128, 16, 128], F32) for i in range(2)])
                Sf = sb(es, "d_S", [128, 16, 128], F32)
                Sb = sb(es, "d_Sb", [128, 16, 128], BF16)
                for b_ in [Ys[0][0], Ys[0][1], Ys[1][0], Ys[1][1], Yx, Qm, Ub, Rf, vnb, Sf, Sb, AQ, Ao] + osr.b:
                    b_.q = [Buf(b_.t) for _ in range(4)]

                def dn_reset():
                    k.op("dve", lambda e: e.memset(Sf.t[:], 0.0), W=Sf.q)
                    k.op("dve", lambda e: e.memset(Sb.t[:], 0.0), W=Sb.q)

                def dn_link():
                    k.op("dve", lambda e: e.tensor_scalar(Sf.t[:], Sf.t[:], linkc.t[:, 0:1], None, ALU.mult), R=Sf.q + [linkc], W=Sf.q)
                    k.op("act", lambda e: e.activation(Sb.t[:], Sf.t[:], AF.Copy), R=Sf.q, W=Sb.q)

                def dn_load(tau):
                    QK = QKr.next()
                    k.dma("sp", QK.t[:, :, 0, :], kTs[tau], W=[QK])
                    k.dma("sp", QK.t[:, :, 1, :], qTs[tau], W=[QK])
                    kt = ktr.next()
                    k.dma("sp", kt.t[:], ktok[tau * 128:(tau + 1) * 128, :].rearrange("p (h c) -> p h c", h=16), W=[kt])
                    Rb = Rbr.next()
                    k.dma("sp", Rb.t[:, :, 0:128], vtok[tau * 128:(tau + 1) * 128, :].rearrange("p (h c) -> p h c", h=16), W=[Rb])
                    zz = zzr.next()
                    k.dma("sp", zz.t[:], zt[tau * 128:(tau + 1) * 128, 2048:2112], W=[zz])
                    return QK, kt, Rb, zz

                def dn_step(tau, d, loaded):
                    QK, kt, Rb, zz = loaded
                    TA = U_LE if d == 0 else U_GE
                    TB = U_GT if d == 0 else U_LT
                    TS = U_LT if d == 0 else U_GT
                    a_ap = zz.t[:, 32 * d:32 * d + 16]
                    b_ap = zz.t[:, 32 * d + 16:32 * d + 32]
                    isM = tau in (M0, M1)
                    mi = 0 if tau == M0 else 1
                    k.op("dve", lambda e: e.tensor_tensor(gt.t[:], a_ap, cst.t[:, 2 * d + 1, :], ALU.add), R=[zz, cst], W=[gt])
                    k.op("act", lambda e: e.activation(gt.t[:], gt.t[:], AF.Exp), R=[gt], W=[gt])
                    k.op("act", lambda e: e.activation(gt.t[:], gt.t[:], AF.Ln, bias=1.0), R=[gt], W=[gt])
                    k.op("dve", lambda e: e.tensor_tensor(gg.t[:], gt.t[:], cst.t[:, 2 * d, :], ALU.mult), R=[gt, cst], W=[gg])
                    k.op("act", lambda e: e.activation(bt.t[:], b_ap, AF.Exp, scale=-1.0), R=[zz], W=[bt])
                    k.op("dve", lambda e: e.tensor_scalar(bt.t[:], bt.t[:], 1.0, None, ALU.add), R=[bt], W=[bt])
                    k.op("dve", lambda e: e.reciprocal(bt.t[:], bt.t[:]), R=[bt], W=[bt])
                    if isM:
                        k.op("dve", lambda e: e.tensor_scalar(gg.t[:], gg.t[:], vmask.t[:, mi:mi + 1], None, ALU.mult), R=[gg, vmask], W=[gg])
                        k.op("dve", lambda e: e.tensor_scalar(bt.t[:], bt.t[:], vmask.t[:, mi:mi + 1], None, ALU.mult), R=[bt, vmask], W=[bt])
                    k.op("dve", lambda e: e.tensor_scalar(nbt.t[:], bt.t[:], -1.0, None, ALU.mult), R=[bt], W=[nbt])
                    pc = psr.next()
                    k.op("dve", lambda e: e.tensor_copy(ghl.t[:, 0, :], gg.t[:]), R=[gg], W=[ghl])
                    k.op("dve", lambda e: e.tensor_tensor(ghl.t[:, 1, :], gg.t[:], ghl.t[:, 0, :], ALU.subtract), R=[gg, ghl], W=[ghl])
                    for hl in range(2):
                        k.op("pe", lambda e: e.matmul(pc.t[:, 0:16], trib.t[:, TA, :], ghl.t[:, hl, :], start=(hl == 0), stop=(hl == 1)), R=[trib, ghl], W=[pc])
                    for hl in range(2):
                        k.op("pe", lambda e: e.matmul(pc.t[:, 16:32], trib.t[:, TB, :], ghl.t[:, hl, :], start=(hl == 0), stop=(hl == 1)), R=[trib, ghl], W=[pc])
                    for hl in range(2):
                        k.op("pe", lambda e: e.matmul(pc.t[:, 32:48], onesb.t[:], ghl.t[:, hl, :], start=(hl == 0), stop=(hl == 1)), R=[onesb, ghl], W=[pc])
                    k.op("act", lambda e: e.activation(ecum.t[:], pc.t[:, 0:48], AF.Exp), R=[pc], W=[ecum])
                    egam = ecum.t[:, 0:16]
                    erest = ecum.t[:, 16:32]
                    etot = ecum.t[:, 32:48]
                    k.op("dve", lambda e: e.tensor_tensor(GUh.t[:], trib.t[:, TA:TA + 1, :].to_broadcast([128, 16, 128]), ghl.t[:, 0, :].unsqueeze(2).to_broadcast([128, 16, 128]), ALU.mult), R=[trib, ghl], W=[GUh])
                    k.op("pool", lambda e: e.tensor_tensor(GUl.t[:], trib.t[:, TA:TA + 1, :].to_broadcast([128, 16, 128]), ghl.t[:, 1, :].unsqueeze(2).to_broadcast([128, 16, 128]), ALU.mult), R=[trib, ghl], W=[GUl])
                    pe_ = [psr.next(), psr.next()]
                    for hq in range(4):
                        pp = pe_[hq // 2]
                        k.op("pe", lambda e: e.matmul(pp.t[:, (hq % 2) * 512:(hq % 2) * 512 + 512], trib.t[:, TB, :], GUh.t[:, 4 * hq:4 * hq + 4, :], start=True, stop=False), R=[trib, GUh], W=[pp])
                        k.op("pe", lambda e: e.matmul(pp.t[:, (hq % 2) * 512:(hq % 2) * 512 + 512], trib.t[:, TB, :], GUl.t[:, 4 * hq:4 * hq + 4, :], start=False, stop=True), R=[trib, GUl], W=[pp])
                    for i2 in range(2):
                        k.op("act", lambda e: e.activation(EX.t[:, 8 * i2:8 * i2 + 8, :].rearrange("p a b -> p (a b)"), pe_[i2].t[:], AF.Exp), R=[pe_[i2]], W=[EX])
                    k.op("dve", lambda e: e.tensor_tensor(GU.t[:], EX.t[:], tri.t[:, TS:TS + 1, :].to_broadcast([128, 16, 128]), ALU.mult), R=[EX, tri], W=[GU])
                    k.op("pool", lambda e: e.tensor_tensor(GU.t[:], GU.t[:], nbt.t[:].unsqueeze(2).to_broadcast([128, 16, 128]), ALU.mult), R=[GU, nbt], W=[GU])
                    k.op("dve", lambda e: e.tensor_tensor(EX.t[:], EX.t[:], tri.t[:, TA:TA + 1, :].to_broadcast([128, 16, 128]), ALU.mult), R=[EX, tri], W=[EX])
                    YT0, Y0 = Ys[0]
                    for hq in range(4):
                        pk = psr.next()
                        pkv = pk.t[:].rearrange("p (a b c) -> p a b c", a=4, b=2)
                        for hl in range(4):
                            h = 4 * hq + hl
                            k.op("pe", lambda e: e.matmul(pk.t[:, hl * 256:(hl + 1) * 256], QK.t[:, h, 0, :], QK.t[:, h, :, :].rearrange("p a b -> p (a b)"), start=True, stop=True), R=[QK], W=[pk])
                        k.op("dve", lambda e: e.tensor_tensor(YT0.t[:, 4 * hq:4 * hq + 4, :], pkv[:, :, 0, :], GU.t[:, 4 * hq:4 * hq + 4, :], ALU.mult), R=[pk, GU], W=[YT0.q[hq]])
                        k.op("dve", lambda e: e.tensor_tensor(AQ.t[:, 4 * hq:4 * hq + 4, :], pkv[:, :, 1, :], EX.t[:, 4 * hq:4 * hq + 4, :], ALU.mult), R=[pk, EX], W=[AQ.q[hq]])
                    k.op("dve", lambda e: e.tensor_tensor(Ao.t[:], YT0.t[:], trib.t[:, BDM:BDM + 1, :].to_broadcast([128, 16, 128]), ALU.mult), R=YT0.q + [trib], W=Ao.q)
                    k.op("dve", lambda e: e.tensor_tensor(YT0.t[:], YT0.t[:], Ao.t[:], ALU.subtract), R=YT0.q + Ao.q, W=YT0.q)
                    AdT, AoT = Ao, YT0
                    YTc, Yc = Ys[1]
                    for i2 in range(2):
                        pt = psr.next()
                        ptv = pt.t[:, 0:512].bitcast(BF16).rearrange("p (a b) -> p a b", a=8)
                        for hl in range(8):
                            h = 8 * i2 + hl
                            k.op("pe", lambda e: e.transpose(ptv[:, hl, :], AdT.t[:, h, :], identb.t[:]), R=[AdT.q[2 * i2], AdT.q[2 * i2 + 1], identb], W=[pt])
                        evac(Yc.t[:, 8 * i2:8 * i2 + 8, :], ptv, R=[pt], W=[Yc.q[2 * i2], Yc.q[2 * i2 + 1]])
                    k.op("dve", lambda e: e.tensor_tensor(Qm.t[:], AdT.t[:], identb.t[:].unsqueeze(1).to_broadcast([128, 16, 128]), ALU.add), R=AdT.q + [identb], W=Qm.q)
                    k.op("act", lambda e: e.activation(Rf.t[:, :, 0:128], Rb.t[:, :, 0:128], AF.Copy), R=[Rb], W=Rf.q)
                    k.op("dve", lambda e: e.tensor_tensor(Rf.t[:, :, 128:256], kt.t[:], egam.unsqueeze(2).to_broadcast([128, 16, 128]), ALU.mult), R=[kt, ecum], W=Rf.q)
                    k.op("dve", lambda e: e.tensor_tensor(Rb.t[:, :, 128:256], kt.t[:], egam.unsqueeze(2).to_broadcast([128, 16, 128]), ALU.mult), R=[kt, ecum], W=[Rb])
                    cur = 1
                    ysets = [(Yx, Ys[0][1]), Ys[1]]
                    for lv in range(1, 5):
                        YT, Y = ysets[cur]
                        if lv == 1:
                            YT = AdT
                        YTn, Yn = ysets[1 - cur]
                        for hq in range(4):
                            py = psr.next()
                            pyv = py.t[:].rearrange("p (a b c) -> p a b c", a=4, b=2)
                            for hl in range(4):
                                h = 4 * hq + hl
                                k.op("pe", lambda e: e.matmul(pyv[:, hl, 0, :], Y.t[:, h, :], YT.t[:, h, :], start=True, stop=True), R=[Y.q[hq], YT.q[hq]], W=[py])
                                k.op("pe", lambda e: e.matmul(pyv[:, hl, 1, :], YT.t[:, h, :], Y.t[:, h, :], start=True, stop=True), R=[Y.q[hq], YT.q[hq]], W=[py])
                            k.op("act", lambda e: e.activation(YTn.t[:, 4 * hq:4 * hq + 4, :], pyv[:, :, 0, :], AF.Copy), R=[py], W=[YTn.q[hq]])
                            k.op("dve", lambda e: e.tensor_copy(Yn.t[:, 4 * hq:4 * hq + 4, :], pyv[:, :, 1, :]), R=[py], W=[Yn.q[hq]])
                        for hq in range(4):
                            pq_ = psr.next()
                            pqv = pq_.t[:, 0:512].rearrange("p (a b) -> p a b", a=4)
                            for hl in range(4):
                                h = 4 * hq + hl
                                k.op("pe", lambda e: e.matmul(pqv[:, hl, :], Yn.t[:, h, :], Qm.t[:, h, :], start=True, stop=True), R=[Yn.q[hq], Qm.q[hq]], W=[pq_])
                            k.op("dve", lambda e: e.tensor_tensor(Qm.t[:, 4 * hq:4 * hq + 4, :], Qm.t[:, 4 * hq:4 * hq + 4, :], pqv, ALU.add), R=[Qm.q[hq], pq_], W=[Qm.q[hq]])
                        cur = 1 - cur
                    for it in range(4):
                        for hq in range(4):
                            if it == 0:
                                zsrc = Rb
                            else:
                                pz = psr.next()
                                pzv = pz.t[:].rearrange("p (a b) -> p a b", a=4)
                                for hl in range(4):
                                    h = 4 * hq + hl
                                    k.op("pe", lambda e: e.matmul(pzv[:, hl, :], AoT.t[:, h, :], Ub.t[:, h, :], start=True, stop=True), R=[AoT.q[hq], Ub.q[hq]], W=[pz])
                                k.op("dve", lambda e: e.tensor_tensor(Ub.t[:, 4 * hq:4 * hq + 4, :], Rf.t[:, 4 * hq:4 * hq + 4, :], pzv, ALU.add), R=[Rf.q[hq], pz], W=[Ub.q[hq]])
                                zsrc = Ub
                            pu = psr.next()
                            puv = pu.t[:].rearrange("p (a b) -> p a b", a=4)
                            for hl in range(4):
                                h = 4 * hq + hl
                                k.op("pe", lambda e: e.matmul(puv[:, hl, :], Qm.t[:, h, :], zsrc.t[:, h, :], start=True, stop=True), R=[Qm.q[hq], (zsrc.q[hq] if zsrc.q else zsrc)], W=[pu])
                            if it < 3:
                                k.op("act", lambda e: e.activation(Ub.t[:, 4 * hq:4 * hq + 4, :], puv, AF.Copy), R=[pu], W=[Ub.q[hq]])
                            else:
                                k.op("act", lambda e: e.activation(Rf.t[:, 4 * hq:4 * hq + 4, :], puv, AF.Copy), R=[pu], W=[Rf.q[hq]])
                    bbc = bt.t[:].unsqueeze(2).to_broadcast([128, 16, 128])
                    k.op("dve", lambda e: e.tensor_tensor(Rf.t[:, :, 0:128], Rf.t[:, :, 0:128], bbc, ALU.mult), R=Rf.q + [bt], W=Rf.q)
                    k.op("dve", lambda e: e.tensor_tensor(Wb_.t[:], Rf.t[:, :, 128:256], bbc, ALU.mult), R=Rf.q + [bt], W=[Wb_])
                    for i2 in range(2):
                        pt = psr.next()
                        ptv = pt.t[:, 0:512].bitcast(BF16).rearrange("p (a b) -> p a b", a=8)
                        for hl in range(8):
                            h = 8 * i2 + hl
                            k.op("pe", lambda e: e.transpose(ptv[:, hl, :], Wb_.t[:, h, :], identb.t[:]), R=[Wb_, identb], W=[pt])
                        evac(WT.t[:, 8 * i2:8 * i2 + 8, :], ptv, R=[pt], W=[WT])
                    k.op("pool", lambda e: e.tensor_tensor(KG.t[:], kt.t[:], erest.unsqueeze(2).to_broadcast([128, 16, 128]), ALU.mult), R=[kt, ecum], W=[KG])
                    osb = osr.next()
                    p12 = []
                    for hq in range(4):
                        pp = psr.next()
                        ppv = pp.t[:].rearrange("p (x a b) -> p x a b", x=2, a=4)
                        for hl in range(4):
                            h = 4 * hq + hl
                            k.op("pe", lambda e: e.matmul(ppv[:, 0, hl, :], WT.t[:, h, :], Sb.t[:, h, :], start=True, stop=True), R=[WT, Sb.q[hq]], W=[pp])
                            k.op("pe", lambda e: e.matmul(ppv[:, 1, hl, :], QK.t[:, h, 1, :], Sb.t[:, h, :], start=True, stop=True), R=[QK, Sb.q[hq]], W=[pp])
                        k.op("dve", lambda e: e.tensor_tensor(vnb.t[:, 4 * hq:4 * hq + 4, :], Rf.t[:, 4 * hq:4 * hq + 4, 0:128], ppv[:, 0, :, :], ALU.subtract), R=[Rf.q[hq], pp], W=[vnb.q[hq]])
                        k.op("dve", lambda e: e.tensor_tensor(osb.t[:, 4 * hq:4 * hq + 4, :], ppv[:, 1, :, :], egam[:, 4 * hq:4 * hq + 4].unsqueeze(2).to_broadcast([128, 4, 128]), ALU.mult), R=[pp, ecum], W=[osb.q[hq]])
                        p12.append(pp)
                    for hq in range(4):
                        pp = psr.next()
                        ppv = pp.t[:].rearrange("p (x a b) -> p x a b", x=2, a=4)
                        for hl in range(4):
                            h = 4 * hq + hl
                            k.op("pe", lambda e: e.matmul(ppv[:, 0, hl, :], AQ.t[:, h, :], vnb.t[:, h, :], start=True, stop=True), R=[AQ.q[hq], vnb.q[hq]], W=[pp])
                            k.op("pe", lambda e: e.matmul(ppv[:, 1, hl, :], KG.t[:, h, :], vnb.t[:, h, :], start=True, stop=True), R=[KG, vnb.q[hq]], W=[pp])
                        k.op("dve", lambda e: e.tensor_tensor(osb.t[:, 4 * hq:4 * hq + 4, :], osb.t[:, 4 * hq:4 * hq + 4, :], ppv[:, 0, :, :], ALU.add), R=[osb.q[hq], pp], W=[osb.q[hq]])
                        k.op("dve", lambda e: e.tensor_tensor(Sf.t[:, 4 * hq:4 * hq + 4, :], Sf.t[:, 4 * hq:4 * hq + 4, :], etot[:, 4 * hq:4 * hq + 4].unsqueeze(2).to_broadcast([128, 4, 128]), ALU.mult), R=[Sf.q[hq], ecum], W=[Sf.q[hq]])
                        k.op("dve", lambda e: e.tensor_tensor(Sf.t[:, 4 * hq:4 * hq + 4, :], Sf.t[:, 4 * hq:4 * hq + 4, :], ppv[:, 1, :, :], ALU.add), R=[Sf.q[hq], pp], W=[Sf.q[hq]])
                        k.op("act", lambda e: e.activation(Sb.t[:, 4 * hq:4 * hq + 4, :], Sf.t[:, 4 * hq:4 * hq + 4, :], AF.Copy), R=[Sf.q[hq]], W=[Sb.q[hq]])
                    return osb

                dn_reset()
                nxt = dn_load(0)
                for tau in range(NTILE):
                    cur = nxt
                    if tau + 1 < NTILE:
                        nxt = dn_load(tau + 1)
                    if tau == M1:
                        dn_link()
                    osb = dn_step(tau, 0, cur)
                    k.dma("sp", ofs[tau * 128:(tau + 1) * 128, :], osb.t[:].rearrange("p a b -> p (a b)"), R=osb.q)
                k.barrier()
                if phase_done():
                    return

                ofr = Ring([sb(es, "d_of%d" % i, [128, 16, 128], F32) for i in range(1)])
                zcr = Ring([sb(es, "d_zc%d" % i, [128, D], F32) for i in range(1)])
                osq = GU
                oss = sb(es, "d_oss", [128, 16], F32)
                ogr = Ring([sb(es, "d_og%d" % i, [128, D], BF16) for i in range(2)])

                dn_reset()
                nxt = dn_load(NTILE - 1)
                for tau in range(NTILE - 1, -1, -1):
                    ld = nxt
                    if tau - 1 >= 0:
                        nxt = dn_load(tau - 1)
                    of_t = ofr.next()
                    k.dma("sp", of_t.t[:].rearrange("p a b -> p (a b)"), ofs[tau * 128:(tau + 1) * 128, :], W=[of_t])
                    zc = zcr.next()
                    k.dma("sp", zc.t[:], zt[tau * 128:(tau + 1) * 128, 0:2048], W=[zc])
                    if tau == LA:
                        dn_link()
                    osb = dn_step(tau, 1, ld)
                    k.op("dve", lambda e: e.tensor_tensor(osb.t[:], osb.t[:], of_t.t[:], ALU.add), R=osb.q + [of_t], W=osb.q)
                    k.op("pool", lambda e: e.tensor_tensor(osq.t[:], osb.t[:], osb.t[:], ALU.mult), R=osb.q, W=[osq])
                    k.op("dve", lambda e: e.tensor_reduce(oss.t[:], osq.t[:], AX.X, ALU.add), R=[osq], W=[oss])
                    rstd_from_ss(oss, 1.0 / 128.0)
                    k.op("dve", lambda e: e.tensor_tensor(osb.t[:], osb.t[:], oss.t[:].unsqueeze(2).to_broadcast([128, 16, 128]), ALU.mult), R=osb.q + [oss], W=osb.q)
                    k.op("pool", lambda e: e.tensor_tensor(osb.t[:], osb.t[:], hnb.t[:, 0:1, :].to_broadcast([128, 16, 128]), ALU.mult), R=osb.q + [hnb], W=osb.q)
                    k.op("act", lambda e: e.activation(zc.t[:], zc.t[:], AF.Silu), R=[zc], W=[zc])
                    og = ogr.next()
                    k.op("dve", lambda e: e.tensor_tensor(og.t[:], osb.t[:].rearrange("p a b -> p (a b)"), zc.t[:], ALU.mult), R=osb.q + [zc], W=[og])
                    k.dma("sp", ogs[tau * 128:(tau + 1) * 128, :], og.t[:], R=[og])
                k.barrier()
            if phase_done():
                return

        try:
          for layer in range(4):
            j = layer // 2
            hsrc = hin if layer == 0 else hs
            if layer % 2 == 0:
                for ph in (lambda: phase_in(hsrc, norm_even[j:j + 1, :], wie[j], EVEN_IN, 0), lambda: phase_even(j), lambda: phase_out(hsrc, woe[j], False)):
                    if not stopped[0]:
                        ph()
            else:
                for ph in (lambda: phase_in(hsrc, norm_odd[j:j + 1, :], wio[j], ODD_IN, 6144), lambda: phase_odd(j), lambda: phase_out(hsrc, woo[j], layer == 3)):
                    if not stopped[0]:
                        ph()
        except _Stop:
            pass
        k.barrier()
        if debug:
            for nm in debug:
                src = scr_all[nm]
                dst = nc.dram_tensor("dbg_" + nm, list(src.shape), src.dtype, kind="ExternalOutput").ap()
                n0 = src.shape[0]
                stp = max(1, n0 // 8)
                for r in range(0, n0, stp):
                    k.dma("sp", dst[r:r + stp], src[r:r + stp])
            k.barrier()
    build.ninstr = k.nins
    build.nop = k.nop_
    return nc


def _core_inputs(xs, meta, S_seg, is_prompt):
    NT_SEG = S_seg // 128
    NTILE = 2 * (NT_SEG + 1)
    T = NTILE * 128
    hin = np.zeros((T, D), np.float32)
    pos = np.zeros((T,), np.float32)
    m0 = 0
    m1 = (NT_SEG + 1) * 128
    r0 = 128
    r1 = (NT_SEG + 2) * 128
    hin[m0 + 112:m0 + 128] = meta
    pos[m0 + 112:m0 + 128] = np.arange(16)
    vm = np.zeros((128, 2), np.float32)
    vm[112:, 0] = 1.0
    if is_prompt:
        x = xs[0]
        hin[r0:r0 + S_seg] = x[:S_seg]
        hin[r1:r1 + S_seg] = x[S_seg:]
        pos[r0:r0 + S_seg] = 16 + np.arange(S_seg)
        pos[r1:r1 + S_seg] = 16 + S_seg + np.arange(S_seg)
        link = 1.0
    else:
        hin[r0:r0 + S_seg] = xs[0]
        hin[r1:r1 + S_seg] = xs[1]
        hin[m1 + 112:m1 + 128] = meta
        pos[m1 + 112:m1 + 128] = np.arange(16)
        pos[r0:r0 + S_seg] = 16 + np.arange(S_seg)
        pos[r1:r1 + S_seg] = 16 + np.arange(S_seg)
        vm[112:, 1] = 1.0
        link = 0.0
    inv = (1.0 / (np.float32(10000.0) ** (np.arange(0, 64, 2, dtype=np.float32) / np.float32(64)))).astype(np.float32)
    ang = pos[:, None].astype(np.float32) * inv[None]
    return {
        "hin": hin,
        "cosr": np.cos(ang).astype(np.float32),
        "sinr": np.sin(ang).astype(np.float32),
        "linkc": np.full((128, 1), link, np.float32),
        "vmask": vm,
    }


def _consts():
    s = np.arange(128)[:, None]
    i = np.arange(128)[None, :]
    tri = np.stack([(s <= i), (s < i), (s >= i), (s > i), (s // 32 == i // 32)]).astype(np.float32)
    return {"ident": np.eye(128, dtype=np.float32), "tri": tri}


_NC_CACHE = {}


def kernel(**inputs):
    xp = np.asarray(inputs["x_prompt"], np.float32)
    xsm = np.asarray(inputs["x_sample"], np.float32)
    nb_p, seq, _ = xp.shape
    nb_s, dseq, _ = xsm.shape
    assert seq == 2 * dseq and nb_s % 2 == 0
    S_seg = dseq
    NT_SEG = S_seg // 128
    meta = np.asarray(inputs["meta_tokens"], np.float32)
    shared = _consts()
    for name in ["norm_even", "w_in_even", "a_sink", "b_gate_up_fwd", "b_gate_bias_fwd", "b_gate_up_bwd",
                 "b_gate_bias_bwd", "b_head_norm", "w_out_even", "norm_odd", "w_in_odd", "c_conv", "c_a_log_fwd",
                 "c_dt_bias_fwd", "c_a_log_bwd", "c_dt_bias_bwd", "c_head_norm", "w_out_odd"]:
        shared[name] = np.ascontiguousarray(np.asarray(inputs[name], np.float32))
    shared["norm_final"] = np.ascontiguousarray(np.asarray(inputs["norm_final"], np.float32).reshape(1, D))
    in_maps = []
    for p in range(nb_p):
        m = dict(shared)
        m.update(_core_inputs([xp[p]], meta, S_seg, True))
        in_maps.append(m)
    for s in range(nb_s // 2):
        m = dict(shared)
        m.update(_core_inputs([xsm[2 * s], xsm[2 * s + 1]], meta, S_seg, False))
        in_maps.append(m)
    ncores = len(in_maps)
    if NT_SEG not in _NC_CACHE:
        _NC_CACHE[NT_SEG] = build(NT_SEG)
    nc = _NC_CACHE[NT_SEG]
    res = run_bass_kernel_spmd(nc, in_maps, core_ids=list(range(ncores)))
    outs = [np.asarray(r["y"], np.float32) for r in res.results]
    y_prompt = np.stack([outs[p].reshape(seq, D) for p in range(nb_p)], axis=0)
    y_sample = np.stack([outs[nb_p + s // 2].reshape(2, dseq, D)[s % 2] for s in range(nb_s)], axis=0)
    return (y_prompt, y_sample)
```
